# Optimizing a Trainium2 kernel written in Bass

```python
import math
import numpy as np
import jax
import jax.numpy as jnp
from jax import lax

D_MODEL = 1024
BATCH = 16
SEQ = 256
DEPTH = 2
DEC_BATCH = 4
DEC_SEQ = 1024
PAST_LEN = 256

GRID_W = 64
D_MIX = D_MODEL // 2
D_FF = ((8 * D_MODEL // 3) + 127) // 128 * 128
N_MOD = 9
SSD_P = 64
SSD_H = D_MIX // SSD_P
SSD_N = 64
SSD_G = 2
SSD_CONV = 3
SSD_CHUNK = 64
SSD_XBC = D_MIX + 2 * SSD_G * SSD_N
SSD_COLS = D_MIX + SSD_XBC + 2 * SSD_H
GLA_H = 4
GLA_DK = D_MIX // 2 // GLA_H
GLA_DV = D_MIX // GLA_H
GLA_LR = 16
GLA_GATE_NORM = 16.0
GLA_CHUNK = 64
GLA_COLS = 2 * GLA_H * GLA_DK + 2 * D_MIX + 2 * GLA_LR
RW_N = 64
RW_H = D_MIX // RW_N
RW_LW = 64
RW_LA = 64
RW_LG = 128
RW_COLS = 3 * D_MIX + 2 * RW_LW + RW_LA + RW_LG
GATE_COLS = 3 * D_MODEL
N_IN = SSD_COLS + GLA_COLS + RW_COLS + GATE_COLS
RMS_EPS = 1e-6
RW_GN_EPS = 64e-5

kernel_name = 'hybrid_ssd_gla_rwkv7_diffusion_step'


def _split(x, sizes):
    return jnp.split(x, np.cumsum(sizes)[:-1].tolist(), axis=-1)


def _rev(t):
    return jnp.flip(t, axis=1)


def rmsnorm(x, g):
    xf = x.astype(jnp.float32)
    y = xf * lax.rsqrt(jnp.mean(xf * xf, axis=-1, keepdims=True) + RMS_EPS)
    return (y * g.astype(jnp.float32)).astype(x.dtype)


def modulate(h, shift, scale):
    return h * (1 + scale) + shift


def swiglu(h, wg, wu, wd):
    return (jax.nn.silu(h @ wg) * (h @ wu)) @ wd


def grid_pos_embed(rows, cols, dim):
    quarter = dim // 4
    omega = 1.0 / (10000.0 ** (jnp.arange(quarter, dtype=jnp.float32) / quarter))
    er = jnp.arange(rows, dtype=jnp.float32)[:, None] * omega
    ec = jnp.arange(cols, dtype=jnp.float32)[:, None] * omega
    er = jnp.concatenate([jnp.sin(er), jnp.cos(er)], axis=-1)
    ec = jnp.concatenate([jnp.sin(ec), jnp.cos(ec)], axis=-1)
    emb = jnp.concatenate([jnp.broadcast_to(er[:, None], (rows, cols, dim // 2)),
                           jnp.broadcast_to(ec[None], (rows, cols, dim // 2))], axis=-1)
    return emb.reshape(rows * cols, dim)


def segsum(x):
    T = x.shape[-1]
    xr = jnp.broadcast_to(x[..., None], x.shape + (T,))
    xr = jnp.where(jnp.tril(jnp.ones((T, T), bool), -1), xr, 0.0)
    ss = jnp.cumsum(xr, axis=-2)
    return jnp.where(jnp.tril(jnp.ones((T, T), bool)), ss, -jnp.inf)


def centred_conv(x, w, b):
    K = w.shape[0]
    half = K // 2
    L = x.shape[1]
    xp = jnp.pad(x, ((0, 0), (half, half), (0, 0)))
    out = b + xp[:, 0:L] * w[0]
    for i in range(1, K):
        out = out + xp[:, i:i + L] * w[i]
    return out


def centred_shift_mix(p, mu):
    zero = jnp.zeros_like(p[:, :1])
    prev = jnp.concatenate([zero, p[:, :-1]], axis=1)
    nxt = jnp.concatenate([p[:, 1:], zero], axis=1)
    return p + (0.5 * (prev + nxt) - p) * mu


def ssd_chunked(X, Adt, Bm, Cm, S0):
    b, l, H, P = X.shape
    dt = X.dtype
    G = Bm.shape[2]
    J = H // G
    N = Bm.shape[3]
    nc = l // SSD_CHUNK
    Xc = X.reshape(b, nc, SSD_CHUNK, G, J, P)
    Bc = Bm.reshape(b, nc, SSD_CHUNK, G, N)
    Cc = Cm.reshape(b, nc, SSD_CHUNK, G, N)
    A = Adt.astype(jnp.float32).reshape(b, nc, SSD_CHUNK, G, J).transpose(0, 3, 4, 1, 2)
    A_cs = jnp.cumsum(A, axis=-1)
    Lmat = jnp.exp(segsum(A))
    Y_diag = jnp.einsum('bclgn,bcsgn,bgjcls,bcsgjp->bclgjp', Cc, Bc, Lmat, Xc)
    decay_states = jnp.exp(A_cs[..., -1:] - A_cs)
    states = jnp.einsum('bcsgn,bgjcs,bcsgjp->bcgjpn', Bc, decay_states, Xc)
    S0r = S0.reshape(b, G, J, P, N).astype(states.dtype)
    states = jnp.concatenate([S0r[:, None], states], axis=1)
    chunk_end = jnp.pad(A_cs[..., -1], ((0, 0), (0, 0), (0, 0), (1, 0)))
    decay_chunk = jnp.exp(segsum(chunk_end))
    new_states = jnp.einsum('bgjzc,bcgjpn->bzgjpn', decay_chunk, states)
    S_in, S_fin = new_states[:, :-1], new_states[:, -1]
    Y_off = jnp.einsum('bclgn,bcgjpn,bgjcl->bclgjp', Cc, S_in, jnp.exp(A_cs))
    Y = (Y_diag + Y_off).reshape(b, l, H, P)
    return Y.astype(dt), S_fin.reshape(b, H, P, N).astype(dt)


def gla_chunked(q, k, v, lg, S0):
    b, l, h, dk = q.shape
    dv = v.shape[-1]
    dt = q.dtype
    C = GLA_CHUNK
    nc = l // C
    f32 = jnp.float32
    qc = q.astype(f32).reshape(b, nc, C, h, dk)
    kc = k.astype(f32).reshape(b, nc, C, h, dk)
    vc = v.astype(f32).reshape(b, nc, C, h, dv)
    bc = jnp.cumsum(lg.astype(f32).reshape(b, nc, C, h, dk), axis=2)
    causal = jnp.tril(jnp.ones((C, C), bool))[None, None, :, :, None, None]
    diff = bc[:, :, :, None] - bc[:, :, None, :]
    decay = jnp.exp(jnp.where(causal, diff, -jnp.inf))
    A = jnp.einsum('bnthk,bnshk,bntshk->bnhts', qc, kc, decay)
    o_intra = jnp.einsum('bnhts,bnshv->bnthv', A, vc)
    last = bc[:, :, -1]
    q_dec = qc * jnp.exp(bc)
    k_dec = kc * jnp.exp(last[:, :, None] - bc)
    kv = jnp.einsum('bnshk,bnshv->bnhkv', k_dec, vc)

    def step(S, inp):
        dec, kv_n = inp
        return dec[..., None] * S + kv_n, S

    S_fin, S_in = lax.scan(step, S0.astype(f32), (jnp.moveaxis(jnp.exp(last), 1, 0), jnp.moveaxis(kv, 1, 0)))
    S_in = jnp.moveaxis(S_in, 0, 1)
    o_inter = jnp.einsum('bnthk,bnhkv->bnthv', q_dec, S_in)
    o = (o_intra + o_inter).reshape(b, l, h, dv)
    return o.astype(dt), S_fin.astype(dt)


def rwkv_scan(r, w, k, v, kk, a, S0, reverse):
    def step(S, inp):
        r_t, w_t, k_t, v_t, kk_t, a_t = inp
        Skk = jnp.einsum('bhvk,bhk->bhv', S, kk_t)
        S = S * w_t[:, :, None, :] - Skk[..., None] * (kk_t * a_t)[:, :, None, :] + v_t[..., None] * k_t[:, :, None, :]
        return S, jnp.einsum('bhvk,bhk->bhv', S, r_t)

    xs = tuple(jnp.moveaxis(t, 1, 0) for t in (r, w, k, v, kk, a))
    S_fin, o = lax.scan(step, S0.astype(jnp.float32), xs, reverse=reverse)
    return jnp.moveaxis(o, 0, 1), S_fin


def ssd_branch(p, st, lp):
    b, L, _ = p.shape
    z, xbc, dt_raw = _split(p, [D_MIX, SSD_XBC, 2 * SSD_H])
    xbc = jax.nn.silu(centred_conv(xbc, lp['ssd_conv_w'], lp['ssd_conv_b']))
    xs, bm, cm = _split(xbc, [D_MIX, SSD_G * SSD_N, SSD_G * SSD_N])
    xs = xs.reshape(b, L, SSD_H, SSD_P)
    bm = bm.reshape(b, L, SSD_G, SSD_N)
    cm = cm.reshape(b, L, SSD_G, SSD_N)
    dt = jax.nn.softplus((dt_raw.reshape(b, L, 2, SSD_H) + lp['ssd_dt_bias']).astype(jnp.float32))
    a = -jnp.exp(lp['ssd_A_log'].astype(jnp.float32))
    y = xs * (lp['ssd_D'][0] + lp['ssd_D'][1])[:, None]
    finals = []
    for d in range(2):
        xd = (xs * dt[:, :, d, :, None]).astype(xs.dtype)
        ad = dt[:, :, d] * a[d]
        if d == 0:
            yd, sd = ssd_chunked(xd, ad, bm, cm, st[:, 0])
        else:
            yd, sd = ssd_chunked(_rev(xd), _rev(ad), _rev(bm), _rev(cm), st[:, 1])
            yd = _rev(yd)
        y = y + yd
        finals.append(sd)
    y = rmsnorm(y.reshape(b, L, D_MIX) * jax.nn.silu(z), lp['ssd_norm'])
    return y @ lp['w_ssd_o'], jnp.stack(finals, axis=1)


def gla_branch(p, st, lp):
    b, L, _ = p.shape
    q, k, v, g, lr = _split(p, [GLA_H * GLA_DK, GLA_H * GLA_DK, D_MIX, D_MIX, 2 * GLA_LR])
    q = q.reshape(b, L, GLA_H, GLA_DK) * (GLA_DK ** -0.5)
    k = k.reshape(b, L, GLA_H, GLA_DK)
    v = v.reshape(b, L, GLA_H, GLA_DV)
    lr = lr.reshape(b, L, 2, GLA_LR)
    outs = []
    finals = []
    for d in range(2):
        lg = jax.nn.log_sigmoid((lr[:, :, d] @ lp['gla_gk_w'][d] + lp['gla_gk_b'][d]).astype(jnp.float32)) / GLA_GATE_NORM
        lg = lg.reshape(b, L, GLA_H, GLA_DK)
        if d == 0:
            od, sd = gla_chunked(q, k, v, lg, st[:, 0])
        else:
            od, sd = gla_chunked(_rev(q), _rev(k), _rev(v), _rev(lg), st[:, 1])
            od = _rev(od)
        outs.append(od)
        finals.append(sd)
    o = rmsnorm(outs[0] + outs[1], lp['gla_norm']).reshape(b, L, D_MIX) * jax.nn.silu(g)
    return o @ lp['w_gla_o'], jnp.stack(finals, axis=1)


def rwkv_branch(p, st, lp):
    b, L, _ = p.shape
    f32 = jnp.float32
    p = centred_shift_mix(p, lp['rw_mu'])
    r, k, v, wlr, alr, glr = _split(p, [D_MIX, D_MIX, D_MIX, 2 * RW_LW, RW_LA, RW_LG])
    a = jax.nn.sigmoid((lp['rw_a0'] + alr @ lp['rw_a2']).astype(f32))
    g = jax.nn.sigmoid(glr) @ lp['rw_g2']

    def hd(t):
        return t.reshape(b, L, RW_H, RW_N)

    r, k, v, a = hd(r.astype(f32)), hd(k.astype(f32)), hd(v.astype(f32)), hd(a)
    kk = k * lp['rw_kk'].astype(f32).reshape(RW_H, RW_N)
    kk = kk / jnp.maximum(jnp.sqrt(jnp.sum(kk * kk, axis=-1, keepdims=True)), 1e-12)
    k = k * (1 + (a - 1) * lp['rw_ka'].astype(f32).reshape(RW_H, RW_N))
    wlr = wlr.reshape(b, L, 2, RW_LW)
    o = jnp.zeros_like(r)
    finals = []
    for d in range(2):
        wl = -jax.nn.softplus(-(lp['rw_w0'][d] + jnp.tanh(wlr[:, :, d]) @ lp['rw_w2'][d]).astype(f32)) - 0.5
        w = hd(jnp.exp(-jnp.exp(wl)))
        od, sd = rwkv_scan(r, w, k, v, kk, a, st[:, d], reverse=(d == 1))
        o = o + od
        finals.append(sd.astype(p.dtype))
    mu = jnp.mean(o, axis=-1, keepdims=True)
    var = jnp.mean(jnp.square(o - mu), axis=-1, keepdims=True)
    o = ((o - mu) * lax.rsqrt(var + RW_GN_EPS)).reshape(b, L, D_MIX)
    o = o * lp['rw_ln_w'].astype(f32) + lp['rw_ln_b'].astype(f32)
    o = o + (jnp.sum(r * k * lp['rw_rk'].astype(f32), axis=-1, keepdims=True) * v).reshape(b, L, D_MIX)
    o = o.astype(p.dtype) * g
    return o @ lp['w_rw_o'], jnp.stack(finals, axis=1)


def token_mixer(u, st_ssd, st_gla, st_rw, lp):
    b, L, _ = u.shape
    proj = u @ lp['w_in']
    p_ssd, p_gla, p_rw, p_gate = _split(proj, [SSD_COLS, GLA_COLS, RW_COLS, GATE_COLS])
    o_ssd, s_ssd = ssd_branch(p_ssd, st_ssd, lp)
    o_gla, s_gla = gla_branch(p_gla, st_gla, lp)
    o_rw, s_rw = rwkv_branch(p_rw, st_rw, lp)
    gates = jax.nn.sigmoid(p_gate.astype(jnp.float32)).astype(u.dtype).reshape(b, L, 3, D_MODEL)
    merged = gates[:, :, 0] * o_ssd + gates[:, :, 1] * o_gla + gates[:, :, 2] * o_rw
    return merged @ lp['w_out'], s_ssd, s_gla, s_rw


def trunk_layer(x, cond, st_ssd, st_gla, st_rw, lp):
    nb = cond.shape[0]
    ada = (jax.nn.silu(cond) @ lp['w_ada'] + lp['b_ada']).reshape(nb, 1, N_MOD, D_MODEL)
    sh1, sc1, g1, sh2, sc2, g2, sh3, sc3, g3 = [ada[:, :, i] for i in range(N_MOD)]
    h = modulate(rmsnorm(x, lp['norm_g'][0]), sh1, sc1)
    x = x + 0.5 * g1 * swiglu(h, lp['ffn_gate'][0], lp['ffn_up'][0], lp['ffn_down'][0])
    h = modulate(rmsnorm(x, lp['norm_g'][1]), sh2, sc2)
    m, s_ssd, s_gla, s_rw = token_mixer(h, st_ssd, st_gla, st_rw, lp)
    x = x + g2 * m
    h = modulate(rmsnorm(x, lp['norm_g'][2]), sh3, sc3)
    x = x + 0.5 * g3 * swiglu(h, lp['ffn_gate'][1], lp['ffn_up'][1], lp['ffn_down'][1])
    return x, s_ssd, s_gla, s_rw


def setup_inputs(seed: int = 0) -> dict:
    key = jax.random.key(seed)
    ks = iter(jax.random.split(key, 64))
    f32 = jnp.float32

    def nrm(shape, scale):
        return jax.random.normal(next(ks), shape, f32) * scale

    def uni(shape, lo, hi):
        return jax.random.uniform(next(ks), shape, f32, lo, hi)

    x_prompt = nrm((BATCH, SEQ, D_MODEL), 1.0)
    x_sample = nrm((DEC_BATCH, DEC_SEQ, D_MODEL), 1.0)
    state_ssd = nrm((DEC_BATCH, DEPTH, 2, SSD_H, SSD_P, SSD_N), 0.5)
    state_gla = nrm((DEC_BATCH, DEPTH, 2, GLA_H, GLA_DK, GLA_DV), 1.0)
    state_rwkv = nrm((DEC_BATCH, DEPTH, 2, RW_H, RW_N, RW_N), 0.5)
    c = nrm((DEC_BATCH, D_MODEL), 1.0)
    c_ctx = nrm((D_MODEL,), 1.0)
    norm_g = 1.0 + nrm((DEPTH, 3, D_MODEL), 0.02)
    w_ada = nrm((DEPTH, D_MODEL, N_MOD * D_MODEL), 0.5 * D_MODEL ** -0.5)
    b_ada = nrm((DEPTH, N_MOD * D_MODEL), 0.02)
    ffn_gate = nrm((DEPTH, 2, D_MODEL, D_FF), D_MODEL ** -0.5)
    ffn_up = nrm((DEPTH, 2, D_MODEL, D_FF), D_MODEL ** -0.5)
    ffn_down = nrm((DEPTH, 2, D_FF, D_MODEL), D_FF ** -0.5)
    w_in = nrm((DEPTH, D_MODEL, N_IN), D_MODEL ** -0.5)
    ssd_conv_w = nrm((DEPTH, SSD_CONV, SSD_XBC), SSD_CONV ** -0.5)
    ssd_conv_b = nrm((DEPTH, SSD_XBC), 0.02)
    dt0 = jnp.exp(uni((DEPTH, 2, SSD_H), math.log(1e-3), math.log(1e-1)))
    ssd_dt_bias = dt0 + jnp.log(-jnp.expm1(-dt0))
    ssd_A_log = jnp.log(uni((DEPTH, 2, SSD_H), 1.0, 16.0))
    ssd_D = 1.0 + nrm((DEPTH, 2, SSD_H), 0.1)
    ssd_norm = 1.0 + nrm((DEPTH, D_MIX), 0.02)
    w_ssd_o = nrm((DEPTH, D_MIX, D_MODEL), D_MIX ** -0.5)
    gla_gk_w = nrm((DEPTH, 2, GLA_LR, GLA_H * GLA_DK), GLA_LR ** -0.5)
    gla_gk_b = nrm((DEPTH, 2, GLA_H * GLA_DK), 0.5)
    gla_norm = 1.0 + nrm((DEPTH, GLA_DV), 0.02)
    w_gla_o = nrm((DEPTH, D_MIX, D_MODEL), D_MIX ** -0.5)
    rw_mu = uni((DEPTH, RW_COLS), 0.0, 1.0)
    rw_w0 = uni((DEPTH, 2, D_MIX), -6.0, 1.0)
    rw_w2 = nrm((DEPTH, 2, RW_LW, D_MIX), 0.5 * RW_LW ** -0.5)
    rw_a0 = nrm((DEPTH, D_MIX), 0.5)
    rw_a2 = nrm((DEPTH, RW_LA, D_MIX), 0.5 * RW_LA ** -0.5)
    rw_g2 = nrm((DEPTH, RW_LG, D_MIX), RW_LG ** -0.5)
    rw_kk = 0.85 + nrm((DEPTH, D_MIX), 0.02)
    rw_ka = 1.0 + nrm((DEPTH, D_MIX), 0.02)
    rw_rk = nrm((DEPTH, RW_H, RW_N), 0.1)
    rw_ln_w = 1.0 + nrm((DEPTH, D_MIX), 0.02)
    rw_ln_b = nrm((DEPTH, D_MIX), 0.02)
    w_rw_o = nrm((DEPTH, D_MIX, D_MODEL), D_MIX ** -0.5)
    w_out = nrm((DEPTH, D_MODEL, D_MODEL), D_MODEL ** -0.5)
    final_norm = 1.0 + nrm((D_MODEL,), 0.02)
    return {'x_prompt': x_prompt, 'x_sample': x_sample, 'state_ssd': state_ssd, 'state_gla': state_gla,
            'state_rwkv': state_rwkv, 'c': c, 'c_ctx': c_ctx, 'norm_g': norm_g, 'w_ada': w_ada, 'b_ada': b_ada,
            'ffn_gate': ffn_gate, 'ffn_up': ffn_up, 'ffn_down': ffn_down, 'w_in': w_in,
            'ssd_conv_w': ssd_conv_w, 'ssd_conv_b': ssd_conv_b, 'ssd_dt_bias': ssd_dt_bias, 'ssd_A_log': ssd_A_log,
            'ssd_D': ssd_D, 'ssd_norm': ssd_norm, 'w_ssd_o': w_ssd_o, 'gla_gk_w': gla_gk_w, 'gla_gk_b': gla_gk_b,
            'gla_norm': gla_norm, 'w_gla_o': w_gla_o, 'rw_mu': rw_mu, 'rw_w0': rw_w0, 'rw_w2': rw_w2,
            'rw_a0': rw_a0, 'rw_a2': rw_a2, 'rw_g2': rw_g2, 'rw_kk': rw_kk, 'rw_ka': rw_ka, 'rw_rk': rw_rk,
            'rw_ln_w': rw_ln_w, 'rw_ln_b': rw_ln_b, 'w_rw_o': w_rw_o, 'w_out': w_out, 'final_norm': final_norm}


def reference(x_prompt, x_sample, state_ssd, state_gla, state_rwkv, c, c_ctx, norm_g, w_ada, b_ada,
              ffn_gate, ffn_up, ffn_down, w_in, ssd_conv_w, ssd_conv_b, ssd_dt_bias, ssd_A_log, ssd_D,
              ssd_norm, w_ssd_o, gla_gk_w, gla_gk_b, gla_norm, w_gla_o, rw_mu, rw_w0, rw_w2, rw_a0, rw_a2,
              rw_g2, rw_kk, rw_ka, rw_rk, rw_ln_w, rw_ln_b, w_rw_o, w_out, final_norm):
    rows = x_sample.shape[1] // GRID_W
    xs = x_sample + grid_pos_embed(rows, GRID_W, D_MODEL).astype(x_sample.dtype)[None]
    xp = x_prompt
    bp = xp.shape[0]
    cond_ctx = c_ctx[None]
    new_ssd, new_gla, new_rw = [], [], []
    for l in range(DEPTH):
        lp = {'norm_g': norm_g[l], 'w_ada': w_ada[l], 'b_ada': b_ada[l], 'ffn_gate': ffn_gate[l],
              'ffn_up': ffn_up[l], 'ffn_down': ffn_down[l], 'w_in': w_in[l], 'ssd_conv_w': ssd_conv_w[l],
              'ssd_conv_b': ssd_conv_b[l], 'ssd_dt_bias': ssd_dt_bias[l], 'ssd_A_log': ssd_A_log[l],
              'ssd_D': ssd_D[l], 'ssd_norm': ssd_norm[l], 'w_ssd_o': w_ssd_o[l], 'gla_gk_w': gla_gk_w[l],
              'gla_gk_b': gla_gk_b[l], 'gla_norm': gla_norm[l], 'w_gla_o': w_gla_o[l], 'rw_mu': rw_mu[l],
              'rw_w0': rw_w0[l], 'rw_w2': rw_w2[l], 'rw_a0': rw_a0[l], 'rw_a2': rw_a2[l], 'rw_g2': rw_g2[l],
              'rw_kk': rw_kk[l], 'rw_ka': rw_ka[l], 'rw_rk': rw_rk[l], 'rw_ln_w': rw_ln_w[l],
              'rw_ln_b': rw_ln_b[l], 'w_rw_o': w_rw_o[l], 'w_out': w_out[l]}
        z_ssd = jnp.zeros((bp, 2, SSD_H, SSD_P, SSD_N), xp.dtype)
        z_gla = jnp.zeros((bp, 2, GLA_H, GLA_DK, GLA_DV), xp.dtype)
        z_rw = jnp.zeros((bp, 2, RW_H, RW_N, RW_N), xp.dtype)
        xp, s_ssd, s_gla, s_rw = trunk_layer(xp, cond_ctx, z_ssd, z_gla, z_rw, lp)
        new_ssd.append(s_ssd)
        new_gla.append(s_gla)
        new_rw.append(s_rw)
        xs, _, _, _ = trunk_layer(xs, c, state_ssd[:, l], state_gla[:, l], state_rwkv[:, l], lp)
    y_prompt = rmsnorm(xp, final_norm)
    y_sample = rmsnorm(xs, final_norm)
    return (y_prompt, y_sample, jnp.stack(new_ssd, axis=1), jnp.stack(new_gla, axis=1), jnp.stack(new_rw, axis=1))
```

```python
import os
import numpy as np
from contextlib import ExitStack
import concourse.bass as bass
import concourse.mybir as mybir
from concourse.bass_utils import run_bass_kernel_spmd

F32 = mybir.dt.float32
BF16 = mybir.dt.bfloat16
I32 = mybir.dt.int32
ALU = mybir.AluOpType
AF = mybir.ActivationFunctionType

D = 1024
TOK = 1024
NT = 8
DFF = 2816
FC = 22
DEPTH = 2
NIN = 7792
PI = float(np.pi)


class V:
    __slots__ = ("t", "ap")

    def __init__(self, t, ap):
        self.t = t
        self.ap = ap

    def __getitem__(self, idx):
        return V(self.t, self.ap[idx])

    def re(self, s, **kw):
        return V(self.t, self.ap.rearrange(s, **kw))

    def bc(self, shape):
        return V(self.t, self.ap.to_broadcast(list(shape)))

    def bitcast(self, dt):
        return V(self.t, self.ap.bitcast(dt))

    @property
    def shape(self):
        return self.ap.shape


class Tile:
    __slots__ = ("h", "name", "w", "r", "dsem", "dcnt", "psum")

    def __init__(self, h, name, r0=None):
        self.h = h
        self.name = name
        self.psum = False
        self.w = None
        self.r = dict(r0) if r0 else {}
        self.dsem = None
        self.dcnt = 0

    def __getitem__(self, idx):
        return V(self, self.h[idx])


class Eng:
    def __init__(self, name, e):
        self.name = name
        self.e = e
        self.sem = None
        self.cnt = 0
        self.val = 0
        self.ord2val = {}
        self.seen = {}


class K:
    def __init__(self, needed=None):
        self.record = needed is None
        self.needed = {} if needed is None else needed
        self.nc = bass.Bass("TRN2", target_bir_lowering=False)
        self.es = ExitStack()
        self.scopes = [self.es]
        nc = self.nc
        self.pe = Eng("pe", nc.tensor)
        self.act = Eng("act", nc.scalar)
        self.dve = Eng("dve", nc.vector)
        self.pool = Eng("pool", nc.gpsimd)
        self.sp = Eng("sp", nc.sync)
        self.engs = [self.pe, self.act, self.dve, self.pool, self.sp]
        self.nsem = 0
        self.dsem_free = []
        for E in self.engs:
            self._newsem(E)
        self.out_waits = []
        self.nincs = 0
        self.all_dsem = {}
        self.ntile = 0
        self.barrier = {}
        self.scope_tiles = [[]]
        self.ninst = 0

    def _sem(self, name):
        self.nsem += 1
        return self.es.enter_context(self.nc.semaphore(name))

    def _newsem(self, E):
        E.sem = self._sem(f"s_{E.name}_{self.nsem}")
        E.cnt = 0

    def dram(self, name, shape, dt, kind):
        return V(None, self.nc.dram_tensor(name, list(shape), dt, kind=kind).ap())

    def sb(self, name, shape, dt=F32):
        self.ntile += 1
        h = self.scopes[-1].enter_context(self.nc.sbuf_tensor(f"{name}_{self.ntile}", list(shape), dt))
        t = Tile(h, name, self.barrier)
        self.scope_tiles[-1].append(t)
        return t

    def ps(self, name, shape, dt=F32):
        self.ntile += 1
        h = self.es.enter_context(self.nc.psum_tensor(f"{name}_{self.ntile}", list(shape), dt))
        t = Tile(h, name)
        t.psum = True
        return t

    def push(self):
        es = ExitStack()
        self.scopes.append(es)
        self.scope_tiles.append([])

    def pop(self):
        for t in self.scope_tiles.pop():
            if t.w is not None:
                s, v = t.w
                if self.barrier.get(s, 0) < v:
                    self.barrier[s] = v
            for s, v in t.r.items():
                if self.barrier.get(s, 0) < v:
                    self.barrier[s] = v
            if t.dsem is not None:
                self.dsem_free.append((t.dsem, t.dcnt))
        self.scopes.pop().close()

    def _wait(self, E, key, n):
        if E.seen.get(key, 0) >= n:
            return
        E.seen[key] = n
        if isinstance(key, Eng):
            if self.record:
                self.needed.setdefault(key.name, set()).add(n)
                return
            E.e.wait_ge(key.sem, key.ord2val[n])
        else:
            E.e.wait_ge(key, n)

    def _deps(self, E, reads, writes):
        waits = {}

        def need(s, v):
            if waits.get(s, 0) < v:
                waits[s] = v

        for t in reads:
            if t.w is not None:
                need(*t.w)
            if t.psum:
                for s, v in t.r.items():
                    if s is not E:
                        need(s, v)
        strict = E is not self.pe
        for t in writes:
            if t.w is not None and (strict or t.w[0] is not E):
                need(*t.w)
            for s, v in t.r.items():
                if strict or s is not E:
                    need(s, v)
        for s, v in waits.items():
            self._wait(E, s, v)

    def emit(self, E, fn, reads, writes, inc=True):
        reads = [x.t for x in reads if x is not None and x.t is not None]
        writes = [x.t for x in writes if x is not None and x.t is not None]
        self._deps(E, reads, writes)
        ins = fn()
        self.ninst += 1
        if inc:
            E.cnt += 1
            cid = E.cnt
            if (not self.record) and cid in self.needed.get(E.name, ()):
                E.val += 1
                ins.then_inc(E.sem, 1)
                E.ord2val[cid] = E.val
                self.nincs += 1
        else:
            cid = E.cnt + 1
        for t in reads:
            t.r[E] = cid
        for t in writes:
            t.w = (E, cid)
            t.r = {}
        return ins

    def dma(self, Q, out, in_, **kw):
        reads = [in_.t] if in_.t is not None else []
        writes = [out.t] if out.t is not None else []
        self._deps(Q, reads, writes)
        tl = out.t if out.t is not None else in_.t
        if tl.dsem is None:
            if self.dsem_free:
                tl.dsem, tl.dcnt = self.dsem_free.pop()
                self._wait(Q, tl.dsem, tl.dcnt)
            else:
                tl.dsem = self._sem(f"d_{tl.name}_{self.nsem}")
        ins = Q.e.dma_start(out=out.ap, in_=in_.ap, **kw)
        ins.then_inc(tl.dsem, 16)
        self.ninst += 1
        tl.dcnt += 16
        self.all_dsem[tl.dsem] = tl.dcnt
        if out.t is not None:
            out.t.w = (tl.dsem, tl.dcnt)
            out.t.r = {}
        if in_.t is not None:
            in_.t.r[tl.dsem] = tl.dcnt
        if out.t is None:
            self.out_waits.append((tl.dsem, tl.dcnt))
        return ins

    def finish(self):
        for s, v in self.all_dsem.items():
            self._wait(self.sp, s, v)
        for E in self.engs:
            if E is not self.sp and E.cnt > 0:
                self._wait(self.sp, E, E.cnt)

    def mm(self, out, lhsT, rhs, start=True, stop=True, inc=None, **kw):
        if inc is None:
            inc = stop
        return self.emit(self.pe, lambda: self.nc.tensor.matmul(out.ap, lhsT.ap, rhs.ap, start=start, stop=stop, **kw),
                         [lhsT, rhs], [out], inc=inc)

    def tr(self, out, in_, ident, inc=True):
        return self.emit(self.pe, lambda: self.nc.tensor.transpose(out.ap, in_.ap, ident.ap), [in_, ident], [out],
                         inc=inc)

    def actf(self, out, in_, func, bias=None, scale=None, accum=None):
        kw = {}
        rd = [in_]
        if bias is not None:
            if isinstance(bias, V):
                kw["bias"] = bias.ap
                rd.append(bias)
            else:
                kw["bias"] = float(bias)
        if scale is not None:
            if isinstance(scale, V):
                kw["scale"] = scale.ap
                rd.append(scale)
            else:
                kw["scale"] = float(scale)
        wr = [out]
        if accum is not None:
            kw["accum_out"] = accum.ap
            wr.append(accum)
        return self.emit(self.act, lambda: self.nc.scalar.activation(out.ap, in_.ap, func, **kw), rd, wr)

    def _ve(self, E):
        return E if E is not None else self.dve

    def tt(self, out, a, b, op, E=None):
        E = self._ve(E)
        return self.emit(E, lambda: E.e.tensor_tensor(out.ap, a.ap, b.ap, op), [a, b], [out])

    def ts(self, out, a, s1, op0, s2=None, op1=None, E=None):
        E = self._ve(E)
        rd = [a]
        a1 = s1
        if isinstance(s1, V):
            rd.append(s1)
            a1 = s1.ap
        a2 = s2
        if isinstance(s2, V):
            rd.append(s2)
            a2 = s2.ap
        kw = {}
        if op1 is not None:
            kw["op1"] = op1
        return self.emit(E, lambda: E.e.tensor_scalar(out.ap, a.ap, a1, a2, op0, **kw), rd, [out])

    def stt(self, out, a, s, b, op0, op1):
        E = self.dve
        rd = [a, b]
        a1 = s
        if isinstance(s, V):
            rd.append(s)
            a1 = s.ap
        return self.emit(E, lambda: E.e.scalar_tensor_tensor(out.ap, a.ap, a1, b.ap, op0, op1), rd, [out])

    def cp(self, out, in_, E=None):
        E = self._ve(E)
        if E is self.act:
            return self.emit(E, lambda: self.nc.scalar.copy(out.ap, in_.ap), [in_], [out])
        return self.emit(E, lambda: E.e.tensor_copy(out.ap, in_.ap), [in_], [out])

    def memset(self, out, val, E=None):
        E = self._ve(E)
        return self.emit(E, lambda: E.e.memset(out.ap, val), [], [out])

    def scan(self, out, d0, d1, init, op0, op1):
        rd = [d0, d1]
        i = init
        if isinstance(init, V):
            rd.append(init)
            i = init.ap
        return self.emit(self.dve, lambda: self.nc.vector.tensor_tensor_scan(out.ap, d0.ap, d1.ap, i, op0, op1), rd,
                         [out])

    def recip(self, out, in_):
        return self.emit(self.dve, lambda: self.nc.vector.reciprocal(out.ap, in_.ap), [in_], [out])

    def iota(self, out, pattern, base, cm):
        return self.emit(self.pool, lambda: self.nc.gpsimd.iota(out.ap, pattern, base=base, channel_multiplier=cm,
                                                               allow_small_or_imprecise_dtypes=True), [], [out])

    def asel(self, out, in_, pattern, op, fill, base, cm):
        return self.emit(self.pool, lambda: self.nc.gpsimd.affine_select(out.ap, in_.ap, pattern, op, fill, base=base,
                                                                        channel_multiplier=cm), [in_], [out])


PP_SPEC = [("norm_g0", 8), ("norm_g1", 8), ("norm_g2", 8), ("b_ada", 72),
           ("conv_w0", 6), ("conv_w1", 6), ("conv_w2", 6), ("conv_b", 6), ("ssd_norm", 4),
           ("ssd_D0", 4), ("ssd_D1", 4), ("dt_bias", 16), ("A_log", 16), ("gk_b", 4), ("gla_norm", 1),
           ("rw_mu", 15), ("rw_w0", 8), ("rw_a0", 4), ("rw_kk", 4), ("rw_ka", 4), ("rw_rk", 4),
           ("rw_ln_w", 4), ("rw_ln_b", 4)]
PP_OFF = {}
_o = 0
for _n, _c in PP_SPEC:
    PP_OFF[_n] = (_o, _c)
    _o += _c
NPP = _o


def _cm(vec):
    vec = np.asarray(vec, np.float32).reshape(-1)
    n = vec.shape[0] // 128
    return np.ascontiguousarray(vec.reshape(n, 128).T)


def pack_params(inp, l):
    pp = np.zeros((128, NPP), np.float32)

    def put(name, arr):
        o, c = PP_OFF[name]
        assert arr.shape == (128, c), (name, arr.shape, c)
        pp[:, o:o + c] = arr

    for i in range(3):
        put(f"norm_g{i}", _cm(inp["norm_g"][l, i]))
    put("b_ada", _cm(inp["b_ada"][l]))
    for i in range(3):
        put(f"conv_w{i}", _cm(inp["ssd_conv_w"][l, i]))
    put("conv_b", _cm(inp["ssd_conv_b"][l]))
    put("ssd_norm", _cm(inp["ssd_norm"][l]))
    hd = (2 * np.arange(4)[None, :] + (np.arange(128)[:, None] // 64))
    put("ssd_D0", inp["ssd_D"][l, 0][hd])
    put("ssd_D1", inp["ssd_D"][l, 1][hd])
    put("dt_bias", np.broadcast_to(inp["ssd_dt_bias"][l].reshape(1, 16), (128, 16)))
    put("A_log", np.broadcast_to(inp["ssd_A_log"][l].reshape(1, 16), (128, 16)))
    put("gk_b", np.concatenate([_cm(inp["gla_gk_b"][l, 0]), _cm(inp["gla_gk_b"][l, 1])], axis=1))
    put("gla_norm", _cm(inp["gla_norm"][l]))
    mu = inp["rw_mu"][l]
    mucols = np.zeros((128, 15), np.float32)
    mucols[:, 0:13] = _cm(mu[0:1664])
    mucols[0:64, 13] = mu[1664:1728]
    mucols[:, 14] = mu[1728:1856]
    put("rw_mu", mucols)
    put("rw_w0", np.concatenate([_cm(inp["rw_w0"][l, 0]), _cm(inp["rw_w0"][l, 1])], axis=1))
    put("rw_a0", _cm(inp["rw_a0"][l]))
    put("rw_kk", _cm(inp["rw_kk"][l]))
    put("rw_ka", _cm(inp["rw_ka"][l]))
    put("rw_rk", _cm(inp["rw_rk"][l]))
    put("rw_ln_w", _cm(inp["rw_ln_w"][l]))
    put("rw_ln_b", _cm(inp["rw_ln_b"][l]))
    return pp


class WS:
    NST = 2
    NBF = 3
    CAST_PAT = ["dve", "act", "dve", "pool", "dve", "act", "dve", "act"]

    def __init__(self, k):
        self.k = k
        self.st = [k.sb(f"wst{i}", [128, 2048], F32) for i in range(self.NST)]
        self.bf = [k.sb(f"wbf{i}", [128, 2048], BF16) for i in range(self.NBF)]
        self.slabs = []
        self.nd = 0
        self.ncast = 0
        self.nget = 0

    def add(self, dview, kcs, ncols):
        assert kcs * ncols <= 2048
        self.slabs.append((dview, kcs, ncols))

    def _dma(self, j):
        dview, kcs, ncols = self.slabs[j]
        st = self.st[j % self.NST]
        self.k.dma(self.k.sp, st[:, 0:kcs * ncols].re("p (a b) -> p a b", a=kcs), dview)

    def _cast(self, j):
        dview, kcs, ncols = self.slabs[j]
        n = kcs * ncols
        E = getattr(self.k, self.CAST_PAT[j % len(self.CAST_PAT)])
        self.k.cp(self.bf[j % self.NBF][:, 0:n], self.st[j % self.NST][:, 0:n], E=E)

    def next(self):
        j = self.nget
        self.nget += 1
        n = len(self.slabs)
        while self.nd < min(n, j + self.NST):
            self._dma(self.nd)
            self.nd += 1
        while self.ncast < min(n, j + 2):
            self._cast(self.ncast)
            self.ncast += 1
        dview, kcs, ncols = self.slabs[j]
        return self.bf[j % self.NBF][:, 0:kcs * ncols].re("p (a b) -> p a b", a=kcs)


def wview(w2d, k0, k1, c0, c1):
    return w2d.re("(kc p) n -> p kc n", p=128)[:, k0:k1, c0:c1]


class StopBuild(Exception):
    pass


class Prog:
    def __init__(self, debug=(), needed=None):
        import os
        self.stop = os.environ.get("KSTOP", "")
        self.debug = set(debug)
        self.k = K(needed)
        self.dbg_out = {}
        self.build()

    def pp(self, l, name, c0=0, c1=None):
        o, c = PP_OFF[name]
        if c1 is None:
            c1 = c
        return self.PP[l][:, o + c0:o + c1]

    def chk(self, name):
        if self.stop == name:
            raise StopBuild(name)

    def bank(self):
        b = self.banks[self.nbank % 8]
        self.nbank += 1
        return b

    def dbg(self, name, view, shape):
        if name not in self.debug or name in self.dbg_out:
            return
        d = self.k.dram("dbg_" + name, shape, view.ap.dtype, "ExternalOutput")
        self.k.dma(self.k.sp, d, view)
        self.dbg_out[name] = shape

    def build(self):
        k = self.k
        nc = k.nc
        self.xT_d = k.dram("xT", [D, TOK], F32, "ExternalInput")
        self.cond_d = k.dram("cond", [128, 8], F32, "ExternalInput")
        self.flags_d = k.dram("flags", [128, 32], F32, "ExternalInput")
        self.gp_d = k.dram("gp", [128, 8], F32, "ExternalInput")
        self.pp_d = k.dram("pp", [DEPTH, 128, NPP], F32, "ExternalInput")
        self.w = {}
        for name, shape in [("w_ada", [DEPTH, D, 9 * D]), ("ffn_gate", [DEPTH, 2, D, DFF]),
                            ("ffn_up", [DEPTH, 2, D, DFF]), ("ffn_down", [DEPTH, 2, DFF, D]),
                            ("w_in", [DEPTH, D, NIN]), ("w_ssd_o", [DEPTH, 512, D]), ("w_gla_o", [DEPTH, 512, D]),
                            ("w_rw_o", [DEPTH, 512, D]), ("w_out", [DEPTH, D, D])]:
            self.w[name] = k.dram(name, shape, F32, "ExternalInput")
        self.yT_d = k.dram("yT", [D, TOK], F32, "ExternalOutput")
        self.lvlmask_d = k.dram("lvlmask", [2, 128, 7, 128], F32, "ExternalInput")
        self.st_ssd_d = k.dram("st_ssd", [DEPTH, 2, 128, 256], F32, "ExternalInput")
        self.ns_ssd_d = k.dram("ns_ssd", [DEPTH, 4, 2, 128, 256], F32, "ExternalOutput")
        self.st_gla_d = k.dram("st_gla", [DEPTH, 2, 128, 256], F32, "ExternalInput")
        self.ns_gla_d = k.dram("ns_gla", [DEPTH, 4, 2, 128, 256], F32, "ExternalOutput")
        self.gkw_d = k.dram("gla_gk_w", [DEPTH, 2, 16, 256], F32, "ExternalInput")
        self.st_rw_d = k.dram("st_rw", [DEPTH, 2, 128, 256], F32, "ExternalInput")
        self.ns_rw_d = k.dram("ns_rw", [DEPTH, 4, 2, 128, 256], F32, "ExternalOutput")
        self.rw_w2_d = k.dram("rw_w2", [DEPTH, 2, 64, 512], F32, "ExternalInput")
        self.rw_a2_d = k.dram("rw_a2", [DEPTH, 64, 512], F32, "ExternalInput")
        self.rw_g2_d = k.dram("rw_g2", [DEPTH, 128, 512], F32, "ExternalInput")

        self.xT = k.sb("xT", [128, 8, TOK], F32)
        self.hT = k.sb("hT", [128, 8, TOK], BF16)
        self.flags = k.sb("flags", [128, 32], F32)
        self.gp = k.sb("gp", [128, 8], F32)
        self.PP = [k.sb(f"pp{l}", [128, NPP], F32) for l in range(DEPTH)]
        self.ada = [k.sb(f"ada{l}", [128, 72], F32) for l in range(DEPTH)]
        self.modA = [k.sb(f"modA{l}", [128, 24], F32) for l in range(DEPTH)]
        self.gate = [k.sb(f"gate{l}", [128, 24], F32) for l in range(DEPTH)]
        self.ones_bf = k.sb("ones_bf", [128, 128], BF16)
        self.ident_bf = k.sb("ident_bf", [128, 128], BF16)
        self.ident_f = k.sb("ident_f", [128, 128], F32)
        self.cst = k.sb("cst", [128, 8], F32)
        self.eps6 = self.cst[:, 0:1]
        self.epsgn = self.cst[:, 1:2]
        self.ones_f = k.sb("ones_f", [128, 128], F32)
        self.maskU = k.sb("maskU", [128, 128], F32)
        self.maskL = k.sb("maskL", [128, 128], F32)
        self.maskSU = k.sb("maskSU", [128, 128], F32)
        self.maskSL = k.sb("maskSL", [128, 128], F32)
        self.mask2 = [k.sb(f"mask2_{d}", [128, 2, 128], F32) for d in range(2)]
        self.lvlmask = [k.sb(f"lvlmask{d}", [128, 7, 128], BF16) for d in range(2)]
        self.blockones = k.sb("blockones", [128, 128], BF16)
        self.blockmean = k.sb("blockmean", [128, 128], BF16)
        self.banks = [k.ps(f"bank{i}", [128, 512], F32) for i in range(8)]
        self.nbank = 0
        self.ws = WS(k)

        k.dma(k.sp, self.flags[:], self.flags_d)
        k.dma(k.sp, self.gp[:], self.gp_d)
        for l in range(DEPTH):
            k.dma(k.sp, self.PP[l][:], self.pp_d[l])
        xv = self.xT_d.re("(c p) t -> p c t", p=128)
        for c in range(8):
            k.dma(k.sp, self.xT[:, c, :], xv[:, c, :])

        k.push()
        lmf = k.sb("lvlmask_f", [128, 7, 128], F32)
        for d in range(2):
            k.dma(k.sp, lmf[:], self.lvlmask_d[d])
            k.cp(self.lvlmask[d][:], lmf[:], E=k.pool)
        k.pop()
        k.memset(self.cst[:, 0:1], 1e-6, E=k.pool)
        k.memset(self.cst[:, 1:2], 64e-5, E=k.pool)
        k.memset(self.cst[:, 2:3], 1.0, E=k.pool)
        k.memset(self.ones_bf[:], 1.0, E=k.pool)
        k.memset(self.ones_f[:], 1.0, E=k.pool)
        k.memset(self.maskU[:], 1.0, E=k.pool)
        k.asel(self.maskU[:], self.maskU[:], [[1, 128]], ALU.is_ge, 0.0, 0, -1)
        k.memset(self.maskSU[:], 1.0, E=k.pool)
        k.asel(self.maskSU[:], self.maskSU[:], [[1, 128]], ALU.is_ge, 0.0, -1, -1)
        k.memset(self.maskSL[:], 1.0, E=k.pool)
        k.asel(self.maskSL[:], self.maskSL[:], [[-1, 128]], ALU.is_ge, 0.0, -1, 1)
        k.memset(self.blockones[:], 0.0, E=k.pool)
        k.memset(self.blockmean[:], 0.0, E=k.pool)
        for e in range(2):
            k.memset(self.blockones[e * 64:(e + 1) * 64, e * 64:(e + 1) * 64], 1.0, E=k.pool)
            k.memset(self.blockmean[e * 64:(e + 1) * 64, e * 64:(e + 1) * 64], 1.0 / 64, E=k.pool)
        k.memset(self.maskL[:], 1.0, E=k.pool)
        k.asel(self.maskL[:], self.maskL[:], [[-1, 128]], ALU.is_ge, 0.0, 0, 1)
        k.cp(self.mask2[0][:, 0, :], self.maskU[:], E=k.pool)
        k.cp(self.mask2[0][:, 1, :], self.maskSU[:], E=k.pool)
        k.cp(self.mask2[1][:, 0, :], self.maskL[:], E=k.pool)
        k.cp(self.mask2[1][:, 1, :], self.maskSL[:], E=k.pool)
        k.memset(self.ident_f[:], 1.0, E=k.pool)
        k.asel(self.ident_f[:], self.ident_f[:], [[-1, 128]], ALU.is_equal, 0.0, 0, 1)
        k.cp(self.ident_bf[:], self.ident_f[:], E=k.pool)

        for l in range(DEPTH):
            self.plan_ada(l)
        for l in range(DEPTH):
            self.plan_ffn(l, 0)
            self.plan_mixer(l)
            self.plan_ffn(l, 1)

        self.pos_embed()
        self.compute_ada()
        try:
            for l in range(DEPTH):
                self.ffn(l, 0)
                self.dbg(f"x1_{l}", self.xT[:], [128, 8, TOK])
                self.mixer(l)
                self.dbg(f"x2_{l}", self.xT[:], [128, 8, TOK])
                self.ffn(l, 1)
        except StopBuild:
            while len(k.scopes) > 1:
                k.pop()
        self.final_norm()
        k.finish()

    def pos_embed(self):
        k = self.k
        k.push()
        idx = k.sb("pe_idx", [128, 2], F32)
        om = k.sb("pe_om", [128, 2], F32)
        pos = k.sb("pe_pos", [128, 80], F32)
        arg = k.sb("pe_arg", [128, 4, 80], F32)
        ki = k.sb("pe_ki", [128, 4, 80], I32)
        kf = k.sb("pe_kf", [128, 4, 80], F32)
        gt = k.sb("pe_gt", [128, 4, 80], F32)
        emb = k.sb("pe_emb", [128, 4, 80], F32)
        k.iota(idx[:], [[128, 2]], 0, 1)
        k.actf(om[:], idx[:], AF.Exp, scale=-float(np.log(10000.0)) / 256.0)
        k.iota(pos[:, 0:16], [[1, 16]], 0, 0)
        k.iota(pos[:, 16:80], [[1, 64]], 0, 0)
        for j in range(4):
            ph = 0.0 if j < 2 else PI / 2
            k.ts(arg[:, j, :], pos[:], om[:, (j % 2):(j % 2) + 1], ALU.mult, ph, ALU.add)
        k.ts(kf[:], arg[:], 1.0 / (2 * PI), ALU.mult)
        k.cp(ki[:], kf[:])
        k.cp(kf[:], ki[:])
        k.stt(arg[:], kf[:], -2 * PI, arg[:], ALU.mult, ALU.add)
        k.ts(gt[:], arg[:], PI, ALU.is_gt, -2 * PI, ALU.mult)
        k.tt(arg[:], arg[:], gt[:], ALU.add)
        k.ts(gt[:], arg[:], -PI, ALU.is_lt, 2 * PI, ALU.mult)
        k.tt(arg[:], arg[:], gt[:], ALU.add)
        k.actf(emb[:], arg[:], AF.Sin)
        k.ts(emb[:], emb[:], self.flags[:, 16:17], ALU.mult)
        self.dbg("emb", emb[:], [128, 4, 80])
        for fc in range(8):
            xv = self.xT[:, fc, :].re("p (r c) -> p r c", c=64)
            if fc < 4:
                ev = V(emb, emb.h[:, fc, 0:16].unsqueeze(2).to_broadcast([128, 16, 64]))
            else:
                ev = V(emb, emb.h[:, fc - 4, 16:80].unsqueeze(1).to_broadcast([128, 16, 64]))
            k.tt(xv, xv, ev, ALU.add)
        k.pop()

    def plan_ada(self, l):
        for s in range(36):
            self.ws.add(wview(self.w["w_ada"][l], 0, 8, s * 256, (s + 1) * 256), 8, 256)

    def compute_ada(self):
        k = self.k
        k.push()
        cond = k.sb("cond", [128, 8], F32)
        sg = k.sb("cond_sg", [128, 8], F32)
        scond = k.sb("scond", [128, 8], BF16)
        k.dma(k.sp, cond[:], self.cond_d)
        k.actf(sg[:], cond[:], AF.Sigmoid)
        k.tt(scond[:], cond[:], sg[:], ALU.mult)
        for l in range(DEPTH):
            pb = self.bank()
            for s in range(36):
                wb = self.ws.next()
                for m in range(2):
                    j = s * 2 + m
                    for kc in range(8):
                        k.mm(pb[:, j:j + 1], wb[:, kc, m * 128:(m + 1) * 128], scond[:, kc:kc + 1],
                             start=(kc == 0), stop=(kc == 7))
            k.tt(self.ada[l][:], pb[:, 0:72], self.pp(l, "b_ada"), ALU.add)
            for i in range(3):
                k.stt(self.modA[l][:, i * 8:(i + 1) * 8], self.ada[l][:, (3 * i + 1) * 8:(3 * i + 2) * 8], 1.0,
                      self.pp(l, f"norm_g{i}"), ALU.add, ALU.mult)
                k.ts(self.gate[l][:, i * 8:(i + 1) * 8], self.ada[l][:, (3 * i + 2) * 8:(3 * i + 3) * 8],
                     0.5 if i != 1 else 1.0, ALU.mult)
            self.dbg(f"ada{l}", self.ada[l][:], [128, 72])
        k.pop()

    def rstd_of_x(self, rstd):
        k = self.k
        k.push()
        sq = k.sb("sq", [128, 8, TOK], BF16)
        for c in range(8):
            k.actf(sq[:, c, :], self.xT[:, c, :], AF.Square)
        for h in range(2):
            pb = self.bank()
            for c in range(8):
                k.mm(pb[:], self.ones_bf[:], sq[:, c, h * 512:(h + 1) * 512], start=(c == 0), stop=(c == 7))
            k.actf(rstd[:, h * 512:(h + 1) * 512], pb[:], AF.Ln, scale=1.0 / D, bias=self.eps6)
        k.actf(rstd[:], rstd[:], AF.Exp, scale=-0.5)
        k.pop()

    def norm_mod(self, l, i):
        k = self.k
        k.push()
        rstd = k.sb("rstd", [128, TOK], F32)
        tmp = [k.sb(f"nm_tmp{j}", [128, TOK], F32) for j in range(2)]
        self.rstd_of_x(rstd)
        for c in range(8):
            t = tmp[c % 2]
            k.tt(t[:], self.xT[:, c, :], rstd[:], ALU.mult)
            k.actf(self.hT[:, c, :], t[:], AF.Identity, scale=self.modA[l][:, i * 8 + c:i * 8 + c + 1],
                   bias=self.ada[l][:, 3 * i * 8 + c:3 * i * 8 + c + 1])
        k.pop()

    def final_norm(self):
        k = self.k
        k.push()
        rstd = k.sb("rstd", [128, TOK], F32)
        tmp = [k.sb(f"fn_tmp{j}", [128, TOK], F32) for j in range(2)]
        self.rstd_of_x(rstd)
        yv = self.yT_d.re("(c p) t -> p c t", p=128)
        for c in range(8):
            t = tmp[c % 2]
            k.stt(t[:], self.xT[:, c, :], self.gp[:, c:c + 1], rstd[:], ALU.mult, ALU.mult)
            k.dma(k.sp, yv[:, c, :], t[:])
        k.pop()

    def plan_ffn(self, l, which):
        wg = self.w["ffn_gate"][l, which]
        wu = self.w["ffn_up"][l, which]
        wd = self.w["ffn_down"][l, which]
        for s in range(11):
            self.ws.add(wview(wg, 0, 8, s * 256, (s + 1) * 256), 8, 256)
            self.ws.add(wview(wu, 0, 8, s * 256, (s + 1) * 256), 8, 256)
        for s in range(4):
            for (k0, k1) in ((0, 8), (8, 16), (16, 22)):
                self.ws.add(wview(wd, k0, k1, s * 256, (s + 1) * 256), k1 - k0, 256)

    def ffn(self, l, which):
        k = self.k
        i = 0 if which == 0 else 2
        self.norm_mod(l, i)
        k.push()
        actT = k.sb("actT", [128, FC, TOK], BF16)
        sg = [k.sb(f"ffn_sg{j}", [128, 512], F32) for j in range(2)]
        nsg = 0
        for s in range(11):
            wg = self.ws.next()
            wu = self.ws.next()
            for m in range(2):
                fcb = s * 2 + m
                for h in range(2):
                    pg = self.bank()
                    pu = self.bank()
                    for kc in range(8):
                        k.mm(pg[:], wg[:, kc, m * 128:(m + 1) * 128], self.hT[:, kc, h * 512:(h + 1) * 512],
                             start=(kc == 0), stop=(kc == 7))
                    for kc in range(8):
                        k.mm(pu[:], wu[:, kc, m * 128:(m + 1) * 128], self.hT[:, kc, h * 512:(h + 1) * 512],
                             start=(kc == 0), stop=(kc == 7))
                    t = sg[nsg % 2]
                    nsg += 1
                    k.actf(t[:], pg[:], AF.Silu)
                    k.tt(actT[:, fcb, h * 512:(h + 1) * 512], pu[:], t[:], ALU.mult)
        gcol = self.gate[l]
        for s in range(4):
            pbs = [[self.bank() for h in range(2)] for m in range(2)]
            for ksub, (k0, k1) in enumerate(((0, 8), (8, 16), (16, 22))):
                wd = self.ws.next()
                for m in range(2):
                    for h in range(2):
                        for kc in range(k0, k1):
                            k.mm(pbs[m][h][:], wd[:, kc - k0, m * 128:(m + 1) * 128],
                                 actT[:, kc, h * 512:(h + 1) * 512], start=(kc == 0), stop=(kc == FC - 1),
                                 inc=(kc == k1 - 1))
            for m in range(2):
                c = s * 2 + m
                for h in range(2):
                    xs = self.xT[:, c, h * 512:(h + 1) * 512]
                    k.stt(xs, pbs[m][h][:], gcol[:, i * 8 + c:i * 8 + c + 1], xs, ALU.mult, ALU.add)
        k.pop()


    SEC = dict(z=0, xbc=512, dt=1280, q=1296, k=1552, v=1808, g=2320, lr=2832, rr=2864, rk=3376, rv=3888,
               wlr=4400, alr=4528, glr=4592, gate=4720)

    def win_slab(self, l, c0, ncols):
        self.ws.add(wview(self.w["w_in"][l], 0, 8, c0, c0 + ncols), 8, ncols)

    def plan_mixer(self, l):
        self.plan_ssd(l)
        self.plan_gla(l)
        self.plan_rw(l)
        self.plan_out(l, "w_out")

    def plan_out(self, l, name):
        w = self.w[name][l]
        if name == "w_out":
            for s in range(4):
                self.ws.add(wview(w, 0, 8, s * 256, (s + 1) * 256), 8, 256)
        else:
            for s in range(2):
                self.ws.add(wview(w, 0, 4, s * 512, (s + 1) * 512), 4, 512)

    def proj_fm(self, wb, m0, mcols, h):
        k = self.k
        pb = self.bank()
        for kc in range(8):
            k.mm(pb[0:mcols, :], wb[:, kc, m0:m0 + mcols], self.hT[:, kc, h * 512:(h + 1) * 512],
                 start=(kc == 0), stop=(kc == 7))
        return pb

    def mixer(self, l):
        k = self.k
        self.norm_mod(l, 1)
        k.push()
        self.merged = k.sb("merged", [128, 8, TOK], BF16)
        self.ssd_branch(l)
        self.dbg(f"mg0_{l}", self.merged[:], [128, 8, TOK])
        self.chk("F")
        self.gla_branch(l)
        self.dbg(f"mg1_{l}", self.merged[:], [128, 8, TOK])
        self.chk("G")
        self.rw_branch(l)
        self.dbg(f"mg2_{l}", self.merged[:], [128, 8, TOK])
        self.chk("H")
        mbf = self.merged
        gcol = self.gate[l]
        for s in range(4):
            wb = self.ws.next()
            for m in range(2):
                c = s * 2 + m
                for h in range(2):
                    pb = self.bank()
                    for kc in range(8):
                        k.mm(pb[:], wb[:, kc, m * 128:(m + 1) * 128], mbf[:, kc, h * 512:(h + 1) * 512],
                             start=(kc == 0), stop=(kc == 7))
                    xs = self.xT[:, c, h * 512:(h + 1) * 512]
                    k.stt(xs, pb[:], gcol[:, 8 + c:8 + c + 1], xs, ALU.mult, ALU.add)
        k.pop()

    def plan_branch_out(self, l, name, b):
        w = self.w[name][l]
        for s in range(4):
            self.ws.add(wview(w, 0, 4, s * 256, (s + 1) * 256), 4, 256)
            self.win_slab(l, self.SEC["gate"] + b * 1024 + s * 256, 256)

    def branch_out(self, l, yT, b):
        k = self.k
        k.push()
        sgt = [k.sb(f"bo_sg{j}", [128, 512], F32) for j in range(2)]
        tmp = [k.sb(f"bo_tmp{j}", [128, 512], F32) for j in range(2)]
        n = 0
        for s in range(4):
            wo = self.ws.next()
            wg = self.ws.next()
            for m in range(2):
                c = s * 2 + m
                for h in range(2):
                    po = self.bank()
                    for kc in range(4):
                        k.mm(po[:], wo[:, kc, m * 128:(m + 1) * 128],
                             yT[:, kc, h * 512:(h + 1) * 512], start=(kc == 0), stop=(kc == 3))
                    pg = self.proj_fm(wg, m * 128, 128, h)
                    sg = sgt[n % 2]
                    tp = tmp[n % 2]
                    n += 1
                    k.actf(sg[:], pg[:], AF.Sigmoid)
                    mv = self.merged[:, c, h * 512:(h + 1) * 512]
                    if b == 0:
                        k.tt(mv, po[:], sg[:], ALU.mult)
                    else:
                        k.tt(tp[:], po[:], sg[:], ALU.mult)
                        k.tt(mv, mv, tp[:], ALU.add, E=k.pool)
        k.pop()

    def plan_ssd(self, l):
        S = self.SEC
        for c0 in (0, 256):
            self.win_slab(l, S["z"] + c0, 256)
        for c0 in (0, 256, 512):
            self.win_slab(l, S["xbc"] + c0, 256)
        self.win_slab(l, S["dt"], 16)
        self.plan_branch_out(l, "w_ssd_o", 0)

    def ssd_branch(self, l):
        k = self.k
        fl = self.flags
        k.push()
        zs = k.sb("ssd_zs", [128, 4, TOK], BF16)
        xc = k.sb("ssd_xc", [128, 6, TOK], BF16)
        xbt = k.sb("ssd_xbt", [128, NT, 640], BF16)
        dt = k.sb("ssd_dt", [128, NT, 16], F32)
        lndt = k.sb("ssd_lndt", [128, NT, 16], F32)
        a = k.sb("ssd_a", [128, NT, 16], F32)
        cs = k.sb("ssd_cs", [128, NT, 16], F32)
        aneg = k.sb("ssd_aneg", [128, 16], F32)
        dsum = k.sb("ssd_dsum", [128, 4], F32)
        ygT = k.sb("ssd_yg", [128, 4, TOK], BF16)
        sin_bf = [k.sb(f"ssd_sin{d}", [128, NT, 256], BF16) for d in range(2)]
        for s in range(2):
            wb = self.ws.next()
            for m in range(2):
                for h in range(2):
                    pb = self.proj_fm(wb, m * 128, 128, h)
                    k.actf(zs[:, s * 2 + m, h * 512:(h + 1) * 512], pb[:], AF.Silu)
        k.push()
        xp = k.sb("ssd_xp", [128, 6, 4, 258], BF16)
        acc = [k.sb(f"ssd_acc{j}", [128, 4, 256], F32) for j in range(2)]
        k.memset(xp[:, :, :, 0:1], 0.0, E=k.pool)
        k.memset(xp[:, :, :, 257:258], 0.0, E=k.pool)
        for s in range(3):
            wb = self.ws.next()
            for m in range(2):
                for h in range(2):
                    pb = self.proj_fm(wb, m * 128, 128, h)
                    k.cp(xp[:, s * 2 + m, 2 * h:2 * h + 2, 1:257], pb[:].re("p (q t) -> p q t", q=2),
                         E=(k.act if h == 0 else k.dve))
        cfv = V(fl, fl.h[:, 2:8:2].unsqueeze(1).to_broadcast([128, 6, 3]))
        cbv = V(fl, fl.h[:, 9:15:2].unsqueeze(1).to_broadcast([128, 6, 3]))
        k.tt(xp[:, :, 1:4, 0], xp[:, :, 0:3, 256], cfv, ALU.mult)
        k.tt(xp[:, :, 0:3, 257], xp[:, :, 1:4, 1], cbv, ALU.mult)
        for c in range(6):
            t = acc[c % 2]
            k.ts(t[:], xp[:, c, :, 1:257], self.pp(l, "conv_w1", c, c + 1), ALU.mult,
                 self.pp(l, "conv_b", c, c + 1), ALU.add)
            k.stt(t[:], xp[:, c, :, 0:256], self.pp(l, "conv_w0", c, c + 1), t[:], ALU.mult, ALU.add)
            k.stt(t[:], xp[:, c, :, 2:258], self.pp(l, "conv_w2", c, c + 1), t[:], ALU.mult, ALU.add)
            k.actf(xc[:, c, :].re("p (q t) -> p q t", q=4), t[:], AF.Silu)
        k.pop()
        self.dbg("ssd_xc", xc[:], [128, 6, TOK])
        self.chk("A")
        for t in range(NT):
            pb = self.bank()
            pv = pb[:].bitcast(BF16)
            for c in range(5):
                k.tr(pv[:, c * 128:(c + 1) * 128], xc[:, c, t * 128:(t + 1) * 128], self.ident_bf[:], inc=(c == 4))
            k.cp(xbt[:, t, :], pv[:, 0:640], E=(k.act if t % 2 else k.dve))
        self.chk("B")
        wb = self.ws.next()
        pb = self.bank()
        for t in range(NT):
            for kc in range(8):
                k.mm(pb[:, t * 16:(t + 1) * 16], self.hT[:, kc, t * 128:(t + 1) * 128], wb[:, kc, 0:16],
                     start=(kc == 0), stop=(kc == 7))
        k.tt(dt[:], pb[:, 0:128].re("p (t c) -> p t c", t=NT),
             V(self.PP[l], self.pp(l, "dt_bias").ap.unsqueeze(1).to_broadcast([128, NT, 16])), ALU.add)
        k.actf(dt[:], dt[:], AF.Exp)
        k.ts(lndt[:], dt[:], -0.5, ALU.mult, 1.0, ALU.add)
        k.tt(lndt[:], lndt[:], dt[:], ALU.mult)
        k.actf(dt[:], dt[:], AF.Ln, bias=self.cst[:, 2:3])
        k.tt(dt[:], dt[:], lndt[:], ALU.max)
        k.actf(lndt[:], dt[:], AF.Ln)
        k.actf(aneg[:], self.pp(l, "A_log"), AF.Exp)
        k.ts(aneg[:], aneg[:], -1.0, ALU.mult)
        k.tt(a[:], dt[:], V(aneg, aneg.h[:, :].unsqueeze(1).to_broadcast([128, NT, 16])), ALU.mult)
        k.tt(dsum[:], self.pp(l, "ssd_D0"), self.pp(l, "ssd_D1"), ALU.add)
        self.dbg("ssd_dt", dt[:], [128, NT, 16])
        self.chk("B1")
        pb = self.bank()
        for t in range(NT):
            k.mm(pb[:, t * 16:t * 16 + 8], self.maskU[:], a[:, t, 0:8])
            k.mm(pb[:, t * 16 + 8:t * 16 + 16], self.maskL[:], a[:, t, 8:16])
        k.cp(cs[:], pb[:, 0:128].re("p (t c) -> p t c", t=NT))
        self.chk("B2")
        carg = k.sb("ssd_carg", [128, NT, 16], F32)
        k.tt(carg[:], lndt[:], cs[:], ALU.subtract)
        tot = k.sb("ssd_tot", [128, NT, 16], F32)
        wdec = k.sb("ssd_wdec", [128, NT, 16], F32)
        dech = [k.sb(f"ssd_dech{d}", [128, NT, 4], F32) for d in range(2)]
        pb = self.bank()
        k.mm(pb[:, 0:128], self.ones_f[:], a[:].re("p t c -> p (t c)"))
        k.cp(tot[:], pb[:, 0:128].re("p (t c) -> p t c", t=NT))
        self.chk("B3")
        import os
        kvar = os.environ.get("KVAR", "")
        if kvar != "nowdec":
            k.tt(wdec[:], tot[:], carg[:], ALU.add)
            k.actf(wdec[:], wdec[:], AF.Exp)
        if kvar != "nodech":
            for d in range(2):
                for g in range(2):
                    ps_ = slice(g * 64, (g + 1) * 64)
                    k.actf(dech[d][ps_, :, :], tot[ps_, :, d * 8 + g * 4:d * 8 + g * 4 + 4], AF.Exp)
        self.chk("C")
        k.push()
        S = [k.sb(f"ssd_S{d}", [128, 256], F32) for d in range(2)]
        Sin = [k.sb(f"ssd_Sin{d}", [128, 256], F32) for d in range(2)]
        Stmp = [k.sb(f"ssd_Stmp{d}", [128, 256], F32) for d in range(2)]
        xdec = [[k.sb(f"ssd_xdec{d}_{j}", [128, 512], BF16) for j in range(2)] for d in range(2)]
        for d in range(2):
            k.dma(k.sp, Sin[d][:], self.st_ssd_d[l, d])
        for step in range(NT):
            for d in range(2):
                t = step if d == 0 else NT - 1 - step
                if step > 0:
                    fcol = t if d == 0 else 8 + t
                    k.ts(Sin[d][:], S[d][:], fl[:, fcol:fcol + 1], ALU.mult)
                k.cp(sin_bf[d][:, t, :], Sin[d][:], E=k.act)
                xd = xdec[d][step % 2]
                k.tt(xd[:].re("p (h q) -> p h q", h=8), xbt[:, t, 0:512].re("p (h q) -> p h q", h=8),
                     V(wdec, wdec.h[:, t, d * 8:(d + 1) * 8].unsqueeze(2).to_broadcast([128, 8, 64])), ALU.mult)
                pb = self.bank()
                for g in range(2):
                    k.mm(pb[g * 64:(g + 1) * 64, 0:256], xbt[:, t, 512 + g * 64:512 + (g + 1) * 64],
                         xd[:, g * 256:(g + 1) * 256])
                k.tt(Stmp[d][:].re("p (j q) -> p j q", j=4), Sin[d][:].re("p (j q) -> p j q", j=4),
                     V(dech[d], dech[d].h[:, t, :].unsqueeze(2).to_broadcast([128, 4, 64])), ALU.mult)
                k.tt(S[d][:], Stmp[d][:], pb[:, 0:256], ALU.add)
                if (d == 0 and t % 2 == 1) or (d == 1 and t % 2 == 0):
                    k.dma(k.sp, self.ns_ssd_d[l, t // 2, d], S[d][:])
        k.pop()
        self.chk("D")
        k.push()
        MT = [k.sb(f"ssd_MT{j}", [128, 8, 128], BF16) for j in range(2)]
        cdec = [[k.sb(f"ssd_cdec{d}_{j}", [128, 4, 128], BF16) for j in range(2)] for d in range(2)]
        am = k.sb("ssd_am", [128, 8, 128], F32)
        ambf = k.sb("ssd_ambf", [128, 8, 128], BF16)
        dm = [k.sb(f"ssd_dm{j}", [128, 8, 128], F32) for j in range(2)]
        er = [k.sb(f"ssd_er{j}", [128, 8, 128], BF16) for j in range(2)]
        ytmp = [k.sb(f"ssd_ytmp{j}", [128, 4, 128], F32) for j in range(2)]
        if "ssd_yraw" in self.debug:
            self.yraw = k.sb("ssd_yraw", [128, 4, TOK], F32)
        masks = (self.maskU, self.maskL)
        for t in range(NT):
            for d in range(2):
                mk = masks[d]
                k.tt(am[:], V(a, a.h[:, t, d * 8:(d + 1) * 8].unsqueeze(2).to_broadcast([128, 8, 128])),
                     V(mk, mk.h[:, :].unsqueeze(1).to_broadcast([128, 8, 128])), ALU.mult,
                     E=(k.dve if os.environ.get("KVAR", "") == "amdve" else k.pool))
                dmt = dm[d]
                ert = er[d]
                if os.environ.get("KVAR", "") == "rowbf":
                    k.cp(ambf[:], am[:])
                for hh in range(2):
                    pr = self.bank()
                    if os.environ.get("KVAR", "") == "rowbf":
                        k.mm(pr[:], self.ones_bf[:], ambf[:, hh * 4:(hh + 1) * 4, :])
                    else:
                        k.mm(pr[:], self.ones_f[:], am[:, hh * 4:(hh + 1) * 4, :])
                    rv = pr[:].re("p (h l) -> p h l", h=4)
                    k.tt(dmt[:, hh * 4:(hh + 1) * 4, :], rv,
                         V(carg, carg.h[:, t, d * 8 + hh * 4:d * 8 + hh * 4 + 4].unsqueeze(2).to_broadcast([128, 4, 128])),
                         ALU.add)
                    k.actf(ert[:, hh * 4:(hh + 1) * 4, :], rv, AF.Exp)
                if t == 0 and d == 0:
                    self.chk("E1b")
                k.ts(dmt[:], dmt[:], 30.0, ALU.min)
                if t == 0 and d == 0:
                    self.chk("E1c")
                k.actf(dmt[:], dmt[:], AF.Exp)
                if t == 0 and d == 0:
                    self.chk("E1d")
                k.tt(dmt[:], dmt[:], V(mk, mk.h[:, :].unsqueeze(1).to_broadcast([128, 8, 128])), ALU.mult)
                for g in range(2):
                    ps_ = slice(g * 64, (g + 1) * 64)
                    k.tt(cdec[d][t % 2][ps_, :, :], ert[ps_, g * 4:(g + 1) * 4, :],
                         V(xc, xc.h[ps_, 5, t * 128:(t + 1) * 128].unsqueeze(1).to_broadcast([64, 4, 128])), ALU.mult)
            if t == 0:
                self.chk("E1")
            k.tt(dm[0][:], dm[0][:], dm[1][:], ALU.add)
            pgs = [self.bank(), self.bank()]
            for g in range(2):
                ps_ = slice(g * 64, (g + 1) * 64)
                k.mm(pgs[g][:, 0:128], xc[ps_, 4, t * 128:(t + 1) * 128], xc[ps_, 5, t * 128:(t + 1) * 128])
            mt = MT[t % 2]
            for g in range(2):
                k.tt(mt[:, g * 4:(g + 1) * 4, :], dm[0][:, g * 4:(g + 1) * 4, :],
                     V(pgs[g], pgs[g].h[:, 0:128].unsqueeze(1).to_broadcast([128, 4, 128])), ALU.mult)
            if t == 0:
                self.chk("E2")
            pys = [self.bank(), self.bank()]
            for h in range(8):
                pair, e = h // 2, h % 2
                g, j = h // 4, h % 4
                out = pys[g][e * 64:(e + 1) * 64, (pair % 2) * 128:(pair % 2 + 1) * 128]
                gs = slice(g * 64, (g + 1) * 64)
                k.mm(out, xbt[:, t, h * 64:(h + 1) * 64], mt[:, h, :], start=True, stop=False)
                k.mm(out, sin_bf[0][gs, t, j * 64:(j + 1) * 64], cdec[0][t % 2][gs, j, :], start=False, stop=False)
                k.mm(out, sin_bf[1][gs, t, j * 64:(j + 1) * 64], cdec[1][t % 2][gs, j, :], start=False, stop=True)
            if t == 0:
                self.chk("E3")
            if t == 1:
                self.chk("E4")
            yt = ytmp[t % 2]
            k.tt(yt[:], xc[:, 0:4, t * 128:(t + 1) * 128],
                 V(dsum, dsum.h[:, :].unsqueeze(2).to_broadcast([128, 4, 128])), ALU.mult, E=k.pool)
            for g in range(2):
                k.tt(yt[:, g * 2:g * 2 + 2, :], pys[g][:, 0:256].re("p (c l) -> p c l", c=2), yt[:, g * 2:g * 2 + 2, :],
                     ALU.add)
            if "ssd_yraw" in self.debug:
                k.cp(self.yraw[:, :, t * 128:(t + 1) * 128], yt[:])
            k.tt(ygT[:, :, t * 128:(t + 1) * 128], yt[:], zs[:, :, t * 128:(t + 1) * 128], ALU.mult)
        if "ssd_yraw" in self.debug:
            self.dbg("ssd_yraw", self.yraw[:], [128, 4, TOK])
        k.pop()
        self.chk("E")
        yT = k.sb("ssd_yT", [128, 4, TOK], BF16)
        self.group_rmsnorm(ygT, yT, 4, 1.0 / 512, self.eps6, [self.pp(l, "ssd_norm", c, c + 1) for c in range(4)])
        self.dbg("ssd_yT", yT[:], [128, 4, TOK])
        self.branch_out(l, yT, 0)
        k.pop()

    def plan_gla(self, l):
        S = self.SEC
        self.win_slab(l, S["q"], 256)
        self.win_slab(l, S["k"], 256)
        self.win_slab(l, S["lr"], 32)
        for c0 in (0, 256):
            self.win_slab(l, S["v"] + c0, 256)
        for c0 in (0, 256):
            self.win_slab(l, S["g"] + c0, 256)
        self.plan_branch_out(l, "w_gla_o", 1)

    def gla_branch(self, l):
        k = self.k
        fl = self.flags
        k.push()
        og = k.sb("gla_og", [128, 4, TOK], F32)
        k.push()
        qd = [k.sb(f"gla_qd{d}", [128, 2, TOK], BF16) for d in range(2)]
        kd = [k.sb(f"gla_kd{d}", [128, 2, TOK], BF16) for d in range(2)]
        vtok = k.sb("gla_vtok", [128, NT, 512], BF16)
        sin_bf = [k.sb(f"gla_sin{d}", [128, NT, 256], BF16) for d in range(2)]
        etot = [k.sb(f"gla_etot{d}", [128, 2, NT], F32) for d in range(2)]
        gkw = k.sb("gla_gkw", [16, 2, 256], F32)
        gkw_bf = k.sb("gla_gkwbf", [16, 2, 256], BF16)
        lr_bf = [k.sb(f"gla_lr{d}", [16, TOK], BF16) for d in range(2)]
        nb = k.sb("gla_nb", [128, 4], F32)
        k.dma(k.sp, gkw[:], self.gkw_d[l].re("d r c -> r d c"))
        k.cp(gkw_bf[:], gkw[:], E=k.pool)
        k.ts(nb[:], self.pp(l, "gk_b"), -1.0, ALU.mult)
        k.push()
        self.rmask = k.sb("rmask", [128, TOK], F32)
        k.memset(self.rmask[:], 1.0, E=k.pool)
        k.memset(self.rmask[:, 0::128], 0.0, E=k.pool)
        qT = k.sb("gla_qT", [128, 2, TOK], BF16)
        kT = k.sb("gla_kT", [128, 2, TOK], BF16)
        T1 = k.sb("gla_T1", [128, 2, TOK], F32)
        T2 = k.sb("gla_T2", [128, 2, TOK], F32)
        T3 = k.sb("gla_T3", [128, 2, TOK], F32)
        wb = self.ws.next()
        for m in range(2):
            for h in range(2):
                pb = self.proj_fm(wb, m * 128, 128, h)
                k.actf(qT[:, m, h * 512:(h + 1) * 512], pb[:], AF.Copy, scale=0.125)
        wb = self.ws.next()
        for m in range(2):
            for h in range(2):
                pb = self.proj_fm(wb, m * 128, 128, h)
                k.cp(kT[:, m, h * 512:(h + 1) * 512], pb[:])
        wb = self.ws.next()
        for d in range(2):
            for h in range(2):
                pb = self.proj_fm(wb, d * 16, 16, h)
                k.cp(lr_bf[d][:, h * 512:(h + 1) * 512], pb[0:16, :], E=k.act)
        for d in range(2):
            for c in range(2):
                for h in range(2):
                    pb = self.bank()
                    k.mm(pb[:], gkw_bf[:, d, c * 128:(c + 1) * 128], lr_bf[d][:, h * 512:(h + 1) * 512])
                    k.actf(T1[:, c, h * 512:(h + 1) * 512], pb[:], AF.Exp, scale=-1.0, bias=nb[:, d * 2 + c:d * 2 + c + 1])
            k.actf(T1[:], T1[:], AF.Ln, bias=self.cst[:, 2:3])
            for c in range(2):
                k.scan(T2[:, c, :], self.rmask[:], T1[:, c, :], 0.0, ALU.mult, ALU.add)
            if d == 0:
                k.actf(T3[:], T2[:], AF.Exp, scale=-1.0 / 16)
                k.tt(qd[0][:], qT[:], T3[:], ALU.mult)
                k.cp(etot[0][:], T3[:, :, 127::128])
                k.actf(T3[:], T2[:], AF.Exp, scale=1.0 / 16)
                k.tt(kd[0][:], kT[:], T3[:], ALU.mult)
            else:
                k.actf(etot[1][:], T2[:, :, 127::128], AF.Exp, scale=-1.0 / 16)
                k.tt(T2[:], T2[:], T1[:], ALU.subtract)
                k.actf(T3[:], T2[:], AF.Exp, scale=1.0 / 16)
                k.tt(qd[1][:], qT[:], T3[:], ALU.mult)
                k.actf(T3[:], T2[:], AF.Exp, scale=-1.0 / 16)
                k.tt(kd[1][:], kT[:], T3[:], ALU.mult)
        k.pop()
        for s in range(2):
            wb = self.ws.next()
            for t in range(NT):
                pb = self.bank()
                for kc in range(8):
                    k.mm(pb[:, 0:256], self.hT[:, kc, t * 128:(t + 1) * 128], wb[:, kc, :], start=(kc == 0), stop=(kc == 7))
                k.cp(vtok[:, t, s * 256:(s + 1) * 256], pb[:, 0:256], E=(k.act if t % 2 else k.dve))
        k.push()
        ktok = k.sb("gla_ktok", [128, NT, 4, 128], BF16)
        for t in range(NT):
            pb = self.bank()
            pv = pb[:].bitcast(BF16)
            for d in range(2):
                for c in range(2):
                    j = d * 2 + c
                    k.tr(pv[:, j * 128:(j + 1) * 128], kd[d][:, c, t * 128:(t + 1) * 128], self.ident_bf[:], inc=(j == 3))
            k.cp(ktok[:, t, :, :], pv[:, 0:512].re("p (j q) -> p j q", j=4), E=(k.act if t % 2 else k.dve))
        S = [k.sb(f"gla_S{d}", [128, 256], F32) for d in range(2)]
        Sin = [k.sb(f"gla_Sin{d}", [128, 256], F32) for d in range(2)]
        Stmp = [k.sb(f"gla_Stmp{d}", [128, 256], F32) for d in range(2)]
        for d in range(2):
            k.dma(k.sp, Sin[d][:], self.st_gla_d[l, d])
        for step in range(NT):
            for d in range(2):
                t = step if d == 0 else NT - 1 - step
                if step > 0:
                    fcol = t if d == 0 else 8 + t
                    k.ts(Sin[d][:], S[d][:], fl[:, fcol:fcol + 1], ALU.mult)
                pb = self.bank()
                for h in range(4):
                    pair, e = h // 2, h % 2
                    k.mm(pb[e * 64:(e + 1) * 64, pair * 128:(pair + 1) * 128],
                         ktok[:, t, d * 2 + pair, e * 64:(e + 1) * 64], vtok[:, t, h * 128:(h + 1) * 128])
                etv = V(etot[d], etot[d].h[:, :, t].unsqueeze(2).to_broadcast([128, 2, 128]))
                if d == 0:
                    k.cp(sin_bf[0][:, t, :], Sin[0][:], E=k.act)
                    k.tt(Stmp[0][:], Sin[0][:], pb[:, 0:256], ALU.add)
                    k.tt(S[0][:].re("p (c v) -> p c v", c=2), Stmp[0][:].re("p (c v) -> p c v", c=2), etv, ALU.mult)
                else:
                    k.tt(Stmp[1][:].re("p (c v) -> p c v", c=2), Sin[1][:].re("p (c v) -> p c v", c=2), etv, ALU.mult)
                    k.cp(sin_bf[1][:, t, :], Stmp[1][:], E=k.act)
                    k.tt(S[1][:], Stmp[1][:], pb[:, 0:256], ALU.add)
                if (d == 0 and t % 2 == 1) or (d == 1 and t % 2 == 0):
                    k.dma(k.sp, self.ns_gla_d[l, t // 2, d], S[d][:])
        k.pop()
        k.push()
        amat = [[k.sb(f"gla_A{d}_{j}", [128, 4, 128], BF16) for j in range(2)] for d in range(2)]
        masks = (self.maskU, self.maskL)
        for t in range(NT):
            ts_ = slice(t * 128, (t + 1) * 128)
            for d in range(2):
                pas = [self.bank(), self.bank()]
                for h in range(4):
                    pair, e = h // 2, h % 2
                    es = slice(e * 64, (e + 1) * 64)
                    k.mm(pas[e][:, pair * 128:(pair + 1) * 128], kd[d][es, pair, ts_], qd[d][es, pair, ts_])
                mk = masks[d]
                for e in range(2):
                    k.tt(amat[d][t % 2][:, e::2, :], pas[e][:, 0:256].re("p (h q) -> p h q", h=2),
                         V(mk, mk.h[:, :].unsqueeze(1).to_broadcast([128, 2, 128])), ALU.mult)
            pos_ = [self.bank(), self.bank()]
            for h in range(4):
                pair, e = h // 2, h % 2
                es = slice(e * 64, (e + 1) * 64)
                out = pos_[e][:, pair * 128:(pair + 1) * 128]
                vt = vtok[:, t, h * 128:(h + 1) * 128]
                k.mm(out, vt, amat[0][t % 2][:, h, :], start=True, stop=False)
                k.mm(out, vt, amat[1][t % 2][:, h, :], start=False, stop=False)
                k.mm(out, sin_bf[0][es, t, pair * 128:(pair + 1) * 128], qd[0][es, pair, ts_], start=False, stop=False)
                k.mm(out, sin_bf[1][es, t, pair * 128:(pair + 1) * 128], qd[1][es, pair, ts_], start=False, stop=True)
            for e in range(2):
                k.cp(og[:, e::2, ts_], pos_[e][:, 0:256].re("p (h q) -> p h q", h=2), E=(k.act if (t + e) % 2 else k.dve))
        k.pop()
        self.dbg("gla_og", og[:], [128, 4, TOK])
        k.pop()
        gs = k.sb("gla_gs", [128, 4, TOK], BF16)
        yT = k.sb("gla_yT", [128, 4, TOK], BF16)
        for s in range(2):
            wb = self.ws.next()
            for m in range(2):
                for h in range(2):
                    pb = self.proj_fm(wb, m * 128, 128, h)
                    k.actf(gs[:, s * 2 + m, h * 512:(h + 1) * 512], pb[:], AF.Silu)
        k.push()
        sq = [k.sb(f"gla_sq{j}", [128, TOK], BF16) for j in range(2)]
        rstd = [k.sb(f"gla_rstd{j}", [128, TOK], F32) for j in range(2)]
        tmp = [k.sb(f"gla_tmp{j}", [128, TOK], F32) for j in range(2)]
        for h in range(4):
            sqh, rs, tp = sq[h % 2], rstd[h % 2], tmp[h % 2]
            k.actf(sqh[:], og[:, h, :], AF.Square)
            for hf in range(2):
                pb = self.bank()
                k.mm(pb[:], self.ones_bf[:], sqh[:, hf * 512:(hf + 1) * 512])
                k.actf(rs[:, hf * 512:(hf + 1) * 512], pb[:], AF.Ln, scale=1.0 / 128, bias=self.eps6)
            k.actf(rs[:], rs[:], AF.Exp, scale=-0.5)
            k.stt(tp[:], og[:, h, :], self.pp(l, "gla_norm"), rs[:], ALU.mult, ALU.mult)
            k.tt(yT[:, h, :], tp[:], gs[:, h, :], ALU.mult)
        k.pop()
        self.dbg("gla_yT", yT[:], [128, 4, TOK])
        self.branch_out(l, yT, 1)
        k.pop()

    ALPHA = float(np.exp(-0.5))

    def plan_rw(self, l):
        S = self.SEC
        for nm in ("rr", "rv"):
            for c0 in (0, 256):
                self.win_slab(l, S[nm] + c0, 256)
        self.win_slab(l, S["wlr"], 192)
        self.win_slab(l, S["glr"], 128)
        for c0 in (0, 256):
            self.win_slab(l, S["rk"] + c0, 256)
        self.plan_branch_out(l, "w_rw_o", 2)

    def rw_branch(self, l):
        k = self.k
        fl = self.flags
        A = self.ALPHA
        k.push()
        OT = k.sb("rw_OT", [128, 4, TOK], BF16)
        bonus = k.sb("rw_bonus", [128, 4, TOK], BF16)
        sgl = k.sb("rw_sgl", [128, TOK], BF16)
        k.push()
        rT = k.sb("rw_rT", [128, 4, TOK], BF16)
        kapT = k.sb("rw_kapT", [128, 4, TOK], BF16)
        kpT = k.sb("rw_kpT", [128, 4, TOK], BF16)
        nbT = k.sb("rw_nbT", [128, 4, TOK], BF16)
        thT = k.sb("rw_thT", [128, TOK], BF16)
        vtok = k.sb("rw_vtok", [128, NT, 512], BF16)
        w2bf = k.sb("rw_w2bf", [128, 512], BF16)
        k.push()
        a2bf = k.sb("rw_a2bf", [64, 512], BF16)
        k.push()
        w2f = k.sb("rw_w2f", [128, 512], F32)
        a2f = k.sb("rw_a2f", [64, 512], F32)
        k.dma(k.sp, w2f[:], self.rw_w2_d[l].re("d r c -> (d r) c"))
        k.dma(k.sp, a2f[:], self.rw_a2_d[l])
        k.cp(w2bf[:], w2f[:], E=k.pool)
        k.cp(a2bf[:], a2f[:], E=k.pool)
        k.pop()
        kT = k.sb("rw_kT", [128, 2, TOK], BF16)
        vT = k.sb("rw_vT", [128, 4, TOK], BF16)
        aT = k.sb("rw_aT", [128, TOK], BF16)
        wl = k.sb("rw_wl", [128, TOK], BF16)
        alT = k.sb("rw_alT", [64, TOK], BF16)
        glT = k.sb("rw_glT", [128, TOK], BF16)
        hb = k.sb("rw_hb", [128, 4, 258], BF16)
        mt = k.sb("rw_mt", [128, 4, 256], BF16)
        omu = k.sb("rw_omu", [128, 15], F32)
        hmu = k.sb("rw_hmu", [128, 15], F32)
        omka = k.sb("rw_omka", [128, 4], F32)
        k.ts(omu[:], self.pp(l, "rw_mu"), -1.0, ALU.mult, 1.0, ALU.add)
        k.ts(hmu[:], self.pp(l, "rw_mu"), 0.5, ALU.mult)
        k.ts(omka[:], self.pp(l, "rw_ka"), -1.0, ALU.mult, 1.0, ALU.add)
        k.memset(hb[:, :, 0:1], 0.0, E=k.pool)
        k.memset(hb[:, :, 257:258], 0.0, E=k.pool)

        def mix(wb, m0, mc, mucol, dst):
            b = hb
            t = mt
            for h in range(2):
                pb = self.proj_fm(wb, m0, mc, h)
                k.cp(b[0:mc, 2 * h:2 * h + 2, 1:257], pb[0:mc, :].re("p (q t) -> p q t", q=2),
                     E=(k.act if h == 0 else k.dve))
            k.tt(b[0:mc, 1:4, 0], b[0:mc, 0:3, 256], V(fl, fl.h[0:mc, 2:8:2]), ALU.mult)
            k.tt(b[0:mc, 0:3, 257], b[0:mc, 1:4, 1], V(fl, fl.h[0:mc, 9:15:2]), ALU.mult)
            k.tt(t[0:mc], b[0:mc, :, 0:256], b[0:mc, :, 2:258], ALU.add)
            k.ts(t[0:mc], t[0:mc], hmu[0:mc, mucol:mucol + 1], ALU.mult)
            k.stt(dst.re("p (q t) -> p q t", q=4), b[0:mc, :, 1:257], omu[0:mc, mucol:mucol + 1], t[0:mc],
                  ALU.mult, ALU.add)

        for bi, dstT in ((0, rT), (2, vT)):
            for s_ in range(2):
                wb = self.ws.next()
                for m in range(2):
                    c = s_ * 2 + m
                    mix(wb, m * 128, 128, bi * 4 + c, dstT[:, c, :])
        wb = self.ws.next()
        mix(wb, 0, 128, 12, wl[:, :])
        mix(wb, 128, 64, 13, alT[:, :])
        wb = self.ws.next()
        mix(wb, 0, 128, 14, glT[:, :])
        k.actf(thT[:], wl[:], AF.Tanh)
        k.actf(sgl[:], glT[:], AF.Sigmoid)
        f1 = k.sb("rw_f1", [128, TOK], F32)
        f2 = k.sb("rw_f2", [128, TOK], F32)
        b1 = k.sb("rw_b1", [128, TOK], BF16)
        for s_ in range(2):
            wb = self.ws.next()
            for m in range(2):
                mix(wb, m * 128, 128, 4 + s_ * 2 + m, kT[:, m, :])
            for m in range(2):
                c = s_ * 2 + m
                for h in range(2):
                    pb = self.bank()
                    k.mm(pb[:], a2bf[:, c * 128:(c + 1) * 128], alT[:, h * 512:(h + 1) * 512])
                    k.actf(aT[:, h * 512:(h + 1) * 512], pb[:], AF.Sigmoid, bias=self.pp(l, "rw_a0", c, c + 1))
                k.ts(f1[:], kT[:, m, :], self.pp(l, "rw_kk", c, c + 1), ALU.mult)
                k.actf(b1[:], f1[:], AF.Square)
                for h in range(2):
                    pb = self.bank()
                    k.mm(pb[:], self.blockones[:], b1[:, h * 512:(h + 1) * 512])
                    k.ts(f2[:, h * 512:(h + 1) * 512], pb[:], 1e-24, ALU.max)
                k.actf(f2[:], f2[:], AF.Sqrt)
                k.recip(f2[:], f2[:])
                k.tt(kapT[:, c, :], f1[:], f2[:], ALU.mult)
                k.stt(nbT[:, c, :], kapT[:, c, :], -1.0, aT[:], ALU.mult, ALU.mult)
                k.ts(f1[:], aT[:], self.pp(l, "rw_ka", c, c + 1), ALU.mult, omka[:, c:c + 1], ALU.add)
                k.tt(kpT[:, c, :], kT[:, m, :], f1[:], ALU.mult)
                k.stt(b1[:], rT[:, c, :], self.pp(l, "rw_rk", c, c + 1), kpT[:, c, :], ALU.mult, ALU.mult)
                for h in range(2):
                    pb = self.bank()
                    k.mm(pb[:], self.blockones[:], b1[:, h * 512:(h + 1) * 512])
                    k.tt(bonus[:, c, h * 512:(h + 1) * 512], pb[:], vT[:, c, h * 512:(h + 1) * 512], ALU.mult)
        for t in range(NT):
            pb = self.bank()
            pv = pb[:].bitcast(BF16)
            for c in range(4):
                k.tr(pv[:, c * 128:(c + 1) * 128], vT[:, c, t * 128:(t + 1) * 128], self.ident_bf[:], inc=(c == 3))
            k.cp(vtok[:, t, :], pv[:, 0:512], E=(k.act if t % 2 else k.dve))
        k.pop()
        self.dbg("rw_kapT", kapT[:], [128, 4, TOK])
        self.dbg("rw_kpT", kpT[:], [128, 4, TOK])
        self.dbg("rw_rT", rT[:], [128, 4, TOK])
        self.dbg("rw_nbT", nbT[:], [128, 4, TOK])
        k.push()
        sig = k.sb("rw_sig", [128, 4, 128], F32)
        Pc = k.sb("rw_P", [128, 4, 128], F32)
        Cx = k.sb("rw_Cx", [128, 4, 128], F32)
        Ea = k.sb("rw_Ea", [128, 4, 128], BF16)
        Eb = k.sb("rw_Eb", [128, 4, 128], BF16)
        Ec = k.sb("rw_Ec", [128, 4, 128], BF16)
        gam = k.sb("rw_gam", [128, 4], F32)
        RKt = k.sb("rw_RKt", [128, 4, 2, 128], BF16)
        kt = k.sb("rw_kt", [128, 4, 128], BF16)
        nbt = k.sb("rw_nbt", [128, 4, 128], BF16)
        tok = k.sb("rw_tok", [128, 3, 512], BF16)
        M1 = k.sb("rw_M1", [128, 8, 2, 128], BF16)
        M2 = k.sb("rw_M2", [128, 8, 2, 128], BF16)
        XY = [k.sb("rw_X0", [128, 4, 128], BF16),
              [k.sb(f"rw_XM{j}", [128, 4, 128], BF16) for j in range(2)],
              [k.sb(f"rw_ZT{j}", [128, 4, 128], BF16) for j in range(2)],
              [k.sb(f"rw_Tc{j}", [128, 4, 128], BF16) for j in range(2)]]
        TT = k.sb("rw_TT", [128, 8, 128], BF16)
        AV = k.sb("rw_AV", [128, 512], BF16)
        U = k.sb("rw_U", [128, 512], BF16)
        WT = k.sb("rw_WT", [128, 4, 128], BF16)
        Et = k.sb("rw_E", [128, 512], BF16)
        S = k.sb("rw_S", [128, 256], F32)
        Sin = k.sb("rw_Sin", [128, 256], F32)
        Stmp = k.sb("rw_Stmp", [128, 256], F32)
        Sbf = k.sb("rw_Sbf", [128, 256], BF16)
        nev = [0]

        def evac(dst, src):
            E = k.act if nev[0] % 2 else k.dve
            nev[0] += 1
            k.cp(dst, src, E=E)

        for d in range(2):
            m2 = self.mask2[d]
            mx = self.maskSL if d == 0 else self.maskSU
            k.dma(k.sp, Sin[:], self.st_rw_d[l, d])
            for step in range(NT):
                t = step if d == 0 else NT - 1 - step
                ts_ = slice(t * 128, (t + 1) * 128)
                ds_ = slice(d * 64, (d + 1) * 64)
                pb = self.bank()
                for c in range(4):
                    k.mm(pb[:, c * 128:(c + 1) * 128], w2bf[ds_, c * 128:(c + 1) * 128], thT[ds_, ts_])
                for c in range(4):
                    k.actf(sig[:, c, :], pb[:, c * 128:(c + 1) * 128], AF.Sigmoid,
                           bias=self.pp(l, "rw_w0", d * 4 + c, d * 4 + c + 1))
                for c in range(4):
                    k.scan(Pc[:, c, :], self.ones_f[:], sig[:, c, :], 0.0, ALU.mult, ALU.add)
                if d == 0:
                    k.tt(Cx[:], Pc[:], sig[:], ALU.subtract)
                    cin, cex = Pc, Cx
                    k.actf(gam[:], Pc[:, :, 127], AF.Exp, scale=-A)
                else:
                    k.actf(gam[:], Pc[:, :, 127], AF.Exp, scale=-A)
                    k.tt(Cx[:], V(Pc, Pc.h[:, :, 127:128].to_broadcast([128, 4, 128])), Pc[:], ALU.subtract)
                    k.tt(Pc[:], Cx[:], sig[:], ALU.add)
                    cin, cex = Pc, Cx
                k.actf(Ea[:], cin[:], AF.Exp, scale=-A)
                k.actf(Eb[:], cex[:], AF.Exp, scale=-A)
                k.actf(Ec[:], cin[:], AF.Exp, scale=A)
                k.tt(RKt[:, :, 0, :], rT[:, :, ts_], Ea[:], ALU.mult)
                k.tt(RKt[:, :, 1, :], kapT[:, :, ts_], Eb[:], ALU.mult)
                k.tt(kt[:], kpT[:, :, ts_], Ec[:], ALU.mult)
                k.tt(nbt[:], nbT[:, :, ts_], Ec[:], ALU.mult)
                pb = self.bank()
                pv = pb[:].bitcast(BF16)
                for c in range(4):
                    k.tr(pv[:, c * 128:(c + 1) * 128], RKt[:, c, 1, :], self.ident_bf[:], inc=False)
                for c in range(4):
                    k.tr(pv[:, 512 + c * 128:512 + (c + 1) * 128], kt[:, c, :], self.ident_bf[:], inc=(c == 3))
                evac(tok[:, 0:2, :], pv[:, :].re("p (a q) -> p a q", a=2))
                pb = self.bank()
                pv = pb[:].bitcast(BF16)
                for c in range(4):
                    k.tr(pv[:, c * 128:(c + 1) * 128], nbt[:, c, :], self.ident_bf[:], inc=(c == 3))
                evac(tok[:, 2, :], pv[:, 0:512])
                for gq in range(2):
                    bA = [self.bank(), self.bank()]
                    bB = [self.bank(), self.bank()]
                    b5 = [self.bank(), self.bank()]
                    for hh in range(4):
                        h = gq * 4 + hh
                        c, e = h // 2, h % 2
                        cc = hh // 2
                        es = slice(e * 64, (e + 1) * 64)
                        cs = slice(cc * 256, cc * 256 + 256)
                        k.mm(bA[e][:, cs], kt[es, c, :], RKt[es, c, :, :])
                        k.mm(bB[e][:, cs], nbt[es, c, :], RKt[es, c, :, :])
                        k.mm(b5[e][:, cc * 128:(cc + 1) * 128], RKt[es, c, 1, :], nbt[es, c, :])
                    m2v = V(m2, m2.h[:, :, :].unsqueeze(1).to_broadcast([128, 2, 2, 128]))
                    X0 = XY[0]
                    for e in range(2):
                        hs = slice(gq * 4 + e, gq * 4 + 4, 2)
                        k.tt(M1[:, hs, :, :], bA[e][:].re("p (h a t) -> p h a t", h=2, a=2), m2v, ALU.mult)
                        k.tt(M2[:, hs, :, :], bB[e][:].re("p (h a t) -> p h a t", h=2, a=2), m2v, ALU.mult)
                        k.tt(X0[:, e::2, :], b5[e][:, 0:256].re("p (h t) -> p h t", h=2),
                             V(mx, mx.h[:, :].unsqueeze(1).to_broadcast([128, 2, 128])), ALU.mult)
                    Y0 = M2[:, gq * 4:gq * 4 + 4, 1, :]
                    lmx = self.lvlmask[d]
                    l1t = self.lvlmask[1 - d]
                    Tc = XY[3][0]
                    k.tt(Tc[:], Y0, V(l1t, l1t.h[:, 0, :].unsqueeze(1).to_broadcast([128, 4, 128])), ALU.mult)
                    k.tt(Tc[:], Tc[:], V(self.ident_bf, self.ident_bf.h[:, :].unsqueeze(1).to_broadcast([128, 4, 128])),
                         ALU.add)
                    for lvl in range(1, 7):
                        XM = XY[1][lvl % 2]
                        k.tt(XM[:], X0[:], V(lmx, lmx.h[:, lvl, :].unsqueeze(1).to_broadcast([128, 4, 128])), ALU.mult,
                             E=k.pool)
                        bz = self.bank()
                        for hh in range(4):
                            k.mm(bz[:, hh * 128:(hh + 1) * 128], XM[:, hh, :], Tc[:, hh, :])
                        bt = self.bank()
                        btv = bt[:].bitcast(BF16)
                        for hh in range(4):
                            k.tr(btv[:, hh * 128:(hh + 1) * 128], Tc[:, hh, :], self.ident_bf[:], inc=(hh == 3))
                        Zs = XY[2][0]
                        Ts = XY[2][1]
                        evac(Zs[:], bz[:].re("p (h t) -> p h t", h=4))
                        evac(Ts[:], btv[:, 0:512].re("p (h t) -> p h t", h=4))
                        bp = self.bank()
                        for hh in range(4):
                            o_ = bp[:, hh * 128:(hh + 1) * 128]
                            k.mm(o_, self.ident_bf[:], Tc[:, hh, :], start=True, stop=False)
                            k.mm(o_, Ts[:, hh, :], Zs[:, hh, :], start=False, stop=True)
                        if lvl < 6:
                            Tn = XY[3][lvl % 2]
                            evac(Tn[:], bp[:].re("p (h t) -> p h t", h=4))
                            Tc = Tn
                        else:
                            evac(TT[:, gq * 4:gq * 4 + 4, :], bp[:].re("p (h t) -> p h t", h=4))
                pb = self.bank()
                for h in range(8):
                    k.mm(pb[:, h * 64:(h + 1) * 64], M1[:, h, 1, :], vtok[:, t, h * 64:(h + 1) * 64])
                evac(AV[:], pb[:])
                pb = self.bank()
                for h in range(8):
                    k.mm(pb[:, h * 64:(h + 1) * 64], TT[:, h, :], AV[:, h * 64:(h + 1) * 64])
                evac(U[:], pb[:])
                pb = self.bank()
                for h in range(8):
                    c, e = h // 2, h % 2
                    k.mm(pb[e * 64:(e + 1) * 64, c * 128:(c + 1) * 128], tok[:, 0, h * 64:(h + 1) * 64], TT[:, h, :])
                evac(WT[:], pb[:].re("p (c t) -> p c t", c=4))
                if step > 0:
                    fcol = t if d == 0 else 8 + t
                    k.ts(Sin[:], S[:], fl[:, fcol:fcol + 1], ALU.mult)
                k.cp(Sbf[:], Sin[:], E=k.act)
                pbe = [self.bank(), self.bank()]
                for h in range(8):
                    c, e = h // 2, h % 2
                    es = slice(e * 64, (e + 1) * 64)
                    k.mm(pbe[e][:, c * 64:(c + 1) * 64], WT[es, c, :], Sbf[es, c * 64:(c + 1) * 64])
                for e in range(2):
                    k.tt(Et[:].re("p (c e v) -> p c e v", c=4, e=2)[:, :, e, :],
                         pbe[e][:, 0:256].re("p (c v) -> p c v", c=4),
                         U[:].re("p (c e v) -> p c e v", c=4, e=2)[:, :, e, :], ALU.add)
                if self.stop == "R1":
                    for nm, tl_, shp in (("rwd_tok", tok, [128, 3, 512]), ("rwd_E", Et, [128, 512]), ("rwd_U", U, [128, 512]),
                                         ("rwd_TT", TT, [128, 8, 128]), ("rwd_M1", M1, [128, 8, 2, 128]),
                                         ("rwd_M2", M2, [128, 8, 2, 128]), ("rwd_RKt", RKt, [128, 4, 2, 128]),
                                         ("rwd_kt", kt, [128, 4, 128]), ("rwd_nbt", nbt, [128, 4, 128]),
                                         ("rwd_sig", sig, [128, 4, 128]), ("rwd_P", Pc, [128, 4, 128]),
                                         ("rwd_WT", WT, [128, 4, 128]), ("rwd_AV", AV, [128, 512])):
                        self.debug.add(nm)
                        self.dbg(nm, tl_[:], shp)
                    self.chk("R1")
                pos_ = [self.bank(), self.bank()]
                for h in range(8):
                    c, e = h // 2, h % 2
                    es = slice(e * 64, (e + 1) * 64)
                    o_ = pos_[e][es, c * 128:(c + 1) * 128]
                    k.mm(o_, Sbf[es, c * 64:(c + 1) * 64], RKt[es, c, 0, :], start=True, stop=False)
                    k.mm(o_, vtok[:, t, h * 64:(h + 1) * 64], M1[:, h, 0, :], start=False, stop=False)
                    k.mm(o_, Et[:, h * 64:(h + 1) * 64], M2[:, h, 0, :], start=False, stop=True)
                for e in range(2):
                    es = slice(e * 64, (e + 1) * 64)
                    if d == 0:
                        evac(OT[es, :, ts_], pos_[e][es, :].re("p (c t) -> p c t", c=4))
                    else:
                        k.tt(OT[es, :, ts_], pos_[e][es, :].re("p (c t) -> p c t", c=4), OT[es, :, ts_], ALU.add)
                pb = self.bank()
                for h in range(8):
                    c, e = h // 2, h % 2
                    o_ = pb[e * 64:(e + 1) * 64, c * 64:(c + 1) * 64]
                    k.mm(o_, tok[:, 1, h * 64:(h + 1) * 64], vtok[:, t, h * 64:(h + 1) * 64], start=True, stop=False)
                    k.mm(o_, tok[:, 2, h * 64:(h + 1) * 64], Et[:, h * 64:(h + 1) * 64], start=False, stop=True)
                k.tt(Stmp[:], Sin[:], pb[:, 0:256], ALU.add)
                k.tt(S[:].re("p (c v) -> p c v", c=4), Stmp[:].re("p (c v) -> p c v", c=4),
                     V(gam, gam.h[:, :].unsqueeze(2).to_broadcast([128, 4, 64])), ALU.mult)
                if (d == 0 and t % 2 == 1) or (d == 1 and t % 2 == 0):
                    k.dma(k.sp, self.ns_rw_d[l, t // 2, d], S[:])
        k.pop()
        k.pop()
        self.dbg("rw_OT", OT[:], [128, 4, TOK])
        yT = k.sb("rw_yT", [128, 4, TOK], BF16)
        g2f = k.sb("rw_g2f", [128, 512], F32)
        g2bf = k.sb("rw_g2bf", [128, 512], BF16)
        k.dma(k.sp, g2f[:], self.rw_g2_d[l])
        k.cp(g2bf[:], g2f[:], E=k.pool)
        k.push()
        dd = k.sb("rw_dd", [128, TOK], F32)
        sq = k.sb("rw_sq", [128, TOK], BF16)
        rs = k.sb("rw_rs", [128, TOK], F32)
        for c in range(4):
            for h in range(2):
                hs = slice(h * 512, (h + 1) * 512)
                pb = self.bank()
                k.mm(pb[:], self.blockmean[:], OT[:, c, hs])
                k.tt(dd[:, hs], OT[:, c, hs], pb[:], ALU.subtract)
            k.actf(sq[:], dd[:], AF.Square)
            for h in range(2):
                hs = slice(h * 512, (h + 1) * 512)
                pb = self.bank()
                k.mm(pb[:], self.blockmean[:], sq[:, hs])
                k.actf(rs[:, hs], pb[:], AF.Ln, bias=self.epsgn)
            k.actf(rs[:], rs[:], AF.Exp, scale=-0.5)
            k.tt(dd[:], dd[:], rs[:], ALU.mult)
            k.ts(dd[:], dd[:], self.pp(l, "rw_ln_w", c, c + 1), ALU.mult, self.pp(l, "rw_ln_b", c, c + 1), ALU.add)
            k.tt(dd[:], dd[:], bonus[:, c, :], ALU.add)
            for h in range(2):
                hs = slice(h * 512, (h + 1) * 512)
                pb = self.bank()
                k.mm(pb[:], g2bf[:, c * 128:(c + 1) * 128], sgl[:, hs])
                k.tt(yT[:, c, hs], pb[:], dd[:, hs], ALU.mult)
        k.pop()
        self.dbg("rw_yT", yT[:], [128, 4, TOK])
        self.branch_out(l, yT, 2)
        k.pop()

    def group_rmsnorm(self, src, dst, nchunk, inv_n, eps, gains):
        k = self.k
        k.push()
        sq = k.sb("grn_sq", [128, nchunk, TOK], BF16)
        rstd = k.sb("grn_rstd", [128, TOK], F32)
        for c in range(nchunk):
            k.actf(sq[:, c, :], src[:, c, :], AF.Square)
        for h in range(2):
            pb = self.bank()
            for c in range(nchunk):
                k.mm(pb[:], self.ones_bf[:], sq[:, c, h * 512:(h + 1) * 512], start=(c == 0), stop=(c == nchunk - 1))
            k.actf(rstd[:, h * 512:(h + 1) * 512], pb[:], AF.Ln, scale=inv_n, bias=eps)
        k.actf(rstd[:], rstd[:], AF.Exp, scale=-0.5)
        for c in range(nchunk):
            k.stt(dst[:, c, :], src[:, c, :], gains[c], rstd[:], ALU.mult, ALU.mult)
        k.pop()


_PROG = {}


def get_prog(debug=()):
    key = tuple(sorted(debug))
    if key not in _PROG:
        p1 = Prog(debug)
        needed = p1.k.needed
        if os.environ.get("KALLINC", ""):
            needed = {E.name: set(range(1, E.cnt + 2)) for E in p1.k.engs}
        _PROG[key] = Prog(debug, needed)
    return _PROG[key]


def make_in_maps(inp):
    inp = {k_: np.asarray(v) for k_, v in inp.items()}
    pp = np.stack([pack_params(inp, l) for l in range(DEPTH)], axis=0)
    gp = _cm(inp["final_norm"])
    ti = np.arange(128)[:, None]
    si = np.arange(128)[None, :]
    lm = np.zeros((2, 128, 7, 128), np.float32)
    for lvl in range(7):
        m_ = 1 << lvl
        msk = ((ti // (2 * m_)) == (si // (2 * m_))) & ((ti % (2 * m_)) >= m_) & ((si % (2 * m_)) < m_)
        lm[0, :, lvl, :] = msk
        lm[1, :, lvl, :] = msk.T
    shared = {"pp": pp, "gp": gp, "lvlmask": lm}
    shared["gla_gk_w"] = np.ascontiguousarray(inp["gla_gk_w"], dtype=np.float32)
    for nm in ("rw_w2", "rw_a2", "rw_g2"):
        shared[nm] = np.ascontiguousarray(inp[nm], dtype=np.float32)
    for name in ["w_ada", "ffn_gate", "ffn_up", "ffn_down", "w_in", "w_ssd_o", "w_gla_o", "w_rw_o", "w_out"]:
        shared[name] = np.ascontiguousarray(inp[name], dtype=np.float32)
    maps = []
    for core in range(8):
        m = dict(shared)
        flags = np.zeros((128, 32), np.float32)
        if core < 4:
            x = inp["x_prompt"][4 * core:4 * core + 4].reshape(TOK, D)
            cond = inp["c_ctx"]
            cf = np.array([0, 1, 0, 1, 0, 1, 0, 1], np.float32)
            cb = np.array([1, 0, 1, 0, 1, 0, 1, 0], np.float32)
            posf = 0.0
        else:
            x = inp["x_sample"][core - 4]
            cond = inp["c"][core - 4]
            cf = np.array([0, 1, 1, 1, 1, 1, 1, 1], np.float32)
            cb = np.array([1, 1, 1, 1, 1, 1, 1, 0], np.float32)
            posf = 1.0
        flags[:, 0:8] = cf[None]
        flags[:, 8:16] = cb[None]
        flags[:, 16] = posf
        if core < 4:
            st_ssd = np.zeros((DEPTH, 2, 128, 256), np.float32)
        else:
            ss = inp["state_ssd"][core - 4]
            st_ssd = np.ascontiguousarray(
                ss.reshape(DEPTH, 2, 2, 4, 64, 64).transpose(0, 1, 2, 5, 3, 4).reshape(DEPTH, 2, 128, 256))
        m["st_ssd"] = st_ssd
        if core < 4:
            st_gla = np.zeros((DEPTH, 2, 128, 256), np.float32)
        else:
            sg = inp["state_gla"][core - 4]
            st_gla = np.ascontiguousarray(
                sg.reshape(DEPTH, 2, 2, 2, 64, 128).transpose(0, 1, 3, 4, 2, 5).reshape(DEPTH, 2, 128, 256))
        m["st_gla"] = st_gla
        if core < 4:
            st_rw = np.zeros((DEPTH, 2, 128, 256), np.float32)
        else:
            sr = inp["state_rwkv"][core - 4]
            st_rw = np.ascontiguousarray(
                sr.reshape(DEPTH, 2, 4, 2, 64, 64).transpose(0, 1, 3, 5, 2, 4).reshape(DEPTH, 2, 128, 256))
        m["st_rw"] = st_rw
        m["xT"] = np.ascontiguousarray(x.T, dtype=np.float32)
        m["cond"] = _cm(cond)
        m["flags"] = flags
        maps.append(m)
    return maps


def run(inp, debug=(), trace=False):
    prog = get_prog(debug)
    maps = make_in_maps(inp)
    res = run_bass_kernel_spmd(prog.k.nc, maps, core_ids=list(range(8)), trace=trace)
    return prog, res


def kernel(**inputs):
    prog, res = run(inputs)
    r = res.results
    y_prompt = np.zeros((16, 256, D), np.float32)
    y_sample = np.zeros((4, 1024, D), np.float32)
    for core in range(8):
        y = np.ascontiguousarray(r[core]["yT"].T)
        if core < 4:
            y_prompt[4 * core:4 * core + 4] = y.reshape(4, 256, D)
        else:
            y_sample[core - 4] = y
    ns_ssd = np.zeros((16, DEPTH, 2, 8, 64, 64), np.float32)
    for core in range(4):
        raw = r[core]["ns_ssd"]
        v = raw.reshape(DEPTH, 4, 2, 2, 64, 4, 64).transpose(1, 0, 2, 3, 5, 6, 4)
        ns_ssd[4 * core:4 * core + 4] = v.reshape(4, DEPTH, 2, 8, 64, 64)
    ns_gla = np.zeros((16, DEPTH, 2, 4, 64, 128), np.float32)
    for core in range(4):
        raw = r[core]["ns_gla"]
        v = raw.reshape(DEPTH, 4, 2, 2, 64, 2, 128).transpose(1, 0, 2, 5, 3, 4, 6)
        ns_gla[4 * core:4 * core + 4] = v.reshape(4, DEPTH, 2, 4, 64, 128)
    ns_rw = np.zeros((16, DEPTH, 2, 8, 64, 64), np.float32)
    for core in range(4):
        raw = r[core]["ns_rw"]
        v = raw.reshape(DEPTH, 4, 2, 2, 64, 4, 64).transpose(1, 0, 2, 5, 3, 6, 4)
        ns_rw[4 * core:4 * core + 4] = v.reshape(4, DEPTH, 2, 8, 64, 64)
    return (y_prompt, y_sample, ns_ssd, ns_gla, ns_rw)
```

```python
import os
import numpy as np
from contextlib import ExitStack
import concourse.bass as bass
import concourse.mybir as mybir
from concourse.bass_utils import run_bass_kernel_spmd

F32 = mybir.dt.float32
BF16 = mybir.dt.bfloat16
I32 = mybir.dt.int32
ALU = mybir.AluOpType
AF = mybir.ActivationFunctionType

D = 1024
TOK = 1024
NT = 8
DFF = 2816
FC = 22
DEPTH = 2
NIN = 7792
PI = float(np.pi)


class V:
    __slots__ = ("t", "ap")

    def __init__(self, t, ap):
        self.t = t
        self.ap = ap

    def __getitem__(self, idx):
        return V(self.t, self.ap[idx])

    def re(self, s, **kw):
        return V(self.t, self.ap.rearrange(s, **kw))

    def bc(self, shape):
        return V(self.t, self.ap.to_broadcast(list(shape)))

    def bitcast(self, dt):
        return V(self.t, self.ap.bitcast(dt))

    @property
    def shape(self):
        return self.ap.shape


class Tile:
    __slots__ = ("h", "name", "w", "r", "dsem", "dcnt", "psum")

    def __init__(self, h, name, r0=None):
        self.h = h
        self.name = name
        self.psum = False
        self.w = None
        self.r = dict(r0) if r0 else {}
        self.dsem = None
        self.dcnt = 0

    def __getitem__(self, idx):
        return V(self, self.h[idx])


class Eng:
    def __init__(self, name, e):
        self.name = name
        self.e = e
        self.sem = None
        self.cnt = 0
        self.val = 0
        self.ord2val = {}
        self.seen = {}


class K:
    def __init__(self, needed=None):
        self.record = needed is None
        self.needed = {} if needed is None else needed
        self.nc = bass.Bass("TRN2", target_bir_lowering=False)
        self.es = ExitStack()
        self.scopes = [self.es]
        nc = self.nc
        self.pe = Eng("pe", nc.tensor)
        self.act = Eng("act", nc.scalar)
        self.dve = Eng("dve", nc.vector)
        self.pool = Eng("pool", nc.gpsimd)
        self.sp = Eng("sp", nc.sync)
        self.engs = [self.pe, self.act, self.dve, self.pool, self.sp]
        self.nsem = 0
        self.dsem_free = []
        for E in self.engs:
            self._newsem(E)
        self.out_waits = []
        self.nincs = 0
        self.all_dsem = {}
        self.ntile = 0
        self.barrier = {}
        self.scope_tiles = [[]]
        self.ninst = 0

    def _sem(self, name):
        self.nsem += 1
        return self.es.enter_context(self.nc.semaphore(name))

    def _newsem(self, E):
        E.sem = self._sem(f"s_{E.name}_{self.nsem}")
        E.cnt = 0

    def dram(self, name, shape, dt, kind):
        return V(None, self.nc.dram_tensor(name, list(shape), dt, kind=kind).ap())

    def sb(self, name, shape, dt=F32):
        self.ntile += 1
        h = self.scopes[-1].enter_context(self.nc.sbuf_tensor(f"{name}_{self.ntile}", list(shape), dt))
        t = Tile(h, name, self.barrier)
        self.scope_tiles[-1].append(t)
        return t

    def ps(self, name, shape, dt=F32):
        self.ntile += 1
        h = self.es.enter_context(self.nc.psum_tensor(f"{name}_{self.ntile}", list(shape), dt))
        t = Tile(h, name)
        t.psum = True
        return t

    def push(self):
        es = ExitStack()
        self.scopes.append(es)
        self.scope_tiles.append([])

    def pop(self):
        for t in self.scope_tiles.pop():
            if t.w is not None:
                s, v = t.w
                if self.barrier.get(s, 0) < v:
                    self.barrier[s] = v
            for s, v in t.r.items():
                if self.barrier.get(s, 0) < v:
                    self.barrier[s] = v
            if t.dsem is not None:
                self.dsem_free.append((t.dsem, t.dcnt))
        self.scopes.pop().close()

    def _wait(self, E, key, n):
        if E.seen.get(key, 0) >= n:
            return
        E.seen[key] = n
        if isinstance(key, Eng):
            if self.record:
                self.needed.setdefault(key.name, set()).add(n)
                return
            E.e.wait_ge(key.sem, key.ord2val[n])
        else:
            E.e.wait_ge(key, n)

    def _deps(self, E, reads, writes):
        waits = {}

        def need(s, v):
            if waits.get(s, 0) < v:
                waits[s] = v

        for t in reads:
            if t.w is not None:
                need(*t.w)
            if t.psum:
                for s, v in t.r.items():
                    if s is not E:
                        need(s, v)
        strict = E is self.pool or (os.environ.get("KSTRICT", "") != "")
        for t in writes:
            if t.w is not None and (strict or t.w[0] is not E):
                need(*t.w)
            for s, v in t.r.items():
                if strict or s is not E:
                    need(s, v)
        for s, v in waits.items():
            self._wait(E, s, v)

    def emit(self, E, fn, reads, writes, inc=True):
        reads = [x.t for x in reads if x is not None and x.t is not None]
        writes = [x.t for x in writes if x is not None and x.t is not None]
        self._deps(E, reads, writes)
        ins = fn()
        self.ninst += 1
        if inc:
            E.cnt += 1
            cid = E.cnt
            if (not self.record) and cid in self.needed.get(E.name, ()):
                E.val += 1
                ins.then_inc(E.sem, 1)
                E.ord2val[cid] = E.val
                self.nincs += 1
        else:
            cid = E.cnt + 1
        for t in reads:
            t.r[E] = cid
        for t in writes:
            t.w = (E, cid)
            t.r = {}
        return ins

    def dma(self, Q, out, in_, **kw):
        reads = [in_.t] if in_.t is not None else []
        writes = [out.t] if out.t is not None else []
        self._deps(Q, reads, writes)
        tl = out.t if out.t is not None else in_.t
        if tl.dsem is None:
            if self.dsem_free:
                tl.dsem, tl.dcnt = self.dsem_free.pop()
                self._wait(Q, tl.dsem, tl.dcnt)
            else:
                tl.dsem = self._sem(f"d_{tl.name}_{self.nsem}")
        ins = Q.e.dma_start(out=out.ap, in_=in_.ap, **kw)
        ins.then_inc(tl.dsem, 16)
        self.ninst += 1
        tl.dcnt += 16
        self.all_dsem[tl.dsem] = tl.dcnt
        if out.t is not None:
            out.t.w = (tl.dsem, tl.dcnt)
            out.t.r = {}
        if in_.t is not None:
            in_.t.r[tl.dsem] = tl.dcnt
        if out.t is None:
            self.out_waits.append((tl.dsem, tl.dcnt))
        return ins

    def finish(self):
        for s, v in self.all_dsem.items():
            self._wait(self.sp, s, v)
        for E in self.engs:
            if E is not self.sp and E.cnt > 0:
                self._wait(self.sp, E, E.cnt)

    def mm(self, out, lhsT, rhs, start=True, stop=True, inc=None, **kw):
        if inc is None:
            inc = stop
        return self.emit(self.pe, lambda: self.nc.tensor.matmul(out.ap, lhsT.ap, rhs.ap, start=start, stop=stop, **kw),
                         [lhsT, rhs], [out], inc=inc)

    def tr(self, out, in_, ident, inc=True):
        return self.emit(self.pe, lambda: self.nc.tensor.transpose(out.ap, in_.ap, ident.ap), [in_, ident], [out],
                         inc=inc)

    def actf(self, out, in_, func, bias=None, scale=None, accum=None):
        kw = {}
        rd = [in_]
        if bias is not None:
            if isinstance(bias, V):
                kw["bias"] = bias.ap
                rd.append(bias)
            else:
                kw["bias"] = float(bias)
        if scale is not None:
            if isinstance(scale, V):
                kw["scale"] = scale.ap
                rd.append(scale)
            else:
                kw["scale"] = float(scale)
        wr = [out]
        if accum is not None:
            kw["accum_out"] = accum.ap
            wr.append(accum)
        return self.emit(self.act, lambda: self.nc.scalar.activation(out.ap, in_.ap, func, **kw), rd, wr)

    def _ve(self, E):
        return E if E is not None else self.dve

    def tt(self, out, a, b, op, E=None):
        E = self._ve(E)
        return self.emit(E, lambda: E.e.tensor_tensor(out.ap, a.ap, b.ap, op), [a, b], [out])

    def ts(self, out, a, s1, op0, s2=None, op1=None, E=None):
        E = self._ve(E)
        rd = [a]
        a1 = s1
        if isinstance(s1, V):
            rd.append(s1)
            a1 = s1.ap
        a2 = s2
        if isinstance(s2, V):
            rd.append(s2)
            a2 = s2.ap
        kw = {}
        if op1 is not None:
            kw["op1"] = op1
        return self.emit(E, lambda: E.e.tensor_scalar(out.ap, a.ap, a1, a2, op0, **kw), rd, [out])

    def stt(self, out, a, s, b, op0, op1):
        E = self.dve
        rd = [a, b]
        a1 = s
        if isinstance(s, V):
            rd.append(s)
            a1 = s.ap
        return self.emit(E, lambda: E.e.scalar_tensor_tensor(out.ap, a.ap, a1, b.ap, op0, op1), rd, [out])

    def cp(self, out, in_, E=None):
        E = self._ve(E)
        if E is self.act:
            return self.emit(E, lambda: self.nc.scalar.copy(out.ap, in_.ap), [in_], [out])
        return self.emit(E, lambda: E.e.tensor_copy(out.ap, in_.ap), [in_], [out])

    def memset(self, out, val, E=None):
        E = self._ve(E)
        return self.emit(E, lambda: E.e.memset(out.ap, val), [], [out])

    def scan(self, out, d0, d1, init, op0, op1):
        rd = [d0, d1]
        i = init
        if isinstance(init, V):
            rd.append(init)
            i = init.ap
        return self.emit(self.dve, lambda: self.nc.vector.tensor_tensor_scan(out.ap, d0.ap, d1.ap, i, op0, op1), rd,
                         [out])

    def recip(self, out, in_):
        return self.emit(self.dve, lambda: self.nc.vector.reciprocal(out.ap, in_.ap), [in_], [out])

    def iota(self, out, pattern, base, cm):
        return self.emit(self.pool, lambda: self.nc.gpsimd.iota(out.ap, pattern, base=base, channel_multiplier=cm,
                                                               allow_small_or_imprecise_dtypes=True), [], [out])

    def asel(self, out, in_, pattern, op, fill, base, cm):
        return self.emit(self.pool, lambda: self.nc.gpsimd.affine_select(out.ap, in_.ap, pattern, op, fill, base=base,
                                                                        channel_multiplier=cm), [in_], [out])


PP_SPEC = [("norm_g0", 8), ("norm_g1", 8), ("norm_g2", 8), ("b_ada", 72),
           ("conv_w0", 6), ("conv_w1", 6), ("conv_w2", 6), ("conv_b", 6), ("ssd_norm", 4),
           ("ssd_D0", 4), ("ssd_D1", 4), ("dt_bias", 16), ("A_log", 16), ("gk_b", 4), ("gla_norm", 1),
           ("rw_mu", 15), ("rw_w0", 8), ("rw_a0", 4), ("rw_kk", 4), ("rw_ka", 4), ("rw_rk", 4),
           ("rw_ln_w", 4), ("rw_ln_b", 4)]
PP_OFF = {}
_o = 0
for _n, _c in PP_SPEC:
    PP_OFF[_n] = (_o, _c)
    _o += _c
NPP = _o


def _cm(vec):
    vec = np.asarray(vec, np.float32).reshape(-1)
    n = vec.shape[0] // 128
    return np.ascontiguousarray(vec.reshape(n, 128).T)


def pack_params(inp, l):
    pp = np.zeros((128, NPP), np.float32)

    def put(name, arr):
        o, c = PP_OFF[name]
        assert arr.shape == (128, c), (name, arr.shape, c)
        pp[:, o:o + c] = arr

    for i in range(3):
        put(f"norm_g{i}", _cm(inp["norm_g"][l, i]))
    put("b_ada", _cm(inp["b_ada"][l]))
    for i in range(3):
        put(f"conv_w{i}", _cm(inp["ssd_conv_w"][l, i]))
    put("conv_b", _cm(inp["ssd_conv_b"][l]))
    put("ssd_norm", _cm(inp["ssd_norm"][l]))
    hd = (2 * np.arange(4)[None, :] + (np.arange(128)[:, None] // 64))
    put("ssd_D0", inp["ssd_D"][l, 0][hd])
    put("ssd_D1", inp["ssd_D"][l, 1][hd])
    put("dt_bias", np.broadcast_to(inp["ssd_dt_bias"][l].reshape(1, 16), (128, 16)))
    put("A_log", np.broadcast_to(inp["ssd_A_log"][l].reshape(1, 16), (128, 16)))
    put("gk_b", np.concatenate([_cm(inp["gla_gk_b"][l, 0]), _cm(inp["gla_gk_b"][l, 1])], axis=1))
    put("gla_norm", _cm(inp["gla_norm"][l]))
    mu = inp["rw_mu"][l]
    mucols = np.zeros((128, 15), np.float32)
    mucols[:, 0:13] = _cm(mu[0:1664])
    mucols[0:64, 13] = mu[1664:1728]
    mucols[:, 14] = mu[1728:1856]
    put("rw_mu", mucols)
    put("rw_w0", np.concatenate([_cm(inp["rw_w0"][l, 0]), _cm(inp["rw_w0"][l, 1])], axis=1))
    put("rw_a0", _cm(inp["rw_a0"][l]))
    put("rw_kk", _cm(inp["rw_kk"][l]))
    put("rw_ka", _cm(inp["rw_ka"][l]))
    put("rw_rk", _cm(inp["rw_rk"][l]))
    put("rw_ln_w", _cm(inp["rw_ln_w"][l]))
    put("rw_ln_b", _cm(inp["rw_ln_b"][l]))
    return pp


class WS:
    NST = 2
    NBF = 3
    CAST_PAT = ["dve", "act", "dve", "pool", "dve", "act", "dve", "act"]

    def __init__(self, k):
        self.k = k
        self.st = [k.sb(f"wst{i}", [128, 2048], F32) for i in range(self.NST)]
        self.bf = [k.sb(f"wbf{i}", [128, 2048], BF16) for i in range(self.NBF)]
        self.slabs = []
        self.nd = 0
        self.ncast = 0
        self.nget = 0

    def add(self, dview, kcs, ncols):
        assert kcs * ncols <= 2048
        self.slabs.append((dview, kcs, ncols))

    def _dma(self, j):
        dview, kcs, ncols = self.slabs[j]
        st = self.st[j % self.NST]
        self.k.dma(self.k.sp, st[:, 0:kcs * ncols].re("p (a b) -> p a b", a=kcs), dview)

    def _cast(self, j):
        dview, kcs, ncols = self.slabs[j]
        n = kcs * ncols
        E = getattr(self.k, self.CAST_PAT[j % len(self.CAST_PAT)])
        self.k.cp(self.bf[j % self.NBF][:, 0:n], self.st[j % self.NST][:, 0:n], E=E)

    def next(self):
        j = self.nget
        self.nget += 1
        n = len(self.slabs)
        while self.nd < min(n, j + self.NST):
            self._dma(self.nd)
            self.nd += 1
        while self.ncast < min(n, j + 2):
            self._cast(self.ncast)
            self.ncast += 1
        dview, kcs, ncols = self.slabs[j]
        return self.bf[j % self.NBF][:, 0:kcs * ncols].re("p (a b) -> p a b", a=kcs)


def wview(w2d, k0, k1, c0, c1):
    return w2d.re("(kc p) n -> p kc n", p=128)[:, k0:k1, c0:c1]


class StopBuild(Exception):
    pass


class Prog:
    def __init__(self, debug=(), needed=None):
        import os
        self.stop = os.environ.get("KSTOP", "")
        self.debug = set(debug)
        self.k = K(needed)
        self.dbg_out = {}
        self.build()

    def pp(self, l, name, c0=0, c1=None):
        o, c = PP_OFF[name]
        if c1 is None:
            c1 = c
        return self.PP[l][:, o + c0:o + c1]

    def chk(self, name):
        if self.stop == name:
            raise StopBuild(name)

    def bank(self):
        b = self.banks[self.nbank % 8]
        self.nbank += 1
        return b

    def dbg(self, name, view, shape):
        if name not in self.debug or name in self.dbg_out:
            return
        d = self.k.dram("dbg_" + name, shape, view.ap.dtype, "ExternalOutput")
        self.k.dma(self.k.sp, d, view)
        self.dbg_out[name] = shape

    def build(self):
        k = self.k
        nc = k.nc
        self.xT_d = k.dram("xT", [D, TOK], F32, "ExternalInput")
        self.cond_d = k.dram("cond", [128, 8], F32, "ExternalInput")
        self.flags_d = k.dram("flags", [128, 32], F32, "ExternalInput")
        self.gp_d = k.dram("gp", [128, 8], F32, "ExternalInput")
        self.pp_d = k.dram("pp", [DEPTH, 128, NPP], F32, "ExternalInput")
        self.w = {}
        for name, shape in [("w_ada", [DEPTH, D, 9 * D]), ("ffn_gate", [DEPTH, 2, D, DFF]),
                            ("ffn_up", [DEPTH, 2, D, DFF]), ("ffn_down", [DEPTH, 2, DFF, D]),
                            ("w_in", [DEPTH, D, NIN]), ("w_ssd_o", [DEPTH, 512, D]), ("w_gla_o", [DEPTH, 512, D]),
                            ("w_rw_o", [DEPTH, 512, D]), ("w_out", [DEPTH, D, D])]:
            self.w[name] = k.dram(name, shape, F32, "ExternalInput")
        self.yT_d = k.dram("yT", [D, TOK], F32, "ExternalOutput")
        self.lvlmask_d = k.dram("lvlmask", [2, 128, 7, 128], F32, "ExternalInput")
        self.st_ssd_d = k.dram("st_ssd", [DEPTH, 2, 128, 256], F32, "ExternalInput")
        self.ns_ssd_d = k.dram("ns_ssd", [DEPTH, 4, 2, 128, 256], F32, "ExternalOutput")
        self.st_gla_d = k.dram("st_gla", [DEPTH, 2, 128, 256], F32, "ExternalInput")
        self.ns_gla_d = k.dram("ns_gla", [DEPTH, 4, 2, 128, 256], F32, "ExternalOutput")
        self.gkw_d = k.dram("gla_gk_w", [DEPTH, 2, 16, 256], F32, "ExternalInput")
        self.st_rw_d = k.dram("st_rw", [DEPTH, 2, 128, 256], F32, "ExternalInput")
        self.ns_rw_d = k.dram("ns_rw", [DEPTH, 4, 2, 128, 256], F32, "ExternalOutput")
        self.rw_w2_d = k.dram("rw_w2", [DEPTH, 2, 64, 512], F32, "ExternalInput")
        self.rw_a2_d = k.dram("rw_a2", [DEPTH, 64, 512], F32, "ExternalInput")
        self.rw_g2_d = k.dram("rw_g2", [DEPTH, 128, 512], F32, "ExternalInput")

        self.xT = k.sb("xT", [128, 8, TOK], F32)
        self.hT = k.sb("hT", [128, 8, TOK], BF16)
        self.flags = k.sb("flags", [128, 32], F32)
        self.gp = k.sb("gp", [128, 8], F32)
        self.PP = [k.sb(f"pp{l}", [128, NPP], F32) for l in range(DEPTH)]
        self.ada = [k.sb(f"ada{l}", [128, 72], F32) for l in range(DEPTH)]
        self.modA = [k.sb(f"modA{l}", [128, 24], F32) for l in range(DEPTH)]
        self.gate = [k.sb(f"gate{l}", [128, 24], F32) for l in range(DEPTH)]
        self.ones_bf = k.sb("ones_bf", [128, 128], BF16)
        self.ident_bf = k.sb("ident_bf", [128, 128], BF16)
        self.ident_f = k.sb("ident_f", [128, 128], F32)
        self.cst = k.sb("cst", [128, 8], F32)
        self.eps6 = self.cst[:, 0:1]
        self.epsgn = self.cst[:, 1:2]
        self.ones_f = k.sb("ones_f", [128, 128], F32)
        self.maskU = k.sb("maskU", [128, 128], F32)
        self.maskL = k.sb("maskL", [128, 128], F32)
        self.maskSU = k.sb("maskSU", [128, 128], F32)
        self.maskSL = k.sb("maskSL", [128, 128], F32)
        self.mask2 = [k.sb(f"mask2_{d}", [128, 2, 128], F32) for d in range(2)]
        self.lvlmask = [k.sb(f"lvlmask{d}", [128, 7, 128], BF16) for d in range(2)]
        self.blockones = k.sb("blockones", [128, 128], BF16)
        self.blockmean = k.sb("blockmean", [128, 128], BF16)
        self.banks = [k.ps(f"bank{i}", [128, 512], F32) for i in range(8)]
        self.nbank = 0
        self.ws = WS(k)

        k.dma(k.sp, self.flags[:], self.flags_d)
        k.dma(k.sp, self.gp[:], self.gp_d)
        for l in range(DEPTH):
            k.dma(k.sp, self.PP[l][:], self.pp_d[l])
        xv = self.xT_d.re("(c p) t -> p c t", p=128)
        for c in range(8):
            k.dma(k.sp, self.xT[:, c, :], xv[:, c, :])

        k.push()
        lmf = k.sb("lvlmask_f", [128, 7, 128], F32)
        for d in range(2):
            k.dma(k.sp, lmf[:], self.lvlmask_d[d])
            k.cp(self.lvlmask[d][:], lmf[:], E=k.pool)
        k.pop()
        k.memset(self.cst[:, 0:1], 1e-6, E=k.pool)
        k.memset(self.cst[:, 1:2], 64e-5, E=k.pool)
        k.memset(self.cst[:, 2:3], 1.0, E=k.pool)
        k.memset(self.ones_bf[:], 1.0, E=k.pool)
        k.memset(self.ones_f[:], 1.0, E=k.pool)
        k.memset(self.maskU[:], 1.0, E=k.pool)
        k.asel(self.maskU[:], self.maskU[:], [[1, 128]], ALU.is_ge, 0.0, 0, -1)
        k.memset(self.maskSU[:], 1.0, E=k.pool)
        k.asel(self.maskSU[:], self.maskSU[:], [[1, 128]], ALU.is_ge, 0.0, -1, -1)
        k.memset(self.maskSL[:], 1.0, E=k.pool)
        k.asel(self.maskSL[:], self.maskSL[:], [[-1, 128]], ALU.is_ge, 0.0, -1, 1)
        k.memset(self.blockones[:], 0.0, E=k.pool)
        k.memset(self.blockmean[:], 0.0, E=k.pool)
        for e in range(2):
            k.memset(self.blockones[e * 64:(e + 1) * 64, e * 64:(e + 1) * 64], 1.0, E=k.pool)
            k.memset(self.blockmean[e * 64:(e + 1) * 64, e * 64:(e + 1) * 64], 1.0 / 64, E=k.pool)
        k.memset(self.maskL[:], 1.0, E=k.pool)
        k.asel(self.maskL[:], self.maskL[:], [[-1, 128]], ALU.is_ge, 0.0, 0, 1)
        k.cp(self.mask2[0][:, 0, :], self.maskU[:], E=k.pool)
        k.cp(self.mask2[0][:, 1, :], self.maskSU[:], E=k.pool)
        k.cp(self.mask2[1][:, 0, :], self.maskL[:], E=k.pool)
        k.cp(self.mask2[1][:, 1, :], self.maskSL[:], E=k.pool)
        k.memset(self.ident_f[:], 1.0, E=k.pool)
        k.asel(self.ident_f[:], self.ident_f[:], [[-1, 128]], ALU.is_equal, 0.0, 0, 1)
        k.cp(self.ident_bf[:], self.ident_f[:], E=k.pool)

        for l in range(DEPTH):
            self.plan_ada(l)
        for l in range(DEPTH):
            self.plan_ffn(l, 0)
            self.plan_mixer(l)
            self.plan_ffn(l, 1)

        self.pos_embed()
        self.compute_ada()
        try:
            for l in range(DEPTH):
                self.ffn(l, 0)
                self.dbg(f"x1_{l}", self.xT[:], [128, 8, TOK])
                self.mixer(l)
                self.dbg(f"x2_{l}", self.xT[:], [128, 8, TOK])
                self.ffn(l, 1)
        except StopBuild:
            while len(k.scopes) > 1:
                k.pop()
        self.final_norm()
        k.finish()

    def pos_embed(self):
        k = self.k
        k.push()
        idx = k.sb("pe_idx", [128, 2], F32)
        om = k.sb("pe_om", [128, 2], F32)
        pos = k.sb("pe_pos", [128, 80], F32)
        arg = k.sb("pe_arg", [128, 4, 80], F32)
        ki = k.sb("pe_ki", [128, 4, 80], I32)
        kf = k.sb("pe_kf", [128, 4, 80], F32)
        gt = k.sb("pe_gt", [128, 4, 80], F32)
        emb = k.sb("pe_emb", [128, 4, 80], F32)
        k.iota(idx[:], [[128, 2]], 0, 1)
        k.actf(om[:], idx[:], AF.Exp, scale=-float(np.log(10000.0)) / 256.0)
        k.iota(pos[:, 0:16], [[1, 16]], 0, 0)
        k.iota(pos[:, 16:80], [[1, 64]], 0, 0)
        for j in range(4):
            ph = 0.0 if j < 2 else PI / 2
            k.ts(arg[:, j, :], pos[:], om[:, (j % 2):(j % 2) + 1], ALU.mult, ph, ALU.add)
        k.ts(kf[:], arg[:], 1.0 / (2 * PI), ALU.mult)
        k.cp(ki[:], kf[:])
        k.cp(kf[:], ki[:])
        k.stt(arg[:], kf[:], -2 * PI, arg[:], ALU.mult, ALU.add)
        k.ts(gt[:], arg[:], PI, ALU.is_gt, -2 * PI, ALU.mult)
        k.tt(arg[:], arg[:], gt[:], ALU.add)
        k.ts(gt[:], arg[:], -PI, ALU.is_lt, 2 * PI, ALU.mult)
        k.tt(arg[:], arg[:], gt[:], ALU.add)
        k.actf(emb[:], arg[:], AF.Sin)
        k.ts(emb[:], emb[:], self.flags[:, 16:17], ALU.mult)
        self.dbg("emb", emb[:], [128, 4, 80])
        for fc in range(8):
            xv = self.xT[:, fc, :].re("p (r c) -> p r c", c=64)
            if fc < 4:
                ev = V(emb, emb.h[:, fc, 0:16].unsqueeze(2).to_broadcast([128, 16, 64]))
            else:
                ev = V(emb, emb.h[:, fc - 4, 16:80].unsqueeze(1).to_broadcast([128, 16, 64]))
            k.tt(xv, xv, ev, ALU.add)
        k.pop()

    def plan_ada(self, l):
        for s in range(36):
            self.ws.add(wview(self.w["w_ada"][l], 0, 8, s * 256, (s + 1) * 256), 8, 256)

    def compute_ada(self):
        k = self.k
        k.push()
        cond = k.sb("cond", [128, 8], F32)
        sg = k.sb("cond_sg", [128, 8], F32)
        scond = k.sb("scond", [128, 8], BF16)
        k.dma(k.sp, cond[:], self.cond_d)
        k.actf(sg[:], cond[:], AF.Sigmoid)
        k.tt(scond[:], cond[:], sg[:], ALU.mult)
        for l in range(DEPTH):
            pb = self.bank()
            for s in range(36):
                wb = self.ws.next()
                for m in range(2):
                    j = s * 2 + m
                    for kc in range(8):
                        k.mm(pb[:, j:j + 1], wb[:, kc, m * 128:(m + 1) * 128], scond[:, kc:kc + 1],
                             start=(kc == 0), stop=(kc == 7))
            k.tt(self.ada[l][:], pb[:, 0:72], self.pp(l, "b_ada"), ALU.add)
            for i in range(3):
                k.stt(self.modA[l][:, i * 8:(i + 1) * 8], self.ada[l][:, (3 * i + 1) * 8:(3 * i + 2) * 8], 1.0,
                      self.pp(l, f"norm_g{i}"), ALU.add, ALU.mult)
                k.ts(self.gate[l][:, i * 8:(i + 1) * 8], self.ada[l][:, (3 * i + 2) * 8:(3 * i + 3) * 8],
                     0.5 if i != 1 else 1.0, ALU.mult)
            self.dbg(f"ada{l}", self.ada[l][:], [128, 72])
        k.pop()

    def rstd_of_x(self, rstd):
        k = self.k
        k.push()
        sq = k.sb("sq", [128, 8, TOK], BF16)
        for c in range(8):
            k.actf(sq[:, c, :], self.xT[:, c, :], AF.Square)
        for h in range(2):
            pb = self.bank()
            for c in range(8):
                k.mm(pb[:], self.ones_bf[:], sq[:, c, h * 512:(h + 1) * 512], start=(c == 0), stop=(c == 7))
            k.actf(rstd[:, h * 512:(h + 1) * 512], pb[:], AF.Ln, scale=1.0 / D, bias=self.eps6)
        k.actf(rstd[:], rstd[:], AF.Exp, scale=-0.5)
        k.pop()

    def norm_mod(self, l, i):
        k = self.k
        k.push()
        rstd = k.sb("rstd", [128, TOK], F32)
        tmp = [k.sb(f"nm_tmp{j}", [128, TOK], F32) for j in range(2)]
        self.rstd_of_x(rstd)
        for c in range(8):
            t = tmp[c % 2]
            k.tt(t[:], self.xT[:, c, :], rstd[:], ALU.mult)
            k.actf(self.hT[:, c, :], t[:], AF.Identity, scale=self.modA[l][:, i * 8 + c:i * 8 + c + 1],
                   bias=self.ada[l][:, 3 * i * 8 + c:3 * i * 8 + c + 1])
        k.pop()

    def final_norm(self):
        k = self.k
        k.push()
        rstd = k.sb("rstd", [128, TOK], F32)
        tmp = [k.sb(f"fn_tmp{j}", [128, TOK], F32) for j in range(2)]
        self.rstd_of_x(rstd)
        yv = self.yT_d.re("(c p) t -> p c t", p=128)
        for c in range(8):
            t = tmp[c % 2]
            k.stt(t[:], self.xT[:, c, :], self.gp[:, c:c + 1], rstd[:], ALU.mult, ALU.mult)
            k.dma(k.sp, yv[:, c, :], t[:])
        k.pop()

    def plan_ffn(self, l, which):
        wg = self.w["ffn_gate"][l, which]
        wu = self.w["ffn_up"][l, which]
        wd = self.w["ffn_down"][l, which]
        for s in range(11):
            self.ws.add(wview(wg, 0, 8, s * 256, (s + 1) * 256), 8, 256)
            self.ws.add(wview(wu, 0, 8, s * 256, (s + 1) * 256), 8, 256)
        for s in range(4):
            for (k0, k1) in ((0, 8), (8, 16), (16, 22)):
                self.ws.add(wview(wd, k0, k1, s * 256, (s + 1) * 256), k1 - k0, 256)

    def ffn(self, l, which):
        k = self.k
        i = 0 if which == 0 else 2
        self.norm_mod(l, i)
        k.push()
        actT = k.sb("actT", [128, FC, TOK], BF16)
        sg = [k.sb(f"ffn_sg{j}", [128, 512], F32) for j in range(2)]
        nsg = 0
        for s in range(11):
            wg = self.ws.next()
            wu = self.ws.next()
            for m in range(2):
                fcb = s * 2 + m
                for h in range(2):
                    pg = self.bank()
                    pu = self.bank()
                    for kc in range(8):
                        k.mm(pg[:], wg[:, kc, m * 128:(m + 1) * 128], self.hT[:, kc, h * 512:(h + 1) * 512],
                             start=(kc == 0), stop=(kc == 7))
                    for kc in range(8):
                        k.mm(pu[:], wu[:, kc, m * 128:(m + 1) * 128], self.hT[:, kc, h * 512:(h + 1) * 512],
                             start=(kc == 0), stop=(kc == 7))
                    t = sg[nsg % 2]
                    nsg += 1
                    k.actf(t[:], pg[:], AF.Silu)
                    k.tt(actT[:, fcb, h * 512:(h + 1) * 512], pu[:], t[:], ALU.mult)
        gcol = self.gate[l]
        for s in range(4):
            pbs = [[self.bank() for h in range(2)] for m in range(2)]
            for ksub, (k0, k1) in enumerate(((0, 8), (8, 16), (16, 22))):
                wd = self.ws.next()
                for m in range(2):
                    for h in range(2):
                        for kc in range(k0, k1):
                            k.mm(pbs[m][h][:], wd[:, kc - k0, m * 128:(m + 1) * 128],
                                 actT[:, kc, h * 512:(h + 1) * 512], start=(kc == 0), stop=(kc == FC - 1),
                                 inc=(kc == k1 - 1))
            for m in range(2):
                c = s * 2 + m
                for h in range(2):
                    xs = self.xT[:, c, h * 512:(h + 1) * 512]
                    k.stt(xs, pbs[m][h][:], gcol[:, i * 8 + c:i * 8 + c + 1], xs, ALU.mult, ALU.add)
        k.pop()


    SEC = dict(z=0, xbc=512, dt=1280, q=1296, k=1552, v=1808, g=2320, lr=2832, rr=2864, rk=3376, rv=3888,
               wlr=4400, alr=4528, glr=4592, gate=4720)

    def win_slab(self, l, c0, ncols):
        self.ws.add(wview(self.w["w_in"][l], 0, 8, c0, c0 + ncols), 8, ncols)

    def plan_mixer(self, l):
        self.plan_ssd(l)
        self.plan_gla(l)
        self.plan_rw(l)
        self.plan_out(l, "w_out")

    def plan_out(self, l, name):
        w = self.w[name][l]
        if name == "w_out":
            for s in range(4):
                self.ws.add(wview(w, 0, 8, s * 256, (s + 1) * 256), 8, 256)
        else:
            for s in range(2):
                self.ws.add(wview(w, 0, 4, s * 512, (s + 1) * 512), 4, 512)

    def proj_fm(self, wb, m0, mcols, h):
        k = self.k
        pb = self.bank()
        for kc in range(8):
            k.mm(pb[0:mcols, :], wb[:, kc, m0:m0 + mcols], self.hT[:, kc, h * 512:(h + 1) * 512],
                 start=(kc == 0), stop=(kc == 7))
        return pb

    def mixer(self, l):
        k = self.k
        self.norm_mod(l, 1)
        k.push()
        self.merged = k.sb("merged", [128, 8, TOK], BF16)
        self.ssd_branch(l)
        self.dbg(f"mg0_{l}", self.merged[:], [128, 8, TOK])
        self.chk("F")
        self.gla_branch(l)
        self.dbg(f"mg1_{l}", self.merged[:], [128, 8, TOK])
        self.chk("G")
        self.rw_branch(l)
        self.dbg(f"mg2_{l}", self.merged[:], [128, 8, TOK])
        self.chk("H")
        mbf = self.merged
        gcol = self.gate[l]
        for s in range(4):
            wb = self.ws.next()
            for m in range(2):
                c = s * 2 + m
                for h in range(2):
                    pb = self.bank()
                    for kc in range(8):
                        k.mm(pb[:], wb[:, kc, m * 128:(m + 1) * 128], mbf[:, kc, h * 512:(h + 1) * 512],
                             start=(kc == 0), stop=(kc == 7))
                    xs = self.xT[:, c, h * 512:(h + 1) * 512]
                    k.stt(xs, pb[:], gcol[:, 8 + c:8 + c + 1], xs, ALU.mult, ALU.add)
        k.pop()

    def plan_branch_out(self, l, name, b):
        w = self.w[name][l]
        for s in range(4):
            self.ws.add(wview(w, 0, 4, s * 256, (s + 1) * 256), 4, 256)
            self.win_slab(l, self.SEC["gate"] + b * 1024 + s * 256, 256)

    def branch_out(self, l, yT, b):
        k = self.k
        k.push()
        sgt = [k.sb(f"bo_sg{j}", [128, 512], F32) for j in range(2)]
        tmp = [k.sb(f"bo_tmp{j}", [128, 512], F32) for j in range(2)]
        n = 0
        for s in range(4):
            wo = self.ws.next()
            wg = self.ws.next()
            for m in range(2):
                c = s * 2 + m
                for h in range(2):
                    po = self.bank()
                    for kc in range(4):
                        k.mm(po[:], wo[:, kc, m * 128:(m + 1) * 128],
                             yT[:, kc, h * 512:(h + 1) * 512], start=(kc == 0), stop=(kc == 3))
                    pg = self.proj_fm(wg, m * 128, 128, h)
                    sg = sgt[n % 2]
                    tp = tmp[n % 2]
                    n += 1
                    k.actf(sg[:], pg[:], AF.Sigmoid)
                    mv = self.merged[:, c, h * 512:(h + 1) * 512]
                    if b == 0:
                        k.tt(mv, po[:], sg[:], ALU.mult)
                    else:
                        k.tt(tp[:], po[:], sg[:], ALU.mult)
                        k.tt(mv, mv, tp[:], ALU.add, E=k.pool)
        k.pop()

    def plan_ssd(self, l):
        S = self.SEC
        for c0 in (0, 256):
            self.win_slab(l, S["z"] + c0, 256)
        for c0 in (0, 256, 512):
            self.win_slab(l, S["xbc"] + c0, 256)
        self.win_slab(l, S["dt"], 16)
        self.plan_branch_out(l, "w_ssd_o", 0)

    def ssd_branch(self, l):
        k = self.k
        fl = self.flags
        k.push()
        zs = k.sb("ssd_zs", [128, 4, TOK], BF16)
        xc = k.sb("ssd_xc", [128, 6, TOK], BF16)
        xbt = k.sb("ssd_xbt", [128, NT, 640], BF16)
        dt = k.sb("ssd_dt", [128, NT, 16], F32)
        lndt = k.sb("ssd_lndt", [128, NT, 16], F32)
        a = k.sb("ssd_a", [128, NT, 16], F32)
        cs = k.sb("ssd_cs", [128, NT, 16], F32)
        aneg = k.sb("ssd_aneg", [128, 16], F32)
        dsum = k.sb("ssd_dsum", [128, 4], F32)
        ygT = k.sb("ssd_yg", [128, 4, TOK], BF16)
        sin_bf = [k.sb(f"ssd_sin{d}", [128, NT, 256], BF16) for d in range(2)]
        for s in range(2):
            wb = self.ws.next()
            for m in range(2):
                for h in range(2):
                    pb = self.proj_fm(wb, m * 128, 128, h)
                    k.actf(zs[:, s * 2 + m, h * 512:(h + 1) * 512], pb[:], AF.Silu)
        k.push()
        xp = k.sb("ssd_xp", [128, 6, 4, 258], BF16)
        acc = [k.sb(f"ssd_acc{j}", [128, 4, 256], F32) for j in range(2)]
        k.memset(xp[:, :, :, 0:1], 0.0, E=k.pool)
        k.memset(xp[:, :, :, 257:258], 0.0, E=k.pool)
        for s in range(3):
            wb = self.ws.next()
            for m in range(2):
                for h in range(2):
                    pb = self.proj_fm(wb, m * 128, 128, h)
                    k.cp(xp[:, s * 2 + m, 2 * h:2 * h + 2, 1:257], pb[:].re("p (q t) -> p q t", q=2),
                         E=(k.act if h == 0 else k.dve))
        cfv = V(fl, fl.h[:, 2:8:2].unsqueeze(1).to_broadcast([128, 6, 3]))
        cbv = V(fl, fl.h[:, 9:15:2].unsqueeze(1).to_broadcast([128, 6, 3]))
        k.tt(xp[:, :, 1:4, 0], xp[:, :, 0:3, 256], cfv, ALU.mult)
        k.tt(xp[:, :, 0:3, 257], xp[:, :, 1:4, 1], cbv, ALU.mult)
        for c in range(6):
            t = acc[c % 2]
            k.ts(t[:], xp[:, c, :, 1:257], self.pp(l, "conv_w1", c, c + 1), ALU.mult,
                 self.pp(l, "conv_b", c, c + 1), ALU.add)
            k.stt(t[:], xp[:, c, :, 0:256], self.pp(l, "conv_w0", c, c + 1), t[:], ALU.mult, ALU.add)
            k.stt(t[:], xp[:, c, :, 2:258], self.pp(l, "conv_w2", c, c + 1), t[:], ALU.mult, ALU.add)
            k.actf(xc[:, c, :].re("p (q t) -> p q t", q=4), t[:], AF.Silu)
        k.pop()
        self.dbg("ssd_xc", xc[:], [128, 6, TOK])
        self.chk("A")
        for t in range(NT):
            pb = self.bank()
            pv = pb[:].bitcast(BF16)
            for c in range(5):
                k.tr(pv[:, c * 128:(c + 1) * 128], xc[:, c, t * 128:(t + 1) * 128], self.ident_bf[:], inc=(c == 4))
            k.cp(xbt[:, t, :], pv[:, 0:640], E=(k.act if t % 2 else k.dve))
        self.chk("B")
        wb = self.ws.next()
        pb = self.bank()
        for t in range(NT):
            for kc in range(8):
                k.mm(pb[:, t * 16:(t + 1) * 16], self.hT[:, kc, t * 128:(t + 1) * 128], wb[:, kc, 0:16],
                     start=(kc == 0), stop=(kc == 7))
        k.tt(dt[:], pb[:, 0:128].re("p (t c) -> p t c", t=NT),
             V(self.PP[l], self.pp(l, "dt_bias").ap.unsqueeze(1).to_broadcast([128, NT, 16])), ALU.add)
        k.actf(dt[:], dt[:], AF.Exp)
        k.ts(lndt[:], dt[:], -0.5, ALU.mult, 1.0, ALU.add)
        k.tt(lndt[:], lndt[:], dt[:], ALU.mult)
        k.actf(dt[:], dt[:], AF.Ln, bias=self.cst[:, 2:3])
        k.tt(dt[:], dt[:], lndt[:], ALU.max)
        k.actf(lndt[:], dt[:], AF.Ln)
        k.actf(aneg[:], self.pp(l, "A_log"), AF.Exp)
        k.ts(aneg[:], aneg[:], -1.0, ALU.mult)
        k.tt(a[:], dt[:], V(aneg, aneg.h[:, :].unsqueeze(1).to_broadcast([128, NT, 16])), ALU.mult)
        k.tt(dsum[:], self.pp(l, "ssd_D0"), self.pp(l, "ssd_D1"), ALU.add)
        self.dbg("ssd_dt", dt[:], [128, NT, 16])
        self.chk("B1")
        pb = self.bank()
        for t in range(NT):
            k.mm(pb[:, t * 16:t * 16 + 8], self.maskU[:], a[:, t, 0:8])
            k.mm(pb[:, t * 16 + 8:t * 16 + 16], self.maskL[:], a[:, t, 8:16])
        k.cp(cs[:], pb[:, 0:128].re("p (t c) -> p t c", t=NT))
        self.chk("B2")
        carg = k.sb("ssd_carg", [128, NT, 16], F32)
        k.tt(carg[:], lndt[:], cs[:], ALU.subtract)
        tot = k.sb("ssd_tot", [128, NT, 16], F32)
        wdec = k.sb("ssd_wdec", [128, NT, 16], F32)
        dech = [k.sb(f"ssd_dech{d}", [128, NT, 4], F32) for d in range(2)]
        pb = self.bank()
        k.mm(pb[:, 0:128], self.ones_f[:], a[:].re("p t c -> p (t c)"))
        k.cp(tot[:], pb[:, 0:128].re("p (t c) -> p t c", t=NT))
        self.chk("B3")
        import os
        kvar = os.environ.get("KVAR", "")
        if kvar != "nowdec":
            k.tt(wdec[:], tot[:], carg[:], ALU.add)
            k.actf(wdec[:], wdec[:], AF.Exp)
        if kvar != "nodech":
            for d in range(2):
                for g in range(2):
                    ps_ = slice(g * 64, (g + 1) * 64)
                    k.actf(dech[d][ps_, :, :], tot[ps_, :, d * 8 + g * 4:d * 8 + g * 4 + 4], AF.Exp)
        self.chk("C")
        k.push()
        S = [k.sb(f"ssd_S{d}", [128, 256], F32) for d in range(2)]
        Sin = [k.sb(f"ssd_Sin{d}", [128, 256], F32) for d in range(2)]
        Stmp = [k.sb(f"ssd_Stmp{d}", [128, 256], F32) for d in range(2)]
        xdec = [[k.sb(f"ssd_xdec{d}_{j}", [128, 512], BF16) for j in range(2)] for d in range(2)]
        for d in range(2):
            k.dma(k.sp, Sin[d][:], self.st_ssd_d[l, d])
        for step in range(NT):
            for d in range(2):
                t = step if d == 0 else NT - 1 - step
                if step > 0:
                    fcol = t if d == 0 else 8 + t
                    k.ts(Sin[d][:], S[d][:], fl[:, fcol:fcol + 1], ALU.mult)
                k.cp(sin_bf[d][:, t, :], Sin[d][:], E=k.act)
                xd = xdec[d][step % 2]
                k.tt(xd[:].re("p (h q) -> p h q", h=8), xbt[:, t, 0:512].re("p (h q) -> p h q", h=8),
                     V(wdec, wdec.h[:, t, d * 8:(d + 1) * 8].unsqueeze(2).to_broadcast([128, 8, 64])), ALU.mult)
                pb = self.bank()
                for g in range(2):
                    k.mm(pb[g * 64:(g + 1) * 64, 0:256], xbt[:, t, 512 + g * 64:512 + (g + 1) * 64],
                         xd[:, g * 256:(g + 1) * 256])
                k.tt(Stmp[d][:].re("p (j q) -> p j q", j=4), Sin[d][:].re("p (j q) -> p j q", j=4),
                     V(dech[d], dech[d].h[:, t, :].unsqueeze(2).to_broadcast([128, 4, 64])), ALU.mult)
                k.tt(S[d][:], Stmp[d][:], pb[:, 0:256], ALU.add)
                if (d == 0 and t % 2 == 1) or (d == 1 and t % 2 == 0):
                    k.dma(k.sp, self.ns_ssd_d[l, t // 2, d], S[d][:])
        k.pop()
        self.chk("D")
        k.push()
        MT = [k.sb(f"ssd_MT{j}", [128, 8, 128], BF16) for j in range(2)]
        cdec = [[k.sb(f"ssd_cdec{d}_{j}", [128, 4, 128], BF16) for j in range(2)] for d in range(2)]
        am = k.sb("ssd_am", [128, 8, 128], F32)
        ambf = k.sb("ssd_ambf", [128, 8, 128], BF16)
        dm = [k.sb(f"ssd_dm{j}", [128, 8, 128], F32) for j in range(2)]
        er = [k.sb(f"ssd_er{j}", [128, 8, 128], BF16) for j in range(2)]
        ytmp = [k.sb(f"ssd_ytmp{j}", [128, 4, 128], F32) for j in range(2)]
        if "ssd_yraw" in self.debug:
            self.yraw = k.sb("ssd_yraw", [128, 4, TOK], F32)
        masks = (self.maskU, self.maskL)
        for t in range(NT):
            for d in range(2):
                mk = masks[d]
                k.tt(am[:], V(a, a.h[:, t, d * 8:(d + 1) * 8].unsqueeze(2).to_broadcast([128, 8, 128])),
                     V(mk, mk.h[:, :].unsqueeze(1).to_broadcast([128, 8, 128])), ALU.mult,
                     E=(k.dve if os.environ.get("KVAR", "") == "amdve" else k.pool))
                dmt = dm[d]
                ert = er[d]
                if os.environ.get("KVAR", "") == "rowbf":
                    k.cp(ambf[:], am[:])
                for hh in range(2):
                    pr = self.bank()
                    if os.environ.get("KVAR", "") == "rowbf":
                        k.mm(pr[:], self.ones_bf[:], ambf[:, hh * 4:(hh + 1) * 4, :])
                    else:
                        k.mm(pr[:], self.ones_f[:], am[:, hh * 4:(hh + 1) * 4, :])
                    rv = pr[:].re("p (h l) -> p h l", h=4)
                    k.tt(dmt[:, hh * 4:(hh + 1) * 4, :], rv,
                         V(carg, carg.h[:, t, d * 8 + hh * 4:d * 8 + hh * 4 + 4].unsqueeze(2).to_broadcast([128, 4, 128])),
                         ALU.add)
                    k.actf(ert[:, hh * 4:(hh + 1) * 4, :], rv, AF.Exp)
                if t == 0 and d == 0:
                    self.chk("E1b")
                k.ts(dmt[:], dmt[:], 30.0, ALU.min)
                if t == 0 and d == 0:
                    self.chk("E1c")
                k.actf(dmt[:], dmt[:], AF.Exp)
                if t == 0 and d == 0:
                    self.chk("E1d")
                k.tt(dmt[:], dmt[:], V(mk, mk.h[:, :].unsqueeze(1).to_broadcast([128, 8, 128])), ALU.mult)
                for g in range(2):
                    ps_ = slice(g * 64, (g + 1) * 64)
                    k.tt(cdec[d][t % 2][ps_, :, :], ert[ps_, g * 4:(g + 1) * 4, :],
                         V(xc, xc.h[ps_, 5, t * 128:(t + 1) * 128].unsqueeze(1).to_broadcast([64, 4, 128])), ALU.mult)
            if t == 0:
                self.chk("E1")
            k.tt(dm[0][:], dm[0][:], dm[1][:], ALU.add)
            pgs = [self.bank(), self.bank()]
            for g in range(2):
                ps_ = slice(g * 64, (g + 1) * 64)
                k.mm(pgs[g][:, 0:128], xc[ps_, 4, t * 128:(t + 1) * 128], xc[ps_, 5, t * 128:(t + 1) * 128])
            mt = MT[t % 2]
            for g in range(2):
                k.tt(mt[:, g * 4:(g + 1) * 4, :], dm[0][:, g * 4:(g + 1) * 4, :],
                     V(pgs[g], pgs[g].h[:, 0:128].unsqueeze(1).to_broadcast([128, 4, 128])), ALU.mult)
            if t == 0:
                self.chk("E2")
            pys = [self.bank(), self.bank()]
            for h in range(8):
                pair, e = h // 2, h % 2
                g, j = h // 4, h % 4
                out = pys[g][e * 64:(e + 1) * 64, (pair % 2) * 128:(pair % 2 + 1) * 128]
                gs = slice(g * 64, (g + 1) * 64)
                k.mm(out, xbt[:, t, h * 64:(h + 1) * 64], mt[:, h, :], start=True, stop=False)
                k.mm(out, sin_bf[0][gs, t, j * 64:(j + 1) * 64], cdec[0][t % 2][gs, j, :], start=False, stop=False)
                k.mm(out, sin_bf[1][gs, t, j * 64:(j + 1) * 64], cdec[1][t % 2][gs, j, :], start=False, stop=True)
            if t == 0:
                self.chk("E3")
            if t == 1:
                self.chk("E4")
            yt = ytmp[t % 2]
            k.tt(yt[:], xc[:, 0:4, t * 128:(t + 1) * 128],
                 V(dsum, dsum.h[:, :].unsqueeze(2).to_broadcast([128, 4, 128])), ALU.mult, E=k.pool)
            for g in range(2):
                k.tt(yt[:, g * 2:g * 2 + 2, :], pys[g][:, 0:256].re("p (c l) -> p c l", c=2), yt[:, g * 2:g * 2 + 2, :],
                     ALU.add)
            if "ssd_yraw" in self.debug:
                k.cp(self.yraw[:, :, t * 128:(t + 1) * 128], yt[:])
            k.tt(ygT[:, :, t * 128:(t + 1) * 128], yt[:], zs[:, :, t * 128:(t + 1) * 128], ALU.mult)
        if "ssd_yraw" in self.debug:
            self.dbg("ssd_yraw", self.yraw[:], [128, 4, TOK])
        k.pop()
        self.chk("E")
        yT = k.sb("ssd_yT", [128, 4, TOK], BF16)
        self.group_rmsnorm(ygT, yT, 4, 1.0 / 512, self.eps6, [self.pp(l, "ssd_norm", c, c + 1) for c in range(4)])
        self.dbg("ssd_yT", yT[:], [128, 4, TOK])
        self.branch_out(l, yT, 0)
        k.pop()

    def plan_gla(self, l):
        S = self.SEC
        self.win_slab(l, S["q"], 256)
        self.win_slab(l, S["k"], 256)
        self.win_slab(l, S["lr"], 32)
        for c0 in (0, 256):
            self.win_slab(l, S["v"] + c0, 256)
        for c0 in (0, 256):
            self.win_slab(l, S["g"] + c0, 256)
        self.plan_branch_out(l, "w_gla_o", 1)

    def gla_branch(self, l):
        k = self.k
        fl = self.flags
        k.push()
        og = k.sb("gla_og", [128, 4, TOK], F32)
        k.push()
        qd = [k.sb(f"gla_qd{d}", [128, 2, TOK], BF16) for d in range(2)]
        kd = [k.sb(f"gla_kd{d}", [128, 2, TOK], BF16) for d in range(2)]
        vtok = k.sb("gla_vtok", [128, NT, 512], BF16)
        sin_bf = [k.sb(f"gla_sin{d}", [128, NT, 256], BF16) for d in range(2)]
        etot = [k.sb(f"gla_etot{d}", [128, 2, NT], F32) for d in range(2)]
        gkw = k.sb("gla_gkw", [16, 2, 256], F32)
        gkw_bf = k.sb("gla_gkwbf", [16, 2, 256], BF16)
        lr_bf = [k.sb(f"gla_lr{d}", [16, TOK], BF16) for d in range(2)]
        nb = k.sb("gla_nb", [128, 4], F32)
        k.dma(k.sp, gkw[:], self.gkw_d[l].re("d r c -> r d c"))
        k.cp(gkw_bf[:], gkw[:], E=k.pool)
        k.ts(nb[:], self.pp(l, "gk_b"), -1.0, ALU.mult)
        k.push()
        self.rmask = k.sb("rmask", [128, TOK], F32)
        k.memset(self.rmask[:], 1.0, E=k.pool)
        k.memset(self.rmask[:, 0::128], 0.0, E=k.pool)
        qT = k.sb("gla_qT", [128, 2, TOK], BF16)
        kT = k.sb("gla_kT", [128, 2, TOK], BF16)
        T1 = k.sb("gla_T1", [128, 2, TOK], F32)
        T2 = k.sb("gla_T2", [128, 2, TOK], F32)
        T3 = k.sb("gla_T3", [128, 2, TOK], F32)
        wb = self.ws.next()
        for m in range(2):
            for h in range(2):
                pb = self.proj_fm(wb, m * 128, 128, h)
                k.actf(qT[:, m, h * 512:(h + 1) * 512], pb[:], AF.Copy, scale=0.125)
        wb = self.ws.next()
        for m in range(2):
            for h in range(2):
                pb = self.proj_fm(wb, m * 128, 128, h)
                k.cp(kT[:, m, h * 512:(h + 1) * 512], pb[:])
        wb = self.ws.next()
        for d in range(2):
            for h in range(2):
                pb = self.proj_fm(wb, d * 16, 16, h)
                k.cp(lr_bf[d][:, h * 512:(h + 1) * 512], pb[0:16, :], E=k.act)
        for d in range(2):
            for c in range(2):
                for h in range(2):
                    pb = self.bank()
                    k.mm(pb[:], gkw_bf[:, d, c * 128:(c + 1) * 128], lr_bf[d][:, h * 512:(h + 1) * 512])
                    k.actf(T1[:, c, h * 512:(h + 1) * 512], pb[:], AF.Exp, scale=-1.0, bias=nb[:, d * 2 + c:d * 2 + c + 1])
            k.actf(T1[:], T1[:], AF.Ln, bias=self.cst[:, 2:3])
            for c in range(2):
                k.scan(T2[:, c, :], self.rmask[:], T1[:, c, :], 0.0, ALU.mult, ALU.add)
            if d == 0:
                k.actf(T3[:], T2[:], AF.Exp, scale=-1.0 / 16)
                k.tt(qd[0][:], qT[:], T3[:], ALU.mult)
                k.cp(etot[0][:], T3[:, :, 127::128])
                k.actf(T3[:], T2[:], AF.Exp, scale=1.0 / 16)
                k.tt(kd[0][:], kT[:], T3[:], ALU.mult)
            else:
                k.actf(etot[1][:], T2[:, :, 127::128], AF.Exp, scale=-1.0 / 16)
                k.tt(T2[:], T2[:], T1[:], ALU.subtract)
                k.actf(T3[:], T2[:], AF.Exp, scale=1.0 / 16)
                k.tt(qd[1][:], qT[:], T3[:], ALU.mult)
                k.actf(T3[:], T2[:], AF.Exp, scale=-1.0 / 16)
                k.tt(kd[1][:], kT[:], T3[:], ALU.mult)
        k.pop()
        for s in range(2):
            wb = self.ws.next()
            for t in range(NT):
                pb = self.bank()
                for kc in range(8):
                    k.mm(pb[:, 0:256], self.hT[:, kc, t * 128:(t + 1) * 128], wb[:, kc, :], start=(kc == 0), stop=(kc == 7))
                k.cp(vtok[:, t, s * 256:(s + 1) * 256], pb[:, 0:256], E=(k.act if t % 2 else k.dve))
        k.push()
        ktok = k.sb("gla_ktok", [128, NT, 4, 128], BF16)
        for t in range(NT):
            pb = self.bank()
            pv = pb[:].bitcast(BF16)
            for d in range(2):
                for c in range(2):
                    j = d * 2 + c
                    k.tr(pv[:, j * 128:(j + 1) * 128], kd[d][:, c, t * 128:(t + 1) * 128], self.ident_bf[:], inc=(j == 3))
            k.cp(ktok[:, t, :, :], pv[:, 0:512].re("p (j q) -> p j q", j=4), E=(k.act if t % 2 else k.dve))
        S = [k.sb(f"gla_S{d}", [128, 256], F32) for d in range(2)]
        Sin = [k.sb(f"gla_Sin{d}", [128, 256], F32) for d in range(2)]
        Stmp = [k.sb(f"gla_Stmp{d}", [128, 256], F32) for d in range(2)]
        for d in range(2):
            k.dma(k.sp, Sin[d][:], self.st_gla_d[l, d])
        for step in range(NT):
            for d in range(2):
                t = step if d == 0 else NT - 1 - step
                if step > 0:
                    fcol = t if d == 0 else 8 + t
                    k.ts(Sin[d][:], S[d][:], fl[:, fcol:fcol + 1], ALU.mult)
                pb = self.bank()
                for h in range(4):
                    pair, e = h // 2, h % 2
                    k.mm(pb[e * 64:(e + 1) * 64, pair * 128:(pair + 1) * 128],
                         ktok[:, t, d * 2 + pair, e * 64:(e + 1) * 64], vtok[:, t, h * 128:(h + 1) * 128])
                etv = V(etot[d], etot[d].h[:, :, t].unsqueeze(2).to_broadcast([128, 2, 128]))
                if d == 0:
                    k.cp(sin_bf[0][:, t, :], Sin[0][:], E=k.act)
                    k.tt(Stmp[0][:], Sin[0][:], pb[:, 0:256], ALU.add)
                    k.tt(S[0][:].re("p (c v) -> p c v", c=2), Stmp[0][:].re("p (c v) -> p c v", c=2), etv, ALU.mult)
                else:
                    k.tt(Stmp[1][:].re("p (c v) -> p c v", c=2), Sin[1][:].re("p (c v) -> p c v", c=2), etv, ALU.mult)
                    k.cp(sin_bf[1][:, t, :], Stmp[1][:], E=k.act)
                    k.tt(S[1][:], Stmp[1][:], pb[:, 0:256], ALU.add)
                if (d == 0 and t % 2 == 1) or (d == 1 and t % 2 == 0):
                    k.dma(k.sp, self.ns_gla_d[l, t // 2, d], S[d][:])
        k.pop()
        k.push()
        amat = [[k.sb(f"gla_A{d}_{j}", [128, 4, 128], BF16) for j in range(2)] for d in range(2)]
        masks = (self.maskU, self.maskL)
        for t in range(NT):
            ts_ = slice(t * 128, (t + 1) * 128)
            for d in range(2):
                pas = [self.bank(), self.bank()]
                for h in range(4):
                    pair, e = h // 2, h % 2
                    es = slice(e * 64, (e + 1) * 64)
                    k.mm(pas[e][:, pair * 128:(pair + 1) * 128], kd[d][es, pair, ts_], qd[d][es, pair, ts_])
                mk = masks[d]
                for e in range(2):
                    k.tt(amat[d][t % 2][:, e::2, :], pas[e][:, 0:256].re("p (h q) -> p h q", h=2),
                         V(mk, mk.h[:, :].unsqueeze(1).to_broadcast([128, 2, 128])), ALU.mult)
            pos_ = [self.bank(), self.bank()]
            for h in range(4):
                pair, e = h // 2, h % 2
                es = slice(e * 64, (e + 1) * 64)
                out = pos_[e][:, pair * 128:(pair + 1) * 128]
                vt = vtok[:, t, h * 128:(h + 1) * 128]
                k.mm(out, vt, amat[0][t % 2][:, h, :], start=True, stop=False)
                k.mm(out, vt, amat[1][t % 2][:, h, :], start=False, stop=False)
                k.mm(out, sin_bf[0][es, t, pair * 128:(pair + 1) * 128], qd[0][es, pair, ts_], start=False, stop=False)
                k.mm(out, sin_bf[1][es, t, pair * 128:(pair + 1) * 128], qd[1][es, pair, ts_], start=False, stop=True)
            for e in range(2):
                k.cp(og[:, e::2, ts_], pos_[e][:, 0:256].re("p (h q) -> p h q", h=2), E=(k.act if (t + e) % 2 else k.dve))
        k.pop()
        self.dbg("gla_og", og[:], [128, 4, TOK])
        k.pop()
        gs = k.sb("gla_gs", [128, 4, TOK], BF16)
        yT = k.sb("gla_yT", [128, 4, TOK], BF16)
        for s in range(2):
            wb = self.ws.next()
            for m in range(2):
                for h in range(2):
                    pb = self.proj_fm(wb, m * 128, 128, h)
                    k.actf(gs[:, s * 2 + m, h * 512:(h + 1) * 512], pb[:], AF.Silu)
        k.push()
        sq = [k.sb(f"gla_sq{j}", [128, TOK], BF16) for j in range(2)]
        rstd = [k.sb(f"gla_rstd{j}", [128, TOK], F32) for j in range(2)]
        tmp = [k.sb(f"gla_tmp{j}", [128, TOK], F32) for j in range(2)]
        for h in range(4):
            sqh, rs, tp = sq[h % 2], rstd[h % 2], tmp[h % 2]
            k.actf(sqh[:], og[:, h, :], AF.Square)
            for hf in range(2):
                pb = self.bank()
                k.mm(pb[:], self.ones_bf[:], sqh[:, hf * 512:(hf + 1) * 512])
                k.actf(rs[:, hf * 512:(hf + 1) * 512], pb[:], AF.Ln, scale=1.0 / 128, bias=self.eps6)
            k.actf(rs[:], rs[:], AF.Exp, scale=-0.5)
            k.stt(tp[:], og[:, h, :], self.pp(l, "gla_norm"), rs[:], ALU.mult, ALU.mult)
            k.tt(yT[:, h, :], tp[:], gs[:, h, :], ALU.mult)
        k.pop()
        self.dbg("gla_yT", yT[:], [128, 4, TOK])
        self.branch_out(l, yT, 1)
        k.pop()

    ALPHA = float(np.exp(-0.5))

    def plan_rw(self, l):
        S = self.SEC
        for nm in ("rr", "rv"):
            for c0 in (0, 256):
                self.win_slab(l, S[nm] + c0, 256)
        self.win_slab(l, S["wlr"], 192)
        self.win_slab(l, S["glr"], 128)
        for c0 in (0, 256):
            self.win_slab(l, S["rk"] + c0, 256)
        self.plan_branch_out(l, "w_rw_o", 2)

    def rw_branch(self, l):
        k = self.k
        fl = self.flags
        A = self.ALPHA
        k.push()
        OT = k.sb("rw_OT", [128, 4, TOK], BF16)
        bonus = k.sb("rw_bonus", [128, 4, TOK], BF16)
        sgl = k.sb("rw_sgl", [128, TOK], BF16)
        k.push()
        rT = k.sb("rw_rT", [128, 4, TOK], BF16)
        kapT = k.sb("rw_kapT", [128, 4, TOK], BF16)
        kpT = k.sb("rw_kpT", [128, 4, TOK], BF16)
        nbT = k.sb("rw_nbT", [128, 4, TOK], BF16)
        thT = k.sb("rw_thT", [128, TOK], BF16)
        vtok = k.sb("rw_vtok", [128, NT, 512], BF16)
        w2bf = k.sb("rw_w2bf", [128, 512], BF16)
        k.push()
        a2bf = k.sb("rw_a2bf", [64, 512], BF16)
        k.push()
        w2f = k.sb("rw_w2f", [128, 512], F32)
        a2f = k.sb("rw_a2f", [64, 512], F32)
        k.dma(k.sp, w2f[:], self.rw_w2_d[l].re("d r c -> (d r) c"))
        k.dma(k.sp, a2f[:], self.rw_a2_d[l])
        k.cp(w2bf[:], w2f[:], E=k.pool)
        k.cp(a2bf[:], a2f[:], E=k.pool)
        k.pop()
        kT = k.sb("rw_kT", [128, 2, TOK], BF16)
        vT = k.sb("rw_vT", [128, 4, TOK], BF16)
        aT = k.sb("rw_aT", [128, TOK], BF16)
        wl = k.sb("rw_wl", [128, TOK], BF16)
        alT = k.sb("rw_alT", [64, TOK], BF16)
        glT = k.sb("rw_glT", [128, TOK], BF16)
        hb = k.sb("rw_hb", [128, 4, 258], BF16)
        mt = k.sb("rw_mt", [128, 4, 256], BF16)
        omu = k.sb("rw_omu", [128, 15], F32)
        hmu = k.sb("rw_hmu", [128, 15], F32)
        omka = k.sb("rw_omka", [128, 4], F32)
        k.ts(omu[:], self.pp(l, "rw_mu"), -1.0, ALU.mult, 1.0, ALU.add)
        k.ts(hmu[:], self.pp(l, "rw_mu"), 0.5, ALU.mult)
        k.ts(omka[:], self.pp(l, "rw_ka"), -1.0, ALU.mult, 1.0, ALU.add)
        k.memset(hb[:, :, 0:1], 0.0, E=k.pool)
        k.memset(hb[:, :, 257:258], 0.0, E=k.pool)

        def mix(wb, m0, mc, mucol, dst):
            b = hb
            t = mt
            for h in range(2):
                pb = self.proj_fm(wb, m0, mc, h)
                k.cp(b[0:mc, 2 * h:2 * h + 2, 1:257], pb[0:mc, :].re("p (q t) -> p q t", q=2),
                     E=(k.act if h == 0 else k.dve))
            k.tt(b[0:mc, 1:4, 0], b[0:mc, 0:3, 256], V(fl, fl.h[0:mc, 2:8:2]), ALU.mult)
            k.tt(b[0:mc, 0:3, 257], b[0:mc, 1:4, 1], V(fl, fl.h[0:mc, 9:15:2]), ALU.mult)
            k.tt(t[0:mc], b[0:mc, :, 0:256], b[0:mc, :, 2:258], ALU.add)
            k.ts(t[0:mc], t[0:mc], hmu[0:mc, mucol:mucol + 1], ALU.mult)
            k.stt(dst.re("p (q t) -> p q t", q=4), b[0:mc, :, 1:257], omu[0:mc, mucol:mucol + 1], t[0:mc],
                  ALU.mult, ALU.add)

        for bi, dstT in ((0, rT), (2, vT)):
            for s_ in range(2):
                wb = self.ws.next()
                for m in range(2):
                    c = s_ * 2 + m
                    mix(wb, m * 128, 128, bi * 4 + c, dstT[:, c, :])
        wb = self.ws.next()
        mix(wb, 0, 128, 12, wl[:, :])
        mix(wb, 128, 64, 13, alT[:, :])
        wb = self.ws.next()
        mix(wb, 0, 128, 14, glT[:, :])
        k.actf(thT[:], wl[:], AF.Tanh)
        k.actf(sgl[:], glT[:], AF.Sigmoid)
        f1 = k.sb("rw_f1", [128, TOK], F32)
        f2 = k.sb("rw_f2", [128, TOK], F32)
        b1 = k.sb("rw_b1", [128, TOK], BF16)
        for s_ in range(2):
            wb = self.ws.next()
            for m in range(2):
                mix(wb, m * 128, 128, 4 + s_ * 2 + m, kT[:, m, :])
            for m in range(2):
                c = s_ * 2 + m
                for h in range(2):
                    pb = self.bank()
                    k.mm(pb[:], a2bf[:, c * 128:(c + 1) * 128], alT[:, h * 512:(h + 1) * 512])
                    k.actf(aT[:, h * 512:(h + 1) * 512], pb[:], AF.Sigmoid, bias=self.pp(l, "rw_a0", c, c + 1))
                k.ts(f1[:], kT[:, m, :], self.pp(l, "rw_kk", c, c + 1), ALU.mult)
                k.actf(b1[:], f1[:], AF.Square)
                for h in range(2):
                    pb = self.bank()
                    k.mm(pb[:], self.blockones[:], b1[:, h * 512:(h + 1) * 512])
                    k.ts(f2[:, h * 512:(h + 1) * 512], pb[:], 1e-24, ALU.max)
                k.actf(f2[:], f2[:], AF.Sqrt)
                k.recip(f2[:], f2[:])
                k.tt(kapT[:, c, :], f1[:], f2[:], ALU.mult)
                k.stt(nbT[:, c, :], kapT[:, c, :], -1.0, aT[:], ALU.mult, ALU.mult)
                k.ts(f1[:], aT[:], self.pp(l, "rw_ka", c, c + 1), ALU.mult, omka[:, c:c + 1], ALU.add)
                k.tt(kpT[:, c, :], kT[:, m, :], f1[:], ALU.mult)
                k.stt(b1[:], rT[:, c, :], self.pp(l, "rw_rk", c, c + 1), kpT[:, c, :], ALU.mult, ALU.mult)
                for h in range(2):
                    pb = self.bank()
                    k.mm(pb[:], self.blockones[:], b1[:, h * 512:(h + 1) * 512])
                    k.tt(bonus[:, c, h * 512:(h + 1) * 512], pb[:], vT[:, c, h * 512:(h + 1) * 512], ALU.mult)
        for t in range(NT):
            pb = self.bank()
            pv = pb[:].bitcast(BF16)
            for c in range(4):
                k.tr(pv[:, c * 128:(c + 1) * 128], vT[:, c, t * 128:(t + 1) * 128], self.ident_bf[:], inc=(c == 3))
            k.cp(vtok[:, t, :], pv[:, 0:512], E=(k.act if t % 2 else k.dve))
        k.pop()
        self.dbg("rw_kapT", kapT[:], [128, 4, TOK])
        self.dbg("rw_kpT", kpT[:], [128, 4, TOK])
        self.dbg("rw_rT", rT[:], [128, 4, TOK])
        self.dbg("rw_nbT", nbT[:], [128, 4, TOK])
        k.push()
        sig = k.sb("rw_sig", [128, 4, 128], F32)
        Pc = k.sb("rw_P", [128, 4, 128], F32)
        Cx = k.sb("rw_Cx", [128, 4, 128], F32)
        Ea = k.sb("rw_Ea", [128, 4, 128], BF16)
        Eb = k.sb("rw_Eb", [128, 4, 128], BF16)
        Ec = k.sb("rw_Ec", [128, 4, 128], BF16)
        gam = k.sb("rw_gam", [128, 4], F32)
        RKt = k.sb("rw_RKt", [128, 4, 2, 128], BF16)
        kt = k.sb("rw_kt", [128, 4, 128], BF16)
        nbt = k.sb("rw_nbt", [128, 4, 128], BF16)
        tok = k.sb("rw_tok", [128, 3, 512], BF16)
        M1 = k.sb("rw_M1", [128, 8, 2, 128], BF16)
        M2 = k.sb("rw_M2", [128, 8, 2, 128], BF16)
        XY = [k.sb("rw_X0", [128, 4, 128], BF16),
              [k.sb(f"rw_XM{j}", [128, 4, 128], BF16) for j in range(2)],
              [k.sb(f"rw_ZT{j}", [128, 4, 128], BF16) for j in range(2)],
              [k.sb(f"rw_Tc{j}", [128, 4, 128], BF16) for j in range(2)]]
        TT = k.sb("rw_TT", [128, 8, 128], BF16)
        AV = k.sb("rw_AV", [128, 512], BF16)
        U = k.sb("rw_U", [128, 512], BF16)
        WT = k.sb("rw_WT", [128, 4, 128], BF16)
        Et = k.sb("rw_E", [128, 512], BF16)
        S = k.sb("rw_S", [128, 256], F32)
        Sin = k.sb("rw_Sin", [128, 256], F32)
        Stmp = k.sb("rw_Stmp", [128, 256], F32)
        Sbf = k.sb("rw_Sbf", [128, 256], BF16)
        nev = [0]

        def evac(dst, src):
            E = k.act if nev[0] % 2 else k.dve
            nev[0] += 1
            k.cp(dst, src, E=E)

        for d in range(2):
            m2 = self.mask2[d]
            mx = self.maskSL if d == 0 else self.maskSU
            k.dma(k.sp, Sin[:], self.st_rw_d[l, d])
            for step in range(NT):
                t = step if d == 0 else NT - 1 - step
                ts_ = slice(t * 128, (t + 1) * 128)
                ds_ = slice(d * 64, (d + 1) * 64)
                pb = self.bank()
                for c in range(4):
                    k.mm(pb[:, c * 128:(c + 1) * 128], w2bf[ds_, c * 128:(c + 1) * 128], thT[ds_, ts_])
                for c in range(4):
                    k.actf(sig[:, c, :], pb[:, c * 128:(c + 1) * 128], AF.Sigmoid,
                           bias=self.pp(l, "rw_w0", d * 4 + c, d * 4 + c + 1))
                for c in range(4):
                    k.scan(Pc[:, c, :], self.ones_f[:], sig[:, c, :], 0.0, ALU.mult, ALU.add)
                if d == 0:
                    k.tt(Cx[:], Pc[:], sig[:], ALU.subtract)
                    cin, cex = Pc, Cx
                    k.actf(gam[:], Pc[:, :, 127], AF.Exp, scale=-A)
                else:
                    k.actf(gam[:], Pc[:, :, 127], AF.Exp, scale=-A)
                    k.tt(Cx[:], V(Pc, Pc.h[:, :, 127:128].to_broadcast([128, 4, 128])), Pc[:], ALU.subtract)
                    k.tt(Pc[:], Cx[:], sig[:], ALU.add)
                    cin, cex = Pc, Cx
                k.actf(Ea[:], cin[:], AF.Exp, scale=-A)
                k.actf(Eb[:], cex[:], AF.Exp, scale=-A)
                k.actf(Ec[:], cin[:], AF.Exp, scale=A)
                k.tt(RKt[:, :, 0, :], rT[:, :, ts_], Ea[:], ALU.mult)
                k.tt(RKt[:, :, 1, :], kapT[:, :, ts_], Eb[:], ALU.mult)
                k.tt(kt[:], kpT[:, :, ts_], Ec[:], ALU.mult)
                k.tt(nbt[:], nbT[:, :, ts_], Ec[:], ALU.mult)
                pb = self.bank()
                pv = pb[:].bitcast(BF16)
                for c in range(4):
                    k.tr(pv[:, c * 128:(c + 1) * 128], RKt[:, c, 1, :], self.ident_bf[:], inc=False)
                for c in range(4):
                    k.tr(pv[:, 512 + c * 128:512 + (c + 1) * 128], kt[:, c, :], self.ident_bf[:], inc=(c == 3))
                evac(tok[:, 0:2, :], pv[:, :].re("p (a q) -> p a q", a=2))
                pb = self.bank()
                pv = pb[:].bitcast(BF16)
                for c in range(4):
                    k.tr(pv[:, c * 128:(c + 1) * 128], nbt[:, c, :], self.ident_bf[:], inc=(c == 3))
                evac(tok[:, 2, :], pv[:, 0:512])
                for gq in range(2):
                    bA = [self.bank(), self.bank()]
                    bB = [self.bank(), self.bank()]
                    b5 = [self.bank(), self.bank()]
                    for hh in range(4):
                        h = gq * 4 + hh
                        c, e = h // 2, h % 2
                        cc = hh // 2
                        es = slice(e * 64, (e + 1) * 64)
                        cs = slice(cc * 256, cc * 256 + 256)
                        k.mm(bA[e][:, cs], kt[es, c, :], RKt[es, c, :, :])
                        k.mm(bB[e][:, cs], nbt[es, c, :], RKt[es, c, :, :])
                        k.mm(b5[e][:, cc * 128:(cc + 1) * 128], RKt[es, c, 1, :], nbt[es, c, :])
                    m2v = V(m2, m2.h[:, :, :].unsqueeze(1).to_broadcast([128, 2, 2, 128]))
                    X0 = XY[0]
                    for e in range(2):
                        hs = slice(gq * 4 + e, gq * 4 + 4, 2)
                        k.tt(M1[:, hs, :, :], bA[e][:].re("p (h a t) -> p h a t", h=2, a=2), m2v, ALU.mult)
                        k.tt(M2[:, hs, :, :], bB[e][:].re("p (h a t) -> p h a t", h=2, a=2), m2v, ALU.mult)
                        k.tt(X0[:, e::2, :], b5[e][:, 0:256].re("p (h t) -> p h t", h=2),
                             V(mx, mx.h[:, :].unsqueeze(1).to_broadcast([128, 2, 128])), ALU.mult)
                    Y0 = M2[:, gq * 4:gq * 4 + 4, 1, :]
                    lmx = self.lvlmask[d]
                    l1t = self.lvlmask[1 - d]
                    Tc = XY[3][0]
                    k.tt(Tc[:], Y0, V(l1t, l1t.h[:, 0, :].unsqueeze(1).to_broadcast([128, 4, 128])), ALU.mult)
                    k.tt(Tc[:], Tc[:], V(self.ident_bf, self.ident_bf.h[:, :].unsqueeze(1).to_broadcast([128, 4, 128])),
                         ALU.add)
                    for lvl in range(1, 7):
                        XM = XY[1][lvl % 2]
                        k.tt(XM[:], X0[:], V(lmx, lmx.h[:, lvl, :].unsqueeze(1).to_broadcast([128, 4, 128])), ALU.mult,
                             E=k.pool)
                        bz = self.bank()
                        for hh in range(4):
                            k.mm(bz[:, hh * 128:(hh + 1) * 128], XM[:, hh, :], Tc[:, hh, :])
                        bt = self.bank()
                        btv = bt[:].bitcast(BF16)
                        for hh in range(4):
                            k.tr(btv[:, hh * 128:(hh + 1) * 128], Tc[:, hh, :], self.ident_bf[:], inc=(hh == 3))
                        Zs = XY[2][0]
                        Ts = XY[2][1]
                        evac(Zs[:], bz[:].re("p (h t) -> p h t", h=4))
                        evac(Ts[:], btv[:, 0:512].re("p (h t) -> p h t", h=4))
                        bp = self.bank()
                        for hh in range(4):
                            o_ = bp[:, hh * 128:(hh + 1) * 128]
                            k.mm(o_, self.ident_bf[:], Tc[:, hh, :], start=True, stop=False)
                            k.mm(o_, Ts[:, hh, :], Zs[:, hh, :], start=False, stop=True)
                        if lvl < 6:
                            Tn = XY[3][lvl % 2]
                            evac(Tn[:], bp[:].re("p (h t) -> p h t", h=4))
                            Tc = Tn
                        else:
                            evac(TT[:, gq * 4:gq * 4 + 4, :], bp[:].re("p (h t) -> p h t", h=4))
                pb = self.bank()
                for h in range(8):
                    k.mm(pb[:, h * 64:(h + 1) * 64], M1[:, h, 1, :], vtok[:, t, h * 64:(h + 1) * 64])
                evac(AV[:], pb[:])
                pb = self.bank()
                for h in range(8):
                    k.mm(pb[:, h * 64:(h + 1) * 64], TT[:, h, :], AV[:, h * 64:(h + 1) * 64])
                evac(U[:], pb[:])
                pb = self.bank()
                for h in range(8):
                    c, e = h // 2, h % 2
                    k.mm(pb[e * 64:(e + 1) * 64, c * 128:(c + 1) * 128], tok[:, 0, h * 64:(h + 1) * 64], TT[:, h, :])
                evac(WT[:], pb[:].re("p (c t) -> p c t", c=4))
                if step > 0:
                    fcol = t if d == 0 else 8 + t
                    k.ts(Sin[:], S[:], fl[:, fcol:fcol + 1], ALU.mult)
                k.cp(Sbf[:], Sin[:], E=k.act)
                pbe = [self.bank(), self.bank()]
                for h in range(8):
                    c, e = h // 2, h % 2
                    es = slice(e * 64, (e + 1) * 64)
                    k.mm(pbe[e][:, c * 64:(c + 1) * 64], WT[es, c, :], Sbf[es, c * 64:(c + 1) * 64])
                for e in range(2):
                    k.tt(Et[:].re("p (c e v) -> p c e v", c=4, e=2)[:, :, e, :],
                         pbe[e][:, 0:256].re("p (c v) -> p c v", c=4),
                         U[:].re("p (c e v) -> p c e v", c=4, e=2)[:, :, e, :], ALU.add)
                if self.stop == "R1":
                    for nm, tl_, shp in (("rwd_tok", tok, [128, 3, 512]), ("rwd_E", Et, [128, 512]), ("rwd_U", U, [128, 512]),
                                         ("rwd_TT", TT, [128, 8, 128]), ("rwd_M1", M1, [128, 8, 2, 128]),
                                         ("rwd_M2", M2, [128, 8, 2, 128]), ("rwd_RKt", RKt, [128, 4, 2, 128]),
                                         ("rwd_kt", kt, [128, 4, 128]), ("rwd_nbt", nbt, [128, 4, 128]),
                                         ("rwd_sig", sig, [128, 4, 128]), ("rwd_P", Pc, [128, 4, 128]),
                                         ("rwd_WT", WT, [128, 4, 128]), ("rwd_AV", AV, [128, 512])):
                        self.debug.add(nm)
                        self.dbg(nm, tl_[:], shp)
                    self.chk("R1")
                pos_ = [self.bank(), self.bank()]
                for h in range(8):
                    c, e = h // 2, h % 2
                    es = slice(e * 64, (e + 1) * 64)
                    o_ = pos_[e][es, c * 128:(c + 1) * 128]
                    k.mm(o_, Sbf[es, c * 64:(c + 1) * 64], RKt[es, c, 0, :], start=True, stop=False)
                    k.mm(o_, vtok[:, t, h * 64:(h + 1) * 64], M1[:, h, 0, :], start=False, stop=False)
                    k.mm(o_, Et[:, h * 64:(h + 1) * 64], M2[:, h, 0, :], start=False, stop=True)
                for e in range(2):
                    es = slice(e * 64, (e + 1) * 64)
                    if d == 0:
                        evac(OT[es, :, ts_], pos_[e][es, :].re("p (c t) -> p c t", c=4))
                    else:
                        k.tt(OT[es, :, ts_], pos_[e][es, :].re("p (c t) -> p c t", c=4), OT[es, :, ts_], ALU.add)
                pb = self.bank()
                for h in range(8):
                    c, e = h // 2, h % 2
                    o_ = pb[e * 64:(e + 1) * 64, c * 64:(c + 1) * 64]
                    k.mm(o_, tok[:, 1, h * 64:(h + 1) * 64], vtok[:, t, h * 64:(h + 1) * 64], start=True, stop=False)
                    k.mm(o_, tok[:, 2, h * 64:(h + 1) * 64], Et[:, h * 64:(h + 1) * 64], start=False, stop=True)
                k.tt(Stmp[:], Sin[:], pb[:, 0:256], ALU.add)
                k.tt(S[:].re("p (c v) -> p c v", c=4), Stmp[:].re("p (c v) -> p c v", c=4),
                     V(gam, gam.h[:, :].unsqueeze(2).to_broadcast([128, 4, 64])), ALU.mult)
                if (d == 0 and t % 2 == 1) or (d == 1 and t % 2 == 0):
                    k.dma(k.sp, self.ns_rw_d[l, t // 2, d], S[:])
        k.pop()
        k.pop()
        self.dbg("rw_OT", OT[:], [128, 4, TOK])
        yT = k.sb("rw_yT", [128, 4, TOK], BF16)
        g2f = k.sb("rw_g2f", [128, 512], F32)
        g2bf = k.sb("rw_g2bf", [128, 512], BF16)
        k.dma(k.sp, g2f[:], self.rw_g2_d[l])
        k.cp(g2bf[:], g2f[:], E=k.pool)
        k.push()
        dd = k.sb("rw_dd", [128, TOK], F32)
        sq = k.sb("rw_sq", [128, TOK], BF16)
        rs = k.sb("rw_rs", [128, TOK], F32)
        for c in range(4):
            for h in range(2):
                hs = slice(h * 512, (h + 1) * 512)
                pb = self.bank()
                k.mm(pb[:], self.blockmean[:], OT[:, c, hs])
                k.tt(dd[:, hs], OT[:, c, hs], pb[:], ALU.subtract)
            k.actf(sq[:], dd[:], AF.Square)
            for h in range(2):
                hs = slice(h * 512, (h + 1) * 512)
                pb = self.bank()
                k.mm(pb[:], self.blockmean[:], sq[:, hs])
                k.actf(rs[:, hs], pb[:], AF.Ln, bias=self.epsgn)
            k.actf(rs[:], rs[:], AF.Exp, scale=-0.5)
            k.tt(dd[:], dd[:], rs[:], ALU.mult)
            k.ts(dd[:], dd[:], self.pp(l, "rw_ln_w", c, c + 1), ALU.mult, self.pp(l, "rw_ln_b", c, c + 1), ALU.add)
            k.tt(dd[:], dd[:], bonus[:, c, :], ALU.add)
            for h in range(2):
                hs = slice(h * 512, (h + 1) * 512)
                pb = self.bank()
                k.mm(pb[:], g2bf[:, c * 128:(c + 1) * 128], sgl[:, hs])
                k.tt(yT[:, c, hs], pb[:], dd[:, hs], ALU.mult)
        k.pop()
        self.dbg("rw_yT", yT[:], [128, 4, TOK])
        self.branch_out(l, yT, 2)
        k.pop()

    def group_rmsnorm(self, src, dst, nchunk, inv_n, eps, gains):
        k = self.k
        k.push()
        sq = k.sb("grn_sq", [128, nchunk, TOK], BF16)
        rstd = k.sb("grn_rstd", [128, TOK], F32)
        for c in range(nchunk):
            k.actf(sq[:, c, :], src[:, c, :], AF.Square)
        for h in range(2):
            pb = self.bank()
            for c in range(nchunk):
                k.mm(pb[:], self.ones_bf[:], sq[:, c, h * 512:(h + 1) * 512], start=(c == 0), stop=(c == nchunk - 1))
            k.actf(rstd[:, h * 512:(h + 1) * 512], pb[:], AF.Ln, scale=inv_n, bias=eps)
        k.actf(rstd[:], rstd[:], AF.Exp, scale=-0.5)
        for c in range(nchunk):
            k.stt(dst[:, c, :], src[:, c, :], gains[c], rstd[:], ALU.mult, ALU.mult)
        k.pop()


_PROG = {}


def get_prog(debug=()):
    key = tuple(sorted(debug))
    if key not in _PROG:
        p1 = Prog(debug)
        needed = p1.k.needed
        if os.environ.get("KALLINC", ""):
            needed = {E.name: set(range(1, E.cnt + 2)) for E in p1.k.engs}
        _PROG[key] = Prog(debug, needed)
    return _PROG[key]


def make_in_maps(inp):
    inp = {k_: np.asarray(v) for k_, v in inp.items()}
    pp = np.stack([pack_params(inp, l) for l in range(DEPTH)], axis=0)
    gp = _cm(inp["final_norm"])
    ti = np.arange(128)[:, None]
    si = np.arange(128)[None, :]
    lm = np.zeros((2, 128, 7, 128), np.float32)
    for lvl in range(7):
        m_ = 1 << lvl
        msk = ((ti // (2 * m_)) == (si // (2 * m_))) & ((ti % (2 * m_)) >= m_) & ((si % (2 * m_)) < m_)
        lm[0, :, lvl, :] = msk
        lm[1, :, lvl, :] = msk.T
    shared = {"pp": pp, "gp": gp, "lvlmask": lm}
    shared["gla_gk_w"] = np.ascontiguousarray(inp["gla_gk_w"], dtype=np.float32)
    for nm in ("rw_w2", "rw_a2", "rw_g2"):
        shared[nm] = np.ascontiguousarray(inp[nm], dtype=np.float32)
    for name in ["w_ada", "ffn_gate", "ffn_up", "ffn_down", "w_in", "w_ssd_o", "w_gla_o", "w_rw_o", "w_out"]:
        shared[name] = np.ascontiguousarray(inp[name], dtype=np.float32)
    maps = []
    for core in range(8):
        m = dict(shared)
        flags = np.zeros((128, 32), np.float32)
        if core < 4:
            x = inp["x_prompt"][4 * core:4 * core + 4].reshape(TOK, D)
            cond = inp["c_ctx"]
            cf = np.array([0, 1, 0, 1, 0, 1, 0, 1], np.float32)
            cb = np.array([1, 0, 1, 0, 1, 0, 1, 0], np.float32)
            posf = 0.0
        else:
            x = inp["x_sample"][core - 4]
            cond = inp["c"][core - 4]
            cf = np.array([0, 1, 1, 1, 1, 1, 1, 1], np.float32)
            cb = np.array([1, 1, 1, 1, 1, 1, 1, 0], np.float32)
            posf = 1.0
        flags[:, 0:8] = cf[None]
        flags[:, 8:16] = cb[None]
        flags[:, 16] = posf
        if core < 4:
            st_ssd = np.zeros((DEPTH, 2, 128, 256), np.float32)
        else:
            ss = inp["state_ssd"][core - 4]
            st_ssd = np.ascontiguousarray(
                ss.reshape(DEPTH, 2, 2, 4, 64, 64).transpose(0, 1, 2, 5, 3, 4).reshape(DEPTH, 2, 128, 256))
        m["st_ssd"] = st_ssd
        if core < 4:
            st_gla = np.zeros((DEPTH, 2, 128, 256), np.float32)
        else:
            sg = inp["state_gla"][core - 4]
            st_gla = np.ascontiguousarray(
                sg.reshape(DEPTH, 2, 2, 2, 64, 128).transpose(0, 1, 3, 4, 2, 5).reshape(DEPTH, 2, 128, 256))
        m["st_gla"] = st_gla
        if core < 4:
            st_rw = np.zeros((DEPTH, 2, 128, 256), np.float32)
        else:
            sr = inp["state_rwkv"][core - 4]
            st_rw = np.ascontiguousarray(
                sr.reshape(DEPTH, 2, 4, 2, 64, 64).transpose(0, 1, 3, 5, 2, 4).reshape(DEPTH, 2, 128, 256))
        m["st_rw"] = st_rw
        m["xT"] = np.ascontiguousarray(x.T, dtype=np.float32)
        m["cond"] = _cm(cond)
        m["flags"] = flags
        maps.append(m)
    return maps


def run(inp, debug=(), trace=False):
    prog = get_prog(debug)
    maps = make_in_maps(inp)
    res = run_bass_kernel_spmd(prog.k.nc, maps, core_ids=list(range(8)), trace=trace)
    return prog, res


def kernel(**inputs):
    prog, res = run(inputs)
    r = res.results
    y_prompt = np.zeros((16, 256, D), np.float32)
    y_sample = np.zeros((4, 1024, D), np.float32)
    for core in range(8):
        y = np.ascontiguousarray(r[core]["yT"].T)
        if core < 4:
            y_prompt[4 * core:4 * core + 4] = y.reshape(4, 256, D)
        else:
            y_sample[core - 4] = y
    ns_ssd = np.zeros((16, DEPTH, 2, 8, 64, 64), np.float32)
    for core in range(4):
        raw = r[core]["ns_ssd"]
        v = raw.reshape(DEPTH, 4, 2, 2, 64, 4, 64).transpose(1, 0, 2, 3, 5, 6, 4)
        ns_ssd[4 * core:4 * core + 4] = v.reshape(4, DEPTH, 2, 8, 64, 64)
    ns_gla = np.zeros((16, DEPTH, 2, 4, 64, 128), np.float32)
    for core in range(4):
        raw = r[core]["ns_gla"]
        v = raw.reshape(DEPTH, 4, 2, 2, 64, 2, 128).transpose(1, 0, 2, 5, 3, 4, 6)
        ns_gla[4 * core:4 * core + 4] = v.reshape(4, DEPTH, 2, 4, 64, 128)
    ns_rw = np.zeros((16, DEPTH, 2, 8, 64, 64), np.float32)
    for core in range(4):
        raw = r[core]["ns_rw"]
        v = raw.reshape(DEPTH, 4, 2, 2, 64, 4, 64).transpose(1, 0, 2, 5, 3, 6, 4)
        ns_rw[4 * core:4 * core + 4] = v.reshape(4, DEPTH, 2, 8, 64, 64)
    return (y_prompt, y_sample, ns_ssd, ns_gla, ns_rw)
```

```python
import os
import numpy as np
from contextlib import ExitStack
import concourse.bass as bass
import concourse.mybir as mybir
from concourse.bass_utils import run_bass_kernel_spmd

F32 = mybir.dt.float32
BF16 = mybir.dt.bfloat16
I32 = mybir.dt.int32
ALU = mybir.AluOpType
AF = mybir.ActivationFunctionType

D = 1024
TOK = 1024
NT = 8
DFF = 2816
FC = 22
DEPTH = 2
NIN = 7792
PI = float(np.pi)


class V:
    __slots__ = ("t", "ap")

    def __init__(self, t, ap):
        self.t = t
        self.ap = ap

    def __getitem__(self, idx):
        return V(self.t, self.ap[idx])

    def re(self, s, **kw):
        return V(self.t, self.ap.rearrange(s, **kw))

    def bc(self, shape):
        return V(self.t, self.ap.to_broadcast(list(shape)))

    def bitcast(self, dt):
        return V(self.t, self.ap.bitcast(dt))

    @property
    def shape(self):
        return self.ap.shape


class Tile:
    __slots__ = ("h", "name", "w", "r", "dsem", "dcnt", "psum")

    def __init__(self, h, name, r0=None):
        self.h = h
        self.name = name
        self.psum = False
        self.w = None
        self.r = dict(r0) if r0 else {}
        self.dsem = None
        self.dcnt = 0

    def __getitem__(self, idx):
        return V(self, self.h[idx])


class Eng:
    def __init__(self, name, e):
        self.name = name
        self.e = e
        self.sem = None
        self.cnt = 0
        self.val = 0
        self.ord2val = {}
        self.seen = {}


class K:
    def __init__(self, needed=None):
        self.record = needed is None
        self.needed = {} if needed is None else needed
        self.nc = bass.Bass("TRN2", target_bir_lowering=False)
        self.es = ExitStack()
        self.scopes = [self.es]
        nc = self.nc
        self.pe = Eng("pe", nc.tensor)
        self.act = Eng("act", nc.scalar)
        self.dve = Eng("dve", nc.vector)
        self.pool = Eng("pool", nc.gpsimd)
        self.sp = Eng("sp", nc.sync)
        self.engs = [self.pe, self.act, self.dve, self.pool, self.sp]
        self.nsem = 0
        self.dsem_free = []
        for E in self.engs:
            self._newsem(E)
        self.out_waits = []
        self.nincs = 0
        self.all_dsem = {}
        self.ntile = 0
        self.barrier = {}
        self.scope_tiles = [[]]
        self.ninst = 0

    def _sem(self, name):
        self.nsem += 1
        return self.es.enter_context(self.nc.semaphore(name))

    def _newsem(self, E):
        E.sem = self._sem(f"s_{E.name}_{self.nsem}")
        E.cnt = 0

    def dram(self, name, shape, dt, kind):
        return V(None, self.nc.dram_tensor(name, list(shape), dt, kind=kind).ap())

    def sb(self, name, shape, dt=F32):
        self.ntile += 1
        h = self.scopes[-1].enter_context(self.nc.sbuf_tensor(f"{name}_{self.ntile}", list(shape), dt))
        t = Tile(h, name, self.barrier)
        self.scope_tiles[-1].append(t)
        return t

    def ps(self, name, shape, dt=F32):
        self.ntile += 1
        h = self.es.enter_context(self.nc.psum_tensor(f"{name}_{self.ntile}", list(shape), dt))
        t = Tile(h, name)
        t.psum = True
        return t

    def push(self):
        es = ExitStack()
        self.scopes.append(es)
        self.scope_tiles.append([])

    def pop(self):
        for t in self.scope_tiles.pop():
            if t.w is not None:
                s, v = t.w
                if self.barrier.get(s, 0) < v:
                    self.barrier[s] = v
            for s, v in t.r.items():
                if self.barrier.get(s, 0) < v:
                    self.barrier[s] = v
            if t.dsem is not None:
                self.dsem_free.append((t.dsem, t.dcnt))
        self.scopes.pop().close()

    def _wait(self, E, key, n):
        if E.seen.get(key, 0) >= n:
            return
        E.seen[key] = n
        if isinstance(key, Eng):
            if self.record:
                self.needed.setdefault(key.name, set()).add(n)
                return
            E.e.wait_ge(key.sem, key.ord2val[n])
        else:
            E.e.wait_ge(key, n)

    def _deps(self, E, reads, writes):
        waits = {}

        def need(s, v):
            if waits.get(s, 0) < v:
                waits[s] = v

        for t in reads:
            if t.w is not None:
                need(*t.w)
            if t.psum:
                for s, v in t.r.items():
                    if s is not E:
                        need(s, v)
        strict = (E is not self.pe) and (E is self.pool or os.environ.get("KRELAX", "") == "")
        for t in writes:
            if t.w is not None and (strict or t.w[0] is not E):
                need(*t.w)
            for s, v in t.r.items():
                if strict or s is not E:
                    need(s, v)
        for s, v in waits.items():
            self._wait(E, s, v)

    def emit(self, E, fn, reads, writes, inc=True):
        reads = [x.t for x in reads if x is not None and x.t is not None]
        writes = [x.t for x in writes if x is not None and x.t is not None]
        self._deps(E, reads, writes)
        ins = fn()
        self.ninst += 1
        if inc:
            E.cnt += 1
            cid = E.cnt
            if (not self.record) and cid in self.needed.get(E.name, ()):
                E.val += 1
                ins.then_inc(E.sem, 1)
                E.ord2val[cid] = E.val
                self.nincs += 1
        else:
            cid = E.cnt + 1
        for t in reads:
            t.r[E] = cid
        for t in writes:
            t.w = (E, cid)
            t.r = {}
        return ins

    def dma(self, Q, out, in_, **kw):
        reads = [in_.t] if in_.t is not None else []
        writes = [out.t] if out.t is not None else []
        self._deps(Q, reads, writes)
        tl = out.t if out.t is not None else in_.t
        if tl.dsem is None:
            if self.dsem_free:
                tl.dsem, tl.dcnt = self.dsem_free.pop()
                self._wait(Q, tl.dsem, tl.dcnt)
            else:
                tl.dsem = self._sem(f"d_{tl.name}_{self.nsem}")
        ins = Q.e.dma_start(out=out.ap, in_=in_.ap, **kw)
        ins.then_inc(tl.dsem, 16)
        self.ninst += 1
        tl.dcnt += 16
        self.all_dsem[tl.dsem] = tl.dcnt
        if out.t is not None:
            out.t.w = (tl.dsem, tl.dcnt)
            out.t.r = {}
        if in_.t is not None:
            in_.t.r[tl.dsem] = tl.dcnt
        if out.t is None:
            self.out_waits.append((tl.dsem, tl.dcnt))
        return ins

    def finish(self):
        for s, v in self.all_dsem.items():
            self._wait(self.sp, s, v)
        for E in self.engs:
            if E is not self.sp and E.cnt > 0:
                self._wait(self.sp, E, E.cnt)

    def mm(self, out, lhsT, rhs, start=True, stop=True, inc=None, **kw):
        if inc is None:
            inc = stop
        return self.emit(self.pe, lambda: self.nc.tensor.matmul(out.ap, lhsT.ap, rhs.ap, start=start, stop=stop, **kw),
                         [lhsT, rhs], [out], inc=inc)

    def tr(self, out, in_, ident, inc=True):
        return self.emit(self.pe, lambda: self.nc.tensor.transpose(out.ap, in_.ap, ident.ap), [in_, ident], [out],
                         inc=inc)

    def actf(self, out, in_, func, bias=None, scale=None, accum=None):
        kw = {}
        rd = [in_]
        if bias is not None:
            if isinstance(bias, V):
                kw["bias"] = bias.ap
                rd.append(bias)
            else:
                kw["bias"] = float(bias)
        if scale is not None:
            if isinstance(scale, V):
                kw["scale"] = scale.ap
                rd.append(scale)
            else:
                kw["scale"] = float(scale)
        wr = [out]
        if accum is not None:
            kw["accum_out"] = accum.ap
            wr.append(accum)
        return self.emit(self.act, lambda: self.nc.scalar.activation(out.ap, in_.ap, func, **kw), rd, wr)

    def _ve(self, E):
        return E if E is not None else self.dve

    def tt(self, out, a, b, op, E=None):
        E = self._ve(E)
        return self.emit(E, lambda: E.e.tensor_tensor(out.ap, a.ap, b.ap, op), [a, b], [out])

    def ts(self, out, a, s1, op0, s2=None, op1=None, E=None):
        E = self._ve(E)
        rd = [a]
        a1 = s1
        if isinstance(s1, V):
            rd.append(s1)
            a1 = s1.ap
        a2 = s2
        if isinstance(s2, V):
            rd.append(s2)
            a2 = s2.ap
        kw = {}
        if op1 is not None:
            kw["op1"] = op1
        return self.emit(E, lambda: E.e.tensor_scalar(out.ap, a.ap, a1, a2, op0, **kw), rd, [out])

    def stt(self, out, a, s, b, op0, op1):
        E = self.dve
        rd = [a, b]
        a1 = s
        if isinstance(s, V):
            rd.append(s)
            a1 = s.ap
        return self.emit(E, lambda: E.e.scalar_tensor_tensor(out.ap, a.ap, a1, b.ap, op0, op1), rd, [out])

    def cp(self, out, in_, E=None):
        E = self._ve(E)
        if E is self.act:
            return self.emit(E, lambda: self.nc.scalar.copy(out.ap, in_.ap), [in_], [out])
        return self.emit(E, lambda: E.e.tensor_copy(out.ap, in_.ap), [in_], [out])

    def memset(self, out, val, E=None):
        E = self._ve(E)
        return self.emit(E, lambda: E.e.memset(out.ap, val), [], [out])

    def scan(self, out, d0, d1, init, op0, op1):
        rd = [d0, d1]
        i = init
        if isinstance(init, V):
            rd.append(init)
            i = init.ap
        return self.emit(self.dve, lambda: self.nc.vector.tensor_tensor_scan(out.ap, d0.ap, d1.ap, i, op0, op1), rd,
                         [out])

    def recip(self, out, in_):
        return self.emit(self.dve, lambda: self.nc.vector.reciprocal(out.ap, in_.ap), [in_], [out])

    def iota(self, out, pattern, base, cm):
        return self.emit(self.pool, lambda: self.nc.gpsimd.iota(out.ap, pattern, base=base, channel_multiplier=cm,
                                                               allow_small_or_imprecise_dtypes=True), [], [out])

    def asel(self, out, in_, pattern, op, fill, base, cm):
        return self.emit(self.pool, lambda: self.nc.gpsimd.affine_select(out.ap, in_.ap, pattern, op, fill, base=base,
                                                                        channel_multiplier=cm), [in_], [out])


PP_SPEC = [("norm_g0", 8), ("norm_g1", 8), ("norm_g2", 8), ("b_ada", 72),
           ("conv_w0", 6), ("conv_w1", 6), ("conv_w2", 6), ("conv_b", 6), ("ssd_norm", 4),
           ("ssd_D0", 4), ("ssd_D1", 4), ("dt_bias", 16), ("A_log", 16), ("gk_b", 4), ("gla_norm", 1),
           ("rw_mu", 15), ("rw_w0", 8), ("rw_a0", 4), ("rw_kk", 4), ("rw_ka", 4), ("rw_rk", 4),
           ("rw_ln_w", 4), ("rw_ln_b", 4)]
PP_OFF = {}
_o = 0
for _n, _c in PP_SPEC:
    PP_OFF[_n] = (_o, _c)
    _o += _c
NPP = _o


def _cm(vec):
    vec = np.asarray(vec, np.float32).reshape(-1)
    n = vec.shape[0] // 128
    return np.ascontiguousarray(vec.reshape(n, 128).T)


def pack_params(inp, l):
    pp = np.zeros((128, NPP), np.float32)

    def put(name, arr):
        o, c = PP_OFF[name]
        assert arr.shape == (128, c), (name, arr.shape, c)
        pp[:, o:o + c] = arr

    for i in range(3):
        put(f"norm_g{i}", _cm(inp["norm_g"][l, i]))
    put("b_ada", _cm(inp["b_ada"][l]))
    for i in range(3):
        put(f"conv_w{i}", _cm(inp["ssd_conv_w"][l, i]))
    put("conv_b", _cm(inp["ssd_conv_b"][l]))
    put("ssd_norm", _cm(inp["ssd_norm"][l]))
    hd = (2 * np.arange(4)[None, :] + (np.arange(128)[:, None] // 64))
    put("ssd_D0", inp["ssd_D"][l, 0][hd])
    put("ssd_D1", inp["ssd_D"][l, 1][hd])
    put("dt_bias", np.broadcast_to(inp["ssd_dt_bias"][l].reshape(1, 16), (128, 16)))
    put("A_log", np.broadcast_to(inp["ssd_A_log"][l].reshape(1, 16), (128, 16)))
    put("gk_b", np.concatenate([_cm(inp["gla_gk_b"][l, 0]), _cm(inp["gla_gk_b"][l, 1])], axis=1))
    put("gla_norm", _cm(inp["gla_norm"][l]))
    mu = inp["rw_mu"][l]
    mucols = np.zeros((128, 15), np.float32)
    mucols[:, 0:13] = _cm(mu[0:1664])
    mucols[0:64, 13] = mu[1664:1728]
    mucols[:, 14] = mu[1728:1856]
    put("rw_mu", mucols)
    put("rw_w0", np.concatenate([_cm(inp["rw_w0"][l, 0]), _cm(inp["rw_w0"][l, 1])], axis=1))
    put("rw_a0", _cm(inp["rw_a0"][l]))
    put("rw_kk", _cm(inp["rw_kk"][l]))
    put("rw_ka", _cm(inp["rw_ka"][l]))
    put("rw_rk", _cm(inp["rw_rk"][l]))
    put("rw_ln_w", _cm(inp["rw_ln_w"][l]))
    put("rw_ln_b", _cm(inp["rw_ln_b"][l]))
    return pp


class WS:
    NST = 2
    NBF = 3
    CAST_PAT = ["dve", "act", "dve", "pool", "dve", "act", "dve", "act"]

    def __init__(self, k):
        self.k = k
        self.st = [k.sb(f"wst{i}", [128, 2048], F32) for i in range(self.NST)]
        self.bf = [k.sb(f"wbf{i}", [128, 2048], BF16) for i in range(self.NBF)]
        self.slabs = []
        self.base_j = 0
        self.limit = None
        self.old = {}
        self.nd = 0
        self.ncast = 0
        self.nget = 0

    def add(self, dview, kcs, ncols):
        assert kcs * ncols <= 2048
        self.slabs.append((dview, kcs, ncols))

    def _stbuf(self, j):
        return self.st[(j - self.base_j) % self.NST] if j >= self.base_j else self.old[j]

    def rebase(self, nst):
        self.old = {j: self._stbuf(j) for j in range(self.ncast, self.nd)}
        assert all(b in self.st[:nst] for b in self.old.values()) or not self.old, "in-flight slab in a dropped buffer"
        self.st = self.st[:nst]
        self.NST = nst
        self.limit = None
        self.base_j = self.nd

    def _dma(self, j):
        dview, kcs, ncols = self.slabs[j]
        st = self._stbuf(j)
        self.k.dma(self.k.sp, st[:, 0:kcs * ncols].re("p (a b) -> p a b", a=kcs), dview)

    def _cast(self, j):
        dview, kcs, ncols = self.slabs[j]
        n = kcs * ncols
        E = getattr(self.k, self.CAST_PAT[j % len(self.CAST_PAT)])
        self.k.cp(self.bf[j % self.NBF][:, 0:n], self._stbuf(j)[:, 0:n], E=E)

    def next(self):
        j = self.nget
        self.nget += 1
        n = len(self.slabs) if self.limit is None else min(len(self.slabs), self.limit)
        while self.nd < min(n, j + self.NST):
            self._dma(self.nd)
            self.nd += 1
        while self.ncast < min(n, j + 2):
            self._cast(self.ncast)
            self.ncast += 1
        dview, kcs, ncols = self.slabs[j]
        return self.bf[j % self.NBF][:, 0:kcs * ncols].re("p (a b) -> p a b", a=kcs)


def wview(w2d, k0, k1, c0, c1):
    return w2d.re("(kc p) n -> p kc n", p=128)[:, k0:k1, c0:c1]


class StopBuild(Exception):
    pass


class Prog:
    def __init__(self, debug=(), needed=None):
        import os
        self.stop = os.environ.get("KSTOP", "")
        self.debug = set(debug)
        self.k = K(needed)
        self.dbg_out = {}
        self.build()

    def pp(self, l, name, c0=0, c1=None):
        o, c = PP_OFF[name]
        if c1 is None:
            c1 = c
        return self.PP[l][:, o + c0:o + c1]

    def chk(self, name):
        if self.stop == name:
            raise StopBuild(name)

    def bank(self):
        b = self.banks[self.nbank % 8]
        self.nbank += 1
        return b

    def dbg(self, name, view, shape):
        if name not in self.debug or name in self.dbg_out:
            return
        d = self.k.dram("dbg_" + name, shape, view.ap.dtype, "ExternalOutput")
        self.k.dma(self.k.sp, d, view)
        self.dbg_out[name] = shape

    def build(self):
        k = self.k
        nc = k.nc
        self.xT_d = k.dram("xT", [D, TOK], F32, "ExternalInput")
        self.cond_d = k.dram("cond", [128, 8], F32, "ExternalInput")
        self.flags_d = k.dram("flags", [128, 32], F32, "ExternalInput")
        self.gp_d = k.dram("gp", [128, 8], F32, "ExternalInput")
        self.pp_d = k.dram("pp", [DEPTH, 128, NPP], F32, "ExternalInput")
        self.w = {}
        for name, shape in [("w_ada", [DEPTH, D, 9 * D]), ("ffn_gate", [DEPTH, 2, D, DFF]),
                            ("ffn_up", [DEPTH, 2, D, DFF]), ("ffn_down", [DEPTH, 2, DFF, D]),
                            ("w_in", [DEPTH, D, NIN]), ("w_ssd_o", [DEPTH, 512, D]), ("w_gla_o", [DEPTH, 512, D]),
                            ("w_rw_o", [DEPTH, 512, D]), ("w_out", [DEPTH, D, D])]:
            self.w[name] = k.dram(name, shape, F32, "ExternalInput")
        self.yT_d = k.dram("yT", [D, TOK], F32, "ExternalOutput")
        self.lvlmask_d = k.dram("lvlmask", [2, 128, 7, 128], F32, "ExternalInput")
        self.st_ssd_d = k.dram("st_ssd", [DEPTH, 2, 128, 256], F32, "ExternalInput")
        self.ns_ssd_d = k.dram("ns_ssd", [DEPTH, 4, 2, 128, 256], F32, "ExternalOutput")
        self.st_gla_d = k.dram("st_gla", [DEPTH, 2, 128, 256], F32, "ExternalInput")
        self.ns_gla_d = k.dram("ns_gla", [DEPTH, 4, 2, 128, 256], F32, "ExternalOutput")
        self.gkw_d = k.dram("gla_gk_w", [DEPTH, 2, 16, 256], F32, "ExternalInput")
        self.st_rw_d = k.dram("st_rw", [DEPTH, 2, 128, 256], F32, "ExternalInput")
        self.ns_rw_d = k.dram("ns_rw", [DEPTH, 4, 2, 128, 256], F32, "ExternalOutput")
        self.rw_w2_d = k.dram("rw_w2", [DEPTH, 2, 64, 512], F32, "ExternalInput")
        self.rw_a2_d = k.dram("rw_a2", [DEPTH, 64, 512], F32, "ExternalInput")
        self.rw_g2_d = k.dram("rw_g2", [DEPTH, 128, 512], F32, "ExternalInput")

        self.xT = k.sb("xT", [128, 8, TOK], F32)
        self.hT = k.sb("hT", [128, 8, TOK], BF16)
        self.flags = k.sb("flags", [128, 32], F32)
        self.gp = k.sb("gp", [128, 8], F32)
        self.PP = [k.sb(f"pp{l}", [128, NPP], F32) for l in range(DEPTH)]
        self.ada = [k.sb(f"ada{l}", [128, 72], F32) for l in range(DEPTH)]
        self.modA = [k.sb(f"modA{l}", [128, 24], F32) for l in range(DEPTH)]
        self.gate = [k.sb(f"gate{l}", [128, 24], F32) for l in range(DEPTH)]
        self.ones_bf = k.sb("ones_bf", [128, 128], BF16)
        self.ident_bf = k.sb("ident_bf", [128, 128], BF16)
        self.ident_f = k.sb("ident_f", [128, 128], F32)
        self.cst = k.sb("cst", [128, 8], F32)
        self.eps6 = self.cst[:, 0:1]
        self.epsgn = self.cst[:, 1:2]
        self.ones_f = k.sb("ones_f", [128, 128], F32)
        self.maskU = k.sb("maskU", [128, 128], F32)
        self.maskL = k.sb("maskL", [128, 128], F32)
        self.maskSU = k.sb("maskSU", [128, 128], F32)
        self.maskSL = k.sb("maskSL", [128, 128], F32)
        self.mask2 = [k.sb(f"mask2_{d}", [128, 2, 128], F32) for d in range(2)]
        self.lvlmask = [k.sb(f"lvlmask{d}", [128, 7, 128], BF16) for d in range(2)]
        self.blockones = k.sb("blockones", [128, 128], BF16)
        self.blockmean = k.sb("blockmean", [128, 128], BF16)
        self.banks = [k.ps(f"bank{i}", [128, 512], F32) for i in range(8)]
        self.nbank = 0
        self.ws = WS(k)

        k.dma(k.sp, self.flags[:], self.flags_d)
        k.dma(k.sp, self.gp[:], self.gp_d)
        for l in range(DEPTH):
            k.dma(k.sp, self.PP[l][:], self.pp_d[l])
        xv = self.xT_d.re("(c p) t -> p c t", p=128)
        for c in range(8):
            k.dma(k.sp, self.xT[:, c, :], xv[:, c, :])

        k.push()
        lmf = k.sb("lvlmask_f", [128, 7, 128], F32)
        for d in range(2):
            k.dma(k.sp, lmf[:], self.lvlmask_d[d])
            k.cp(self.lvlmask[d][:], lmf[:], E=k.pool)
        k.pop()
        k.memset(self.cst[:, 0:1], 1e-6, E=k.pool)
        k.memset(self.cst[:, 1:2], 64e-5, E=k.pool)
        k.memset(self.cst[:, 2:3], 1.0, E=k.pool)
        k.memset(self.ones_bf[:], 1.0, E=k.pool)
        k.memset(self.ones_f[:], 1.0, E=k.pool)
        k.memset(self.maskU[:], 1.0, E=k.pool)
        k.asel(self.maskU[:], self.maskU[:], [[1, 128]], ALU.is_ge, 0.0, 0, -1)
        k.memset(self.maskSU[:], 1.0, E=k.pool)
        k.asel(self.maskSU[:], self.maskSU[:], [[1, 128]], ALU.is_ge, 0.0, -1, -1)
        k.memset(self.maskSL[:], 1.0, E=k.pool)
        k.asel(self.maskSL[:], self.maskSL[:], [[-1, 128]], ALU.is_ge, 0.0, -1, 1)
        k.memset(self.blockones[:], 0.0, E=k.pool)
        k.memset(self.blockmean[:], 0.0, E=k.pool)
        for e in range(2):
            k.memset(self.blockones[e * 64:(e + 1) * 64, e * 64:(e + 1) * 64], 1.0, E=k.pool)
            k.memset(self.blockmean[e * 64:(e + 1) * 64, e * 64:(e + 1) * 64], 1.0 / 64, E=k.pool)
        k.memset(self.maskL[:], 1.0, E=k.pool)
        k.asel(self.maskL[:], self.maskL[:], [[-1, 128]], ALU.is_ge, 0.0, 0, 1)
        k.cp(self.mask2[0][:, 0, :], self.maskU[:], E=k.pool)
        k.cp(self.mask2[0][:, 1, :], self.maskSU[:], E=k.pool)
        k.cp(self.mask2[1][:, 0, :], self.maskL[:], E=k.pool)
        k.cp(self.mask2[1][:, 1, :], self.maskSL[:], E=k.pool)
        k.memset(self.ident_f[:], 1.0, E=k.pool)
        k.asel(self.ident_f[:], self.ident_f[:], [[-1, 128]], ALU.is_equal, 0.0, 0, 1)
        k.cp(self.ident_bf[:], self.ident_f[:], E=k.pool)

        for l in range(DEPTH):
            self.plan_ada(l)
        for l in range(DEPTH):
            self.plan_ffn(l, 0)
            self.plan_mixer(l)
            self.plan_ffn(l, 1)

        self.pos_embed()
        self.compute_ada()
        try:
            for l in range(DEPTH):
                self.ffn(l, 0)
                self.dbg(f"x1_{l}", self.xT[:], [128, 8, TOK])
                self.mixer(l)
                self.dbg(f"x2_{l}", self.xT[:], [128, 8, TOK])
                self.ffn(l, 1)
        except StopBuild:
            while len(k.scopes) > 1:
                k.pop()
        self.final_norm()
        k.finish()

    def pos_embed(self):
        k = self.k
        k.push()
        idx = k.sb("pe_idx", [128, 2], F32)
        om = k.sb("pe_om", [128, 2], F32)
        pos = k.sb("pe_pos", [128, 80], F32)
        arg = k.sb("pe_arg", [128, 4, 80], F32)
        ki = k.sb("pe_ki", [128, 4, 80], I32)
        kf = k.sb("pe_kf", [128, 4, 80], F32)
        gt = k.sb("pe_gt", [128, 4, 80], F32)
        emb = k.sb("pe_emb", [128, 4, 80], F32)
        k.iota(idx[:], [[128, 2]], 0, 1)
        k.actf(om[:], idx[:], AF.Exp, scale=-float(np.log(10000.0)) / 256.0)
        k.iota(pos[:, 0:16], [[1, 16]], 0, 0)
        k.iota(pos[:, 16:80], [[1, 64]], 0, 0)
        for j in range(4):
            ph = 0.0 if j < 2 else PI / 2
            k.ts(arg[:, j, :], pos[:], om[:, (j % 2):(j % 2) + 1], ALU.mult, ph, ALU.add)
        k.ts(kf[:], arg[:], 1.0 / (2 * PI), ALU.mult)
        k.cp(ki[:], kf[:])
        k.cp(kf[:], ki[:])
        k.stt(arg[:], kf[:], -2 * PI, arg[:], ALU.mult, ALU.add)
        k.ts(gt[:], arg[:], PI, ALU.is_gt, -2 * PI, ALU.mult)
        k.tt(arg[:], arg[:], gt[:], ALU.add)
        k.ts(gt[:], arg[:], -PI, ALU.is_lt, 2 * PI, ALU.mult)
        k.tt(arg[:], arg[:], gt[:], ALU.add)
        k.actf(emb[:], arg[:], AF.Sin)
        k.ts(emb[:], emb[:], self.flags[:, 16:17], ALU.mult)
        self.dbg("emb", emb[:], [128, 4, 80])
        for fc in range(8):
            xv = self.xT[:, fc, :].re("p (r c) -> p r c", c=64)
            if fc < 4:
                ev = V(emb, emb.h[:, fc, 0:16].unsqueeze(2).to_broadcast([128, 16, 64]))
            else:
                ev = V(emb, emb.h[:, fc - 4, 16:80].unsqueeze(1).to_broadcast([128, 16, 64]))
            k.tt(xv, xv, ev, ALU.add)
        k.pop()

    def plan_ada(self, l):
        for s in range(36):
            self.ws.add(wview(self.w["w_ada"][l], 0, 8, s * 256, (s + 1) * 256), 8, 256)

    def compute_ada(self):
        k = self.k
        k.push()
        cond = k.sb("cond", [128, 8], F32)
        sg = k.sb("cond_sg", [128, 8], F32)
        scond = k.sb("scond", [128, 8], BF16)
        k.dma(k.sp, cond[:], self.cond_d)
        extra = [k.sb(f"wst_x{i}", [128, 2048], F32) for i in range(4)]
        self.ws.st = self.ws.st + extra
        self.ws.NST = len(self.ws.st)
        self.ws.limit = 36 * DEPTH
        k.actf(sg[:], cond[:], AF.Sigmoid)
        k.tt(scond[:], cond[:], sg[:], ALU.mult)
        for l in range(DEPTH):
            pb = self.bank()
            for s in range(36):
                wb = self.ws.next()
                for m in range(2):
                    j = s * 2 + m
                    for kc in range(8):
                        k.mm(pb[:, j:j + 1], wb[:, kc, m * 128:(m + 1) * 128], scond[:, kc:kc + 1],
                             start=(kc == 0), stop=(kc == 7))
            k.tt(self.ada[l][:], pb[:, 0:72], self.pp(l, "b_ada"), ALU.add)
            for i in range(3):
                k.stt(self.modA[l][:, i * 8:(i + 1) * 8], self.ada[l][:, (3 * i + 1) * 8:(3 * i + 2) * 8], 1.0,
                      self.pp(l, f"norm_g{i}"), ALU.add, ALU.mult)
                k.ts(self.gate[l][:, i * 8:(i + 1) * 8], self.ada[l][:, (3 * i + 2) * 8:(3 * i + 3) * 8],
                     0.5 if i != 1 else 1.0, ALU.mult)
            self.dbg(f"ada{l}", self.ada[l][:], [128, 72])
        self.ws.rebase(2)
        k.pop()

    def rstd_of_x(self, rstd):
        k = self.k
        k.push()
        sq = k.sb("sq", [128, 8, TOK], BF16)
        for c in range(8):
            k.actf(sq[:, c, :], self.xT[:, c, :], AF.Square)
        for h in range(2):
            pb = self.bank()
            for c in range(8):
                k.mm(pb[:], self.ones_bf[:], sq[:, c, h * 512:(h + 1) * 512], start=(c == 0), stop=(c == 7))
            k.actf(rstd[:, h * 512:(h + 1) * 512], pb[:], AF.Ln, scale=1.0 / D, bias=self.eps6)
        k.actf(rstd[:], rstd[:], AF.Exp, scale=-0.5)
        k.pop()

    def norm_mod(self, l, i):
        k = self.k
        k.push()
        rstd = k.sb("rstd", [128, TOK], F32)
        tmp = [k.sb(f"nm_tmp{j}", [128, TOK], F32) for j in range(2)]
        self.rstd_of_x(rstd)
        for c in range(8):
            t = tmp[c % 2]
            k.tt(t[:], self.xT[:, c, :], rstd[:], ALU.mult)
            k.actf(self.hT[:, c, :], t[:], AF.Identity, scale=self.modA[l][:, i * 8 + c:i * 8 + c + 1],
                   bias=self.ada[l][:, 3 * i * 8 + c:3 * i * 8 + c + 1])
        k.pop()

    def final_norm(self):
        k = self.k
        k.push()
        rstd = k.sb("rstd", [128, TOK], F32)
        tmp = [k.sb(f"fn_tmp{j}", [128, TOK], F32) for j in range(2)]
        self.rstd_of_x(rstd)
        yv = self.yT_d.re("(c p) t -> p c t", p=128)
        for c in range(8):
            t = tmp[c % 2]
            k.stt(t[:], self.xT[:, c, :], self.gp[:, c:c + 1], rstd[:], ALU.mult, ALU.mult)
            k.dma(k.sp, yv[:, c, :], t[:])
        k.pop()

    def plan_ffn(self, l, which):
        wg = self.w["ffn_gate"][l, which]
        wu = self.w["ffn_up"][l, which]
        wd = self.w["ffn_down"][l, which]
        for s in range(11):
            self.ws.add(wview(wg, 0, 8, s * 256, (s + 1) * 256), 8, 256)
            self.ws.add(wview(wu, 0, 8, s * 256, (s + 1) * 256), 8, 256)
        for s in range(4):
            for (k0, k1) in ((0, 8), (8, 16), (16, 22)):
                self.ws.add(wview(wd, k0, k1, s * 256, (s + 1) * 256), k1 - k0, 256)

    def ffn(self, l, which):
        k = self.k
        i = 0 if which == 0 else 2
        self.norm_mod(l, i)
        k.push()
        actT = k.sb("actT", [128, FC, TOK], BF16)
        sg = [k.sb(f"ffn_sg{j}", [128, 512], F32) for j in range(2)]
        nsg = 0
        for s in range(11):
            wg = self.ws.next()
            wu = self.ws.next()
            for m in range(2):
                fcb = s * 2 + m
                for h in range(2):
                    pg = self.bank()
                    pu = self.bank()
                    for kc in range(8):
                        k.mm(pg[:], wg[:, kc, m * 128:(m + 1) * 128], self.hT[:, kc, h * 512:(h + 1) * 512],
                             start=(kc == 0), stop=(kc == 7))
                    for kc in range(8):
                        k.mm(pu[:], wu[:, kc, m * 128:(m + 1) * 128], self.hT[:, kc, h * 512:(h + 1) * 512],
                             start=(kc == 0), stop=(kc == 7))
                    t = sg[nsg % 2]
                    nsg += 1
                    k.actf(t[:], pg[:], AF.Silu)
                    k.tt(actT[:, fcb, h * 512:(h + 1) * 512], pu[:], t[:], ALU.mult)
        gcol = self.gate[l]
        for s in range(4):
            pbs = [[self.bank() for h in range(2)] for m in range(2)]
            for ksub, (k0, k1) in enumerate(((0, 8), (8, 16), (16, 22))):
                wd = self.ws.next()
                for m in range(2):
                    for h in range(2):
                        for kc in range(k0, k1):
                            k.mm(pbs[m][h][:], wd[:, kc - k0, m * 128:(m + 1) * 128],
                                 actT[:, kc, h * 512:(h + 1) * 512], start=(kc == 0), stop=(kc == FC - 1),
                                 inc=(kc == k1 - 1))
            for m in range(2):
                c = s * 2 + m
                for h in range(2):
                    xs = self.xT[:, c, h * 512:(h + 1) * 512]
                    k.stt(xs, pbs[m][h][:], gcol[:, i * 8 + c:i * 8 + c + 1], xs, ALU.mult, ALU.add)
        k.pop()


    SEC = dict(z=0, xbc=512, dt=1280, q=1296, k=1552, v=1808, g=2320, lr=2832, rr=2864, rk=3376, rv=3888,
               wlr=4400, alr=4528, glr=4592, gate=4720)

    def win_slab(self, l, c0, ncols):
        self.ws.add(wview(self.w["w_in"][l], 0, 8, c0, c0 + ncols), 8, ncols)

    def plan_mixer(self, l):
        self.plan_ssd(l)
        self.plan_gla(l)
        self.plan_rw(l)
        self.plan_out(l, "w_out")

    def plan_out(self, l, name):
        w = self.w[name][l]
        if name == "w_out":
            for s in range(4):
                self.ws.add(wview(w, 0, 8, s * 256, (s + 1) * 256), 8, 256)
        else:
            for s in range(2):
                self.ws.add(wview(w, 0, 4, s * 512, (s + 1) * 512), 4, 512)

    def proj_fm(self, wb, m0, mcols, h):
        k = self.k
        pb = self.bank()
        for kc in range(8):
            k.mm(pb[0:mcols, :], wb[:, kc, m0:m0 + mcols], self.hT[:, kc, h * 512:(h + 1) * 512],
                 start=(kc == 0), stop=(kc == 7))
        return pb

    def mixer(self, l):
        k = self.k
        self.norm_mod(l, 1)
        k.push()
        self.merged = k.sb("merged", [128, 8, TOK], BF16)
        self.ssd_branch(l)
        self.dbg(f"mg0_{l}", self.merged[:], [128, 8, TOK])
        self.chk("F")
        self.gla_branch(l)
        self.dbg(f"mg1_{l}", self.merged[:], [128, 8, TOK])
        self.chk("G")
        self.rw_branch(l)
        self.dbg(f"mg2_{l}", self.merged[:], [128, 8, TOK])
        self.chk("H")
        mbf = self.merged
        gcol = self.gate[l]
        for s in range(4):
            wb = self.ws.next()
            for m in range(2):
                c = s * 2 + m
                for h in range(2):
                    pb = self.bank()
                    for kc in range(8):
                        k.mm(pb[:], wb[:, kc, m * 128:(m + 1) * 128], mbf[:, kc, h * 512:(h + 1) * 512],
                             start=(kc == 0), stop=(kc == 7))
                    xs = self.xT[:, c, h * 512:(h + 1) * 512]
                    k.stt(xs, pb[:], gcol[:, 8 + c:8 + c + 1], xs, ALU.mult, ALU.add)
        k.pop()

    def plan_branch_out(self, l, name, b):
        w = self.w[name][l]
        for s in range(4):
            self.ws.add(wview(w, 0, 4, s * 256, (s + 1) * 256), 4, 256)
            self.win_slab(l, self.SEC["gate"] + b * 1024 + s * 256, 256)

    def branch_out(self, l, yT, b):
        k = self.k
        k.push()
        sgt = [k.sb(f"bo_sg{j}", [128, 512], F32) for j in range(2)]
        tmp = [k.sb(f"bo_tmp{j}", [128, 512], F32) for j in range(2)]
        n = 0
        for s in range(4):
            wo = self.ws.next()
            wg = self.ws.next()
            for m in range(2):
                c = s * 2 + m
                for h in range(2):
                    po = self.bank()
                    for kc in range(4):
                        k.mm(po[:], wo[:, kc, m * 128:(m + 1) * 128],
                             yT[:, kc, h * 512:(h + 1) * 512], start=(kc == 0), stop=(kc == 3))
                    pg = self.proj_fm(wg, m * 128, 128, h)
                    sg = sgt[n % 2]
                    tp = tmp[n % 2]
                    n += 1
                    k.actf(sg[:], pg[:], AF.Sigmoid)
                    mv = self.merged[:, c, h * 512:(h + 1) * 512]
                    if b == 0:
                        k.tt(mv, po[:], sg[:], ALU.mult)
                    else:
                        k.tt(tp[:], po[:], sg[:], ALU.mult)
                        k.tt(mv, mv, tp[:], ALU.add, E=k.pool)
        k.pop()

    def plan_ssd(self, l):
        S = self.SEC
        for c0 in (0, 256):
            self.win_slab(l, S["z"] + c0, 256)
        for c0 in (0, 256, 512):
            self.win_slab(l, S["xbc"] + c0, 256)
        self.win_slab(l, S["dt"], 16)
        self.plan_branch_out(l, "w_ssd_o", 0)

    def ssd_branch(self, l):
        k = self.k
        fl = self.flags
        k.push()
        zs = k.sb("ssd_zs", [128, 4, TOK], BF16)
        xc = k.sb("ssd_xc", [128, 6, TOK], BF16)
        xbt = k.sb("ssd_xbt", [128, NT, 640], BF16)
        dt = k.sb("ssd_dt", [128, NT, 16], F32)
        lndt = k.sb("ssd_lndt", [128, NT, 16], F32)
        a = k.sb("ssd_a", [128, NT, 16], F32)
        cs = k.sb("ssd_cs", [128, NT, 16], F32)
        aneg = k.sb("ssd_aneg", [128, 16], F32)
        dsum = k.sb("ssd_dsum", [128, 4], F32)
        ygT = k.sb("ssd_yg", [128, 4, TOK], BF16)
        sin_bf = [k.sb(f"ssd_sin{d}", [128, NT, 256], BF16) for d in range(2)]
        for s in range(2):
            wb = self.ws.next()
            for m in range(2):
                for h in range(2):
                    pb = self.proj_fm(wb, m * 128, 128, h)
                    k.actf(zs[:, s * 2 + m, h * 512:(h + 1) * 512], pb[:], AF.Silu)
        k.push()
        xp = k.sb("ssd_xp", [128, 6, 4, 258], BF16)
        acc = [k.sb(f"ssd_acc{j}", [128, 4, 256], F32) for j in range(2)]
        k.memset(xp[:, :, :, 0:1], 0.0, E=k.pool)
        k.memset(xp[:, :, :, 257:258], 0.0, E=k.pool)
        for s in range(3):
            wb = self.ws.next()
            for m in range(2):
                for h in range(2):
                    pb = self.proj_fm(wb, m * 128, 128, h)
                    k.cp(xp[:, s * 2 + m, 2 * h:2 * h + 2, 1:257], pb[:].re("p (q t) -> p q t", q=2),
                         E=(k.act if h == 0 else k.dve))
        cfv = V(fl, fl.h[:, 2:8:2].unsqueeze(1).to_broadcast([128, 6, 3]))
        cbv = V(fl, fl.h[:, 9:15:2].unsqueeze(1).to_broadcast([128, 6, 3]))
        k.tt(xp[:, :, 1:4, 0], xp[:, :, 0:3, 256], cfv, ALU.mult)
        k.tt(xp[:, :, 0:3, 257], xp[:, :, 1:4, 1], cbv, ALU.mult)
        for c in range(6):
            t = acc[c % 2]
            k.ts(t[:], xp[:, c, :, 1:257], self.pp(l, "conv_w1", c, c + 1), ALU.mult,
                 self.pp(l, "conv_b", c, c + 1), ALU.add)
            k.stt(t[:], xp[:, c, :, 0:256], self.pp(l, "conv_w0", c, c + 1), t[:], ALU.mult, ALU.add)
            k.stt(t[:], xp[:, c, :, 2:258], self.pp(l, "conv_w2", c, c + 1), t[:], ALU.mult, ALU.add)
            k.actf(xc[:, c, :].re("p (q t) -> p q t", q=4), t[:], AF.Silu)
        k.pop()
        self.dbg("ssd_xc", xc[:], [128, 6, TOK])
        self.chk("A")
        for t in range(NT):
            pb = self.bank()
            pv = pb[:].bitcast(BF16)
            for c in range(5):
                k.tr(pv[:, c * 128:(c + 1) * 128], xc[:, c, t * 128:(t + 1) * 128], self.ident_bf[:], inc=(c == 4))
            k.cp(xbt[:, t, :], pv[:, 0:640], E=(k.act if t % 2 else k.dve))
        self.chk("B")
        wb = self.ws.next()
        pb = self.bank()
        for t in range(NT):
            for kc in range(8):
                k.mm(pb[:, t * 16:(t + 1) * 16], self.hT[:, kc, t * 128:(t + 1) * 128], wb[:, kc, 0:16],
                     start=(kc == 0), stop=(kc == 7))
        k.tt(dt[:], pb[:, 0:128].re("p (t c) -> p t c", t=NT),
             V(self.PP[l], self.pp(l, "dt_bias").ap.unsqueeze(1).to_broadcast([128, NT, 16])), ALU.add)
        k.actf(dt[:], dt[:], AF.Exp)
        k.ts(lndt[:], dt[:], -0.5, ALU.mult, 1.0, ALU.add)
        k.tt(lndt[:], lndt[:], dt[:], ALU.mult)
        k.actf(dt[:], dt[:], AF.Ln, bias=self.cst[:, 2:3])
        k.tt(dt[:], dt[:], lndt[:], ALU.max)
        k.actf(lndt[:], dt[:], AF.Ln)
        k.actf(aneg[:], self.pp(l, "A_log"), AF.Exp)
        k.ts(aneg[:], aneg[:], -1.0, ALU.mult)
        k.tt(a[:], dt[:], V(aneg, aneg.h[:, :].unsqueeze(1).to_broadcast([128, NT, 16])), ALU.mult)
        k.tt(dsum[:], self.pp(l, "ssd_D0"), self.pp(l, "ssd_D1"), ALU.add)
        self.dbg("ssd_dt", dt[:], [128, NT, 16])
        self.chk("B1")
        pb = self.bank()
        for t in range(NT):
            k.mm(pb[:, t * 16:t * 16 + 8], self.maskU[:], a[:, t, 0:8])
            k.mm(pb[:, t * 16 + 8:t * 16 + 16], self.maskL[:], a[:, t, 8:16])
        k.cp(cs[:], pb[:, 0:128].re("p (t c) -> p t c", t=NT))
        self.chk("B2")
        carg = k.sb("ssd_carg", [128, NT, 16], F32)
        k.tt(carg[:], lndt[:], cs[:], ALU.subtract)
        tot = k.sb("ssd_tot", [128, NT, 16], F32)
        wdec = k.sb("ssd_wdec", [128, NT, 16], F32)
        dech = [k.sb(f"ssd_dech{d}", [128, NT, 4], F32) for d in range(2)]
        pb = self.bank()
        k.mm(pb[:, 0:128], self.ones_f[:], a[:].re("p t c -> p (t c)"))
        k.cp(tot[:], pb[:, 0:128].re("p (t c) -> p t c", t=NT))
        self.chk("B3")
        import os
        kvar = os.environ.get("KVAR", "")
        if kvar != "nowdec":
            k.tt(wdec[:], tot[:], carg[:], ALU.add)
            k.actf(wdec[:], wdec[:], AF.Exp)
        if kvar != "nodech":
            for d in range(2):
                for g in range(2):
                    ps_ = slice(g * 64, (g + 1) * 64)
                    k.actf(dech[d][ps_, :, :], tot[ps_, :, d * 8 + g * 4:d * 8 + g * 4 + 4], AF.Exp)
        self.chk("C")
        k.push()
        S = [k.sb(f"ssd_S{d}", [128, 256], F32) for d in range(2)]
        Sin = [k.sb(f"ssd_Sin{d}", [128, 256], F32) for d in range(2)]
        Stmp = [k.sb(f"ssd_Stmp{d}", [128, 256], F32) for d in range(2)]
        xdec = [[k.sb(f"ssd_xdec{d}_{j}", [128, 512], BF16) for j in range(2)] for d in range(2)]
        for d in range(2):
            k.dma(k.sp, Sin[d][:], self.st_ssd_d[l, d])
        for step in range(NT):
            for d in range(2):
                t = step if d == 0 else NT - 1 - step
                if step > 0:
                    fcol = t if d == 0 else 8 + t
                    k.ts(Sin[d][:], S[d][:], fl[:, fcol:fcol + 1], ALU.mult)
                k.cp(sin_bf[d][:, t, :], Sin[d][:], E=k.act)
                xd = xdec[d][step % 2]
                k.tt(xd[:].re("p (h q) -> p h q", h=8), xbt[:, t, 0:512].re("p (h q) -> p h q", h=8),
                     V(wdec, wdec.h[:, t, d * 8:(d + 1) * 8].unsqueeze(2).to_broadcast([128, 8, 64])), ALU.mult)
                pb = self.bank()
                for g in range(2):
                    k.mm(pb[g * 64:(g + 1) * 64, 0:256], xbt[:, t, 512 + g * 64:512 + (g + 1) * 64],
                         xd[:, g * 256:(g + 1) * 256])
                k.tt(Stmp[d][:].re("p (j q) -> p j q", j=4), Sin[d][:].re("p (j q) -> p j q", j=4),
                     V(dech[d], dech[d].h[:, t, :].unsqueeze(2).to_broadcast([128, 4, 64])), ALU.mult)
                k.tt(S[d][:], Stmp[d][:], pb[:, 0:256], ALU.add)
                if (d == 0 and t % 2 == 1) or (d == 1 and t % 2 == 0):
                    k.dma(k.sp, self.ns_ssd_d[l, t // 2, d], S[d][:])
        k.pop()
        self.chk("D")
        k.push()
        MT = [k.sb(f"ssd_MT{j}", [128, 8, 128], BF16) for j in range(2)]
        cdec = [[k.sb(f"ssd_cdec{d}_{j}", [128, 4, 128], BF16) for j in range(2)] for d in range(2)]
        am = k.sb("ssd_am", [128, 8, 128], F32)
        ambf = k.sb("ssd_ambf", [128, 8, 128], BF16)
        dm = [k.sb(f"ssd_dm{j}", [128, 8, 128], F32) for j in range(2)]
        er = [k.sb(f"ssd_er{j}", [128, 8, 128], BF16) for j in range(2)]
        ytmp = [k.sb(f"ssd_ytmp{j}", [128, 4, 128], F32) for j in range(2)]
        if "ssd_yraw" in self.debug:
            self.yraw = k.sb("ssd_yraw", [128, 4, TOK], F32)
        masks = (self.maskU, self.maskL)
        for t in range(NT):
            for d in range(2):
                mk = masks[d]
                k.tt(am[:], V(a, a.h[:, t, d * 8:(d + 1) * 8].unsqueeze(2).to_broadcast([128, 8, 128])),
                     V(mk, mk.h[:, :].unsqueeze(1).to_broadcast([128, 8, 128])), ALU.mult,
                     E=(k.dve if os.environ.get("KVAR", "") == "amdve" else k.pool))
                dmt = dm[d]
                ert = er[d]
                if os.environ.get("KVAR", "") == "rowbf":
                    k.cp(ambf[:], am[:])
                for hh in range(2):
                    pr = self.bank()
                    if os.environ.get("KVAR", "") == "rowbf":
                        k.mm(pr[:], self.ones_bf[:], ambf[:, hh * 4:(hh + 1) * 4, :])
                    else:
                        k.mm(pr[:], self.ones_f[:], am[:, hh * 4:(hh + 1) * 4, :])
                    rv = pr[:].re("p (h l) -> p h l", h=4)
                    k.tt(dmt[:, hh * 4:(hh + 1) * 4, :], rv,
                         V(carg, carg.h[:, t, d * 8 + hh * 4:d * 8 + hh * 4 + 4].unsqueeze(2).to_broadcast([128, 4, 128])),
                         ALU.add)
                    k.actf(ert[:, hh * 4:(hh + 1) * 4, :], rv, AF.Exp)
                if t == 0 and d == 0:
                    self.chk("E1b")
                k.ts(dmt[:], dmt[:], 30.0, ALU.min)
                if t == 0 and d == 0:
                    self.chk("E1c")
                k.actf(dmt[:], dmt[:], AF.Exp)
                if t == 0 and d == 0:
                    self.chk("E1d")
                k.tt(dmt[:], dmt[:], V(mk, mk.h[:, :].unsqueeze(1).to_broadcast([128, 8, 128])), ALU.mult)
                for g in range(2):
                    ps_ = slice(g * 64, (g + 1) * 64)
                    k.tt(cdec[d][t % 2][ps_, :, :], ert[ps_, g * 4:(g + 1) * 4, :],
                         V(xc, xc.h[ps_, 5, t * 128:(t + 1) * 128].unsqueeze(1).to_broadcast([64, 4, 128])), ALU.mult)
            if t == 0:
                self.chk("E1")
            k.tt(dm[0][:], dm[0][:], dm[1][:], ALU.add)
            pgs = [self.bank(), self.bank()]
            for g in range(2):
                ps_ = slice(g * 64, (g + 1) * 64)
                k.mm(pgs[g][:, 0:128], xc[ps_, 4, t * 128:(t + 1) * 128], xc[ps_, 5, t * 128:(t + 1) * 128])
            mt = MT[t % 2]
            for g in range(2):
                k.tt(mt[:, g * 4:(g + 1) * 4, :], dm[0][:, g * 4:(g + 1) * 4, :],
                     V(pgs[g], pgs[g].h[:, 0:128].unsqueeze(1).to_broadcast([128, 4, 128])), ALU.mult)
            if t == 0:
                self.chk("E2")
            pys = [self.bank(), self.bank()]
            for h in range(8):
                pair, e = h // 2, h % 2
                g, j = h // 4, h % 4
                out = pys[g][e * 64:(e + 1) * 64, (pair % 2) * 128:(pair % 2 + 1) * 128]
                gs = slice(g * 64, (g + 1) * 64)
                k.mm(out, xbt[:, t, h * 64:(h + 1) * 64], mt[:, h, :], start=True, stop=False)
                k.mm(out, sin_bf[0][gs, t, j * 64:(j + 1) * 64], cdec[0][t % 2][gs, j, :], start=False, stop=False)
                k.mm(out, sin_bf[1][gs, t, j * 64:(j + 1) * 64], cdec[1][t % 2][gs, j, :], start=False, stop=True)
            if t == 0:
                self.chk("E3")
            if t == 1:
                self.chk("E4")
            yt = ytmp[t % 2]
            k.tt(yt[:], xc[:, 0:4, t * 128:(t + 1) * 128],
                 V(dsum, dsum.h[:, :].unsqueeze(2).to_broadcast([128, 4, 128])), ALU.mult, E=k.pool)
            for g in range(2):
                k.tt(yt[:, g * 2:g * 2 + 2, :], pys[g][:, 0:256].re("p (c l) -> p c l", c=2), yt[:, g * 2:g * 2 + 2, :],
                     ALU.add)
            if "ssd_yraw" in self.debug:
                k.cp(self.yraw[:, :, t * 128:(t + 1) * 128], yt[:])
            k.tt(ygT[:, :, t * 128:(t + 1) * 128], yt[:], zs[:, :, t * 128:(t + 1) * 128], ALU.mult)
        if "ssd_yraw" in self.debug:
            self.dbg("ssd_yraw", self.yraw[:], [128, 4, TOK])
        k.pop()
        self.chk("E")
        yT = k.sb("ssd_yT", [128, 4, TOK], BF16)
        self.group_rmsnorm(ygT, yT, 4, 1.0 / 512, self.eps6, [self.pp(l, "ssd_norm", c, c + 1) for c in range(4)])
        self.dbg("ssd_yT", yT[:], [128, 4, TOK])
        self.branch_out(l, yT, 0)
        k.pop()

    def plan_gla(self, l):
        S = self.SEC
        self.win_slab(l, S["q"], 256)
        self.win_slab(l, S["k"], 256)
        self.win_slab(l, S["lr"], 32)
        for c0 in (0, 256):
            self.win_slab(l, S["v"] + c0, 256)
        for c0 in (0, 256):
            self.win_slab(l, S["g"] + c0, 256)
        self.plan_branch_out(l, "w_gla_o", 1)

    def gla_branch(self, l):
        k = self.k
        fl = self.flags
        k.push()
        og = k.sb("gla_og", [128, 4, TOK], F32)
        k.push()
        qd = [k.sb(f"gla_qd{d}", [128, 2, TOK], BF16) for d in range(2)]
        kd = [k.sb(f"gla_kd{d}", [128, 2, TOK], BF16) for d in range(2)]
        vtok = k.sb("gla_vtok", [128, NT, 512], BF16)
        sin_bf = [k.sb(f"gla_sin{d}", [128, NT, 256], BF16) for d in range(2)]
        etot = [k.sb(f"gla_etot{d}", [128, 2, NT], F32) for d in range(2)]
        gkw = k.sb("gla_gkw", [16, 2, 256], F32)
        gkw_bf = k.sb("gla_gkwbf", [16, 2, 256], BF16)
        lr_bf = [k.sb(f"gla_lr{d}", [16, TOK], BF16) for d in range(2)]
        nb = k.sb("gla_nb", [128, 4], F32)
        k.dma(k.sp, gkw[:], self.gkw_d[l].re("d r c -> r d c"))
        k.cp(gkw_bf[:], gkw[:], E=k.pool)
        k.ts(nb[:], self.pp(l, "gk_b"), -1.0, ALU.mult)
        k.push()
        self.rmask = k.sb("rmask", [128, TOK], F32)
        k.memset(self.rmask[:], 1.0, E=k.pool)
        k.memset(self.rmask[:, 0::128], 0.0, E=k.pool)
        qT = k.sb("gla_qT", [128, 2, TOK], BF16)
        kT = k.sb("gla_kT", [128, 2, TOK], BF16)
        T1 = k.sb("gla_T1", [128, 2, TOK], F32)
        T2 = k.sb("gla_T2", [128, 2, TOK], F32)
        T3 = k.sb("gla_T3", [128, 2, TOK], F32)
        wb = self.ws.next()
        for m in range(2):
            for h in range(2):
                pb = self.proj_fm(wb, m * 128, 128, h)
                k.actf(qT[:, m, h * 512:(h + 1) * 512], pb[:], AF.Copy, scale=0.125)
        wb = self.ws.next()
        for m in range(2):
            for h in range(2):
                pb = self.proj_fm(wb, m * 128, 128, h)
                k.cp(kT[:, m, h * 512:(h + 1) * 512], pb[:])
        wb = self.ws.next()
        for d in range(2):
            for h in range(2):
                pb = self.proj_fm(wb, d * 16, 16, h)
                k.cp(lr_bf[d][:, h * 512:(h + 1) * 512], pb[0:16, :], E=k.act)
        for d in range(2):
            for c in range(2):
                for h in range(2):
                    pb = self.bank()
                    k.mm(pb[:], gkw_bf[:, d, c * 128:(c + 1) * 128], lr_bf[d][:, h * 512:(h + 1) * 512])
                    k.actf(T1[:, c, h * 512:(h + 1) * 512], pb[:], AF.Exp, scale=-1.0, bias=nb[:, d * 2 + c:d * 2 + c + 1])
            k.actf(T1[:], T1[:], AF.Ln, bias=self.cst[:, 2:3])
            for c in range(2):
                k.scan(T2[:, c, :], self.rmask[:], T1[:, c, :], 0.0, ALU.mult, ALU.add)
            if d == 0:
                k.actf(T3[:], T2[:], AF.Exp, scale=-1.0 / 16)
                k.tt(qd[0][:], qT[:], T3[:], ALU.mult)
                k.cp(etot[0][:], T3[:, :, 127::128])
                k.actf(T3[:], T2[:], AF.Exp, scale=1.0 / 16)
                k.tt(kd[0][:], kT[:], T3[:], ALU.mult)
            else:
                k.actf(etot[1][:], T2[:, :, 127::128], AF.Exp, scale=-1.0 / 16)
                k.tt(T2[:], T2[:], T1[:], ALU.subtract)
                k.actf(T3[:], T2[:], AF.Exp, scale=1.0 / 16)
                k.tt(qd[1][:], qT[:], T3[:], ALU.mult)
                k.actf(T3[:], T2[:], AF.Exp, scale=-1.0 / 16)
                k.tt(kd[1][:], kT[:], T3[:], ALU.mult)
        k.pop()
        for s in range(2):
            wb = self.ws.next()
            for t in range(NT):
                pb = self.bank()
                for kc in range(8):
                    k.mm(pb[:, 0:256], self.hT[:, kc, t * 128:(t + 1) * 128], wb[:, kc, :], start=(kc == 0), stop=(kc == 7))
                k.cp(vtok[:, t, s * 256:(s + 1) * 256], pb[:, 0:256], E=(k.act if t % 2 else k.dve))
        k.push()
        ktok = k.sb("gla_ktok", [128, NT, 4, 128], BF16)
        for t in range(NT):
            pb = self.bank()
            pv = pb[:].bitcast(BF16)
            for d in range(2):
                for c in range(2):
                    j = d * 2 + c
                    k.tr(pv[:, j * 128:(j + 1) * 128], kd[d][:, c, t * 128:(t + 1) * 128], self.ident_bf[:], inc=(j == 3))
            k.cp(ktok[:, t, :, :], pv[:, 0:512].re("p (j q) -> p j q", j=4), E=(k.act if t % 2 else k.dve))
        S = [k.sb(f"gla_S{d}", [128, 256], F32) for d in range(2)]
        Sin = [k.sb(f"gla_Sin{d}", [128, 256], F32) for d in range(2)]
        Stmp = [k.sb(f"gla_Stmp{d}", [128, 256], F32) for d in range(2)]
        for d in range(2):
            k.dma(k.sp, Sin[d][:], self.st_gla_d[l, d])
        for step in range(NT):
            for d in range(2):
                t = step if d == 0 else NT - 1 - step
                if step > 0:
                    fcol = t if d == 0 else 8 + t
                    k.ts(Sin[d][:], S[d][:], fl[:, fcol:fcol + 1], ALU.mult)
                pb = self.bank()
                for h in range(4):
                    pair, e = h // 2, h % 2
                    k.mm(pb[e * 64:(e + 1) * 64, pair * 128:(pair + 1) * 128],
                         ktok[:, t, d * 2 + pair, e * 64:(e + 1) * 64], vtok[:, t, h * 128:(h + 1) * 128])
                etv = V(etot[d], etot[d].h[:, :, t].unsqueeze(2).to_broadcast([128, 2, 128]))
                if d == 0:
                    k.cp(sin_bf[0][:, t, :], Sin[0][:], E=k.act)
                    k.tt(Stmp[0][:], Sin[0][:], pb[:, 0:256], ALU.add)
                    k.tt(S[0][:].re("p (c v) -> p c v", c=2), Stmp[0][:].re("p (c v) -> p c v", c=2), etv, ALU.mult)
                else:
                    k.tt(Stmp[1][:].re("p (c v) -> p c v", c=2), Sin[1][:].re("p (c v) -> p c v", c=2), etv, ALU.mult)
                    k.cp(sin_bf[1][:, t, :], Stmp[1][:], E=k.act)
                    k.tt(S[1][:], Stmp[1][:], pb[:, 0:256], ALU.add)
                if (d == 0 and t % 2 == 1) or (d == 1 and t % 2 == 0):
                    k.dma(k.sp, self.ns_gla_d[l, t // 2, d], S[d][:])
        k.pop()
        k.push()
        amat = [[k.sb(f"gla_A{d}_{j}", [128, 4, 128], BF16) for j in range(2)] for d in range(2)]
        masks = (self.maskU, self.maskL)
        for t in range(NT):
            ts_ = slice(t * 128, (t + 1) * 128)
            for d in range(2):
                pas = [self.bank(), self.bank()]
                for h in range(4):
                    pair, e = h // 2, h % 2
                    es = slice(e * 64, (e + 1) * 64)
                    k.mm(pas[e][:, pair * 128:(pair + 1) * 128], kd[d][es, pair, ts_], qd[d][es, pair, ts_])
                mk = masks[d]
                for e in range(2):
                    k.tt(amat[d][t % 2][:, e::2, :], pas[e][:, 0:256].re("p (h q) -> p h q", h=2),
                         V(mk, mk.h[:, :].unsqueeze(1).to_broadcast([128, 2, 128])), ALU.mult)
            pos_ = [self.bank(), self.bank()]
            for h in range(4):
                pair, e = h // 2, h % 2
                es = slice(e * 64, (e + 1) * 64)
                out = pos_[e][:, pair * 128:(pair + 1) * 128]
                vt = vtok[:, t, h * 128:(h + 1) * 128]
                k.mm(out, vt, amat[0][t % 2][:, h, :], start=True, stop=False)
                k.mm(out, vt, amat[1][t % 2][:, h, :], start=False, stop=False)
                k.mm(out, sin_bf[0][es, t, pair * 128:(pair + 1) * 128], qd[0][es, pair, ts_], start=False, stop=False)
                k.mm(out, sin_bf[1][es, t, pair * 128:(pair + 1) * 128], qd[1][es, pair, ts_], start=False, stop=True)
            for e in range(2):
                k.cp(og[:, e::2, ts_], pos_[e][:, 0:256].re("p (h q) -> p h q", h=2), E=(k.act if (t + e) % 2 else k.dve))
        k.pop()
        self.dbg("gla_og", og[:], [128, 4, TOK])
        k.pop()
        gs = k.sb("gla_gs", [128, 4, TOK], BF16)
        yT = k.sb("gla_yT", [128, 4, TOK], BF16)
        for s in range(2):
            wb = self.ws.next()
            for m in range(2):
                for h in range(2):
                    pb = self.proj_fm(wb, m * 128, 128, h)
                    k.actf(gs[:, s * 2 + m, h * 512:(h + 1) * 512], pb[:], AF.Silu)
        k.push()
        sq = [k.sb(f"gla_sq{j}", [128, TOK], BF16) for j in range(2)]
        rstd = [k.sb(f"gla_rstd{j}", [128, TOK], F32) for j in range(2)]
        tmp = [k.sb(f"gla_tmp{j}", [128, TOK], F32) for j in range(2)]
        for h in range(4):
            sqh, rs, tp = sq[h % 2], rstd[h % 2], tmp[h % 2]
            k.actf(sqh[:], og[:, h, :], AF.Square)
            for hf in range(2):
                pb = self.bank()
                k.mm(pb[:], self.ones_bf[:], sqh[:, hf * 512:(hf + 1) * 512])
                k.actf(rs[:, hf * 512:(hf + 1) * 512], pb[:], AF.Ln, scale=1.0 / 128, bias=self.eps6)
            k.actf(rs[:], rs[:], AF.Exp, scale=-0.5)
            k.stt(tp[:], og[:, h, :], self.pp(l, "gla_norm"), rs[:], ALU.mult, ALU.mult)
            k.tt(yT[:, h, :], tp[:], gs[:, h, :], ALU.mult)
        k.pop()
        self.dbg("gla_yT", yT[:], [128, 4, TOK])
        self.branch_out(l, yT, 1)
        k.pop()

    ALPHA = float(np.exp(-0.5))

    def plan_rw(self, l):
        S = self.SEC
        for nm in ("rr", "rv"):
            for c0 in (0, 256):
                self.win_slab(l, S[nm] + c0, 256)
        self.win_slab(l, S["wlr"], 192)
        self.win_slab(l, S["glr"], 128)
        for c0 in (0, 256):
            self.win_slab(l, S["rk"] + c0, 256)
        self.plan_branch_out(l, "w_rw_o", 2)

    def rw_branch(self, l):
        k = self.k
        fl = self.flags
        A = self.ALPHA
        k.push()
        OT = k.sb("rw_OT", [128, 4, TOK], BF16)
        bonus = k.sb("rw_bonus", [128, 4, TOK], BF16)
        sgl = k.sb("rw_sgl", [128, TOK], BF16)
        k.push()
        rT = k.sb("rw_rT", [128, 4, TOK], BF16)
        kapT = k.sb("rw_kapT", [128, 4, TOK], BF16)
        kpT = k.sb("rw_kpT", [128, 4, TOK], BF16)
        nbT = k.sb("rw_nbT", [128, 4, TOK], BF16)
        thT = k.sb("rw_thT", [128, TOK], BF16)
        vtok = k.sb("rw_vtok", [128, NT, 512], BF16)
        w2bf = k.sb("rw_w2bf", [128, 512], BF16)
        k.push()
        a2bf = k.sb("rw_a2bf", [64, 512], BF16)
        k.push()
        w2f = k.sb("rw_w2f", [128, 512], F32)
        a2f = k.sb("rw_a2f", [64, 512], F32)
        k.dma(k.sp, w2f[:], self.rw_w2_d[l].re("d r c -> (d r) c"))
        k.dma(k.sp, a2f[:], self.rw_a2_d[l])
        k.cp(w2bf[:], w2f[:], E=k.pool)
        k.cp(a2bf[:], a2f[:], E=k.pool)
        k.pop()
        kT = k.sb("rw_kT", [128, 2, TOK], BF16)
        vT = k.sb("rw_vT", [128, 4, TOK], BF16)
        aT = k.sb("rw_aT", [128, TOK], BF16)
        wl = k.sb("rw_wl", [128, TOK], BF16)
        alT = k.sb("rw_alT", [64, TOK], BF16)
        glT = k.sb("rw_glT", [128, TOK], BF16)
        hb = k.sb("rw_hb", [128, 4, 258], BF16)
        mt = k.sb("rw_mt", [128, 4, 256], BF16)
        omu = k.sb("rw_omu", [128, 15], F32)
        hmu = k.sb("rw_hmu", [128, 15], F32)
        omka = k.sb("rw_omka", [128, 4], F32)
        k.ts(omu[:], self.pp(l, "rw_mu"), -1.0, ALU.mult, 1.0, ALU.add)
        k.ts(hmu[:], self.pp(l, "rw_mu"), 0.5, ALU.mult)
        k.ts(omka[:], self.pp(l, "rw_ka"), -1.0, ALU.mult, 1.0, ALU.add)
        k.memset(hb[:, :, 0:1], 0.0, E=k.pool)
        k.memset(hb[:, :, 257:258], 0.0, E=k.pool)

        def mix(wb, m0, mc, mucol, dst):
            b = hb
            t = mt
            for h in range(2):
                pb = self.proj_fm(wb, m0, mc, h)
                k.cp(b[0:mc, 2 * h:2 * h + 2, 1:257], pb[0:mc, :].re("p (q t) -> p q t", q=2),
                     E=(k.act if h == 0 else k.dve))
            k.tt(b[0:mc, 1:4, 0], b[0:mc, 0:3, 256], V(fl, fl.h[0:mc, 2:8:2]), ALU.mult)
            k.tt(b[0:mc, 0:3, 257], b[0:mc, 1:4, 1], V(fl, fl.h[0:mc, 9:15:2]), ALU.mult)
            k.tt(t[0:mc], b[0:mc, :, 0:256], b[0:mc, :, 2:258], ALU.add)
            k.ts(t[0:mc], t[0:mc], hmu[0:mc, mucol:mucol + 1], ALU.mult)
            k.stt(dst.re("p (q t) -> p q t", q=4), b[0:mc, :, 1:257], omu[0:mc, mucol:mucol + 1], t[0:mc],
                  ALU.mult, ALU.add)

        for bi, dstT in ((0, rT), (2, vT)):
            for s_ in range(2):
                wb = self.ws.next()
                for m in range(2):
                    c = s_ * 2 + m
                    mix(wb, m * 128, 128, bi * 4 + c, dstT[:, c, :])
        wb = self.ws.next()
        mix(wb, 0, 128, 12, wl[:, :])
        mix(wb, 128, 64, 13, alT[:, :])
        wb = self.ws.next()
        mix(wb, 0, 128, 14, glT[:, :])
        k.actf(thT[:], wl[:], AF.Tanh)
        k.actf(sgl[:], glT[:], AF.Sigmoid)
        f1 = k.sb("rw_f1", [128, TOK], F32)
        f2 = k.sb("rw_f2", [128, TOK], F32)
        b1 = k.sb("rw_b1", [128, TOK], BF16)
        for s_ in range(2):
            wb = self.ws.next()
            for m in range(2):
                mix(wb, m * 128, 128, 4 + s_ * 2 + m, kT[:, m, :])
            for m in range(2):
                c = s_ * 2 + m
                for h in range(2):
                    pb = self.bank()
                    k.mm(pb[:], a2bf[:, c * 128:(c + 1) * 128], alT[:, h * 512:(h + 1) * 512])
                    k.actf(aT[:, h * 512:(h + 1) * 512], pb[:], AF.Sigmoid, bias=self.pp(l, "rw_a0", c, c + 1))
                k.ts(f1[:], kT[:, m, :], self.pp(l, "rw_kk", c, c + 1), ALU.mult)
                k.actf(b1[:], f1[:], AF.Square)
                for h in range(2):
                    pb = self.bank()
                    k.mm(pb[:], self.blockones[:], b1[:, h * 512:(h + 1) * 512])
                    k.ts(f2[:, h * 512:(h + 1) * 512], pb[:], 1e-24, ALU.max)
                k.actf(f2[:], f2[:], AF.Sqrt)
                k.recip(f2[:], f2[:])
                k.tt(kapT[:, c, :], f1[:], f2[:], ALU.mult)
                k.stt(nbT[:, c, :], kapT[:, c, :], -1.0, aT[:], ALU.mult, ALU.mult)
                k.ts(f1[:], aT[:], self.pp(l, "rw_ka", c, c + 1), ALU.mult, omka[:, c:c + 1], ALU.add)
                k.tt(kpT[:, c, :], kT[:, m, :], f1[:], ALU.mult)
                k.stt(b1[:], rT[:, c, :], self.pp(l, "rw_rk", c, c + 1), kpT[:, c, :], ALU.mult, ALU.mult)
                for h in range(2):
                    pb = self.bank()
                    k.mm(pb[:], self.blockones[:], b1[:, h * 512:(h + 1) * 512])
                    k.tt(bonus[:, c, h * 512:(h + 1) * 512], pb[:], vT[:, c, h * 512:(h + 1) * 512], ALU.mult)
        for t in range(NT):
            pb = self.bank()
            pv = pb[:].bitcast(BF16)
            for c in range(4):
                k.tr(pv[:, c * 128:(c + 1) * 128], vT[:, c, t * 128:(t + 1) * 128], self.ident_bf[:], inc=(c == 3))
            k.cp(vtok[:, t, :], pv[:, 0:512], E=(k.act if t % 2 else k.dve))
        k.pop()
        self.dbg("rw_kapT", kapT[:], [128, 4, TOK])
        self.dbg("rw_kpT", kpT[:], [128, 4, TOK])
        self.dbg("rw_rT", rT[:], [128, 4, TOK])
        self.dbg("rw_nbT", nbT[:], [128, 4, TOK])
        k.push()
        sig = k.sb("rw_sig", [128, 4, 128], F32)
        Pc = k.sb("rw_P", [128, 4, 128], F32)
        Cx = k.sb("rw_Cx", [128, 4, 128], F32)
        Ea = k.sb("rw_Ea", [128, 4, 128], BF16)
        Eb = k.sb("rw_Eb", [128, 4, 128], BF16)
        Ec = k.sb("rw_Ec", [128, 4, 128], BF16)
        gam = k.sb("rw_gam", [128, 4], F32)
        RKt = k.sb("rw_RKt", [128, 4, 2, 128], BF16)
        kt = k.sb("rw_kt", [128, 4, 128], BF16)
        nbt = k.sb("rw_nbt", [128, 4, 128], BF16)
        tok = k.sb("rw_tok", [128, 3, 512], BF16)
        M1 = k.sb("rw_M1", [128, 8, 2, 128], BF16)
        M2 = k.sb("rw_M2", [128, 8, 2, 128], BF16)
        XY = [k.sb("rw_X0", [128, 4, 128], BF16),
              [k.sb(f"rw_XM{j}", [128, 4, 128], BF16) for j in range(2)],
              [k.sb(f"rw_ZT{j}", [128, 4, 128], BF16) for j in range(2)],
              [k.sb(f"rw_Tc{j}", [128, 4, 128], BF16) for j in range(2)]]
        TT = k.sb("rw_TT", [128, 8, 128], BF16)
        AV = k.sb("rw_AV", [128, 512], BF16)
        U = k.sb("rw_U", [128, 512], BF16)
        WT = k.sb("rw_WT", [128, 4, 128], BF16)
        Et = k.sb("rw_E", [128, 512], BF16)
        S = k.sb("rw_S", [128, 256], F32)
        Sin = k.sb("rw_Sin", [128, 256], F32)
        Stmp = k.sb("rw_Stmp", [128, 256], F32)
        Sbf = k.sb("rw_Sbf", [128, 256], BF16)
        nev = [0]

        def evac(dst, src):
            E = k.act if nev[0] % 2 else k.dve
            nev[0] += 1
            k.cp(dst, src, E=E)

        for d in range(2):
            m2 = self.mask2[d]
            mx = self.maskSL if d == 0 else self.maskSU
            k.dma(k.sp, Sin[:], self.st_rw_d[l, d])
            for step in range(NT):
                t = step if d == 0 else NT - 1 - step
                ts_ = slice(t * 128, (t + 1) * 128)
                ds_ = slice(d * 64, (d + 1) * 64)
                pb = self.bank()
                for c in range(4):
                    k.mm(pb[:, c * 128:(c + 1) * 128], w2bf[ds_, c * 128:(c + 1) * 128], thT[ds_, ts_])
                for c in range(4):
                    k.actf(sig[:, c, :], pb[:, c * 128:(c + 1) * 128], AF.Sigmoid,
                           bias=self.pp(l, "rw_w0", d * 4 + c, d * 4 + c + 1))
                for c in range(4):
                    k.scan(Pc[:, c, :], self.ones_f[:], sig[:, c, :], 0.0, ALU.mult, ALU.add)
                if d == 0:
                    k.tt(Cx[:], Pc[:], sig[:], ALU.subtract)
                    cin, cex = Pc, Cx
                    k.actf(gam[:], Pc[:, :, 127], AF.Exp, scale=-A)
                else:
                    k.actf(gam[:], Pc[:, :, 127], AF.Exp, scale=-A)
                    k.tt(Cx[:], V(Pc, Pc.h[:, :, 127:128].to_broadcast([128, 4, 128])), Pc[:], ALU.subtract)
                    k.tt(Pc[:], Cx[:], sig[:], ALU.add)
                    cin, cex = Pc, Cx
                k.actf(Ea[:], cin[:], AF.Exp, scale=-A)
                k.actf(Eb[:], cex[:], AF.Exp, scale=-A)
                k.actf(Ec[:], cin[:], AF.Exp, scale=A)
                k.tt(RKt[:, :, 0, :], rT[:, :, ts_], Ea[:], ALU.mult)
                k.tt(RKt[:, :, 1, :], kapT[:, :, ts_], Eb[:], ALU.mult)
                k.tt(kt[:], kpT[:, :, ts_], Ec[:], ALU.mult)
                k.tt(nbt[:], nbT[:, :, ts_], Ec[:], ALU.mult)
                pb = self.bank()
                pv = pb[:].bitcast(BF16)
                for c in range(4):
                    k.tr(pv[:, c * 128:(c + 1) * 128], RKt[:, c, 1, :], self.ident_bf[:], inc=False)
                for c in range(4):
                    k.tr(pv[:, 512 + c * 128:512 + (c + 1) * 128], kt[:, c, :], self.ident_bf[:], inc=(c == 3))
                evac(tok[:, 0:2, :], pv[:, :].re("p (a q) -> p a q", a=2))
                pb = self.bank()
                pv = pb[:].bitcast(BF16)
                for c in range(4):
                    k.tr(pv[:, c * 128:(c + 1) * 128], nbt[:, c, :], self.ident_bf[:], inc=(c == 3))
                evac(tok[:, 2, :], pv[:, 0:512])
                for gq in range(2):
                    bA = [self.bank(), self.bank()]
                    bB = [self.bank(), self.bank()]
                    b5 = [self.bank(), self.bank()]
                    for hh in range(4):
                        h = gq * 4 + hh
                        c, e = h // 2, h % 2
                        cc = hh // 2
                        es = slice(e * 64, (e + 1) * 64)
                        cs = slice(cc * 256, cc * 256 + 256)
                        k.mm(bA[e][:, cs], kt[es, c, :], RKt[es, c, :, :])
                        k.mm(bB[e][:, cs], nbt[es, c, :], RKt[es, c, :, :])
                        k.mm(b5[e][:, cc * 128:(cc + 1) * 128], RKt[es, c, 1, :], nbt[es, c, :])
                    m2v = V(m2, m2.h[:, :, :].unsqueeze(1).to_broadcast([128, 2, 2, 128]))
                    X0 = XY[0]
                    for e in range(2):
                        hs = slice(gq * 4 + e, gq * 4 + 4, 2)
                        k.tt(M1[:, hs, :, :], bA[e][:].re("p (h a t) -> p h a t", h=2, a=2), m2v, ALU.mult)
                        k.tt(M2[:, hs, :, :], bB[e][:].re("p (h a t) -> p h a t", h=2, a=2), m2v, ALU.mult)
                        k.tt(X0[:, e::2, :], b5[e][:, 0:256].re("p (h t) -> p h t", h=2),
                             V(mx, mx.h[:, :].unsqueeze(1).to_broadcast([128, 2, 128])), ALU.mult)
                    Y0 = M2[:, gq * 4:gq * 4 + 4, 1, :]
                    lmx = self.lvlmask[d]
                    l1t = self.lvlmask[1 - d]
                    Tc = XY[3][0]
                    k.tt(Tc[:], Y0, V(l1t, l1t.h[:, 0, :].unsqueeze(1).to_broadcast([128, 4, 128])), ALU.mult)
                    k.tt(Tc[:], Tc[:], V(self.ident_bf, self.ident_bf.h[:, :].unsqueeze(1).to_broadcast([128, 4, 128])),
                         ALU.add)
                    for lvl in range(1, 7):
                        XM = XY[1][lvl % 2]
                        k.tt(XM[:], X0[:], V(lmx, lmx.h[:, lvl, :].unsqueeze(1).to_broadcast([128, 4, 128])), ALU.mult,
                             E=k.pool)
                        bz = self.bank()
                        for hh in range(4):
                            k.mm(bz[:, hh * 128:(hh + 1) * 128], XM[:, hh, :], Tc[:, hh, :])
                        bt = self.bank()
                        btv = bt[:].bitcast(BF16)
                        for hh in range(4):
                            k.tr(btv[:, hh * 128:(hh + 1) * 128], Tc[:, hh, :], self.ident_bf[:], inc=(hh == 3))
                        Zs = XY[2][0]
                        Ts = XY[2][1]
                        evac(Zs[:], bz[:].re("p (h t) -> p h t", h=4))
                        evac(Ts[:], btv[:, 0:512].re("p (h t) -> p h t", h=4))
                        bp = self.bank()
                        for hh in range(4):
                            o_ = bp[:, hh * 128:(hh + 1) * 128]
                            k.mm(o_, self.ident_bf[:], Tc[:, hh, :], start=True, stop=False)
                            k.mm(o_, Ts[:, hh, :], Zs[:, hh, :], start=False, stop=True)
                        if lvl < 6:
                            Tn = XY[3][lvl % 2]
                            evac(Tn[:], bp[:].re("p (h t) -> p h t", h=4))
                            Tc = Tn
                        else:
                            evac(TT[:, gq * 4:gq * 4 + 4, :], bp[:].re("p (h t) -> p h t", h=4))
                pb = self.bank()
                for h in range(8):
                    k.mm(pb[:, h * 64:(h + 1) * 64], M1[:, h, 1, :], vtok[:, t, h * 64:(h + 1) * 64])
                evac(AV[:], pb[:])
                pb = self.bank()
                for h in range(8):
                    k.mm(pb[:, h * 64:(h + 1) * 64], TT[:, h, :], AV[:, h * 64:(h + 1) * 64])
                evac(U[:], pb[:])
                pb = self.bank()
                for h in range(8):
                    c, e = h // 2, h % 2
                    k.mm(pb[e * 64:(e + 1) * 64, c * 128:(c + 1) * 128], tok[:, 0, h * 64:(h + 1) * 64], TT[:, h, :])
                evac(WT[:], pb[:].re("p (c t) -> p c t", c=4))
                if step > 0:
                    fcol = t if d == 0 else 8 + t
                    k.ts(Sin[:], S[:], fl[:, fcol:fcol + 1], ALU.mult)
                k.cp(Sbf[:], Sin[:], E=k.act)
                pbe = [self.bank(), self.bank()]
                for h in range(8):
                    c, e = h // 2, h % 2
                    es = slice(e * 64, (e + 1) * 64)
                    k.mm(pbe[e][:, c * 64:(c + 1) * 64], WT[es, c, :], Sbf[es, c * 64:(c + 1) * 64])
                for e in range(2):
                    k.tt(Et[:].re("p (c e v) -> p c e v", c=4, e=2)[:, :, e, :],
                         pbe[e][:, 0:256].re("p (c v) -> p c v", c=4),
                         U[:].re("p (c e v) -> p c e v", c=4, e=2)[:, :, e, :], ALU.add)
                if self.stop == "R1":
                    for nm, tl_, shp in (("rwd_tok", tok, [128, 3, 512]), ("rwd_E", Et, [128, 512]), ("rwd_U", U, [128, 512]),
                                         ("rwd_TT", TT, [128, 8, 128]), ("rwd_M1", M1, [128, 8, 2, 128]),
                                         ("rwd_M2", M2, [128, 8, 2, 128]), ("rwd_RKt", RKt, [128, 4, 2, 128]),
                                         ("rwd_kt", kt, [128, 4, 128]), ("rwd_nbt", nbt, [128, 4, 128]),
                                         ("rwd_sig", sig, [128, 4, 128]), ("rwd_P", Pc, [128, 4, 128]),
                                         ("rwd_WT", WT, [128, 4, 128]), ("rwd_AV", AV, [128, 512])):
                        self.debug.add(nm)
                        self.dbg(nm, tl_[:], shp)
                    self.chk("R1")
                pos_ = [self.bank(), self.bank()]
                for h in range(8):
                    c, e = h // 2, h % 2
                    es = slice(e * 64, (e + 1) * 64)
                    o_ = pos_[e][es, c * 128:(c + 1) * 128]
                    k.mm(o_, Sbf[es, c * 64:(c + 1) * 64], RKt[es, c, 0, :], start=True, stop=False)
                    k.mm(o_, vtok[:, t, h * 64:(h + 1) * 64], M1[:, h, 0, :], start=False, stop=False)
                    k.mm(o_, Et[:, h * 64:(h + 1) * 64], M2[:, h, 0, :], start=False, stop=True)
                for e in range(2):
                    es = slice(e * 64, (e + 1) * 64)
                    if d == 0:
                        evac(OT[es, :, ts_], pos_[e][es, :].re("p (c t) -> p c t", c=4))
                    else:
                        k.tt(OT[es, :, ts_], pos_[e][es, :].re("p (c t) -> p c t", c=4), OT[es, :, ts_], ALU.add)
                pb = self.bank()
                for h in range(8):
                    c, e = h // 2, h % 2
                    o_ = pb[e * 64:(e + 1) * 64, c * 64:(c + 1) * 64]
                    k.mm(o_, tok[:, 1, h * 64:(h + 1) * 64], vtok[:, t, h * 64:(h + 1) * 64], start=True, stop=False)
                    k.mm(o_, tok[:, 2, h * 64:(h + 1) * 64], Et[:, h * 64:(h + 1) * 64], start=False, stop=True)
                k.tt(Stmp[:], Sin[:], pb[:, 0:256], ALU.add)
                k.tt(S[:].re("p (c v) -> p c v", c=4), Stmp[:].re("p (c v) -> p c v", c=4),
                     V(gam, gam.h[:, :].unsqueeze(2).to_broadcast([128, 4, 64])), ALU.mult)
                if (d == 0 and t % 2 == 1) or (d == 1 and t % 2 == 0):
                    k.dma(k.sp, self.ns_rw_d[l, t // 2, d], S[:])
        k.pop()
        k.pop()
        self.dbg("rw_OT", OT[:], [128, 4, TOK])
        yT = k.sb("rw_yT", [128, 4, TOK], BF16)
        g2f = k.sb("rw_g2f", [128, 512], F32)
        g2bf = k.sb("rw_g2bf", [128, 512], BF16)
        k.dma(k.sp, g2f[:], self.rw_g2_d[l])
        k.cp(g2bf[:], g2f[:], E=k.pool)
        k.push()
        dd = k.sb("rw_dd", [128, TOK], F32)
        sq = k.sb("rw_sq", [128, TOK], BF16)
        rs = k.sb("rw_rs", [128, TOK], F32)
        for c in range(4):
            for h in range(2):
                hs = slice(h * 512, (h + 1) * 512)
                pb = self.bank()
                k.mm(pb[:], self.blockmean[:], OT[:, c, hs])
                k.tt(dd[:, hs], OT[:, c, hs], pb[:], ALU.subtract)
            k.actf(sq[:], dd[:], AF.Square)
            for h in range(2):
                hs = slice(h * 512, (h + 1) * 512)
                pb = self.bank()
                k.mm(pb[:], self.blockmean[:], sq[:, hs])
                k.actf(rs[:, hs], pb[:], AF.Ln, bias=self.epsgn)
            k.actf(rs[:], rs[:], AF.Exp, scale=-0.5)
            k.tt(dd[:], dd[:], rs[:], ALU.mult)
            k.ts(dd[:], dd[:], self.pp(l, "rw_ln_w", c, c + 1), ALU.mult, self.pp(l, "rw_ln_b", c, c + 1), ALU.add)
            k.tt(dd[:], dd[:], bonus[:, c, :], ALU.add)
            for h in range(2):
                hs = slice(h * 512, (h + 1) * 512)
                pb = self.bank()
                k.mm(pb[:], g2bf[:, c * 128:(c + 1) * 128], sgl[:, hs])
                k.tt(yT[:, c, hs], pb[:], dd[:, hs], ALU.mult)
        k.pop()
        self.dbg("rw_yT", yT[:], [128, 4, TOK])
        self.branch_out(l, yT, 2)
        k.pop()

    def group_rmsnorm(self, src, dst, nchunk, inv_n, eps, gains):
        k = self.k
        k.push()
        sq = k.sb("grn_sq", [128, nchunk, TOK], BF16)
        rstd = k.sb("grn_rstd", [128, TOK], F32)
        for c in range(nchunk):
            k.actf(sq[:, c, :], src[:, c, :], AF.Square)
        for h in range(2):
            pb = self.bank()
            for c in range(nchunk):
                k.mm(pb[:], self.ones_bf[:], sq[:, c, h * 512:(h + 1) * 512], start=(c == 0), stop=(c == nchunk - 1))
            k.actf(rstd[:, h * 512:(h + 1) * 512], pb[:], AF.Ln, scale=inv_n, bias=eps)
        k.actf(rstd[:], rstd[:], AF.Exp, scale=-0.5)
        for c in range(nchunk):
            k.stt(dst[:, c, :], src[:, c, :], gains[c], rstd[:], ALU.mult, ALU.mult)
        k.pop()


_PROG = {}


def get_prog(debug=()):
    key = tuple(sorted(debug))
    if key not in _PROG:
        p1 = Prog(debug)
        needed = p1.k.needed
        if os.environ.get("KALLINC", ""):
            needed = {E.name: set(range(1, E.cnt + 2)) for E in p1.k.engs}
        _PROG[key] = Prog(debug, needed)
    return _PROG[key]


def make_in_maps(inp):
    inp = {k_: np.asarray(v) for k_, v in inp.items()}
    pp = np.stack([pack_params(inp, l) for l in range(DEPTH)], axis=0)
    gp = _cm(inp["final_norm"])
    ti = np.arange(128)[:, None]
    si = np.arange(128)[None, :]
    lm = np.zeros((2, 128, 7, 128), np.float32)
    for lvl in range(7):
        m_ = 1 << lvl
        msk = ((ti // (2 * m_)) == (si // (2 * m_))) & ((ti % (2 * m_)) >= m_) & ((si % (2 * m_)) < m_)
        lm[0, :, lvl, :] = msk
        lm[1, :, lvl, :] = msk.T
    shared = {"pp": pp, "gp": gp, "lvlmask": lm}
    shared["gla_gk_w"] = np.ascontiguousarray(inp["gla_gk_w"], dtype=np.float32)
    for nm in ("rw_w2", "rw_a2", "rw_g2"):
        shared[nm] = np.ascontiguousarray(inp[nm], dtype=np.float32)
    for name in ["w_ada", "ffn_gate", "ffn_up", "ffn_down", "w_in", "w_ssd_o", "w_gla_o", "w_rw_o", "w_out"]:
        shared[name] = np.ascontiguousarray(inp[name], dtype=np.float32)
    maps = []
    for core in range(8):
        m = dict(shared)
        flags = np.zeros((128, 32), np.float32)
        if core < 4:
            x = inp["x_prompt"][4 * core:4 * core + 4].reshape(TOK, D)
            cond = inp["c_ctx"]
            cf = np.array([0, 1, 0, 1, 0, 1, 0, 1], np.float32)
            cb = np.array([1, 0, 1, 0, 1, 0, 1, 0], np.float32)
            posf = 0.0
        else:
            x = inp["x_sample"][core - 4]
            cond = inp["c"][core - 4]
            cf = np.array([0, 1, 1, 1, 1, 1, 1, 1], np.float32)
            cb = np.array([1, 1, 1, 1, 1, 1, 1, 0], np.float32)
            posf = 1.0
        flags[:, 0:8] = cf[None]
        flags[:, 8:16] = cb[None]
        flags[:, 16] = posf
        if core < 4:
            st_ssd = np.zeros((DEPTH, 2, 128, 256), np.float32)
        else:
            ss = inp["state_ssd"][core - 4]
            st_ssd = np.ascontiguousarray(
                ss.reshape(DEPTH, 2, 2, 4, 64, 64).transpose(0, 1, 2, 5, 3, 4).reshape(DEPTH, 2, 128, 256))
        m["st_ssd"] = st_ssd
        if core < 4:
            st_gla = np.zeros((DEPTH, 2, 128, 256), np.float32)
        else:
            sg = inp["state_gla"][core - 4]
            st_gla = np.ascontiguousarray(
                sg.reshape(DEPTH, 2, 2, 2, 64, 128).transpose(0, 1, 3, 4, 2, 5).reshape(DEPTH, 2, 128, 256))
        m["st_gla"] = st_gla
        if core < 4:
            st_rw = np.zeros((DEPTH, 2, 128, 256), np.float32)
        else:
            sr = inp["state_rwkv"][core - 4]
            st_rw = np.ascontiguousarray(
                sr.reshape(DEPTH, 2, 4, 2, 64, 64).transpose(0, 1, 3, 5, 2, 4).reshape(DEPTH, 2, 128, 256))
        m["st_rw"] = st_rw
        m["xT"] = np.ascontiguousarray(x.T, dtype=np.float32)
        m["cond"] = _cm(cond)
        m["flags"] = flags
        maps.append(m)
    return maps


def run(inp, debug=(), trace=False):
    prog = get_prog(debug)
    maps = make_in_maps(inp)
    res = run_bass_kernel_spmd(prog.k.nc, maps, core_ids=list(range(8)), trace=trace)
    return prog, res


def kernel(**inputs):
    prog, res = run(inputs)
    r = res.results
    y_prompt = np.zeros((16, 256, D), np.float32)
    y_sample = np.zeros((4, 1024, D), np.float32)
    for core in range(8):
        y = np.ascontiguousarray(r[core]["yT"].T)
        if core < 4:
            y_prompt[4 * core:4 * core + 4] = y.reshape(4, 256, D)
        else:
            y_sample[core - 4] = y
    ns_ssd = np.zeros((16, DEPTH, 2, 8, 64, 64), np.float32)
    for core in range(4):
        raw = r[core]["ns_ssd"]
        v = raw.reshape(DEPTH, 4, 2, 2, 64, 4, 64).transpose(1, 0, 2, 3, 5, 6, 4)
        ns_ssd[4 * core:4 * core + 4] = v.reshape(4, DEPTH, 2, 8, 64, 64)
    ns_gla = np.zeros((16, DEPTH, 2, 4, 64, 128), np.float32)
    for core in range(4):
        raw = r[core]["ns_gla"]
        v = raw.reshape(DEPTH, 4, 2, 2, 64, 2, 128).transpose(1, 0, 2, 5, 3, 4, 6)
        ns_gla[4 * core:4 * core + 4] = v.reshape(4, DEPTH, 2, 4, 64, 128)
    ns_rw = np.zeros((16, DEPTH, 2, 8, 64, 64), np.float32)
    for core in range(4):
        raw = r[core]["ns_rw"]
        v = raw.reshape(DEPTH, 4, 2, 2, 64, 4, 64).transpose(1, 0, 2, 5, 3, 6, 4)
        ns_rw[4 * core:4 * core + 4] = v.reshape(4, DEPTH, 2, 8, 64, 64)
    return (y_prompt, y_sample, ns_ssd, ns_gla, ns_rw)
```

```python
import os
import numpy as np
from contextlib import ExitStack
import concourse.bass as bass
import concourse.mybir as mybir
from concourse.bass_utils import run_bass_kernel_spmd

F32 = mybir.dt.float32
BF16 = mybir.dt.bfloat16
I32 = mybir.dt.int32
ALU = mybir.AluOpType
AF = mybir.ActivationFunctionType

D = 1024
TOK = 1024
NT = 8
DFF = 2816
FC = 22
DEPTH = 2
NIN = 7792
PI = float(np.pi)


class V:
    __slots__ = ("t", "ap")

    def __init__(self, t, ap):
        self.t = t
        self.ap = ap

    def __getitem__(self, idx):
        return V(self.t, self.ap[idx])

    def re(self, s, **kw):
        return V(self.t, self.ap.rearrange(s, **kw))

    def bc(self, shape):
        return V(self.t, self.ap.to_broadcast(list(shape)))

    def bitcast(self, dt):
        return V(self.t, self.ap.bitcast(dt))

    @property
    def shape(self):
        return self.ap.shape


class Tile:
    __slots__ = ("h", "name", "w", "r", "dsem", "dcnt", "psum")

    def __init__(self, h, name, r0=None):
        self.h = h
        self.name = name
        self.psum = False
        self.w = None
        self.r = dict(r0) if r0 else {}
        self.dsem = None
        self.dcnt = 0

    def __getitem__(self, idx):
        return V(self, self.h[idx])


class Eng:
    def __init__(self, name, e):
        self.name = name
        self.e = e
        self.sem = None
        self.cnt = 0
        self.val = 0
        self.ord2val = {}
        self.seen = {}


class K:
    def __init__(self, needed=None):
        self.record = needed is None
        self.needed = {} if needed is None else needed
        self.nc = bass.Bass("TRN2", target_bir_lowering=False)
        self.es = ExitStack()
        self.scopes = [self.es]
        nc = self.nc
        self.pe = Eng("pe", nc.tensor)
        self.act = Eng("act", nc.scalar)
        self.dve = Eng("dve", nc.vector)
        self.pool = Eng("pool", nc.gpsimd)
        self.sp = Eng("sp", nc.sync)
        self.engs = [self.pe, self.act, self.dve, self.pool, self.sp]
        self.nsem = 0
        self.dsem_free = []
        for E in self.engs:
            self._newsem(E)
        self.out_waits = []
        self.nincs = 0
        self.all_dsem = {}
        self.ntile = 0
        self.barrier = {}
        self.scope_tiles = [[]]
        self.ninst = 0

    def _sem(self, name):
        self.nsem += 1
        return self.es.enter_context(self.nc.semaphore(name))

    def _newsem(self, E):
        E.sem = self._sem(f"s_{E.name}_{self.nsem}")
        E.cnt = 0

    def dram(self, name, shape, dt, kind):
        return V(None, self.nc.dram_tensor(name, list(shape), dt, kind=kind).ap())

    def sb(self, name, shape, dt=F32):
        self.ntile += 1
        h = self.scopes[-1].enter_context(self.nc.sbuf_tensor(f"{name}_{self.ntile}", list(shape), dt))
        t = Tile(h, name, self.barrier)
        self.scope_tiles[-1].append(t)
        return t

    def ps(self, name, shape, dt=F32):
        self.ntile += 1
        h = self.es.enter_context(self.nc.psum_tensor(f"{name}_{self.ntile}", list(shape), dt))
        t = Tile(h, name)
        t.psum = True
        return t

    def push(self):
        es = ExitStack()
        self.scopes.append(es)
        self.scope_tiles.append([])

    def pop(self):
        for t in self.scope_tiles.pop():
            if t.w is not None:
                s, v = t.w
                if self.barrier.get(s, 0) < v:
                    self.barrier[s] = v
            for s, v in t.r.items():
                if self.barrier.get(s, 0) < v:
                    self.barrier[s] = v
            if t.dsem is not None:
                self.dsem_free.append((t.dsem, t.dcnt))
        self.scopes.pop().close()

    def _wait(self, E, key, n):
        if E.seen.get(key, 0) >= n:
            return
        E.seen[key] = n
        if isinstance(key, Eng):
            if self.record:
                self.needed.setdefault(key.name, set()).add(n)
                return
            E.e.wait_ge(key.sem, key.ord2val[n])
        else:
            E.e.wait_ge(key, n)

    def _deps(self, E, reads, writes):
        waits = {}

        def need(s, v):
            if waits.get(s, 0) < v:
                waits[s] = v

        for t in reads:
            if t.w is not None:
                need(*t.w)
            if t.psum:
                for s, v in t.r.items():
                    if s is not E:
                        need(s, v)
        strict = (E is not self.pe) and (E is self.pool or os.environ.get("KRELAX", "") == "")
        for t in writes:
            if t.w is not None and (strict or t.w[0] is not E):
                need(*t.w)
            for s, v in t.r.items():
                if strict or s is not E:
                    need(s, v)
        for s, v in waits.items():
            self._wait(E, s, v)

    def emit(self, E, fn, reads, writes, inc=True):
        reads = [x.t for x in reads if x is not None and x.t is not None]
        writes = [x.t for x in writes if x is not None and x.t is not None]
        self._deps(E, reads, writes)
        ins = fn()
        self.ninst += 1
        if inc:
            E.cnt += 1
            cid = E.cnt
            if (not self.record) and cid in self.needed.get(E.name, ()):
                E.val += 1
                ins.then_inc(E.sem, 1)
                E.ord2val[cid] = E.val
                self.nincs += 1
        else:
            cid = E.cnt + 1
        for t in reads:
            t.r[E] = cid
        for t in writes:
            t.w = (E, cid)
            t.r = {}
        return ins

    def dma(self, Q, out, in_, **kw):
        reads = [in_.t] if in_.t is not None else []
        writes = [out.t] if out.t is not None else []
        self._deps(Q, reads, writes)
        tl = out.t if out.t is not None else in_.t
        if tl.dsem is None:
            if self.dsem_free:
                tl.dsem, tl.dcnt = self.dsem_free.pop()
                self._wait(Q, tl.dsem, tl.dcnt)
            else:
                tl.dsem = self._sem(f"d_{tl.name}_{self.nsem}")
        ins = Q.e.dma_start(out=out.ap, in_=in_.ap, **kw)
        ins.then_inc(tl.dsem, 16)
        self.ninst += 1
        tl.dcnt += 16
        self.all_dsem[tl.dsem] = tl.dcnt
        if out.t is not None:
            out.t.w = (tl.dsem, tl.dcnt)
            out.t.r = {}
        if in_.t is not None:
            in_.t.r[tl.dsem] = tl.dcnt
        if out.t is None:
            self.out_waits.append((tl.dsem, tl.dcnt))
        return ins

    def finish(self):
        for s, v in self.all_dsem.items():
            self._wait(self.sp, s, v)
        for E in self.engs:
            if E is not self.sp and E.cnt > 0:
                self._wait(self.sp, E, E.cnt)

    def mm(self, out, lhsT, rhs, start=True, stop=True, inc=None, **kw):
        if inc is None:
            inc = stop
        return self.emit(self.pe, lambda: self.nc.tensor.matmul(out.ap, lhsT.ap, rhs.ap, start=start, stop=stop, **kw),
                         [lhsT, rhs], [out], inc=inc)

    def tr(self, out, in_, ident, inc=True):
        return self.emit(self.pe, lambda: self.nc.tensor.transpose(out.ap, in_.ap, ident.ap), [in_, ident], [out],
                         inc=inc)

    def actf(self, out, in_, func, bias=None, scale=None, accum=None):
        kw = {}
        rd = [in_]
        if bias is not None:
            if isinstance(bias, V):
                kw["bias"] = bias.ap
                rd.append(bias)
            else:
                kw["bias"] = float(bias)
        if scale is not None:
            if isinstance(scale, V):
                kw["scale"] = scale.ap
                rd.append(scale)
            else:
                kw["scale"] = float(scale)
        wr = [out]
        if accum is not None:
            kw["accum_out"] = accum.ap
            wr.append(accum)
        return self.emit(self.act, lambda: self.nc.scalar.activation(out.ap, in_.ap, func, **kw), rd, wr)

    def _ve(self, E):
        return E if E is not None else self.dve

    def tt(self, out, a, b, op, E=None):
        E = self._ve(E)
        return self.emit(E, lambda: E.e.tensor_tensor(out.ap, a.ap, b.ap, op), [a, b], [out])

    def ts(self, out, a, s1, op0, s2=None, op1=None, E=None):
        E = self._ve(E)
        rd = [a]
        a1 = s1
        if isinstance(s1, V):
            rd.append(s1)
            a1 = s1.ap
        a2 = s2
        if isinstance(s2, V):
            rd.append(s2)
            a2 = s2.ap
        kw = {}
        if op1 is not None:
            kw["op1"] = op1
        return self.emit(E, lambda: E.e.tensor_scalar(out.ap, a.ap, a1, a2, op0, **kw), rd, [out])

    def stt(self, out, a, s, b, op0, op1):
        E = self.dve
        rd = [a, b]
        a1 = s
        if isinstance(s, V):
            rd.append(s)
            a1 = s.ap
        return self.emit(E, lambda: E.e.scalar_tensor_tensor(out.ap, a.ap, a1, b.ap, op0, op1), rd, [out])

    def cp(self, out, in_, E=None):
        E = self._ve(E)
        if E is self.act:
            return self.emit(E, lambda: self.nc.scalar.copy(out.ap, in_.ap), [in_], [out])
        return self.emit(E, lambda: E.e.tensor_copy(out.ap, in_.ap), [in_], [out])

    def memset(self, out, val, E=None):
        E = self._ve(E)
        return self.emit(E, lambda: E.e.memset(out.ap, val), [], [out])

    def scan(self, out, d0, d1, init, op0, op1):
        rd = [d0, d1]
        i = init
        if isinstance(init, V):
            rd.append(init)
            i = init.ap
        return self.emit(self.dve, lambda: self.nc.vector.tensor_tensor_scan(out.ap, d0.ap, d1.ap, i, op0, op1), rd,
                         [out])

    def recip(self, out, in_):
        return self.emit(self.dve, lambda: self.nc.vector.reciprocal(out.ap, in_.ap), [in_], [out])

    def iota(self, out, pattern, base, cm):
        return self.emit(self.pool, lambda: self.nc.gpsimd.iota(out.ap, pattern, base=base, channel_multiplier=cm,
                                                               allow_small_or_imprecise_dtypes=True), [], [out])

    def asel(self, out, in_, pattern, op, fill, base, cm):
        return self.emit(self.pool, lambda: self.nc.gpsimd.affine_select(out.ap, in_.ap, pattern, op, fill, base=base,
                                                                        channel_multiplier=cm), [in_], [out])


PP_SPEC = [("norm_g0", 8), ("norm_g1", 8), ("norm_g2", 8), ("b_ada", 72),
           ("conv_w0", 6), ("conv_w1", 6), ("conv_w2", 6), ("conv_b", 6), ("ssd_norm", 4),
           ("ssd_D0", 4), ("ssd_D1", 4), ("dt_bias", 16), ("A_log", 16), ("gk_b", 4), ("gla_norm", 1),
           ("rw_mu", 15), ("rw_w0", 8), ("rw_a0", 4), ("rw_kk", 4), ("rw_ka", 4), ("rw_rk", 4),
           ("rw_ln_w", 4), ("rw_ln_b", 4)]
PP_OFF = {}
_o = 0
for _n, _c in PP_SPEC:
    PP_OFF[_n] = (_o, _c)
    _o += _c
NPP = _o


def _cm(vec):
    vec = np.asarray(vec, np.float32).reshape(-1)
    n = vec.shape[0] // 128
    return np.ascontiguousarray(vec.reshape(n, 128).T)


def pack_params(inp, l):
    pp = np.zeros((128, NPP), np.float32)

    def put(name, arr):
        o, c = PP_OFF[name]
        assert arr.shape == (128, c), (name, arr.shape, c)
        pp[:, o:o + c] = arr

    for i in range(3):
        put(f"norm_g{i}", _cm(inp["norm_g"][l, i]))
    put("b_ada", _cm(inp["b_ada"][l]))
    for i in range(3):
        put(f"conv_w{i}", _cm(inp["ssd_conv_w"][l, i]))
    put("conv_b", _cm(inp["ssd_conv_b"][l]))
    put("ssd_norm", _cm(inp["ssd_norm"][l]))
    hd = (2 * np.arange(4)[None, :] + (np.arange(128)[:, None] // 64))
    put("ssd_D0", inp["ssd_D"][l, 0][hd])
    put("ssd_D1", inp["ssd_D"][l, 1][hd])
    put("dt_bias", np.broadcast_to(inp["ssd_dt_bias"][l].reshape(1, 16), (128, 16)))
    put("A_log", np.broadcast_to(inp["ssd_A_log"][l].reshape(1, 16), (128, 16)))
    put("gk_b", np.concatenate([_cm(inp["gla_gk_b"][l, 0]), _cm(inp["gla_gk_b"][l, 1])], axis=1))
    put("gla_norm", _cm(inp["gla_norm"][l]))
    mu = inp["rw_mu"][l]
    mucols = np.zeros((128, 15), np.float32)
    mucols[:, 0:13] = _cm(mu[0:1664])
    mucols[0:64, 13] = mu[1664:1728]
    mucols[:, 14] = mu[1728:1856]
    put("rw_mu", mucols)
    put("rw_w0", np.concatenate([_cm(inp["rw_w0"][l, 0]), _cm(inp["rw_w0"][l, 1])], axis=1))
    put("rw_a0", _cm(inp["rw_a0"][l]))
    put("rw_kk", _cm(inp["rw_kk"][l]))
    put("rw_ka", _cm(inp["rw_ka"][l]))
    put("rw_rk", _cm(inp["rw_rk"][l]))
    put("rw_ln_w", _cm(inp["rw_ln_w"][l]))
    put("rw_ln_b", _cm(inp["rw_ln_b"][l]))
    return pp


class WS:
    NST = 2
    NBF = 3
    CAST_PAT = ["dve", "act", "dve", "pool", "dve", "act", "dve", "act"]

    def __init__(self, k):
        self.k = k
        self.st = [k.sb(f"wst{i}", [128, 2048], F32) for i in range(self.NST)]
        self.bf = [k.sb(f"wbf{i}", [128, 2048], BF16) for i in range(self.NBF)]
        self.slabs = []
        self.base_j = 0
        self.limit = None
        self.old = {}
        self.nd = 0
        self.ncast = 0
        self.nget = 0

    def add(self, dview, kcs, ncols):
        assert kcs * ncols <= 2048
        self.slabs.append((dview, kcs, ncols))

    def _stbuf(self, j):
        return self.st[(j - self.base_j) % self.NST] if j >= self.base_j else self.old[j]

    def rebase(self, nst):
        self.old = {j: self._stbuf(j) for j in range(self.ncast, self.nd)}
        assert all(b in self.st[:nst] for b in self.old.values()) or not self.old, "in-flight slab in a dropped buffer"
        self.st = self.st[:nst]
        self.NST = nst
        self.limit = None
        self.base_j = self.nd

    def _dma(self, j):
        dview, kcs, ncols = self.slabs[j]
        st = self._stbuf(j)
        self.k.dma(self.k.sp, st[:, 0:kcs * ncols].re("p (a b) -> p a b", a=kcs), dview)

    def _cast(self, j):
        dview, kcs, ncols = self.slabs[j]
        n = kcs * ncols
        E = getattr(self.k, self.CAST_PAT[j % len(self.CAST_PAT)])
        self.k.cp(self.bf[j % self.NBF][:, 0:n], self._stbuf(j)[:, 0:n], E=E)

    def next(self):
        j = self.nget
        self.nget += 1
        n = len(self.slabs) if self.limit is None else min(len(self.slabs), self.limit)
        while self.nd < min(n, j + self.NST):
            self._dma(self.nd)
            self.nd += 1
        while self.ncast < min(n, j + 2):
            self._cast(self.ncast)
            self.ncast += 1
        dview, kcs, ncols = self.slabs[j]
        return self.bf[j % self.NBF][:, 0:kcs * ncols].re("p (a b) -> p a b", a=kcs)


def wview(w2d, k0, k1, c0, c1):
    return w2d.re("(kc p) n -> p kc n", p=128)[:, k0:k1, c0:c1]


class StopBuild(Exception):
    pass


class Prog:
    def __init__(self, debug=(), needed=None):
        import os
        self.stop = os.environ.get("KSTOP", "")
        self.debug = set(debug)
        self.k = K(needed)
        self.dbg_out = {}
        self.build()

    def pp(self, l, name, c0=0, c1=None):
        o, c = PP_OFF[name]
        if c1 is None:
            c1 = c
        return self.PP[l][:, o + c0:o + c1]

    def chk(self, name):
        if self.stop == name:
            raise StopBuild(name)

    def bank(self):
        b = self.banks[self.nbank % 8]
        self.nbank += 1
        return b

    def dbg(self, name, view, shape):
        if name not in self.debug or name in self.dbg_out:
            return
        d = self.k.dram("dbg_" + name, shape, view.ap.dtype, "ExternalOutput")
        self.k.dma(self.k.sp, d, view)
        self.dbg_out[name] = shape

    def build(self):
        k = self.k
        nc = k.nc
        self.xT_d = k.dram("xT", [D, TOK], F32, "ExternalInput")
        self.cond_d = k.dram("cond", [128, 8], F32, "ExternalInput")
        self.flags_d = k.dram("flags", [128, 32], F32, "ExternalInput")
        self.gp_d = k.dram("gp", [128, 8], F32, "ExternalInput")
        self.pp_d = k.dram("pp", [DEPTH, 128, NPP], F32, "ExternalInput")
        self.w = {}
        for name, shape in [("w_ada", [DEPTH, D, 9 * D]), ("ffn_gate", [DEPTH, 2, D, DFF]),
                            ("ffn_up", [DEPTH, 2, D, DFF]), ("ffn_down", [DEPTH, 2, DFF, D]),
                            ("w_in", [DEPTH, D, NIN]), ("w_ssd_o", [DEPTH, 512, D]), ("w_gla_o", [DEPTH, 512, D]),
                            ("w_rw_o", [DEPTH, 512, D]), ("w_out", [DEPTH, D, D])]:
            self.w[name] = k.dram(name, shape, F32, "ExternalInput")
        self.yT_d = k.dram("yT", [D, TOK], F32, "ExternalOutput")
        self.lvlmask_d = k.dram("lvlmask", [2, 128, 7, 128], F32, "ExternalInput")
        self.st_ssd_d = k.dram("st_ssd", [DEPTH, 2, 128, 256], F32, "ExternalInput")
        self.ns_ssd_d = k.dram("ns_ssd", [DEPTH, 4, 2, 128, 256], F32, "ExternalOutput")
        self.st_gla_d = k.dram("st_gla", [DEPTH, 2, 128, 256], F32, "ExternalInput")
        self.ns_gla_d = k.dram("ns_gla", [DEPTH, 4, 2, 128, 256], F32, "ExternalOutput")
        self.gkw_d = k.dram("gla_gk_w", [DEPTH, 2, 16, 256], F32, "ExternalInput")
        self.st_rw_d = k.dram("st_rw", [DEPTH, 2, 128, 256], F32, "ExternalInput")
        self.ns_rw_d = k.dram("ns_rw", [DEPTH, 4, 2, 128, 256], F32, "ExternalOutput")
        self.rw_w2_d = k.dram("rw_w2", [DEPTH, 2, 64, 512], F32, "ExternalInput")
        self.rw_a2_d = k.dram("rw_a2", [DEPTH, 64, 512], F32, "ExternalInput")
        self.rw_g2_d = k.dram("rw_g2", [DEPTH, 128, 512], F32, "ExternalInput")

        self.xT = k.sb("xT", [128, 8, TOK], F32)
        self.hT = k.sb("hT", [128, 8, TOK], BF16)
        self.flags = k.sb("flags", [128, 32], F32)
        self.gp = k.sb("gp", [128, 8], F32)
        self.PP = [k.sb(f"pp{l}", [128, NPP], F32) for l in range(DEPTH)]
        self.ada = [k.sb(f"ada{l}", [128, 72], F32) for l in range(DEPTH)]
        self.modA = [k.sb(f"modA{l}", [128, 24], F32) for l in range(DEPTH)]
        self.gate = [k.sb(f"gate{l}", [128, 24], F32) for l in range(DEPTH)]
        self.ones_bf = k.sb("ones_bf", [128, 128], BF16)
        self.ident_bf = k.sb("ident_bf", [128, 128], BF16)
        self.ident_f = k.sb("ident_f", [128, 128], F32)
        self.cst = k.sb("cst", [128, 8], F32)
        self.eps6 = self.cst[:, 0:1]
        self.epsgn = self.cst[:, 1:2]
        self.ones_f = k.sb("ones_f", [128, 128], F32)
        self.maskU = k.sb("maskU", [128, 128], F32)
        self.maskL = k.sb("maskL", [128, 128], F32)
        self.maskSU = k.sb("maskSU", [128, 128], F32)
        self.maskSL = k.sb("maskSL", [128, 128], F32)
        self.mask2 = [k.sb(f"mask2_{d}", [128, 2, 128], F32) for d in range(2)]
        self.lvlmask = [k.sb(f"lvlmask{d}", [128, 7, 128], BF16) for d in range(2)]
        self.blockones = k.sb("blockones", [128, 128], BF16)
        self.blockmean = k.sb("blockmean", [128, 128], BF16)
        self.banks = [k.ps(f"bank{i}", [128, 512], F32) for i in range(8)]
        self.nbank = 0
        self.ws = WS(k)

        k.dma(k.sp, self.flags[:], self.flags_d)
        k.dma(k.sp, self.gp[:], self.gp_d)
        for l in range(DEPTH):
            k.dma(k.sp, self.PP[l][:], self.pp_d[l])
        xv = self.xT_d.re("(c p) t -> p c t", p=128)
        for c in range(8):
            k.dma(k.sp, self.xT[:, c, :], xv[:, c, :])

        k.push()
        lmf = k.sb("lvlmask_f", [128, 7, 128], F32)
        for d in range(2):
            k.dma(k.sp, lmf[:], self.lvlmask_d[d])
            k.cp(self.lvlmask[d][:], lmf[:], E=k.pool)
        k.pop()
        k.memset(self.cst[:, 0:1], 1e-6, E=k.pool)
        k.memset(self.cst[:, 1:2], 64e-5, E=k.pool)
        k.memset(self.cst[:, 2:3], 1.0, E=k.pool)
        k.memset(self.ones_bf[:], 1.0, E=k.pool)
        k.memset(self.ones_f[:], 1.0, E=k.pool)
        k.memset(self.maskU[:], 1.0, E=k.pool)
        k.asel(self.maskU[:], self.maskU[:], [[1, 128]], ALU.is_ge, 0.0, 0, -1)
        k.memset(self.maskSU[:], 1.0, E=k.pool)
        k.asel(self.maskSU[:], self.maskSU[:], [[1, 128]], ALU.is_ge, 0.0, -1, -1)
        k.memset(self.maskSL[:], 1.0, E=k.pool)
        k.asel(self.maskSL[:], self.maskSL[:], [[-1, 128]], ALU.is_ge, 0.0, -1, 1)
        k.memset(self.blockones[:], 0.0, E=k.pool)
        k.memset(self.blockmean[:], 0.0, E=k.pool)
        for e in range(2):
            k.memset(self.blockones[e * 64:(e + 1) * 64, e * 64:(e + 1) * 64], 1.0, E=k.pool)
            k.memset(self.blockmean[e * 64:(e + 1) * 64, e * 64:(e + 1) * 64], 1.0 / 64, E=k.pool)
        k.memset(self.maskL[:], 1.0, E=k.pool)
        k.asel(self.maskL[:], self.maskL[:], [[-1, 128]], ALU.is_ge, 0.0, 0, 1)
        k.cp(self.mask2[0][:, 0, :], self.maskU[:], E=k.pool)
        k.cp(self.mask2[0][:, 1, :], self.maskSU[:], E=k.pool)
        k.cp(self.mask2[1][:, 0, :], self.maskL[:], E=k.pool)
        k.cp(self.mask2[1][:, 1, :], self.maskSL[:], E=k.pool)
        k.memset(self.ident_f[:], 1.0, E=k.pool)
        k.asel(self.ident_f[:], self.ident_f[:], [[-1, 128]], ALU.is_equal, 0.0, 0, 1)
        k.cp(self.ident_bf[:], self.ident_f[:], E=k.pool)

        for l in range(DEPTH):
            self.plan_ada(l)
        for l in range(DEPTH):
            self.plan_ffn(l, 0)
            self.plan_mixer(l)
            self.plan_ffn(l, 1)

        self.pos_embed()
        self.compute_ada()
        try:
            for l in range(DEPTH):
                self.ffn(l, 0)
                self.dbg(f"x1_{l}", self.xT[:], [128, 8, TOK])
                self.mixer(l)
                self.dbg(f"x2_{l}", self.xT[:], [128, 8, TOK])
                self.ffn(l, 1)
        except StopBuild:
            while len(k.scopes) > 1:
                k.pop()
        self.final_norm()
        k.finish()

    def pos_embed(self):
        k = self.k
        k.push()
        idx = k.sb("pe_idx", [128, 2], F32)
        om = k.sb("pe_om", [128, 2], F32)
        pos = k.sb("pe_pos", [128, 80], F32)
        arg = k.sb("pe_arg", [128, 4, 80], F32)
        ki = k.sb("pe_ki", [128, 4, 80], I32)
        kf = k.sb("pe_kf", [128, 4, 80], F32)
        gt = k.sb("pe_gt", [128, 4, 80], F32)
        emb = k.sb("pe_emb", [128, 4, 80], F32)
        k.iota(idx[:], [[128, 2]], 0, 1)
        k.actf(om[:], idx[:], AF.Exp, scale=-float(np.log(10000.0)) / 256.0)
        k.iota(pos[:, 0:16], [[1, 16]], 0, 0)
        k.iota(pos[:, 16:80], [[1, 64]], 0, 0)
        for j in range(4):
            ph = 0.0 if j < 2 else PI / 2
            k.ts(arg[:, j, :], pos[:], om[:, (j % 2):(j % 2) + 1], ALU.mult, ph, ALU.add)
        k.ts(kf[:], arg[:], 1.0 / (2 * PI), ALU.mult)
        k.cp(ki[:], kf[:])
        k.cp(kf[:], ki[:])
        k.stt(arg[:], kf[:], -2 * PI, arg[:], ALU.mult, ALU.add)
        k.ts(gt[:], arg[:], PI, ALU.is_gt, -2 * PI, ALU.mult)
        k.tt(arg[:], arg[:], gt[:], ALU.add)
        k.ts(gt[:], arg[:], -PI, ALU.is_lt, 2 * PI, ALU.mult)
        k.tt(arg[:], arg[:], gt[:], ALU.add)
        k.actf(emb[:], arg[:], AF.Sin)
        k.ts(emb[:], emb[:], self.flags[:, 16:17], ALU.mult)
        self.dbg("emb", emb[:], [128, 4, 80])
        for fc in range(8):
            xv = self.xT[:, fc, :].re("p (r c) -> p r c", c=64)
            if fc < 4:
                ev = V(emb, emb.h[:, fc, 0:16].unsqueeze(2).to_broadcast([128, 16, 64]))
            else:
                ev = V(emb, emb.h[:, fc - 4, 16:80].unsqueeze(1).to_broadcast([128, 16, 64]))
            k.tt(xv, xv, ev, ALU.add)
        k.pop()

    def plan_ada(self, l):
        for s in range(36):
            self.ws.add(wview(self.w["w_ada"][l], 0, 8, s * 256, (s + 1) * 256), 8, 256)

    def compute_ada(self):
        k = self.k
        k.push()
        cond = k.sb("cond", [128, 8], F32)
        sg = k.sb("cond_sg", [128, 8], F32)
        scond = k.sb("scond", [128, 8], BF16)
        k.dma(k.sp, cond[:], self.cond_d)
        extra = [k.sb(f"wst_x{i}", [128, 2048], F32) for i in range(4)]
        self.ws.st = self.ws.st + extra
        self.ws.NST = len(self.ws.st)
        self.ws.limit = 36 * DEPTH
        k.actf(sg[:], cond[:], AF.Sigmoid)
        k.tt(scond[:], cond[:], sg[:], ALU.mult)
        for l in range(DEPTH):
            pb = self.bank()
            for s in range(36):
                wb = self.ws.next()
                for m in range(2):
                    j = s * 2 + m
                    for kc in range(8):
                        k.mm(pb[:, j:j + 1], wb[:, kc, m * 128:(m + 1) * 128], scond[:, kc:kc + 1],
                             start=(kc == 0), stop=(kc == 7))
            k.tt(self.ada[l][:], pb[:, 0:72], self.pp(l, "b_ada"), ALU.add)
            for i in range(3):
                k.stt(self.modA[l][:, i * 8:(i + 1) * 8], self.ada[l][:, (3 * i + 1) * 8:(3 * i + 2) * 8], 1.0,
                      self.pp(l, f"norm_g{i}"), ALU.add, ALU.mult)
                k.ts(self.gate[l][:, i * 8:(i + 1) * 8], self.ada[l][:, (3 * i + 2) * 8:(3 * i + 3) * 8],
                     0.5 if i != 1 else 1.0, ALU.mult)
            self.dbg(f"ada{l}", self.ada[l][:], [128, 72])
        self.ws.rebase(2)
        k.pop()

    def rstd_of_x(self, rstd):
        k = self.k
        k.push()
        sq = k.sb("sq", [128, 8, TOK], BF16)
        for c in range(8):
            k.actf(sq[:, c, :], self.xT[:, c, :], AF.Square)
        for h in range(2):
            pb = self.bank()
            for c in range(8):
                k.mm(pb[:], self.ones_bf[:], sq[:, c, h * 512:(h + 1) * 512], start=(c == 0), stop=(c == 7))
            k.actf(rstd[:, h * 512:(h + 1) * 512], pb[:], AF.Ln, scale=1.0 / D, bias=self.eps6)
        k.actf(rstd[:], rstd[:], AF.Exp, scale=-0.5)
        k.pop()

    def norm_mod(self, l, i):
        k = self.k
        k.push()
        rstd = k.sb("rstd", [128, TOK], F32)
        tmp = [k.sb(f"nm_tmp{j}", [128, TOK], F32) for j in range(2)]
        self.rstd_of_x(rstd)
        for c in range(8):
            t = tmp[c % 2]
            k.tt(t[:], self.xT[:, c, :], rstd[:], ALU.mult)
            k.actf(self.hT[:, c, :], t[:], AF.Identity, scale=self.modA[l][:, i * 8 + c:i * 8 + c + 1],
                   bias=self.ada[l][:, 3 * i * 8 + c:3 * i * 8 + c + 1])
        k.pop()

    def final_norm(self):
        k = self.k
        k.push()
        rstd = k.sb("rstd", [128, TOK], F32)
        tmp = [k.sb(f"fn_tmp{j}", [128, TOK], F32) for j in range(2)]
        self.rstd_of_x(rstd)
        yv = self.yT_d.re("(c p) t -> p c t", p=128)
        for c in range(8):
            t = tmp[c % 2]
            k.stt(t[:], self.xT[:, c, :], self.gp[:, c:c + 1], rstd[:], ALU.mult, ALU.mult)
            k.dma(k.sp, yv[:, c, :], t[:])
        k.pop()

    def plan_ffn(self, l, which):
        wg = self.w["ffn_gate"][l, which]
        wu = self.w["ffn_up"][l, which]
        wd = self.w["ffn_down"][l, which]
        for s in range(11):
            self.ws.add(wview(wg, 0, 8, s * 256, (s + 1) * 256), 8, 256)
            self.ws.add(wview(wu, 0, 8, s * 256, (s + 1) * 256), 8, 256)
        for s in range(4):
            for (k0, k1) in ((0, 8), (8, 16), (16, 22)):
                self.ws.add(wview(wd, k0, k1, s * 256, (s + 1) * 256), k1 - k0, 256)

    def ffn(self, l, which):
        k = self.k
        i = 0 if which == 0 else 2
        self.norm_mod(l, i)
        k.push()
        actT = k.sb("actT", [128, FC, TOK], BF16)
        sg = [k.sb(f"ffn_sg{j}", [128, 512], F32) for j in range(2)]
        nsg = 0
        for s in range(11):
            wg = self.ws.next()
            wu = self.ws.next()
            for m in range(2):
                fcb = s * 2 + m
                for h in range(2):
                    pg = self.bank()
                    pu = self.bank()
                    for kc in range(8):
                        k.mm(pg[:], wg[:, kc, m * 128:(m + 1) * 128], self.hT[:, kc, h * 512:(h + 1) * 512],
                             start=(kc == 0), stop=(kc == 7))
                    for kc in range(8):
                        k.mm(pu[:], wu[:, kc, m * 128:(m + 1) * 128], self.hT[:, kc, h * 512:(h + 1) * 512],
                             start=(kc == 0), stop=(kc == 7))
                    t = sg[nsg % 2]
                    nsg += 1
                    k.actf(t[:], pg[:], AF.Silu)
                    k.tt(actT[:, fcb, h * 512:(h + 1) * 512], pu[:], t[:], ALU.mult)
        gcol = self.gate[l]
        for s in range(4):
            pbs = [[self.bank() for h in range(2)] for m in range(2)]
            for ksub, (k0, k1) in enumerate(((0, 8), (8, 16), (16, 22))):
                wd = self.ws.next()
                for m in range(2):
                    for h in range(2):
                        for kc in range(k0, k1):
                            k.mm(pbs[m][h][:], wd[:, kc - k0, m * 128:(m + 1) * 128],
                                 actT[:, kc, h * 512:(h + 1) * 512], start=(kc == 0), stop=(kc == FC - 1),
                                 inc=(kc == k1 - 1))
            for m in range(2):
                c = s * 2 + m
                for h in range(2):
                    xs = self.xT[:, c, h * 512:(h + 1) * 512]
                    k.stt(xs, pbs[m][h][:], gcol[:, i * 8 + c:i * 8 + c + 1], xs, ALU.mult, ALU.add)
        k.pop()


    SEC = dict(z=0, xbc=512, dt=1280, q=1296, k=1552, v=1808, g=2320, lr=2832, rr=2864, rk=3376, rv=3888,
               wlr=4400, alr=4528, glr=4592, gate=4720)

    def win_slab(self, l, c0, ncols):
        self.ws.add(wview(self.w["w_in"][l], 0, 8, c0, c0 + ncols), 8, ncols)

    def plan_mixer(self, l):
        self.plan_ssd(l)
        self.plan_gla(l)
        self.plan_rw(l)
        self.plan_out(l, "w_out")

    def plan_out(self, l, name):
        w = self.w[name][l]
        if name == "w_out":
            for s in range(4):
                self.ws.add(wview(w, 0, 8, s * 256, (s + 1) * 256), 8, 256)
        else:
            for s in range(2):
                self.ws.add(wview(w, 0, 4, s * 512, (s + 1) * 512), 4, 512)

    def proj_fm(self, wb, m0, mcols, h):
        k = self.k
        pb = self.bank()
        for kc in range(8):
            k.mm(pb[0:mcols, :], wb[:, kc, m0:m0 + mcols], self.hT[:, kc, h * 512:(h + 1) * 512],
                 start=(kc == 0), stop=(kc == 7))
        return pb

    def mixer(self, l):
        k = self.k
        self.norm_mod(l, 1)
        k.push()
        self.merged = k.sb("merged", [128, 8, TOK], BF16)
        self.ssd_branch(l)
        self.dbg(f"mg0_{l}", self.merged[:], [128, 8, TOK])
        self.chk("F")
        self.gla_branch(l)
        self.dbg(f"mg1_{l}", self.merged[:], [128, 8, TOK])
        self.chk("G")
        self.rw_branch(l)
        self.dbg(f"mg2_{l}", self.merged[:], [128, 8, TOK])
        self.chk("H")
        mbf = self.merged
        gcol = self.gate[l]
        for s in range(4):
            wb = self.ws.next()
            for m in range(2):
                c = s * 2 + m
                for h in range(2):
                    pb = self.bank()
                    for kc in range(8):
                        k.mm(pb[:], wb[:, kc, m * 128:(m + 1) * 128], mbf[:, kc, h * 512:(h + 1) * 512],
                             start=(kc == 0), stop=(kc == 7))
                    xs = self.xT[:, c, h * 512:(h + 1) * 512]
                    k.stt(xs, pb[:], gcol[:, 8 + c:8 + c + 1], xs, ALU.mult, ALU.add)
        k.pop()

    def plan_branch_out(self, l, name, b):
        w = self.w[name][l]
        for s in range(4):
            self.ws.add(wview(w, 0, 4, s * 256, (s + 1) * 256), 4, 256)
            self.win_slab(l, self.SEC["gate"] + b * 1024 + s * 256, 256)

    def branch_out(self, l, yT, b):
        k = self.k
        k.push()
        sgt = [k.sb(f"bo_sg{j}", [128, 512], F32) for j in range(2)]
        tmp = [k.sb(f"bo_tmp{j}", [128, 512], F32) for j in range(2)]
        n = 0
        for s in range(4):
            wo = self.ws.next()
            wg = self.ws.next()
            for m in range(2):
                c = s * 2 + m
                for h in range(2):
                    po = self.bank()
                    for kc in range(4):
                        k.mm(po[:], wo[:, kc, m * 128:(m + 1) * 128],
                             yT[:, kc, h * 512:(h + 1) * 512], start=(kc == 0), stop=(kc == 3))
                    pg = self.proj_fm(wg, m * 128, 128, h)
                    sg = sgt[n % 2]
                    tp = tmp[n % 2]
                    n += 1
                    k.actf(sg[:], pg[:], AF.Sigmoid)
                    mv = self.merged[:, c, h * 512:(h + 1) * 512]
                    if b == 0:
                        k.tt(mv, po[:], sg[:], ALU.mult)
                    else:
                        k.tt(tp[:], po[:], sg[:], ALU.mult)
                        k.tt(mv, mv, tp[:], ALU.add, E=k.pool)
        k.pop()

    def plan_ssd(self, l):
        S = self.SEC
        for c0 in (0, 256):
            self.win_slab(l, S["z"] + c0, 256)
        for c0 in (0, 256, 512):
            self.win_slab(l, S["xbc"] + c0, 256)
        self.win_slab(l, S["dt"], 16)
        self.plan_branch_out(l, "w_ssd_o", 0)

    def ssd_branch(self, l):
        k = self.k
        fl = self.flags
        k.push()
        zs = k.sb("ssd_zs", [128, 4, TOK], BF16)
        xc = k.sb("ssd_xc", [128, 6, TOK], BF16)
        xbt = k.sb("ssd_xbt", [128, NT, 640], BF16)
        dt = k.sb("ssd_dt", [128, NT, 16], F32)
        lndt = k.sb("ssd_lndt", [128, NT, 16], F32)
        a = k.sb("ssd_a", [128, NT, 16], F32)
        cs = k.sb("ssd_cs", [128, NT, 16], F32)
        aneg = k.sb("ssd_aneg", [128, 16], F32)
        dsum = k.sb("ssd_dsum", [128, 4], F32)
        ygT = k.sb("ssd_yg", [128, 4, TOK], BF16)
        sin_bf = [k.sb(f"ssd_sin{d}", [128, NT, 256], BF16) for d in range(2)]
        for s in range(2):
            wb = self.ws.next()
            for m in range(2):
                for h in range(2):
                    pb = self.proj_fm(wb, m * 128, 128, h)
                    k.actf(zs[:, s * 2 + m, h * 512:(h + 1) * 512], pb[:], AF.Silu)
        k.push()
        xp = k.sb("ssd_xp", [128, 6, 4, 258], BF16)
        acc = [k.sb(f"ssd_acc{j}", [128, 4, 256], F32) for j in range(2)]
        k.memset(xp[:, :, :, 0:1], 0.0, E=k.pool)
        k.memset(xp[:, :, :, 257:258], 0.0, E=k.pool)
        for s in range(3):
            wb = self.ws.next()
            for m in range(2):
                for h in range(2):
                    pb = self.proj_fm(wb, m * 128, 128, h)
                    k.cp(xp[:, s * 2 + m, 2 * h:2 * h + 2, 1:257], pb[:].re("p (q t) -> p q t", q=2),
                         E=(k.act if h == 0 else k.dve))
        cfv = V(fl, fl.h[:, 2:8:2].unsqueeze(1).to_broadcast([128, 6, 3]))
        cbv = V(fl, fl.h[:, 9:15:2].unsqueeze(1).to_broadcast([128, 6, 3]))
        k.tt(xp[:, :, 1:4, 0], xp[:, :, 0:3, 256], cfv, ALU.mult)
        k.tt(xp[:, :, 0:3, 257], xp[:, :, 1:4, 1], cbv, ALU.mult)
        for c in range(6):
            t = acc[c % 2]
            k.ts(t[:], xp[:, c, :, 1:257], self.pp(l, "conv_w1", c, c + 1), ALU.mult,
                 self.pp(l, "conv_b", c, c + 1), ALU.add)
            k.stt(t[:], xp[:, c, :, 0:256], self.pp(l, "conv_w0", c, c + 1), t[:], ALU.mult, ALU.add)
            k.stt(t[:], xp[:, c, :, 2:258], self.pp(l, "conv_w2", c, c + 1), t[:], ALU.mult, ALU.add)
            k.actf(xc[:, c, :].re("p (q t) -> p q t", q=4), t[:], AF.Silu)
        k.pop()
        self.dbg("ssd_xc", xc[:], [128, 6, TOK])
        self.chk("A")
        for t in range(NT):
            pb = self.bank()
            pv = pb[:].bitcast(BF16)
            for c in range(5):
                k.tr(pv[:, c * 128:(c + 1) * 128], xc[:, c, t * 128:(t + 1) * 128], self.ident_bf[:], inc=(c == 4))
            k.cp(xbt[:, t, :], pv[:, 0:640], E=(k.act if t % 2 else k.dve))
        self.chk("B")
        wb = self.ws.next()
        pb = self.bank()
        for t in range(NT):
            for kc in range(8):
                k.mm(pb[:, t * 16:(t + 1) * 16], self.hT[:, kc, t * 128:(t + 1) * 128], wb[:, kc, 0:16],
                     start=(kc == 0), stop=(kc == 7))
        k.tt(dt[:], pb[:, 0:128].re("p (t c) -> p t c", t=NT),
             V(self.PP[l], self.pp(l, "dt_bias").ap.unsqueeze(1).to_broadcast([128, NT, 16])), ALU.add)
        k.actf(dt[:], dt[:], AF.Exp)
        k.ts(lndt[:], dt[:], -0.5, ALU.mult, 1.0, ALU.add)
        k.tt(lndt[:], lndt[:], dt[:], ALU.mult)
        k.actf(dt[:], dt[:], AF.Ln, bias=self.cst[:, 2:3])
        k.tt(dt[:], dt[:], lndt[:], ALU.max)
        k.actf(lndt[:], dt[:], AF.Ln)
        k.actf(aneg[:], self.pp(l, "A_log"), AF.Exp)
        k.ts(aneg[:], aneg[:], -1.0, ALU.mult)
        k.tt(a[:], dt[:], V(aneg, aneg.h[:, :].unsqueeze(1).to_broadcast([128, NT, 16])), ALU.mult)
        k.tt(dsum[:], self.pp(l, "ssd_D0"), self.pp(l, "ssd_D1"), ALU.add)
        self.dbg("ssd_dt", dt[:], [128, NT, 16])
        self.chk("B1")
        pb = self.bank()
        for t in range(NT):
            k.mm(pb[:, t * 16:t * 16 + 8], self.maskU[:], a[:, t, 0:8])
            k.mm(pb[:, t * 16 + 8:t * 16 + 16], self.maskL[:], a[:, t, 8:16])
        k.cp(cs[:], pb[:, 0:128].re("p (t c) -> p t c", t=NT))
        self.chk("B2")
        carg = k.sb("ssd_carg", [128, NT, 16], F32)
        k.tt(carg[:], lndt[:], cs[:], ALU.subtract)
        tot = k.sb("ssd_tot", [128, NT, 16], F32)
        wdec = k.sb("ssd_wdec", [128, NT, 16], F32)
        dech = [k.sb(f"ssd_dech{d}", [128, NT, 4], F32) for d in range(2)]
        pb = self.bank()
        k.mm(pb[:, 0:128], self.ones_f[:], a[:].re("p t c -> p (t c)"))
        k.cp(tot[:], pb[:, 0:128].re("p (t c) -> p t c", t=NT))
        self.chk("B3")
        import os
        kvar = os.environ.get("KVAR", "")
        if kvar != "nowdec":
            k.tt(wdec[:], tot[:], carg[:], ALU.add)
            k.actf(wdec[:], wdec[:], AF.Exp)
        if kvar != "nodech":
            for d in range(2):
                for g in range(2):
                    ps_ = slice(g * 64, (g + 1) * 64)
                    k.actf(dech[d][ps_, :, :], tot[ps_, :, d * 8 + g * 4:d * 8 + g * 4 + 4], AF.Exp)
        self.chk("C")
        k.push()
        S = [k.sb(f"ssd_S{d}", [128, 256], F32) for d in range(2)]
        Sin = [k.sb(f"ssd_Sin{d}", [128, 256], F32) for d in range(2)]
        Stmp = [k.sb(f"ssd_Stmp{d}", [128, 256], F32) for d in range(2)]
        xdec = [[k.sb(f"ssd_xdec{d}_{j}", [128, 512], BF16) for j in range(2)] for d in range(2)]
        for d in range(2):
            k.dma(k.sp, Sin[d][:], self.st_ssd_d[l, d])
        for step in range(NT):
            for d in range(2):
                t = step if d == 0 else NT - 1 - step
                if step > 0:
                    fcol = t if d == 0 else 8 + t
                    k.ts(Sin[d][:], S[d][:], fl[:, fcol:fcol + 1], ALU.mult)
                k.cp(sin_bf[d][:, t, :], Sin[d][:], E=k.act)
                xd = xdec[d][step % 2]
                k.tt(xd[:].re("p (h q) -> p h q", h=8), xbt[:, t, 0:512].re("p (h q) -> p h q", h=8),
                     V(wdec, wdec.h[:, t, d * 8:(d + 1) * 8].unsqueeze(2).to_broadcast([128, 8, 64])), ALU.mult)
                pb = self.bank()
                for g in range(2):
                    k.mm(pb[g * 64:(g + 1) * 64, 0:256], xbt[:, t, 512 + g * 64:512 + (g + 1) * 64],
                         xd[:, g * 256:(g + 1) * 256])
                k.tt(Stmp[d][:].re("p (j q) -> p j q", j=4), Sin[d][:].re("p (j q) -> p j q", j=4),
                     V(dech[d], dech[d].h[:, t, :].unsqueeze(2).to_broadcast([128, 4, 64])), ALU.mult)
                k.tt(S[d][:], Stmp[d][:], pb[:, 0:256], ALU.add)
                if (d == 0 and t % 2 == 1) or (d == 1 and t % 2 == 0):
                    k.dma(k.sp, self.ns_ssd_d[l, t // 2, d], S[d][:])
        k.pop()
        self.chk("D")
        k.push()
        MT = [k.sb(f"ssd_MT{j}", [128, 8, 128], BF16) for j in range(2)]
        cdec = [[k.sb(f"ssd_cdec{d}_{j}", [128, 4, 128], BF16) for j in range(2)] for d in range(2)]
        am = k.sb("ssd_am", [128, 8, 128], F32)
        ambf = k.sb("ssd_ambf", [128, 8, 128], BF16)
        dm = [k.sb(f"ssd_dm{j}", [128, 8, 128], F32) for j in range(2)]
        er = [k.sb(f"ssd_er{j}", [128, 8, 128], BF16) for j in range(2)]
        ytmp = [k.sb(f"ssd_ytmp{j}", [128, 4, 128], F32) for j in range(2)]
        if "ssd_yraw" in self.debug:
            self.yraw = k.sb("ssd_yraw", [128, 4, TOK], F32)
        masks = (self.maskU, self.maskL)
        for t in range(NT):
            for d in range(2):
                mk = masks[d]
                k.tt(am[:], V(a, a.h[:, t, d * 8:(d + 1) * 8].unsqueeze(2).to_broadcast([128, 8, 128])),
                     V(mk, mk.h[:, :].unsqueeze(1).to_broadcast([128, 8, 128])), ALU.mult,
                     E=(k.dve if os.environ.get("KVAR", "") == "amdve" else k.pool))
                dmt = dm[d]
                ert = er[d]
                if os.environ.get("KVAR", "") == "rowbf":
                    k.cp(ambf[:], am[:])
                for hh in range(2):
                    pr = self.bank()
                    if os.environ.get("KVAR", "") == "rowbf":
                        k.mm(pr[:], self.ones_bf[:], ambf[:, hh * 4:(hh + 1) * 4, :])
                    else:
                        k.mm(pr[:], self.ones_f[:], am[:, hh * 4:(hh + 1) * 4, :])
                    rv = pr[:].re("p (h l) -> p h l", h=4)
                    k.tt(dmt[:, hh * 4:(hh + 1) * 4, :], rv,
                         V(carg, carg.h[:, t, d * 8 + hh * 4:d * 8 + hh * 4 + 4].unsqueeze(2).to_broadcast([128, 4, 128])),
                         ALU.add)
                    k.actf(ert[:, hh * 4:(hh + 1) * 4, :], rv, AF.Exp)
                if t == 0 and d == 0:
                    self.chk("E1b")
                k.ts(dmt[:], dmt[:], 30.0, ALU.min)
                if t == 0 and d == 0:
                    self.chk("E1c")
                k.actf(dmt[:], dmt[:], AF.Exp)
                if t == 0 and d == 0:
                    self.chk("E1d")
                k.tt(dmt[:], dmt[:], V(mk, mk.h[:, :].unsqueeze(1).to_broadcast([128, 8, 128])), ALU.mult)
                for g in range(2):
                    ps_ = slice(g * 64, (g + 1) * 64)
                    k.tt(cdec[d][t % 2][ps_, :, :], ert[ps_, g * 4:(g + 1) * 4, :],
                         V(xc, xc.h[ps_, 5, t * 128:(t + 1) * 128].unsqueeze(1).to_broadcast([64, 4, 128])), ALU.mult)
            if t == 0:
                self.chk("E1")
            k.tt(dm[0][:], dm[0][:], dm[1][:], ALU.add)
            pgs = [self.bank(), self.bank()]
            for g in range(2):
                ps_ = slice(g * 64, (g + 1) * 64)
                k.mm(pgs[g][:, 0:128], xc[ps_, 4, t * 128:(t + 1) * 128], xc[ps_, 5, t * 128:(t + 1) * 128])
            mt = MT[t % 2]
            for g in range(2):
                k.tt(mt[:, g * 4:(g + 1) * 4, :], dm[0][:, g * 4:(g + 1) * 4, :],
                     V(pgs[g], pgs[g].h[:, 0:128].unsqueeze(1).to_broadcast([128, 4, 128])), ALU.mult)
            if t == 0:
                self.chk("E2")
            pys = [self.bank(), self.bank()]
            for h in range(8):
                pair, e = h // 2, h % 2
                g, j = h // 4, h % 4
                out = pys[g][e * 64:(e + 1) * 64, (pair % 2) * 128:(pair % 2 + 1) * 128]
                gs = slice(g * 64, (g + 1) * 64)
                k.mm(out, xbt[:, t, h * 64:(h + 1) * 64], mt[:, h, :], start=True, stop=False)
                k.mm(out, sin_bf[0][gs, t, j * 64:(j + 1) * 64], cdec[0][t % 2][gs, j, :], start=False, stop=False)
                k.mm(out, sin_bf[1][gs, t, j * 64:(j + 1) * 64], cdec[1][t % 2][gs, j, :], start=False, stop=True)
            if t == 0:
                self.chk("E3")
            if t == 1:
                self.chk("E4")
            yt = ytmp[t % 2]
            k.tt(yt[:], xc[:, 0:4, t * 128:(t + 1) * 128],
                 V(dsum, dsum.h[:, :].unsqueeze(2).to_broadcast([128, 4, 128])), ALU.mult, E=k.pool)
            for g in range(2):
                k.tt(yt[:, g * 2:g * 2 + 2, :], pys[g][:, 0:256].re("p (c l) -> p c l", c=2), yt[:, g * 2:g * 2 + 2, :],
                     ALU.add)
            if "ssd_yraw" in self.debug:
                k.cp(self.yraw[:, :, t * 128:(t + 1) * 128], yt[:])
            k.tt(ygT[:, :, t * 128:(t + 1) * 128], yt[:], zs[:, :, t * 128:(t + 1) * 128], ALU.mult)
        if "ssd_yraw" in self.debug:
            self.dbg("ssd_yraw", self.yraw[:], [128, 4, TOK])
        k.pop()
        self.chk("E")
        yT = k.sb("ssd_yT", [128, 4, TOK], BF16)
        self.group_rmsnorm(ygT, yT, 4, 1.0 / 512, self.eps6, [self.pp(l, "ssd_norm", c, c + 1) for c in range(4)])
        self.dbg("ssd_yT", yT[:], [128, 4, TOK])
        self.branch_out(l, yT, 0)
        k.pop()

    def plan_gla(self, l):
        S = self.SEC
        self.win_slab(l, S["q"], 256)
        self.win_slab(l, S["k"], 256)
        self.win_slab(l, S["lr"], 32)
        for c0 in (0, 256):
            self.win_slab(l, S["v"] + c0, 256)
        for c0 in (0, 256):
            self.win_slab(l, S["g"] + c0, 256)
        self.plan_branch_out(l, "w_gla_o", 1)

    def gla_branch(self, l):
        k = self.k
        fl = self.flags
        k.push()
        og = k.sb("gla_og", [128, 4, TOK], F32)
        k.push()
        qd = [k.sb(f"gla_qd{d}", [128, 2, TOK], BF16) for d in range(2)]
        kd = [k.sb(f"gla_kd{d}", [128, 2, TOK], BF16) for d in range(2)]
        vtok = k.sb("gla_vtok", [128, NT, 512], BF16)
        sin_bf = [k.sb(f"gla_sin{d}", [128, NT, 256], BF16) for d in range(2)]
        etot = [k.sb(f"gla_etot{d}", [128, 2, NT], F32) for d in range(2)]
        gkw = k.sb("gla_gkw", [16, 2, 256], F32)
        gkw_bf = k.sb("gla_gkwbf", [16, 2, 256], BF16)
        lr_bf = [k.sb(f"gla_lr{d}", [16, TOK], BF16) for d in range(2)]
        nb = k.sb("gla_nb", [128, 4], F32)
        k.dma(k.sp, gkw[:], self.gkw_d[l].re("d r c -> r d c"))
        k.cp(gkw_bf[:], gkw[:], E=k.pool)
        k.ts(nb[:], self.pp(l, "gk_b"), -1.0, ALU.mult)
        k.push()
        self.rmask = k.sb("rmask", [128, TOK], F32)
        k.memset(self.rmask[:], 1.0, E=k.pool)
        k.memset(self.rmask[:, 0::128], 0.0, E=k.pool)
        qT = k.sb("gla_qT", [128, 2, TOK], BF16)
        kT = k.sb("gla_kT", [128, 2, TOK], BF16)
        T1 = k.sb("gla_T1", [128, 2, TOK], F32)
        T2 = k.sb("gla_T2", [128, 2, TOK], F32)
        T3 = k.sb("gla_T3", [128, 2, TOK], F32)
        wb = self.ws.next()
        for m in range(2):
            for h in range(2):
                pb = self.proj_fm(wb, m * 128, 128, h)
                k.actf(qT[:, m, h * 512:(h + 1) * 512], pb[:], AF.Copy, scale=0.125)
        wb = self.ws.next()
        for m in range(2):
            for h in range(2):
                pb = self.proj_fm(wb, m * 128, 128, h)
                k.cp(kT[:, m, h * 512:(h + 1) * 512], pb[:])
        wb = self.ws.next()
        for d in range(2):
            for h in range(2):
                pb = self.proj_fm(wb, d * 16, 16, h)
                k.cp(lr_bf[d][:, h * 512:(h + 1) * 512], pb[0:16, :], E=k.act)
        for d in range(2):
            for c in range(2):
                for h in range(2):
                    pb = self.bank()
                    k.mm(pb[:], gkw_bf[:, d, c * 128:(c + 1) * 128], lr_bf[d][:, h * 512:(h + 1) * 512])
                    k.actf(T1[:, c, h * 512:(h + 1) * 512], pb[:], AF.Exp, scale=-1.0, bias=nb[:, d * 2 + c:d * 2 + c + 1])
            k.actf(T1[:], T1[:], AF.Ln, bias=self.cst[:, 2:3])
            for c in range(2):
                k.scan(T2[:, c, :], self.rmask[:], T1[:, c, :], 0.0, ALU.mult, ALU.add)
            if d == 0:
                k.actf(T3[:], T2[:], AF.Exp, scale=-1.0 / 16)
                k.tt(qd[0][:], qT[:], T3[:], ALU.mult)
                k.cp(etot[0][:], T3[:, :, 127::128])
                k.actf(T3[:], T2[:], AF.Exp, scale=1.0 / 16)
                k.tt(kd[0][:], kT[:], T3[:], ALU.mult)
            else:
                k.actf(etot[1][:], T2[:, :, 127::128], AF.Exp, scale=-1.0 / 16)
                k.tt(T2[:], T2[:], T1[:], ALU.subtract)
                k.actf(T3[:], T2[:], AF.Exp, scale=1.0 / 16)
                k.tt(qd[1][:], qT[:], T3[:], ALU.mult)
                k.actf(T3[:], T2[:], AF.Exp, scale=-1.0 / 16)
                k.tt(kd[1][:], kT[:], T3[:], ALU.mult)
        k.pop()
        for s in range(2):
            wb = self.ws.next()
            for t in range(NT):
                pb = self.bank()
                for kc in range(8):
                    k.mm(pb[:, 0:256], self.hT[:, kc, t * 128:(t + 1) * 128], wb[:, kc, :], start=(kc == 0), stop=(kc == 7))
                k.cp(vtok[:, t, s * 256:(s + 1) * 256], pb[:, 0:256], E=(k.act if t % 2 else k.dve))
        k.push()
        ktok = k.sb("gla_ktok", [128, NT, 4, 128], BF16)
        for t in range(NT):
            pb = self.bank()
            pv = pb[:].bitcast(BF16)
            for d in range(2):
                for c in range(2):
                    j = d * 2 + c
                    k.tr(pv[:, j * 128:(j + 1) * 128], kd[d][:, c, t * 128:(t + 1) * 128], self.ident_bf[:], inc=(j == 3))
            k.cp(ktok[:, t, :, :], pv[:, 0:512].re("p (j q) -> p j q", j=4), E=(k.act if t % 2 else k.dve))
        S = [k.sb(f"gla_S{d}", [128, 256], F32) for d in range(2)]
        Sin = [k.sb(f"gla_Sin{d}", [128, 256], F32) for d in range(2)]
        Stmp = [k.sb(f"gla_Stmp{d}", [128, 256], F32) for d in range(2)]
        for d in range(2):
            k.dma(k.sp, Sin[d][:], self.st_gla_d[l, d])
        for step in range(NT):
            for d in range(2):
                t = step if d == 0 else NT - 1 - step
                if step > 0:
                    fcol = t if d == 0 else 8 + t
                    k.ts(Sin[d][:], S[d][:], fl[:, fcol:fcol + 1], ALU.mult)
                pb = self.bank()
                for h in range(4):
                    pair, e = h // 2, h % 2
                    k.mm(pb[e * 64:(e + 1) * 64, pair * 128:(pair + 1) * 128],
                         ktok[:, t, d * 2 + pair, e * 64:(e + 1) * 64], vtok[:, t, h * 128:(h + 1) * 128])
                etv = V(etot[d], etot[d].h[:, :, t].unsqueeze(2).to_broadcast([128, 2, 128]))
                if d == 0:
                    k.cp(sin_bf[0][:, t, :], Sin[0][:], E=k.act)
                    k.tt(Stmp[0][:], Sin[0][:], pb[:, 0:256], ALU.add)
                    k.tt(S[0][:].re("p (c v) -> p c v", c=2), Stmp[0][:].re("p (c v) -> p c v", c=2), etv, ALU.mult)
                else:
                    k.tt(Stmp[1][:].re("p (c v) -> p c v", c=2), Sin[1][:].re("p (c v) -> p c v", c=2), etv, ALU.mult)
                    k.cp(sin_bf[1][:, t, :], Stmp[1][:], E=k.act)
                    k.tt(S[1][:], Stmp[1][:], pb[:, 0:256], ALU.add)
                if (d == 0 and t % 2 == 1) or (d == 1 and t % 2 == 0):
                    k.dma(k.sp, self.ns_gla_d[l, t // 2, d], S[d][:])
        k.pop()
        k.push()
        amat = [[k.sb(f"gla_A{d}_{j}", [128, 4, 128], BF16) for j in range(2)] for d in range(2)]
        masks = (self.maskU, self.maskL)
        for t in range(NT):
            ts_ = slice(t * 128, (t + 1) * 128)
            for d in range(2):
                pas = [self.bank(), self.bank()]
                for h in range(4):
                    pair, e = h // 2, h % 2
                    es = slice(e * 64, (e + 1) * 64)
                    k.mm(pas[e][:, pair * 128:(pair + 1) * 128], kd[d][es, pair, ts_], qd[d][es, pair, ts_])
                mk = masks[d]
                for e in range(2):
                    k.tt(amat[d][t % 2][:, e::2, :], pas[e][:, 0:256].re("p (h q) -> p h q", h=2),
                         V(mk, mk.h[:, :].unsqueeze(1).to_broadcast([128, 2, 128])), ALU.mult)
            pos_ = [self.bank(), self.bank()]
            for h in range(4):
                pair, e = h // 2, h % 2
                es = slice(e * 64, (e + 1) * 64)
                out = pos_[e][:, pair * 128:(pair + 1) * 128]
                vt = vtok[:, t, h * 128:(h + 1) * 128]
                k.mm(out, vt, amat[0][t % 2][:, h, :], start=True, stop=False)
                k.mm(out, vt, amat[1][t % 2][:, h, :], start=False, stop=False)
                k.mm(out, sin_bf[0][es, t, pair * 128:(pair + 1) * 128], qd[0][es, pair, ts_], start=False, stop=False)
                k.mm(out, sin_bf[1][es, t, pair * 128:(pair + 1) * 128], qd[1][es, pair, ts_], start=False, stop=True)
            for e in range(2):
                k.cp(og[:, e::2, ts_], pos_[e][:, 0:256].re("p (h q) -> p h q", h=2), E=(k.act if (t + e) % 2 else k.dve))
        k.pop()
        self.dbg("gla_og", og[:], [128, 4, TOK])
        k.pop()
        gs = k.sb("gla_gs", [128, 4, TOK], BF16)
        yT = k.sb("gla_yT", [128, 4, TOK], BF16)
        for s in range(2):
            wb = self.ws.next()
            for m in range(2):
                for h in range(2):
                    pb = self.proj_fm(wb, m * 128, 128, h)
                    k.actf(gs[:, s * 2 + m, h * 512:(h + 1) * 512], pb[:], AF.Silu)
        k.push()
        sq = [k.sb(f"gla_sq{j}", [128, TOK], BF16) for j in range(2)]
        rstd = [k.sb(f"gla_rstd{j}", [128, TOK], F32) for j in range(2)]
        tmp = [k.sb(f"gla_tmp{j}", [128, TOK], F32) for j in range(2)]
        for h in range(4):
            sqh, rs, tp = sq[h % 2], rstd[h % 2], tmp[h % 2]
            k.actf(sqh[:], og[:, h, :], AF.Square)
            for hf in range(2):
                pb = self.bank()
                k.mm(pb[:], self.ones_bf[:], sqh[:, hf * 512:(hf + 1) * 512])
                k.actf(rs[:, hf * 512:(hf + 1) * 512], pb[:], AF.Ln, scale=1.0 / 128, bias=self.eps6)
            k.actf(rs[:], rs[:], AF.Exp, scale=-0.5)
            k.stt(tp[:], og[:, h, :], self.pp(l, "gla_norm"), rs[:], ALU.mult, ALU.mult)
            k.tt(yT[:, h, :], tp[:], gs[:, h, :], ALU.mult)
        k.pop()
        self.dbg("gla_yT", yT[:], [128, 4, TOK])
        self.branch_out(l, yT, 1)
        k.pop()

    ALPHA = float(np.exp(-0.5))

    def plan_rw(self, l):
        S = self.SEC
        for nm in ("rr", "rv"):
            for c0 in (0, 256):
                self.win_slab(l, S[nm] + c0, 256)
        self.win_slab(l, S["wlr"], 192)
        self.win_slab(l, S["glr"], 128)
        for c0 in (0, 256):
            self.win_slab(l, S["rk"] + c0, 256)
        self.plan_branch_out(l, "w_rw_o", 2)

    def rw_branch(self, l):
        k = self.k
        fl = self.flags
        A = self.ALPHA
        k.push()
        OT = k.sb("rw_OT", [128, 4, TOK], BF16)
        bonus = k.sb("rw_bonus", [128, 4, TOK], BF16)
        sgl = k.sb("rw_sgl", [128, TOK], BF16)
        k.push()
        rT = k.sb("rw_rT", [128, 4, TOK], BF16)
        kapT = k.sb("rw_kapT", [128, 4, TOK], BF16)
        kpT = k.sb("rw_kpT", [128, 4, TOK], BF16)
        nbT = k.sb("rw_nbT", [128, 4, TOK], BF16)
        thT = k.sb("rw_thT", [128, TOK], BF16)
        vtok = k.sb("rw_vtok", [128, NT, 512], BF16)
        w2bf = k.sb("rw_w2bf", [128, 512], BF16)
        k.push()
        a2bf = k.sb("rw_a2bf", [64, 512], BF16)
        k.push()
        w2f = k.sb("rw_w2f", [128, 512], F32)
        a2f = k.sb("rw_a2f", [64, 512], F32)
        k.dma(k.sp, w2f[:], self.rw_w2_d[l].re("d r c -> (d r) c"))
        k.dma(k.sp, a2f[:], self.rw_a2_d[l])
        k.cp(w2bf[:], w2f[:], E=k.pool)
        k.cp(a2bf[:], a2f[:], E=k.pool)
        k.pop()
        kT = k.sb("rw_kT", [128, 2, TOK], BF16)
        vT = k.sb("rw_vT", [128, 4, TOK], BF16)
        aT = k.sb("rw_aT", [128, TOK], BF16)
        wl = k.sb("rw_wl", [128, TOK], BF16)
        alT = k.sb("rw_alT", [64, TOK], BF16)
        glT = k.sb("rw_glT", [128, TOK], BF16)
        hb = k.sb("rw_hb", [128, 4, 258], BF16)
        mt = k.sb("rw_mt", [128, 4, 256], BF16)
        omu = k.sb("rw_omu", [128, 15], F32)
        hmu = k.sb("rw_hmu", [128, 15], F32)
        omka = k.sb("rw_omka", [128, 4], F32)
        k.ts(omu[:], self.pp(l, "rw_mu"), -1.0, ALU.mult, 1.0, ALU.add)
        k.ts(hmu[:], self.pp(l, "rw_mu"), 0.5, ALU.mult)
        k.ts(omka[:], self.pp(l, "rw_ka"), -1.0, ALU.mult, 1.0, ALU.add)
        k.memset(hb[:, :, 0:1], 0.0, E=k.pool)
        k.memset(hb[:, :, 257:258], 0.0, E=k.pool)

        def mix(wb, m0, mc, mucol, dst):
            b = hb
            t = mt
            for h in range(2):
                pb = self.proj_fm(wb, m0, mc, h)
                k.cp(b[0:mc, 2 * h:2 * h + 2, 1:257], pb[0:mc, :].re("p (q t) -> p q t", q=2),
                     E=(k.act if h == 0 else k.dve))
            k.tt(b[0:mc, 1:4, 0], b[0:mc, 0:3, 256], V(fl, fl.h[0:mc, 2:8:2]), ALU.mult)
            k.tt(b[0:mc, 0:3, 257], b[0:mc, 1:4, 1], V(fl, fl.h[0:mc, 9:15:2]), ALU.mult)
            k.tt(t[0:mc], b[0:mc, :, 0:256], b[0:mc, :, 2:258], ALU.add)
            k.ts(t[0:mc], t[0:mc], hmu[0:mc, mucol:mucol + 1], ALU.mult)
            k.stt(dst.re("p (q t) -> p q t", q=4), b[0:mc, :, 1:257], omu[0:mc, mucol:mucol + 1], t[0:mc],
                  ALU.mult, ALU.add)

        for bi, dstT in ((0, rT), (2, vT)):
            for s_ in range(2):
                wb = self.ws.next()
                for m in range(2):
                    c = s_ * 2 + m
                    mix(wb, m * 128, 128, bi * 4 + c, dstT[:, c, :])
        wb = self.ws.next()
        mix(wb, 0, 128, 12, wl[:, :])
        mix(wb, 128, 64, 13, alT[:, :])
        wb = self.ws.next()
        mix(wb, 0, 128, 14, glT[:, :])
        k.actf(thT[:], wl[:], AF.Tanh)
        k.actf(sgl[:], glT[:], AF.Sigmoid)
        f1 = k.sb("rw_f1", [128, TOK], F32)
        f2 = k.sb("rw_f2", [128, TOK], F32)
        b1 = k.sb("rw_b1", [128, TOK], BF16)
        for s_ in range(2):
            wb = self.ws.next()
            for m in range(2):
                mix(wb, m * 128, 128, 4 + s_ * 2 + m, kT[:, m, :])
            for m in range(2):
                c = s_ * 2 + m
                for h in range(2):
                    pb = self.bank()
                    k.mm(pb[:], a2bf[:, c * 128:(c + 1) * 128], alT[:, h * 512:(h + 1) * 512])
                    k.actf(aT[:, h * 512:(h + 1) * 512], pb[:], AF.Sigmoid, bias=self.pp(l, "rw_a0", c, c + 1))
                k.ts(f1[:], kT[:, m, :], self.pp(l, "rw_kk", c, c + 1), ALU.mult)
                k.actf(b1[:], f1[:], AF.Square)
                for h in range(2):
                    pb = self.bank()
                    k.mm(pb[:], self.blockones[:], b1[:, h * 512:(h + 1) * 512])
                    k.ts(f2[:, h * 512:(h + 1) * 512], pb[:], 1e-24, ALU.max)
                k.actf(f2[:], f2[:], AF.Sqrt)
                k.recip(f2[:], f2[:])
                k.tt(kapT[:, c, :], f1[:], f2[:], ALU.mult)
                k.stt(nbT[:, c, :], kapT[:, c, :], -1.0, aT[:], ALU.mult, ALU.mult)
                k.ts(f1[:], aT[:], self.pp(l, "rw_ka", c, c + 1), ALU.mult, omka[:, c:c + 1], ALU.add)
                k.tt(kpT[:, c, :], kT[:, m, :], f1[:], ALU.mult)
                k.stt(b1[:], rT[:, c, :], self.pp(l, "rw_rk", c, c + 1), kpT[:, c, :], ALU.mult, ALU.mult)
                for h in range(2):
                    pb = self.bank()
                    k.mm(pb[:], self.blockones[:], b1[:, h * 512:(h + 1) * 512])
                    k.tt(bonus[:, c, h * 512:(h + 1) * 512], pb[:], vT[:, c, h * 512:(h + 1) * 512], ALU.mult)
        for t in range(NT):
            pb = self.bank()
            pv = pb[:].bitcast(BF16)
            for c in range(4):
                k.tr(pv[:, c * 128:(c + 1) * 128], vT[:, c, t * 128:(t + 1) * 128], self.ident_bf[:], inc=(c == 3))
            k.cp(vtok[:, t, :], pv[:, 0:512], E=(k.act if t % 2 else k.dve))
        k.pop()
        self.dbg("rw_kapT", kapT[:], [128, 4, TOK])
        self.dbg("rw_kpT", kpT[:], [128, 4, TOK])
        self.dbg("rw_rT", rT[:], [128, 4, TOK])
        self.dbg("rw_nbT", nbT[:], [128, 4, TOK])
        k.push()
        sig = k.sb("rw_sig", [128, 4, 128], F32)
        Pc = k.sb("rw_P", [128, 4, 128], F32)
        Cx = k.sb("rw_Cx", [128, 4, 128], F32)
        Ea = Eb = Ec = k.sb("rw_Ee", [128, 4, 128], BF16)
        gam = k.sb("rw_gam", [128, 4], F32)
        RKt = k.sb("rw_RKt", [128, 4, 2, 128], BF16)
        kt = k.sb("rw_kt", [128, 4, 128], BF16)
        nbt = k.sb("rw_nbt", [128, 4, 128], BF16)
        tok = k.sb("rw_tok", [128, 3, 512], BF16)
        M1 = k.sb("rw_M1", [128, 8, 2, 128], BF16)
        M2 = k.sb("rw_M2", [128, 8, 2, 128], BF16)
        XYg = [[k.sb(f"rw_X0_{g}", [128, 4, 128], BF16),
                [k.sb(f"rw_XM{g}", [128, 4, 128], BF16)] * 2,
                [k.sb(f"rw_ZT{g}_{j}", [128, 4, 128], BF16) for j in range(2)],
                [k.sb(f"rw_Tc{g}_{j}", [128, 4, 128], BF16) for j in range(2)]] for g in range(2)]
        TT = k.sb("rw_TT", [128, 8, 128], BF16)
        AV = k.sb("rw_AV", [128, 512], BF16)
        U = k.sb("rw_U", [128, 512], BF16)
        WT = k.sb("rw_WT", [128, 4, 128], BF16)
        Et = U
        S = k.sb("rw_S", [128, 256], F32)
        Sin = k.sb("rw_Sin", [128, 256], F32)
        Sbf = k.sb("rw_Sbf", [128, 256], BF16)
        nev = [0]

        def evac(dst, src):
            E = k.act if nev[0] % 2 else k.dve
            nev[0] += 1
            k.cp(dst, src, E=E)

        for d in range(2):
            m2 = self.mask2[d]
            mx = self.maskSL if d == 0 else self.maskSU
            k.dma(k.sp, Sin[:], self.st_rw_d[l, d])
            for step in range(NT):
                t = step if d == 0 else NT - 1 - step
                ts_ = slice(t * 128, (t + 1) * 128)
                ds_ = slice(d * 64, (d + 1) * 64)
                pb = self.bank()
                for c in range(4):
                    k.mm(pb[:, c * 128:(c + 1) * 128], w2bf[ds_, c * 128:(c + 1) * 128], thT[ds_, ts_])
                for c in range(4):
                    k.actf(sig[:, c, :], pb[:, c * 128:(c + 1) * 128], AF.Sigmoid,
                           bias=self.pp(l, "rw_w0", d * 4 + c, d * 4 + c + 1))
                for c in range(4):
                    k.scan(Pc[:, c, :], self.ones_f[:], sig[:, c, :], 0.0, ALU.mult, ALU.add)
                if d == 0:
                    k.tt(Cx[:], Pc[:], sig[:], ALU.subtract)
                    cin, cex = Pc, Cx
                    k.actf(gam[:], Pc[:, :, 127], AF.Exp, scale=-A)
                else:
                    k.actf(gam[:], Pc[:, :, 127], AF.Exp, scale=-A)
                    k.tt(Cx[:], V(Pc, Pc.h[:, :, 127:128].to_broadcast([128, 4, 128])), Pc[:], ALU.subtract)
                    k.tt(Pc[:], Cx[:], sig[:], ALU.add)
                    cin, cex = Pc, Cx
                k.actf(Ea[:], cin[:], AF.Exp, scale=-A)
                k.tt(RKt[:, :, 0, :], rT[:, :, ts_], Ea[:], ALU.mult)
                k.actf(Eb[:], cex[:], AF.Exp, scale=-A)
                k.tt(RKt[:, :, 1, :], kapT[:, :, ts_], Eb[:], ALU.mult)
                k.actf(Ec[:], cin[:], AF.Exp, scale=A)
                k.tt(kt[:], kpT[:, :, ts_], Ec[:], ALU.mult)
                k.tt(nbt[:], nbT[:, :, ts_], Ec[:], ALU.mult)
                pb = self.bank()
                pv = pb[:].bitcast(BF16)
                for c in range(4):
                    k.tr(pv[:, c * 128:(c + 1) * 128], RKt[:, c, 1, :], self.ident_bf[:], inc=False)
                for c in range(4):
                    k.tr(pv[:, 512 + c * 128:512 + (c + 1) * 128], kt[:, c, :], self.ident_bf[:], inc=(c == 3))
                evac(tok[:, 0:2, :], pv[:, :].re("p (a q) -> p a q", a=2))
                pb = self.bank()
                pv = pb[:].bitcast(BF16)
                for c in range(4):
                    k.tr(pv[:, c * 128:(c + 1) * 128], nbt[:, c, :], self.ident_bf[:], inc=(c == 3))
                evac(tok[:, 2, :], pv[:, 0:512])
                lmx = self.lvlmask[d]
                l1t = self.lvlmask[1 - d]
                Tcs = [None, None]
                for gq in range(2):
                    bA = [self.bank(), self.bank()]
                    bB = [self.bank(), self.bank()]
                    b5 = [self.bank(), self.bank()]
                    for hh in range(4):
                        h = gq * 4 + hh
                        c, e = h // 2, h % 2
                        cc = hh // 2
                        es = slice(e * 64, (e + 1) * 64)
                        cs = slice(cc * 256, cc * 256 + 256)
                        k.mm(bA[e][:, cs], kt[es, c, :], RKt[es, c, :, :])
                        k.mm(bB[e][:, cs], nbt[es, c, :], RKt[es, c, :, :])
                        k.mm(b5[e][:, cc * 128:(cc + 1) * 128], RKt[es, c, 1, :], nbt[es, c, :])
                    m2v = V(m2, m2.h[:, :, :].unsqueeze(1).to_broadcast([128, 2, 2, 128]))
                    X0 = XYg[gq][0]
                    for e in range(2):
                        hs = slice(gq * 4 + e, gq * 4 + 4, 2)
                        k.tt(M1[:, hs, :, :], bA[e][:].re("p (h a t) -> p h a t", h=2, a=2), m2v, ALU.mult)
                        k.tt(M2[:, hs, :, :], bB[e][:].re("p (h a t) -> p h a t", h=2, a=2), m2v, ALU.mult)
                        k.tt(X0[:, e::2, :], b5[e][:, 0:256].re("p (h t) -> p h t", h=2),
                             V(mx, mx.h[:, :].unsqueeze(1).to_broadcast([128, 2, 128])), ALU.mult)
                    Y0 = M2[:, gq * 4:gq * 4 + 4, 1, :]
                    Tc = XYg[gq][3][0]
                    k.tt(Tc[:], Y0, V(l1t, l1t.h[:, 0, :].unsqueeze(1).to_broadcast([128, 4, 128])), ALU.mult)
                    k.tt(Tc[:], Tc[:], V(self.ident_bf, self.ident_bf.h[:, :].unsqueeze(1).to_broadcast([128, 4, 128])),
                         ALU.add)
                    Tcs[gq] = Tc
                for lvl in range(1, 7):
                    XMs, bzs, bts, bps = [], [], [], []
                    for gq in range(2):
                        XM = XYg[gq][1][lvl % 2]
                        k.tt(XM[:], XYg[gq][0][:], V(lmx, lmx.h[:, lvl, :].unsqueeze(1).to_broadcast([128, 4, 128])),
                             ALU.mult, E=k.pool)
                        XMs.append(XM)
                    for gq in range(2):
                        Tc = Tcs[gq]
                        bz = self.bank()
                        for hh in range(4):
                            k.mm(bz[:, hh * 128:(hh + 1) * 128], XMs[gq][:, hh, :], Tc[:, hh, :])
                        bt = self.bank()
                        btv = bt[:].bitcast(BF16)
                        for hh in range(4):
                            k.tr(btv[:, hh * 128:(hh + 1) * 128], Tc[:, hh, :], self.ident_bf[:], inc=(hh == 3))
                        bzs.append(bz)
                        bts.append(btv)
                    for gq in range(2):
                        Zs = XYg[gq][2][0]
                        Ts = XYg[gq][2][1]
                        k.cp(Zs[:], bzs[gq][:].re("p (h t) -> p h t", h=4), E=(k.dve if gq == 0 else k.act))
                        k.cp(Ts[:], bts[gq][:, 0:512].re("p (h t) -> p h t", h=4), E=(k.act if gq == 0 else k.dve))
                    for gq in range(2):
                        Tc = Tcs[gq]
                        Zs = XYg[gq][2][0]
                        Ts = XYg[gq][2][1]
                        bp = self.bank()
                        for hh in range(4):
                            o_ = bp[:, hh * 128:(hh + 1) * 128]
                            k.mm(o_, self.ident_bf[:], Tc[:, hh, :], start=True, stop=False)
                            k.mm(o_, Ts[:, hh, :], Zs[:, hh, :], start=False, stop=True)
                        bps.append(bp)
                    for gq in range(2):
                        E_ = k.dve if gq == 0 else k.act
                        if lvl < 6:
                            Tn = XYg[gq][3][lvl % 2]
                            k.cp(Tn[:], bps[gq][:].re("p (h t) -> p h t", h=4), E=E_)
                            Tcs[gq] = Tn
                        else:
                            k.cp(TT[:, gq * 4:gq * 4 + 4, :], bps[gq][:].re("p (h t) -> p h t", h=4), E=E_)
                pb = self.bank()
                for h in range(8):
                    k.mm(pb[:, h * 64:(h + 1) * 64], M1[:, h, 1, :], vtok[:, t, h * 64:(h + 1) * 64])
                evac(AV[:], pb[:])
                pb = self.bank()
                for h in range(8):
                    k.mm(pb[:, h * 64:(h + 1) * 64], TT[:, h, :], AV[:, h * 64:(h + 1) * 64])
                evac(U[:], pb[:])
                pb = self.bank()
                for h in range(8):
                    c, e = h // 2, h % 2
                    k.mm(pb[e * 64:(e + 1) * 64, c * 128:(c + 1) * 128], tok[:, 0, h * 64:(h + 1) * 64], TT[:, h, :])
                evac(WT[:], pb[:].re("p (c t) -> p c t", c=4))
                if step > 0:
                    fcol = t if d == 0 else 8 + t
                    k.ts(Sin[:], S[:], fl[:, fcol:fcol + 1], ALU.mult)
                k.cp(Sbf[:], Sin[:], E=k.act)
                pbe = [self.bank(), self.bank()]
                for h in range(8):
                    c, e = h // 2, h % 2
                    es = slice(e * 64, (e + 1) * 64)
                    k.mm(pbe[e][:, c * 64:(c + 1) * 64], WT[es, c, :], Sbf[es, c * 64:(c + 1) * 64])
                for e in range(2):
                    k.tt(Et[:].re("p (c e v) -> p c e v", c=4, e=2)[:, :, e, :],
                         pbe[e][:, 0:256].re("p (c v) -> p c v", c=4),
                         U[:].re("p (c e v) -> p c e v", c=4, e=2)[:, :, e, :], ALU.add)
                if self.stop == "R1":
                    for nm, tl_, shp in (("rwd_tok", tok, [128, 3, 512]), ("rwd_E", Et, [128, 512]), ("rwd_U", U, [128, 512]),
                                         ("rwd_TT", TT, [128, 8, 128]), ("rwd_M1", M1, [128, 8, 2, 128]),
                                         ("rwd_M2", M2, [128, 8, 2, 128]), ("rwd_RKt", RKt, [128, 4, 2, 128]),
                                         ("rwd_kt", kt, [128, 4, 128]), ("rwd_nbt", nbt, [128, 4, 128]),
                                         ("rwd_sig", sig, [128, 4, 128]), ("rwd_P", Pc, [128, 4, 128]),
                                         ("rwd_WT", WT, [128, 4, 128]), ("rwd_AV", AV, [128, 512])):
                        self.debug.add(nm)
                        self.dbg(nm, tl_[:], shp)
                    self.chk("R1")
                pos_ = [self.bank(), self.bank()]
                for h in range(8):
                    c, e = h // 2, h % 2
                    es = slice(e * 64, (e + 1) * 64)
                    o_ = pos_[e][es, c * 128:(c + 1) * 128]
                    k.mm(o_, Sbf[es, c * 64:(c + 1) * 64], RKt[es, c, 0, :], start=True, stop=False)
                    k.mm(o_, vtok[:, t, h * 64:(h + 1) * 64], M1[:, h, 0, :], start=False, stop=False)
                    k.mm(o_, Et[:, h * 64:(h + 1) * 64], M2[:, h, 0, :], start=False, stop=True)
                for e in range(2):
                    es = slice(e * 64, (e + 1) * 64)
                    if d == 0:
                        evac(OT[es, :, ts_], pos_[e][es, :].re("p (c t) -> p c t", c=4))
                    else:
                        k.tt(OT[es, :, ts_], pos_[e][es, :].re("p (c t) -> p c t", c=4), OT[es, :, ts_], ALU.add)
                pb = self.bank()
                for h in range(8):
                    c, e = h // 2, h % 2
                    o_ = pb[e * 64:(e + 1) * 64, c * 64:(c + 1) * 64]
                    k.mm(o_, tok[:, 1, h * 64:(h + 1) * 64], vtok[:, t, h * 64:(h + 1) * 64], start=True, stop=False)
                    k.mm(o_, tok[:, 2, h * 64:(h + 1) * 64], Et[:, h * 64:(h + 1) * 64], start=False, stop=True)
                k.tt(S[:], Sin[:], pb[:, 0:256], ALU.add)
                k.tt(S[:].re("p (c v) -> p c v", c=4), S[:].re("p (c v) -> p c v", c=4),
                     V(gam, gam.h[:, :].unsqueeze(2).to_broadcast([128, 4, 64])), ALU.mult)
                if (d == 0 and t % 2 == 1) or (d == 1 and t % 2 == 0):
                    k.dma(k.sp, self.ns_rw_d[l, t // 2, d], S[:])
        k.pop()
        k.pop()
        self.dbg("rw_OT", OT[:], [128, 4, TOK])
        yT = k.sb("rw_yT", [128, 4, TOK], BF16)
        g2f = k.sb("rw_g2f", [128, 512], F32)
        g2bf = k.sb("rw_g2bf", [128, 512], BF16)
        k.dma(k.sp, g2f[:], self.rw_g2_d[l])
        k.cp(g2bf[:], g2f[:], E=k.pool)
        k.push()
        dd = k.sb("rw_dd", [128, TOK], F32)
        sq = k.sb("rw_sq", [128, TOK], BF16)
        rs = k.sb("rw_rs", [128, TOK], F32)
        for c in range(4):
            for h in range(2):
                hs = slice(h * 512, (h + 1) * 512)
                pb = self.bank()
                k.mm(pb[:], self.blockmean[:], OT[:, c, hs])
                k.tt(dd[:, hs], OT[:, c, hs], pb[:], ALU.subtract)
            k.actf(sq[:], dd[:], AF.Square)
            for h in range(2):
                hs = slice(h * 512, (h + 1) * 512)
                pb = self.bank()
                k.mm(pb[:], self.blockmean[:], sq[:, hs])
                k.actf(rs[:, hs], pb[:], AF.Ln, bias=self.epsgn)
            k.actf(rs[:], rs[:], AF.Exp, scale=-0.5)
            k.tt(dd[:], dd[:], rs[:], ALU.mult)
            k.ts(dd[:], dd[:], self.pp(l, "rw_ln_w", c, c + 1), ALU.mult, self.pp(l, "rw_ln_b", c, c + 1), ALU.add)
            k.tt(dd[:], dd[:], bonus[:, c, :], ALU.add)
            for h in range(2):
                hs = slice(h * 512, (h + 1) * 512)
                pb = self.bank()
                k.mm(pb[:], g2bf[:, c * 128:(c + 1) * 128], sgl[:, hs])
                k.tt(yT[:, c, hs], pb[:], dd[:, hs], ALU.mult)
        k.pop()
        self.dbg("rw_yT", yT[:], [128, 4, TOK])
        self.branch_out(l, yT, 2)
        k.pop()

    def group_rmsnorm(self, src, dst, nchunk, inv_n, eps, gains):
        k = self.k
        k.push()
        sq = k.sb("grn_sq", [128, nchunk, TOK], BF16)
        rstd = k.sb("grn_rstd", [128, TOK], F32)
        for c in range(nchunk):
            k.actf(sq[:, c, :], src[:, c, :], AF.Square)
        for h in range(2):
            pb = self.bank()
            for c in range(nchunk):
                k.mm(pb[:], self.ones_bf[:], sq[:, c, h * 512:(h + 1) * 512], start=(c == 0), stop=(c == nchunk - 1))
            k.actf(rstd[:, h * 512:(h + 1) * 512], pb[:], AF.Ln, scale=inv_n, bias=eps)
        k.actf(rstd[:], rstd[:], AF.Exp, scale=-0.5)
        for c in range(nchunk):
            k.stt(dst[:, c, :], src[:, c, :], gains[c], rstd[:], ALU.mult, ALU.mult)
        k.pop()


_PROG = {}


def get_prog(debug=()):
    key = tuple(sorted(debug))
    if key not in _PROG:
        p1 = Prog(debug)
        needed = p1.k.needed
        if os.environ.get("KALLINC", ""):
            needed = {E.name: set(range(1, E.cnt + 2)) for E in p1.k.engs}
        _PROG[key] = Prog(debug, needed)
    return _PROG[key]


def make_in_maps(inp):
    inp = {k_: np.asarray(v) for k_, v in inp.items()}
    pp = np.stack([pack_params(inp, l) for l in range(DEPTH)], axis=0)
    gp = _cm(inp["final_norm"])
    ti = np.arange(128)[:, None]
    si = np.arange(128)[None, :]
    lm = np.zeros((2, 128, 7, 128), np.float32)
    for lvl in range(7):
        m_ = 1 << lvl
        msk = ((ti // (2 * m_)) == (si // (2 * m_))) & ((ti % (2 * m_)) >= m_) & ((si % (2 * m_)) < m_)
        lm[0, :, lvl, :] = msk
        lm[1, :, lvl, :] = msk.T
    shared = {"pp": pp, "gp": gp, "lvlmask": lm}
    shared["gla_gk_w"] = np.ascontiguousarray(inp["gla_gk_w"], dtype=np.float32)
    for nm in ("rw_w2", "rw_a2", "rw_g2"):
        shared[nm] = np.ascontiguousarray(inp[nm], dtype=np.float32)
    for name in ["w_ada", "ffn_gate", "ffn_up", "ffn_down", "w_in", "w_ssd_o", "w_gla_o", "w_rw_o", "w_out"]:
        shared[name] = np.ascontiguousarray(inp[name], dtype=np.float32)
    maps = []
    for core in range(8):
        m = dict(shared)
        flags = np.zeros((128, 32), np.float32)
        if core < 4:
            x = inp["x_prompt"][4 * core:4 * core + 4].reshape(TOK, D)
            cond = inp["c_ctx"]
            cf = np.array([0, 1, 0, 1, 0, 1, 0, 1], np.float32)
            cb = np.array([1, 0, 1, 0, 1, 0, 1, 0], np.float32)
            posf = 0.0
        else:
            x = inp["x_sample"][core - 4]
            cond = inp["c"][core - 4]
            cf = np.array([0, 1, 1, 1, 1, 1, 1, 1], np.float32)
            cb = np.array([1, 1, 1, 1, 1, 1, 1, 0], np.float32)
            posf = 1.0
        flags[:, 0:8] = cf[None]
        flags[:, 8:16] = cb[None]
        flags[:, 16] = posf
        if core < 4:
            st_ssd = np.zeros((DEPTH, 2, 128, 256), np.float32)
        else:
            ss = inp["state_ssd"][core - 4]
            st_ssd = np.ascontiguousarray(
                ss.reshape(DEPTH, 2, 2, 4, 64, 64).transpose(0, 1, 2, 5, 3, 4).reshape(DEPTH, 2, 128, 256))
        m["st_ssd"] = st_ssd
        if core < 4:
            st_gla = np.zeros((DEPTH, 2, 128, 256), np.float32)
        else:
            sg = inp["state_gla"][core - 4]
            st_gla = np.ascontiguousarray(
                sg.reshape(DEPTH, 2, 2, 2, 64, 128).transpose(0, 1, 3, 4, 2, 5).reshape(DEPTH, 2, 128, 256))
        m["st_gla"] = st_gla
        if core < 4:
            st_rw = np.zeros((DEPTH, 2, 128, 256), np.float32)
        else:
            sr = inp["state_rwkv"][core - 4]
            st_rw = np.ascontiguousarray(
                sr.reshape(DEPTH, 2, 4, 2, 64, 64).transpose(0, 1, 3, 5, 2, 4).reshape(DEPTH, 2, 128, 256))
        m["st_rw"] = st_rw
        m["xT"] = np.ascontiguousarray(x.T, dtype=np.float32)
        m["cond"] = _cm(cond)
        m["flags"] = flags
        maps.append(m)
    return maps


def run(inp, debug=(), trace=False):
    prog = get_prog(debug)
    maps = make_in_maps(inp)
    res = run_bass_kernel_spmd(prog.k.nc, maps, core_ids=list(range(8)), trace=trace)
    return prog, res


def kernel(**inputs):
    prog, res = run(inputs)
    r = res.results
    y_prompt = np.zeros((16, 256, D), np.float32)
    y_sample = np.zeros((4, 1024, D), np.float32)
    for core in range(8):
        y = np.ascontiguousarray(r[core]["yT"].T)
        if core < 4:
            y_prompt[4 * core:4 * core + 4] = y.reshape(4, 256, D)
        else:
            y_sample[core - 4] = y
    ns_ssd = np.zeros((16, DEPTH, 2, 8, 64, 64), np.float32)
    for core in range(4):
        raw = r[core]["ns_ssd"]
        v = raw.reshape(DEPTH, 4, 2, 2, 64, 4, 64).transpose(1, 0, 2, 3, 5, 6, 4)
        ns_ssd[4 * core:4 * core + 4] = v.reshape(4, DEPTH, 2, 8, 64, 64)
    ns_gla = np.zeros((16, DEPTH, 2, 4, 64, 128), np.float32)
    for core in range(4):
        raw = r[core]["ns_gla"]
        v = raw.reshape(DEPTH, 4, 2, 2, 64, 2, 128).transpose(1, 0, 2, 5, 3, 4, 6)
        ns_gla[4 * core:4 * core + 4] = v.reshape(4, DEPTH, 2, 4, 64, 128)
    ns_rw = np.zeros((16, DEPTH, 2, 8, 64, 64), np.float32)
    for core in range(4):
        raw = r[core]["ns_rw"]
        v = raw.reshape(DEPTH, 4, 2, 2, 64, 4, 64).transpose(1, 0, 2, 5, 3, 6, 4)
        ns_rw[4 * core:4 * core + 4] = v.reshape(4, DEPTH, 2, 8, 64, 64)
    return (y_prompt, y_sample, ns_ssd, ns_gla, ns_rw)
```

```python
import os
import numpy as np
from contextlib import ExitStack
import concourse.bass as bass
import concourse.mybir as mybir
from concourse.bass_utils import run_bass_kernel_spmd

F32 = mybir.dt.float32
BF16 = mybir.dt.bfloat16
I32 = mybir.dt.int32
ALU = mybir.AluOpType
AF = mybir.ActivationFunctionType

D = 1024
TOK = 1024
NT = 8
DFF = 2816
FC = 22
DEPTH = 2
NIN = 7792
PI = float(np.pi)


class V:
    __slots__ = ("t", "ap")

    def __init__(self, t, ap):
        self.t = t
        self.ap = ap

    def __getitem__(self, idx):
        return V(self.t, self.ap[idx])

    def re(self, s, **kw):
        return V(self.t, self.ap.rearrange(s, **kw))

    def bc(self, shape):
        return V(self.t, self.ap.to_broadcast(list(shape)))

    def bitcast(self, dt):
        return V(self.t, self.ap.bitcast(dt))

    @property
    def shape(self):
        return self.ap.shape


class Tile:
    __slots__ = ("h", "name", "w", "r", "dsem", "dcnt", "psum")

    def __init__(self, h, name, r0=None):
        self.h = h
        self.name = name
        self.psum = False
        self.w = None
        self.r = dict(r0) if r0 else {}
        self.dsem = None
        self.dcnt = 0

    def __getitem__(self, idx):
        return V(self, self.h[idx])


class Eng:
    def __init__(self, name, e):
        self.name = name
        self.e = e
        self.sem = None
        self.cnt = 0
        self.val = 0
        self.ord2val = {}
        self.seen = {}


class K:
    def __init__(self, needed=None):
        self.record = needed is None
        self.needed = {} if needed is None else needed
        self.nc = bass.Bass("TRN2", target_bir_lowering=False)
        self.es = ExitStack()
        self.scopes = [self.es]
        nc = self.nc
        self.pe = Eng("pe", nc.tensor)
        self.act = Eng("act", nc.scalar)
        self.dve = Eng("dve", nc.vector)
        self.pool = Eng("pool", nc.gpsimd)
        self.sp = Eng("sp", nc.sync)
        self.engs = [self.pe, self.act, self.dve, self.pool, self.sp]
        self.nsem = 0
        self.dsem_free = []
        for E in self.engs:
            self._newsem(E)
        self.out_waits = []
        self.nincs = 0
        self.all_dsem = {}
        self.ntile = 0
        self.barrier = {}
        self.scope_tiles = [[]]
        self.ninst = 0

    def _sem(self, name):
        self.nsem += 1
        return self.es.enter_context(self.nc.semaphore(name))

    def _newsem(self, E):
        E.sem = self._sem(f"s_{E.name}_{self.nsem}")
        E.cnt = 0

    def dram(self, name, shape, dt, kind):
        return V(None, self.nc.dram_tensor(name, list(shape), dt, kind=kind).ap())

    def sb(self, name, shape, dt=F32):
        self.ntile += 1
        h = self.scopes[-1].enter_context(self.nc.sbuf_tensor(f"{name}_{self.ntile}", list(shape), dt))
        t = Tile(h, name, self.barrier)
        self.scope_tiles[-1].append(t)
        return t

    def ps(self, name, shape, dt=F32):
        self.ntile += 1
        h = self.es.enter_context(self.nc.psum_tensor(f"{name}_{self.ntile}", list(shape), dt))
        t = Tile(h, name)
        t.psum = True
        return t

    def push(self):
        es = ExitStack()
        self.scopes.append(es)
        self.scope_tiles.append([])

    def pop(self):
        for t in self.scope_tiles.pop():
            if t.w is not None:
                s, v = t.w
                if self.barrier.get(s, 0) < v:
                    self.barrier[s] = v
            for s, v in t.r.items():
                if self.barrier.get(s, 0) < v:
                    self.barrier[s] = v
            if t.dsem is not None:
                self.dsem_free.append((t.dsem, t.dcnt))
        self.scopes.pop().close()

    def _wait(self, E, key, n):
        if E.seen.get(key, 0) >= n:
            return
        E.seen[key] = n
        if isinstance(key, Eng):
            if self.record:
                self.needed.setdefault(key.name, set()).add(n)
                return
            E.e.wait_ge(key.sem, key.ord2val[n])
        else:
            E.e.wait_ge(key, n)

    def _deps(self, E, reads, writes):
        waits = {}

        def need(s, v):
            if waits.get(s, 0) < v:
                waits[s] = v

        for t in reads:
            if t.w is not None:
                need(*t.w)
            if t.psum:
                for s, v in t.r.items():
                    if s is not E:
                        need(s, v)
        strict = (E is not self.pe) and (E is self.pool or os.environ.get("KRELAX", "") == "")
        for t in writes:
            if t.w is not None and (strict or t.w[0] is not E):
                need(*t.w)
            for s, v in t.r.items():
                if strict or s is not E:
                    need(s, v)
        for s, v in waits.items():
            self._wait(E, s, v)

    def emit(self, E, fn, reads, writes, inc=True):
        reads = [x.t for x in reads if x is not None and x.t is not None]
        writes = [x.t for x in writes if x is not None and x.t is not None]
        self._deps(E, reads, writes)
        ins = fn()
        self.ninst += 1
        if inc:
            E.cnt += 1
            cid = E.cnt
            if (not self.record) and cid in self.needed.get(E.name, ()):
                E.val += 1
                ins.then_inc(E.sem, 1)
                E.ord2val[cid] = E.val
                self.nincs += 1
        else:
            cid = E.cnt + 1
        for t in reads:
            t.r[E] = cid
        for t in writes:
            t.w = (E, cid)
            t.r = {}
        return ins

    def dma(self, Q, out, in_, **kw):
        reads = [in_.t] if in_.t is not None else []
        writes = [out.t] if out.t is not None else []
        self._deps(Q, reads, writes)
        tl = out.t if out.t is not None else in_.t
        if tl.dsem is None:
            if self.dsem_free:
                tl.dsem, tl.dcnt = self.dsem_free.pop()
                self._wait(Q, tl.dsem, tl.dcnt)
            else:
                tl.dsem = self._sem(f"d_{tl.name}_{self.nsem}")
        ins = Q.e.dma_start(out=out.ap, in_=in_.ap, **kw)
        ins.then_inc(tl.dsem, 16)
        self.ninst += 1
        tl.dcnt += 16
        self.all_dsem[tl.dsem] = tl.dcnt
        if out.t is not None:
            out.t.w = (tl.dsem, tl.dcnt)
            out.t.r = {}
        if in_.t is not None:
            in_.t.r[tl.dsem] = tl.dcnt
        if out.t is None:
            self.out_waits.append((tl.dsem, tl.dcnt))
        return ins

    def finish(self):
        for s, v in self.all_dsem.items():
            self._wait(self.sp, s, v)
        for E in self.engs:
            if E is not self.sp and E.cnt > 0:
                self._wait(self.sp, E, E.cnt)

    def mm(self, out, lhsT, rhs, start=True, stop=True, inc=None, **kw):
        if inc is None:
            inc = stop
        return self.emit(self.pe, lambda: self.nc.tensor.matmul(out.ap, lhsT.ap, rhs.ap, start=start, stop=stop, **kw),
                         [lhsT, rhs], [out], inc=inc)

    def tr(self, out, in_, ident, inc=True):
        return self.emit(self.pe, lambda: self.nc.tensor.transpose(out.ap, in_.ap, ident.ap), [in_, ident], [out],
                         inc=inc)

    def actf(self, out, in_, func, bias=None, scale=None, accum=None):
        kw = {}
        rd = [in_]
        if bias is not None:
            if isinstance(bias, V):
                kw["bias"] = bias.ap
                rd.append(bias)
            else:
                kw["bias"] = float(bias)
        if scale is not None:
            if isinstance(scale, V):
                kw["scale"] = scale.ap
                rd.append(scale)
            else:
                kw["scale"] = float(scale)
        wr = [out]
        if accum is not None:
            kw["accum_out"] = accum.ap
            wr.append(accum)
        return self.emit(self.act, lambda: self.nc.scalar.activation(out.ap, in_.ap, func, **kw), rd, wr)

    def _ve(self, E):
        return E if E is not None else self.dve

    def tt(self, out, a, b, op, E=None):
        E = self._ve(E)
        return self.emit(E, lambda: E.e.tensor_tensor(out.ap, a.ap, b.ap, op), [a, b], [out])

    def ts(self, out, a, s1, op0, s2=None, op1=None, E=None):
        E = self._ve(E)
        rd = [a]
        a1 = s1
        if isinstance(s1, V):
            rd.append(s1)
            a1 = s1.ap
        a2 = s2
        if isinstance(s2, V):
            rd.append(s2)
            a2 = s2.ap
        kw = {}
        if op1 is not None:
            kw["op1"] = op1
        return self.emit(E, lambda: E.e.tensor_scalar(out.ap, a.ap, a1, a2, op0, **kw), rd, [out])

    def stt(self, out, a, s, b, op0, op1):
        E = self.dve
        rd = [a, b]
        a1 = s
        if isinstance(s, V):
            rd.append(s)
            a1 = s.ap
        return self.emit(E, lambda: E.e.scalar_tensor_tensor(out.ap, a.ap, a1, b.ap, op0, op1), rd, [out])

    def cp(self, out, in_, E=None):
        E = self._ve(E)
        if E is self.act:
            return self.emit(E, lambda: self.nc.scalar.copy(out.ap, in_.ap), [in_], [out])
        return self.emit(E, lambda: E.e.tensor_copy(out.ap, in_.ap), [in_], [out])

    def memset(self, out, val, E=None):
        E = self._ve(E)
        return self.emit(E, lambda: E.e.memset(out.ap, val), [], [out])

    def scan(self, out, d0, d1, init, op0, op1):
        rd = [d0, d1]
        i = init
        if isinstance(init, V):
            rd.append(init)
            i = init.ap
        return self.emit(self.dve, lambda: self.nc.vector.tensor_tensor_scan(out.ap, d0.ap, d1.ap, i, op0, op1), rd,
                         [out])

    def recip(self, out, in_):
        return self.emit(self.dve, lambda: self.nc.vector.reciprocal(out.ap, in_.ap), [in_], [out])

    def iota(self, out, pattern, base, cm):
        return self.emit(self.pool, lambda: self.nc.gpsimd.iota(out.ap, pattern, base=base, channel_multiplier=cm,
                                                               allow_small_or_imprecise_dtypes=True), [], [out])

    def asel(self, out, in_, pattern, op, fill, base, cm):
        return self.emit(self.pool, lambda: self.nc.gpsimd.affine_select(out.ap, in_.ap, pattern, op, fill, base=base,
                                                                        channel_multiplier=cm), [in_], [out])


PP_SPEC = [("norm_g0", 8), ("norm_g1", 8), ("norm_g2", 8), ("b_ada", 72),
           ("conv_w0", 6), ("conv_w1", 6), ("conv_w2", 6), ("conv_b", 6), ("ssd_norm", 4),
           ("ssd_D0", 4), ("ssd_D1", 4), ("dt_bias", 16), ("A_log", 16), ("gk_b", 4), ("gla_norm", 1),
           ("rw_mu", 15), ("rw_w0", 8), ("rw_a0", 4), ("rw_kk", 4), ("rw_ka", 4), ("rw_rk", 4),
           ("rw_ln_w", 4), ("rw_ln_b", 4)]
PP_OFF = {}
_o = 0
for _n, _c in PP_SPEC:
    PP_OFF[_n] = (_o, _c)
    _o += _c
NPP = _o


def _cm(vec):
    vec = np.asarray(vec, np.float32).reshape(-1)
    n = vec.shape[0] // 128
    return np.ascontiguousarray(vec.reshape(n, 128).T)


def pack_params(inp, l):
    pp = np.zeros((128, NPP), np.float32)

    def put(name, arr):
        o, c = PP_OFF[name]
        assert arr.shape == (128, c), (name, arr.shape, c)
        pp[:, o:o + c] = arr

    for i in range(3):
        put(f"norm_g{i}", _cm(inp["norm_g"][l, i]))
    put("b_ada", _cm(inp["b_ada"][l]))
    for i in range(3):
        put(f"conv_w{i}", _cm(inp["ssd_conv_w"][l, i]))
    put("conv_b", _cm(inp["ssd_conv_b"][l]))
    put("ssd_norm", _cm(inp["ssd_norm"][l]))
    hd = (2 * np.arange(4)[None, :] + (np.arange(128)[:, None] // 64))
    put("ssd_D0", inp["ssd_D"][l, 0][hd])
    put("ssd_D1", inp["ssd_D"][l, 1][hd])
    put("dt_bias", np.broadcast_to(inp["ssd_dt_bias"][l].reshape(1, 16), (128, 16)))
    put("A_log", np.broadcast_to(inp["ssd_A_log"][l].reshape(1, 16), (128, 16)))
    put("gk_b", np.concatenate([_cm(inp["gla_gk_b"][l, 0]), _cm(inp["gla_gk_b"][l, 1])], axis=1))
    put("gla_norm", _cm(inp["gla_norm"][l]))
    mu = inp["rw_mu"][l]
    mucols = np.zeros((128, 15), np.float32)
    mucols[:, 0:13] = _cm(mu[0:1664])
    mucols[0:64, 13] = mu[1664:1728]
    mucols[:, 14] = mu[1728:1856]
    put("rw_mu", mucols)
    put("rw_w0", np.concatenate([_cm(inp["rw_w0"][l, 0]), _cm(inp["rw_w0"][l, 1])], axis=1))
    put("rw_a0", _cm(inp["rw_a0"][l]))
    put("rw_kk", _cm(inp["rw_kk"][l]))
    put("rw_ka", _cm(inp["rw_ka"][l]))
    put("rw_rk", _cm(inp["rw_rk"][l]))
    put("rw_ln_w", _cm(inp["rw_ln_w"][l]))
    put("rw_ln_b", _cm(inp["rw_ln_b"][l]))
    return pp


class WS:
    NST = 2
    NBF = 3
    CAST_PAT = ["dve", "act", "dve", "pool", "dve", "act", "dve", "act"]

    def __init__(self, k):
        self.k = k
        self.st = [k.sb(f"wst{i}", [128, 2048], F32) for i in range(self.NST)]
        self.bf = [k.sb(f"wbf{i}", [128, 2048], BF16) for i in range(self.NBF)]
        self.slabs = []
        self.base_j = 0
        self.limit = None
        self.old = {}
        self.nd = 0
        self.ncast = 0
        self.nget = 0

    def add(self, dview, kcs, ncols):
        assert kcs * ncols <= 2048
        self.slabs.append((dview, kcs, ncols))

    def _stbuf(self, j):
        return self.st[(j - self.base_j) % self.NST] if j >= self.base_j else self.old[j]

    def rebase(self, nst):
        self.old = {j: self._stbuf(j) for j in range(self.ncast, self.nd)}
        assert all(b in self.st[:nst] for b in self.old.values()) or not self.old, "in-flight slab in a dropped buffer"
        self.st = self.st[:nst]
        self.NST = nst
        self.limit = None
        self.base_j = self.nd

    def _dma(self, j):
        dview, kcs, ncols = self.slabs[j]
        st = self._stbuf(j)
        self.k.dma(self.k.sp, st[:, 0:kcs * ncols].re("p (a b) -> p a b", a=kcs), dview)

    def _cast(self, j):
        dview, kcs, ncols = self.slabs[j]
        n = kcs * ncols
        E = getattr(self.k, self.CAST_PAT[j % len(self.CAST_PAT)])
        self.k.cp(self.bf[j % self.NBF][:, 0:n], self._stbuf(j)[:, 0:n], E=E)

    def next(self):
        j = self.nget
        self.nget += 1
        n = len(self.slabs) if self.limit is None else min(len(self.slabs), self.limit)
        while self.nd < min(n, j + self.NST):
            self._dma(self.nd)
            self.nd += 1
        while self.ncast < min(n, j + 2):
            self._cast(self.ncast)
            self.ncast += 1
        dview, kcs, ncols = self.slabs[j]
        return self.bf[j % self.NBF][:, 0:kcs * ncols].re("p (a b) -> p a b", a=kcs)


def wview(w2d, k0, k1, c0, c1):
    return w2d.re("(kc p) n -> p kc n", p=128)[:, k0:k1, c0:c1]


class StopBuild(Exception):
    pass


class Prog:
    def __init__(self, debug=(), needed=None):
        import os
        self.stop = os.environ.get("KSTOP", "")
        self.debug = set(debug)
        self.k = K(needed)
        self.dbg_out = {}
        self.build()

    def pp(self, l, name, c0=0, c1=None):
        o, c = PP_OFF[name]
        if c1 is None:
            c1 = c
        return self.PP[l][:, o + c0:o + c1]

    def chk(self, name):
        if self.stop == name:
            raise StopBuild(name)

    def bank(self):
        b = self.banks[self.nbank % 8]
        self.nbank += 1
        return b

    def dbg(self, name, view, shape):
        if name not in self.debug or name in self.dbg_out:
            return
        d = self.k.dram("dbg_" + name, shape, view.ap.dtype, "ExternalOutput")
        self.k.dma(self.k.sp, d, view)
        self.dbg_out[name] = shape

    def build(self):
        k = self.k
        nc = k.nc
        self.xT_d = k.dram("xT", [D, TOK], F32, "ExternalInput")
        self.cond_d = k.dram("cond", [128, 8], F32, "ExternalInput")
        self.flags_d = k.dram("flags", [128, 32], F32, "ExternalInput")
        self.gp_d = k.dram("gp", [128, 8], F32, "ExternalInput")
        self.pp_d = k.dram("pp", [DEPTH, 128, NPP], F32, "ExternalInput")
        self.w = {}
        for name, shape in [("w_ada", [DEPTH, D, 9 * D]), ("ffn_gate", [DEPTH, 2, D, DFF]),
                            ("ffn_up", [DEPTH, 2, D, DFF]), ("ffn_down", [DEPTH, 2, DFF, D]),
                            ("w_in", [DEPTH, D, NIN]), ("w_ssd_o", [DEPTH, 512, D]), ("w_gla_o", [DEPTH, 512, D]),
                            ("w_rw_o", [DEPTH, 512, D]), ("w_out", [DEPTH, D, D])]:
            self.w[name] = k.dram(name, shape, F32, "ExternalInput")
        self.yT_d = k.dram("yT", [D, TOK], F32, "ExternalOutput")
        self.lvlmask_d = k.dram("lvlmask", [2, 128, 7, 128], F32, "ExternalInput")
        self.st_ssd_d = k.dram("st_ssd", [DEPTH, 2, 128, 256], F32, "ExternalInput")
        self.ns_ssd_d = k.dram("ns_ssd", [DEPTH, 4, 2, 128, 256], F32, "ExternalOutput")
        self.st_gla_d = k.dram("st_gla", [DEPTH, 2, 128, 256], F32, "ExternalInput")
        self.ns_gla_d = k.dram("ns_gla", [DEPTH, 4, 2, 128, 256], F32, "ExternalOutput")
        self.gkw_d = k.dram("gla_gk_w", [DEPTH, 2, 16, 256], F32, "ExternalInput")
        self.st_rw_d = k.dram("st_rw", [DEPTH, 2, 128, 256], F32, "ExternalInput")
        self.ns_rw_d = k.dram("ns_rw", [DEPTH, 4, 2, 128, 256], F32, "ExternalOutput")
        self.rw_w2_d = k.dram("rw_w2", [DEPTH, 2, 64, 512], F32, "ExternalInput")
        self.rw_a2_d = k.dram("rw_a2", [DEPTH, 64, 512], F32, "ExternalInput")
        self.rw_g2_d = k.dram("rw_g2", [DEPTH, 128, 512], F32, "ExternalInput")

        self.xT = k.sb("xT", [128, 8, TOK], F32)
        self.hT = k.sb("hT", [128, 8, TOK], BF16)
        self.flags = k.sb("flags", [128, 32], F32)
        self.gp = k.sb("gp", [128, 8], F32)
        self.PP = [k.sb(f"pp{l}", [128, NPP], F32) for l in range(DEPTH)]
        self.ada = [k.sb(f"ada{l}", [128, 72], F32) for l in range(DEPTH)]
        self.modA = [k.sb(f"modA{l}", [128, 24], F32) for l in range(DEPTH)]
        self.gate = [k.sb(f"gate{l}", [128, 24], F32) for l in range(DEPTH)]
        self.ones_bf = k.sb("ones_bf", [128, 128], BF16)
        self.ident_bf = k.sb("ident_bf", [128, 128], BF16)
        self.ident_f = k.sb("ident_f", [128, 128], F32)
        self.cst = k.sb("cst", [128, 8], F32)
        self.eps6 = self.cst[:, 0:1]
        self.epsgn = self.cst[:, 1:2]
        self.ones_f = k.sb("ones_f", [128, 128], F32)
        self.maskU = k.sb("maskU", [128, 128], F32)
        self.maskL = k.sb("maskL", [128, 128], F32)
        self.maskSU = k.sb("maskSU", [128, 128], F32)
        self.maskSL = k.sb("maskSL", [128, 128], F32)
        self.mask2 = [k.sb(f"mask2_{d}", [128, 2, 128], F32) for d in range(2)]
        self.lvlmask = [k.sb(f"lvlmask{d}", [128, 7, 128], BF16) for d in range(2)]
        self.blockones = k.sb("blockones", [128, 128], BF16)
        self.blockmean = k.sb("blockmean", [128, 128], BF16)
        self.banks = [k.ps(f"bank{i}", [128, 512], F32) for i in range(8)]
        self.nbank = 0
        self.ws = WS(k)

        k.dma(k.sp, self.flags[:], self.flags_d)
        k.dma(k.sp, self.gp[:], self.gp_d)
        for l in range(DEPTH):
            k.dma(k.sp, self.PP[l][:], self.pp_d[l])
        xv = self.xT_d.re("(c p) t -> p c t", p=128)
        for c in range(8):
            k.dma(k.sp, self.xT[:, c, :], xv[:, c, :])

        k.push()
        lmf = k.sb("lvlmask_f", [128, 7, 128], F32)
        for d in range(2):
            k.dma(k.sp, lmf[:], self.lvlmask_d[d])
            k.cp(self.lvlmask[d][:], lmf[:], E=k.pool)
        k.pop()
        k.memset(self.cst[:, 0:1], 1e-6, E=k.pool)
        k.memset(self.cst[:, 1:2], 64e-5, E=k.pool)
        k.memset(self.cst[:, 2:3], 1.0, E=k.pool)
        k.memset(self.ones_bf[:], 1.0, E=k.pool)
        k.memset(self.ones_f[:], 1.0, E=k.pool)
        k.memset(self.maskU[:], 1.0, E=k.pool)
        k.asel(self.maskU[:], self.maskU[:], [[1, 128]], ALU.is_ge, 0.0, 0, -1)
        k.memset(self.maskSU[:], 1.0, E=k.pool)
        k.asel(self.maskSU[:], self.maskSU[:], [[1, 128]], ALU.is_ge, 0.0, -1, -1)
        k.memset(self.maskSL[:], 1.0, E=k.pool)
        k.asel(self.maskSL[:], self.maskSL[:], [[-1, 128]], ALU.is_ge, 0.0, -1, 1)
        k.memset(self.blockones[:], 0.0, E=k.pool)
        k.memset(self.blockmean[:], 0.0, E=k.pool)
        for e in range(2):
            k.memset(self.blockones[e * 64:(e + 1) * 64, e * 64:(e + 1) * 64], 1.0, E=k.pool)
            k.memset(self.blockmean[e * 64:(e + 1) * 64, e * 64:(e + 1) * 64], 1.0 / 64, E=k.pool)
        k.memset(self.maskL[:], 1.0, E=k.pool)
        k.asel(self.maskL[:], self.maskL[:], [[-1, 128]], ALU.is_ge, 0.0, 0, 1)
        k.cp(self.mask2[0][:, 0, :], self.maskU[:], E=k.pool)
        k.cp(self.mask2[0][:, 1, :], self.maskSU[:], E=k.pool)
        k.cp(self.mask2[1][:, 0, :], self.maskL[:], E=k.pool)
        k.cp(self.mask2[1][:, 1, :], self.maskSL[:], E=k.pool)
        k.memset(self.ident_f[:], 1.0, E=k.pool)
        k.asel(self.ident_f[:], self.ident_f[:], [[-1, 128]], ALU.is_equal, 0.0, 0, 1)
        k.cp(self.ident_bf[:], self.ident_f[:], E=k.pool)

        for l in range(DEPTH):
            self.plan_ada(l)
        for l in range(DEPTH):
            self.plan_ffn(l, 0)
            self.plan_mixer(l)
            self.plan_ffn(l, 1)

        self.pos_embed()
        self.compute_ada()
        try:
            for l in range(DEPTH):
                self.ffn(l, 0)
                self.dbg(f"x1_{l}", self.xT[:], [128, 8, TOK])
                self.mixer(l)
                self.dbg(f"x2_{l}", self.xT[:], [128, 8, TOK])
                self.ffn(l, 1)
        except StopBuild:
            while len(k.scopes) > 1:
                k.pop()
        self.final_norm()
        k.finish()

    def pos_embed(self):
        k = self.k
        k.push()
        idx = k.sb("pe_idx", [128, 2], F32)
        om = k.sb("pe_om", [128, 2], F32)
        pos = k.sb("pe_pos", [128, 80], F32)
        arg = k.sb("pe_arg", [128, 4, 80], F32)
        ki = k.sb("pe_ki", [128, 4, 80], I32)
        kf = k.sb("pe_kf", [128, 4, 80], F32)
        gt = k.sb("pe_gt", [128, 4, 80], F32)
        emb = k.sb("pe_emb", [128, 4, 80], F32)
        k.iota(idx[:], [[128, 2]], 0, 1)
        k.actf(om[:], idx[:], AF.Exp, scale=-float(np.log(10000.0)) / 256.0)
        k.iota(pos[:, 0:16], [[1, 16]], 0, 0)
        k.iota(pos[:, 16:80], [[1, 64]], 0, 0)
        for j in range(4):
            ph = 0.0 if j < 2 else PI / 2
            k.ts(arg[:, j, :], pos[:], om[:, (j % 2):(j % 2) + 1], ALU.mult, ph, ALU.add)
        k.ts(kf[:], arg[:], 1.0 / (2 * PI), ALU.mult)
        k.cp(ki[:], kf[:])
        k.cp(kf[:], ki[:])
        k.stt(arg[:], kf[:], -2 * PI, arg[:], ALU.mult, ALU.add)
        k.ts(gt[:], arg[:], PI, ALU.is_gt, -2 * PI, ALU.mult)
        k.tt(arg[:], arg[:], gt[:], ALU.add)
        k.ts(gt[:], arg[:], -PI, ALU.is_lt, 2 * PI, ALU.mult)
        k.tt(arg[:], arg[:], gt[:], ALU.add)
        k.actf(emb[:], arg[:], AF.Sin)
        k.ts(emb[:], emb[:], self.flags[:, 16:17], ALU.mult)
        self.dbg("emb", emb[:], [128, 4, 80])
        for fc in range(8):
            xv = self.xT[:, fc, :].re("p (r c) -> p r c", c=64)
            if fc < 4:
                ev = V(emb, emb.h[:, fc, 0:16].unsqueeze(2).to_broadcast([128, 16, 64]))
            else:
                ev = V(emb, emb.h[:, fc - 4, 16:80].unsqueeze(1).to_broadcast([128, 16, 64]))
            k.tt(xv, xv, ev, ALU.add)
        k.pop()

    def plan_ada(self, l):
        for s in range(36):
            self.ws.add(wview(self.w["w_ada"][l], 0, 8, s * 256, (s + 1) * 256), 8, 256)

    def compute_ada(self):
        k = self.k
        k.push()
        cond = k.sb("cond", [128, 8], F32)
        sg = k.sb("cond_sg", [128, 8], F32)
        scond = k.sb("scond", [128, 8], BF16)
        k.dma(k.sp, cond[:], self.cond_d)
        extra = [k.sb(f"wst_x{i}", [128, 2048], F32) for i in range(4)]
        self.ws.st = self.ws.st + extra
        self.ws.NST = len(self.ws.st)
        self.ws.limit = 36 * DEPTH
        k.actf(sg[:], cond[:], AF.Sigmoid)
        k.tt(scond[:], cond[:], sg[:], ALU.mult)
        for l in range(DEPTH):
            pb = self.bank()
            for s in range(36):
                wb = self.ws.next()
                for m in range(2):
                    j = s * 2 + m
                    for kc in range(8):
                        k.mm(pb[:, j:j + 1], wb[:, kc, m * 128:(m + 1) * 128], scond[:, kc:kc + 1],
                             start=(kc == 0), stop=(kc == 7))
            k.tt(self.ada[l][:], pb[:, 0:72], self.pp(l, "b_ada"), ALU.add)
            for i in range(3):
                k.stt(self.modA[l][:, i * 8:(i + 1) * 8], self.ada[l][:, (3 * i + 1) * 8:(3 * i + 2) * 8], 1.0,
                      self.pp(l, f"norm_g{i}"), ALU.add, ALU.mult)
                k.ts(self.gate[l][:, i * 8:(i + 1) * 8], self.ada[l][:, (3 * i + 2) * 8:(3 * i + 3) * 8],
                     0.5 if i != 1 else 1.0, ALU.mult)
            self.dbg(f"ada{l}", self.ada[l][:], [128, 72])
        self.ws.rebase(2)
        k.pop()

    def rstd_of_x(self, rstd):
        k = self.k
        k.push()
        sq = k.sb("sq", [128, 8, TOK], BF16)
        for c in range(8):
            k.actf(sq[:, c, :], self.xT[:, c, :], AF.Square)
        for h in range(2):
            pb = self.bank()
            for c in range(8):
                k.mm(pb[:], self.ones_bf[:], sq[:, c, h * 512:(h + 1) * 512], start=(c == 0), stop=(c == 7))
            k.actf(rstd[:, h * 512:(h + 1) * 512], pb[:], AF.Ln, scale=1.0 / D, bias=self.eps6)
        k.actf(rstd[:], rstd[:], AF.Exp, scale=-0.5)
        k.pop()

    def norm_mod(self, l, i):
        k = self.k
        k.push()
        rstd = k.sb("rstd", [128, TOK], F32)
        tmp = [k.sb(f"nm_tmp{j}", [128, TOK], F32) for j in range(2)]
        self.rstd_of_x(rstd)
        for c in range(8):
            t = tmp[c % 2]
            k.tt(t[:], self.xT[:, c, :], rstd[:], ALU.mult)
            k.actf(self.hT[:, c, :], t[:], AF.Identity, scale=self.modA[l][:, i * 8 + c:i * 8 + c + 1],
                   bias=self.ada[l][:, 3 * i * 8 + c:3 * i * 8 + c + 1])
        k.pop()

    def final_norm(self):
        k = self.k
        k.push()
        rstd = k.sb("rstd", [128, TOK], F32)
        tmp = [k.sb(f"fn_tmp{j}", [128, TOK], F32) for j in range(2)]
        self.rstd_of_x(rstd)
        yv = self.yT_d.re("(c p) t -> p c t", p=128)
        for c in range(8):
            t = tmp[c % 2]
            k.stt(t[:], self.xT[:, c, :], self.gp[:, c:c + 1], rstd[:], ALU.mult, ALU.mult)
            k.dma(k.sp, yv[:, c, :], t[:])
        k.pop()

    def plan_ffn(self, l, which):
        wg = self.w["ffn_gate"][l, which]
        wu = self.w["ffn_up"][l, which]
        wd = self.w["ffn_down"][l, which]
        for s in range(11):
            self.ws.add(wview(wg, 0, 8, s * 256, (s + 1) * 256), 8, 256)
            self.ws.add(wview(wu, 0, 8, s * 256, (s + 1) * 256), 8, 256)
        for s in range(4):
            for (k0, k1) in ((0, 8), (8, 16), (16, 22)):
                self.ws.add(wview(wd, k0, k1, s * 256, (s + 1) * 256), k1 - k0, 256)

    def ffn(self, l, which):
        k = self.k
        i = 0 if which == 0 else 2
        self.norm_mod(l, i)
        k.push()
        actT = k.sb("actT", [128, FC, TOK], BF16)
        sg = [k.sb(f"ffn_sg{j}", [128, 512], F32) for j in range(2)]
        nsg = 0
        for s in range(11):
            wg = self.ws.next()
            wu = self.ws.next()
            for m in range(2):
                fcb = s * 2 + m
                for h in range(2):
                    pg = self.bank()
                    pu = self.bank()
                    for kc in range(8):
                        k.mm(pg[:], wg[:, kc, m * 128:(m + 1) * 128], self.hT[:, kc, h * 512:(h + 1) * 512],
                             start=(kc == 0), stop=(kc == 7))
                    for kc in range(8):
                        k.mm(pu[:], wu[:, kc, m * 128:(m + 1) * 128], self.hT[:, kc, h * 512:(h + 1) * 512],
                             start=(kc == 0), stop=(kc == 7))
                    t = sg[nsg % 2]
                    nsg += 1
                    k.actf(t[:], pg[:], AF.Silu)
                    k.tt(actT[:, fcb, h * 512:(h + 1) * 512], pu[:], t[:], ALU.mult)
        gcol = self.gate[l]
        for s in range(4):
            pbs = [[self.bank() for h in range(2)] for m in range(2)]
            for ksub, (k0, k1) in enumerate(((0, 8), (8, 16), (16, 22))):
                wd = self.ws.next()
                for m in range(2):
                    for h in range(2):
                        for kc in range(k0, k1):
                            k.mm(pbs[m][h][:], wd[:, kc - k0, m * 128:(m + 1) * 128],
                                 actT[:, kc, h * 512:(h + 1) * 512], start=(kc == 0), stop=(kc == FC - 1),
                                 inc=(kc == k1 - 1))
            for m in range(2):
                c = s * 2 + m
                for h in range(2):
                    xs = self.xT[:, c, h * 512:(h + 1) * 512]
                    k.stt(xs, pbs[m][h][:], gcol[:, i * 8 + c:i * 8 + c + 1], xs, ALU.mult, ALU.add)
        k.pop()


    SEC = dict(z=0, xbc=512, dt=1280, q=1296, k=1552, v=1808, g=2320, lr=2832, rr=2864, rk=3376, rv=3888,
               wlr=4400, alr=4528, glr=4592, gate=4720)

    def win_slab(self, l, c0, ncols):
        self.ws.add(wview(self.w["w_in"][l], 0, 8, c0, c0 + ncols), 8, ncols)

    def plan_mixer(self, l):
        self.plan_ssd(l)
        self.plan_gla(l)
        self.plan_rw(l)
        self.plan_out(l, "w_out")

    def plan_out(self, l, name):
        w = self.w[name][l]
        if name == "w_out":
            for s in range(4):
                self.ws.add(wview(w, 0, 8, s * 256, (s + 1) * 256), 8, 256)
        else:
            for s in range(2):
                self.ws.add(wview(w, 0, 4, s * 512, (s + 1) * 512), 4, 512)

    def proj_fm(self, wb, m0, mcols, h):
        k = self.k
        pb = self.bank()
        for kc in range(8):
            k.mm(pb[0:mcols, :], wb[:, kc, m0:m0 + mcols], self.hT[:, kc, h * 512:(h + 1) * 512],
                 start=(kc == 0), stop=(kc == 7))
        return pb

    def mixer(self, l):
        k = self.k
        self.norm_mod(l, 1)
        k.push()
        self.merged = k.sb("merged", [128, 8, TOK], BF16)
        self.ssd_branch(l)
        self.dbg(f"mg0_{l}", self.merged[:], [128, 8, TOK])
        self.chk("F")
        self.gla_branch(l)
        self.dbg(f"mg1_{l}", self.merged[:], [128, 8, TOK])
        self.chk("G")
        self.rw_branch(l)
        self.dbg(f"mg2_{l}", self.merged[:], [128, 8, TOK])
        self.chk("H")
        mbf = self.merged
        gcol = self.gate[l]
        for s in range(4):
            wb = self.ws.next()
            for m in range(2):
                c = s * 2 + m
                for h in range(2):
                    pb = self.bank()
                    for kc in range(8):
                        k.mm(pb[:], wb[:, kc, m * 128:(m + 1) * 128], mbf[:, kc, h * 512:(h + 1) * 512],
                             start=(kc == 0), stop=(kc == 7))
                    xs = self.xT[:, c, h * 512:(h + 1) * 512]
                    k.stt(xs, pb[:], gcol[:, 8 + c:8 + c + 1], xs, ALU.mult, ALU.add)
        k.pop()

    def plan_branch_out(self, l, name, b):
        w = self.w[name][l]
        for s in range(4):
            self.ws.add(wview(w, 0, 4, s * 256, (s + 1) * 256), 4, 256)
            self.win_slab(l, self.SEC["gate"] + b * 1024 + s * 256, 256)

    def branch_out(self, l, yT, b):
        k = self.k
        k.push()
        sgt = [k.sb(f"bo_sg{j}", [128, 512], F32) for j in range(2)]
        tmp = [k.sb(f"bo_tmp{j}", [128, 512], F32) for j in range(2)]
        n = 0
        for s in range(4):
            wo = self.ws.next()
            wg = self.ws.next()
            for m in range(2):
                c = s * 2 + m
                for h in range(2):
                    po = self.bank()
                    for kc in range(4):
                        k.mm(po[:], wo[:, kc, m * 128:(m + 1) * 128],
                             yT[:, kc, h * 512:(h + 1) * 512], start=(kc == 0), stop=(kc == 3))
                    pg = self.proj_fm(wg, m * 128, 128, h)
                    sg = sgt[n % 2]
                    tp = tmp[n % 2]
                    n += 1
                    k.actf(sg[:], pg[:], AF.Sigmoid)
                    mv = self.merged[:, c, h * 512:(h + 1) * 512]
                    if b == 0:
                        k.tt(mv, po[:], sg[:], ALU.mult)
                    else:
                        k.tt(tp[:], po[:], sg[:], ALU.mult)
                        k.tt(mv, mv, tp[:], ALU.add, E=k.pool)
        k.pop()

    def plan_ssd(self, l):
        S = self.SEC
        for c0 in (0, 256):
            self.win_slab(l, S["z"] + c0, 256)
        for c0 in (0, 256, 512):
            self.win_slab(l, S["xbc"] + c0, 256)
        self.win_slab(l, S["dt"], 16)
        self.plan_branch_out(l, "w_ssd_o", 0)

    def ssd_branch(self, l):
        k = self.k
        fl = self.flags
        k.push()
        zs = k.sb("ssd_zs", [128, 4, TOK], BF16)
        xc = k.sb("ssd_xc", [128, 6, TOK], BF16)
        xbt = k.sb("ssd_xbt", [128, NT, 640], BF16)
        dt = k.sb("ssd_dt", [128, NT, 16], F32)
        lndt = k.sb("ssd_lndt", [128, NT, 16], F32)
        a = k.sb("ssd_a", [128, NT, 16], F32)
        cs = k.sb("ssd_cs", [128, NT, 16], F32)
        aneg = k.sb("ssd_aneg", [128, 16], F32)
        dsum = k.sb("ssd_dsum", [128, 4], F32)
        ygT = k.sb("ssd_yg", [128, 4, TOK], BF16)
        sin_bf = [k.sb(f"ssd_sin{d}", [128, NT, 256], BF16) for d in range(2)]
        for s in range(2):
            wb = self.ws.next()
            for m in range(2):
                for h in range(2):
                    pb = self.proj_fm(wb, m * 128, 128, h)
                    k.actf(zs[:, s * 2 + m, h * 512:(h + 1) * 512], pb[:], AF.Silu)
        k.push()
        xp = k.sb("ssd_xp", [128, 6, 4, 258], BF16)
        acc = [k.sb(f"ssd_acc{j}", [128, 4, 256], F32) for j in range(2)]
        k.memset(xp[:, :, :, 0:1], 0.0, E=k.pool)
        k.memset(xp[:, :, :, 257:258], 0.0, E=k.pool)
        for s in range(3):
            wb = self.ws.next()
            for m in range(2):
                for h in range(2):
                    pb = self.proj_fm(wb, m * 128, 128, h)
                    k.cp(xp[:, s * 2 + m, 2 * h:2 * h + 2, 1:257], pb[:].re("p (q t) -> p q t", q=2),
                         E=(k.act if h == 0 else k.dve))
        cfv = V(fl, fl.h[:, 2:8:2].unsqueeze(1).to_broadcast([128, 6, 3]))
        cbv = V(fl, fl.h[:, 9:15:2].unsqueeze(1).to_broadcast([128, 6, 3]))
        k.tt(xp[:, :, 1:4, 0], xp[:, :, 0:3, 256], cfv, ALU.mult)
        k.tt(xp[:, :, 0:3, 257], xp[:, :, 1:4, 1], cbv, ALU.mult)
        for c in range(6):
            t = acc[c % 2]
            k.ts(t[:], xp[:, c, :, 1:257], self.pp(l, "conv_w1", c, c + 1), ALU.mult,
                 self.pp(l, "conv_b", c, c + 1), ALU.add)
            k.stt(t[:], xp[:, c, :, 0:256], self.pp(l, "conv_w0", c, c + 1), t[:], ALU.mult, ALU.add)
            k.stt(t[:], xp[:, c, :, 2:258], self.pp(l, "conv_w2", c, c + 1), t[:], ALU.mult, ALU.add)
            k.actf(xc[:, c, :].re("p (q t) -> p q t", q=4), t[:], AF.Silu)
        k.pop()
        self.dbg("ssd_xc", xc[:], [128, 6, TOK])
        self.chk("A")
        for t in range(NT):
            pb = self.bank()
            pv = pb[:].bitcast(BF16)
            for c in range(5):
                k.tr(pv[:, c * 128:(c + 1) * 128], xc[:, c, t * 128:(t + 1) * 128], self.ident_bf[:], inc=(c == 4))
            k.cp(xbt[:, t, :], pv[:, 0:640], E=(k.act if t % 2 else k.dve))
        self.chk("B")
        wb = self.ws.next()
        pb = self.bank()
        for t in range(NT):
            for kc in range(8):
                k.mm(pb[:, t * 16:(t + 1) * 16], self.hT[:, kc, t * 128:(t + 1) * 128], wb[:, kc, 0:16],
                     start=(kc == 0), stop=(kc == 7))
        k.tt(dt[:], pb[:, 0:128].re("p (t c) -> p t c", t=NT),
             V(self.PP[l], self.pp(l, "dt_bias").ap.unsqueeze(1).to_broadcast([128, NT, 16])), ALU.add)
        k.actf(dt[:], dt[:], AF.Exp)
        k.ts(lndt[:], dt[:], -0.5, ALU.mult, 1.0, ALU.add)
        k.tt(lndt[:], lndt[:], dt[:], ALU.mult)
        k.actf(dt[:], dt[:], AF.Ln, bias=self.cst[:, 2:3])
        k.tt(dt[:], dt[:], lndt[:], ALU.max)
        k.actf(lndt[:], dt[:], AF.Ln)
        k.actf(aneg[:], self.pp(l, "A_log"), AF.Exp)
        k.ts(aneg[:], aneg[:], -1.0, ALU.mult)
        k.tt(a[:], dt[:], V(aneg, aneg.h[:, :].unsqueeze(1).to_broadcast([128, NT, 16])), ALU.mult)
        k.tt(dsum[:], self.pp(l, "ssd_D0"), self.pp(l, "ssd_D1"), ALU.add)
        self.dbg("ssd_dt", dt[:], [128, NT, 16])
        self.chk("B1")
        pb = self.bank()
        for t in range(NT):
            k.mm(pb[:, t * 16:t * 16 + 8], self.maskU[:], a[:, t, 0:8])
            k.mm(pb[:, t * 16 + 8:t * 16 + 16], self.maskL[:], a[:, t, 8:16])
        k.cp(cs[:], pb[:, 0:128].re("p (t c) -> p t c", t=NT))
        self.chk("B2")
        carg = k.sb("ssd_carg", [128, NT, 16], F32)
        k.tt(carg[:], lndt[:], cs[:], ALU.subtract)
        tot = k.sb("ssd_tot", [128, NT, 16], F32)
        wdec = k.sb("ssd_wdec", [128, NT, 16], F32)
        dech = [k.sb(f"ssd_dech{d}", [128, NT, 4], F32) for d in range(2)]
        pb = self.bank()
        k.mm(pb[:, 0:128], self.ones_f[:], a[:].re("p t c -> p (t c)"))
        k.cp(tot[:], pb[:, 0:128].re("p (t c) -> p t c", t=NT))
        self.chk("B3")
        import os
        kvar = os.environ.get("KVAR", "")
        if kvar != "nowdec":
            k.tt(wdec[:], tot[:], carg[:], ALU.add)
            k.actf(wdec[:], wdec[:], AF.Exp)
        if kvar != "nodech":
            for d in range(2):
                for g in range(2):
                    ps_ = slice(g * 64, (g + 1) * 64)
                    k.actf(dech[d][ps_, :, :], tot[ps_, :, d * 8 + g * 4:d * 8 + g * 4 + 4], AF.Exp)
        self.chk("C")
        k.push()
        S = [k.sb(f"ssd_S{d}", [128, 256], F32) for d in range(2)]
        Sin = [k.sb(f"ssd_Sin{d}", [128, 256], F32) for d in range(2)]
        Stmp = [k.sb(f"ssd_Stmp{d}", [128, 256], F32) for d in range(2)]
        xdec = [[k.sb(f"ssd_xdec{d}_{j}", [128, 512], BF16) for j in range(2)] for d in range(2)]
        for d in range(2):
            k.dma(k.sp, Sin[d][:], self.st_ssd_d[l, d])
        for step in range(NT):
            for d in range(2):
                t = step if d == 0 else NT - 1 - step
                if step > 0:
                    fcol = t if d == 0 else 8 + t
                    k.ts(Sin[d][:], S[d][:], fl[:, fcol:fcol + 1], ALU.mult)
                k.cp(sin_bf[d][:, t, :], Sin[d][:], E=k.act)
                xd = xdec[d][step % 2]
                k.tt(xd[:].re("p (h q) -> p h q", h=8), xbt[:, t, 0:512].re("p (h q) -> p h q", h=8),
                     V(wdec, wdec.h[:, t, d * 8:(d + 1) * 8].unsqueeze(2).to_broadcast([128, 8, 64])), ALU.mult)
                pb = self.bank()
                for g in range(2):
                    k.mm(pb[g * 64:(g + 1) * 64, 0:256], xbt[:, t, 512 + g * 64:512 + (g + 1) * 64],
                         xd[:, g * 256:(g + 1) * 256])
                k.tt(Stmp[d][:].re("p (j q) -> p j q", j=4), Sin[d][:].re("p (j q) -> p j q", j=4),
                     V(dech[d], dech[d].h[:, t, :].unsqueeze(2).to_broadcast([128, 4, 64])), ALU.mult)
                k.tt(S[d][:], Stmp[d][:], pb[:, 0:256], ALU.add)
                if (d == 0 and t % 2 == 1) or (d == 1 and t % 2 == 0):
                    k.dma(k.sp, self.ns_ssd_d[l, t // 2, d], S[d][:])
        k.pop()
        self.chk("D")
        k.push()
        MT = [k.sb(f"ssd_MT{j}", [128, 8, 128], BF16) for j in range(2)]
        cdec = [[k.sb(f"ssd_cdec{d}_{j}", [128, 4, 128], BF16) for j in range(2)] for d in range(2)]
        am = k.sb("ssd_am", [128, 8, 128], F32)
        ambf = k.sb("ssd_ambf", [128, 8, 128], BF16)
        dm = [k.sb(f"ssd_dm{j}", [128, 8, 128], F32) for j in range(2)]
        er = [k.sb(f"ssd_er{j}", [128, 8, 128], BF16) for j in range(2)]
        ytmp = [k.sb(f"ssd_ytmp{j}", [128, 4, 128], F32) for j in range(2)]
        if "ssd_yraw" in self.debug:
            self.yraw = k.sb("ssd_yraw", [128, 4, TOK], F32)
        masks = (self.maskU, self.maskL)
        for t in range(NT):
            for d in range(2):
                mk = masks[d]
                k.tt(am[:], V(a, a.h[:, t, d * 8:(d + 1) * 8].unsqueeze(2).to_broadcast([128, 8, 128])),
                     V(mk, mk.h[:, :].unsqueeze(1).to_broadcast([128, 8, 128])), ALU.mult,
                     E=(k.dve if os.environ.get("KVAR", "") == "amdve" else k.pool))
                dmt = dm[d]
                ert = er[d]
                if os.environ.get("KVAR", "") == "rowbf":
                    k.cp(ambf[:], am[:])
                for hh in range(2):
                    pr = self.bank()
                    if os.environ.get("KVAR", "") == "rowbf":
                        k.mm(pr[:], self.ones_bf[:], ambf[:, hh * 4:(hh + 1) * 4, :])
                    else:
                        k.mm(pr[:], self.ones_f[:], am[:, hh * 4:(hh + 1) * 4, :])
                    rv = pr[:].re("p (h l) -> p h l", h=4)
                    k.tt(dmt[:, hh * 4:(hh + 1) * 4, :], rv,
                         V(carg, carg.h[:, t, d * 8 + hh * 4:d * 8 + hh * 4 + 4].unsqueeze(2).to_broadcast([128, 4, 128])),
                         ALU.add)
                    k.actf(ert[:, hh * 4:(hh + 1) * 4, :], rv, AF.Exp)
                if t == 0 and d == 0:
                    self.chk("E1b")
                k.ts(dmt[:], dmt[:], 30.0, ALU.min)
                if t == 0 and d == 0:
                    self.chk("E1c")
                k.actf(dmt[:], dmt[:], AF.Exp)
                if t == 0 and d == 0:
                    self.chk("E1d")
                k.tt(dmt[:], dmt[:], V(mk, mk.h[:, :].unsqueeze(1).to_broadcast([128, 8, 128])), ALU.mult)
                for g in range(2):
                    ps_ = slice(g * 64, (g + 1) * 64)
                    k.tt(cdec[d][t % 2][ps_, :, :], ert[ps_, g * 4:(g + 1) * 4, :],
                         V(xc, xc.h[ps_, 5, t * 128:(t + 1) * 128].unsqueeze(1).to_broadcast([64, 4, 128])), ALU.mult)
            if t == 0:
                self.chk("E1")
            k.tt(dm[0][:], dm[0][:], dm[1][:], ALU.add)
            pgs = [self.bank(), self.bank()]
            for g in range(2):
                ps_ = slice(g * 64, (g + 1) * 64)
                k.mm(pgs[g][:, 0:128], xc[ps_, 4, t * 128:(t + 1) * 128], xc[ps_, 5, t * 128:(t + 1) * 128])
            mt = MT[t % 2]
            for g in range(2):
                k.tt(mt[:, g * 4:(g + 1) * 4, :], dm[0][:, g * 4:(g + 1) * 4, :],
                     V(pgs[g], pgs[g].h[:, 0:128].unsqueeze(1).to_broadcast([128, 4, 128])), ALU.mult)
            if t == 0:
                self.chk("E2")
            pys = [self.bank(), self.bank()]
            for h in range(8):
                pair, e = h // 2, h % 2
                g, j = h // 4, h % 4
                out = pys[g][e * 64:(e + 1) * 64, (pair % 2) * 128:(pair % 2 + 1) * 128]
                gs = slice(g * 64, (g + 1) * 64)
                k.mm(out, xbt[:, t, h * 64:(h + 1) * 64], mt[:, h, :], start=True, stop=False)
                k.mm(out, sin_bf[0][gs, t, j * 64:(j + 1) * 64], cdec[0][t % 2][gs, j, :], start=False, stop=False)
                k.mm(out, sin_bf[1][gs, t, j * 64:(j + 1) * 64], cdec[1][t % 2][gs, j, :], start=False, stop=True)
            if t == 0:
                self.chk("E3")
            if t == 1:
                self.chk("E4")
            yt = ytmp[t % 2]
            k.tt(yt[:], xc[:, 0:4, t * 128:(t + 1) * 128],
                 V(dsum, dsum.h[:, :].unsqueeze(2).to_broadcast([128, 4, 128])), ALU.mult, E=k.pool)
            for g in range(2):
                k.tt(yt[:, g * 2:g * 2 + 2, :], pys[g][:, 0:256].re("p (c l) -> p c l", c=2), yt[:, g * 2:g * 2 + 2, :],
                     ALU.add)
            if "ssd_yraw" in self.debug:
                k.cp(self.yraw[:, :, t * 128:(t + 1) * 128], yt[:])
            k.tt(ygT[:, :, t * 128:(t + 1) * 128], yt[:], zs[:, :, t * 128:(t + 1) * 128], ALU.mult)
        if "ssd_yraw" in self.debug:
            self.dbg("ssd_yraw", self.yraw[:], [128, 4, TOK])
        k.pop()
        self.chk("E")
        yT = k.sb("ssd_yT", [128, 4, TOK], BF16)
        self.group_rmsnorm(ygT, yT, 4, 1.0 / 512, self.eps6, [self.pp(l, "ssd_norm", c, c + 1) for c in range(4)])
        self.dbg("ssd_yT", yT[:], [128, 4, TOK])
        self.branch_out(l, yT, 0)
        k.pop()

    def plan_gla(self, l):
        S = self.SEC
        self.win_slab(l, S["q"], 256)
        self.win_slab(l, S["k"], 256)
        self.win_slab(l, S["lr"], 32)
        for c0 in (0, 256):
            self.win_slab(l, S["v"] + c0, 256)
        for c0 in (0, 256):
            self.win_slab(l, S["g"] + c0, 256)
        self.plan_branch_out(l, "w_gla_o", 1)

    def gla_branch(self, l):
        k = self.k
        fl = self.flags
        k.push()
        og = k.sb("gla_og", [128, 4, TOK], F32)
        k.push()
        qd = [k.sb(f"gla_qd{d}", [128, 2, TOK], BF16) for d in range(2)]
        kd = [k.sb(f"gla_kd{d}", [128, 2, TOK], BF16) for d in range(2)]
        vtok = k.sb("gla_vtok", [128, NT, 512], BF16)
        sin_bf = [k.sb(f"gla_sin{d}", [128, NT, 256], BF16) for d in range(2)]
        etot = [k.sb(f"gla_etot{d}", [128, 2, NT], F32) for d in range(2)]
        gkw = k.sb("gla_gkw", [16, 2, 256], F32)
        gkw_bf = k.sb("gla_gkwbf", [16, 2, 256], BF16)
        lr_bf = [k.sb(f"gla_lr{d}", [16, TOK], BF16) for d in range(2)]
        nb = k.sb("gla_nb", [128, 4], F32)
        k.dma(k.sp, gkw[:], self.gkw_d[l].re("d r c -> r d c"))
        k.cp(gkw_bf[:], gkw[:], E=k.pool)
        k.ts(nb[:], self.pp(l, "gk_b"), -1.0, ALU.mult)
        k.push()
        self.rmask = k.sb("rmask", [128, TOK], F32)
        k.memset(self.rmask[:], 1.0, E=k.pool)
        k.memset(self.rmask[:, 0::128], 0.0, E=k.pool)
        qT = k.sb("gla_qT", [128, 2, TOK], BF16)
        kT = k.sb("gla_kT", [128, 2, TOK], BF16)
        T1 = k.sb("gla_T1", [128, 2, TOK], F32)
        T2 = k.sb("gla_T2", [128, 2, TOK], F32)
        T3 = k.sb("gla_T3", [128, 2, TOK], F32)
        wb = self.ws.next()
        for m in range(2):
            for h in range(2):
                pb = self.proj_fm(wb, m * 128, 128, h)
                k.actf(qT[:, m, h * 512:(h + 1) * 512], pb[:], AF.Copy, scale=0.125)
        wb = self.ws.next()
        for m in range(2):
            for h in range(2):
                pb = self.proj_fm(wb, m * 128, 128, h)
                k.cp(kT[:, m, h * 512:(h + 1) * 512], pb[:])
        wb = self.ws.next()
        for d in range(2):
            for h in range(2):
                pb = self.proj_fm(wb, d * 16, 16, h)
                k.cp(lr_bf[d][:, h * 512:(h + 1) * 512], pb[0:16, :], E=k.act)
        for d in range(2):
            for c in range(2):
                for h in range(2):
                    pb = self.bank()
                    k.mm(pb[:], gkw_bf[:, d, c * 128:(c + 1) * 128], lr_bf[d][:, h * 512:(h + 1) * 512])
                    k.actf(T1[:, c, h * 512:(h + 1) * 512], pb[:], AF.Exp, scale=-1.0, bias=nb[:, d * 2 + c:d * 2 + c + 1])
            k.actf(T1[:], T1[:], AF.Ln, bias=self.cst[:, 2:3])
            for c in range(2):
                k.scan(T2[:, c, :], self.rmask[:], T1[:, c, :], 0.0, ALU.mult, ALU.add)
            if d == 0:
                k.actf(T3[:], T2[:], AF.Exp, scale=-1.0 / 16)
                k.tt(qd[0][:], qT[:], T3[:], ALU.mult)
                k.cp(etot[0][:], T3[:, :, 127::128])
                k.actf(T3[:], T2[:], AF.Exp, scale=1.0 / 16)
                k.tt(kd[0][:], kT[:], T3[:], ALU.mult)
            else:
                k.actf(etot[1][:], T2[:, :, 127::128], AF.Exp, scale=-1.0 / 16)
                k.tt(T2[:], T2[:], T1[:], ALU.subtract)
                k.actf(T3[:], T2[:], AF.Exp, scale=1.0 / 16)
                k.tt(qd[1][:], qT[:], T3[:], ALU.mult)
                k.actf(T3[:], T2[:], AF.Exp, scale=-1.0 / 16)
                k.tt(kd[1][:], kT[:], T3[:], ALU.mult)
        k.pop()
        for s in range(2):
            wb = self.ws.next()
            for t in range(NT):
                pb = self.bank()
                for kc in range(8):
                    k.mm(pb[:, 0:256], self.hT[:, kc, t * 128:(t + 1) * 128], wb[:, kc, :], start=(kc == 0), stop=(kc == 7))
                k.cp(vtok[:, t, s * 256:(s + 1) * 256], pb[:, 0:256], E=(k.act if t % 2 else k.dve))
        k.push()
        ktok = k.sb("gla_ktok", [128, NT, 4, 128], BF16)
        for t in range(NT):
            pb = self.bank()
            pv = pb[:].bitcast(BF16)
            for d in range(2):
                for c in range(2):
                    j = d * 2 + c
                    k.tr(pv[:, j * 128:(j + 1) * 128], kd[d][:, c, t * 128:(t + 1) * 128], self.ident_bf[:], inc=(j == 3))
            k.cp(ktok[:, t, :, :], pv[:, 0:512].re("p (j q) -> p j q", j=4), E=(k.act if t % 2 else k.dve))
        S = [k.sb(f"gla_S{d}", [128, 256], F32) for d in range(2)]
        Sin = [k.sb(f"gla_Sin{d}", [128, 256], F32) for d in range(2)]
        Stmp = [k.sb(f"gla_Stmp{d}", [128, 256], F32) for d in range(2)]
        for d in range(2):
            k.dma(k.sp, Sin[d][:], self.st_gla_d[l, d])
        for step in range(NT):
            for d in range(2):
                t = step if d == 0 else NT - 1 - step
                if step > 0:
                    fcol = t if d == 0 else 8 + t
                    k.ts(Sin[d][:], S[d][:], fl[:, fcol:fcol + 1], ALU.mult)
                pb = self.bank()
                for h in range(4):
                    pair, e = h // 2, h % 2
                    k.mm(pb[e * 64:(e + 1) * 64, pair * 128:(pair + 1) * 128],
                         ktok[:, t, d * 2 + pair, e * 64:(e + 1) * 64], vtok[:, t, h * 128:(h + 1) * 128])
                etv = V(etot[d], etot[d].h[:, :, t].unsqueeze(2).to_broadcast([128, 2, 128]))
                if d == 0:
                    k.cp(sin_bf[0][:, t, :], Sin[0][:], E=k.act)
                    k.tt(Stmp[0][:], Sin[0][:], pb[:, 0:256], ALU.add)
                    k.tt(S[0][:].re("p (c v) -> p c v", c=2), Stmp[0][:].re("p (c v) -> p c v", c=2), etv, ALU.mult)
                else:
                    k.tt(Stmp[1][:].re("p (c v) -> p c v", c=2), Sin[1][:].re("p (c v) -> p c v", c=2), etv, ALU.mult)
                    k.cp(sin_bf[1][:, t, :], Stmp[1][:], E=k.act)
                    k.tt(S[1][:], Stmp[1][:], pb[:, 0:256], ALU.add)
                if (d == 0 and t % 2 == 1) or (d == 1 and t % 2 == 0):
                    k.dma(k.sp, self.ns_gla_d[l, t // 2, d], S[d][:])
        k.pop()
        k.push()
        amat = [[k.sb(f"gla_A{d}_{j}", [128, 4, 128], BF16) for j in range(2)] for d in range(2)]
        masks = (self.maskU, self.maskL)
        for t in range(NT):
            ts_ = slice(t * 128, (t + 1) * 128)
            for d in range(2):
                pas = [self.bank(), self.bank()]
                for h in range(4):
                    pair, e = h // 2, h % 2
                    es = slice(e * 64, (e + 1) * 64)
                    k.mm(pas[e][:, pair * 128:(pair + 1) * 128], kd[d][es, pair, ts_], qd[d][es, pair, ts_])
                mk = masks[d]
                for e in range(2):
                    k.tt(amat[d][t % 2][:, e::2, :], pas[e][:, 0:256].re("p (h q) -> p h q", h=2),
                         V(mk, mk.h[:, :].unsqueeze(1).to_broadcast([128, 2, 128])), ALU.mult)
            pos_ = [self.bank(), self.bank()]
            for h in range(4):
                pair, e = h // 2, h % 2
                es = slice(e * 64, (e + 1) * 64)
                out = pos_[e][:, pair * 128:(pair + 1) * 128]
                vt = vtok[:, t, h * 128:(h + 1) * 128]
                k.mm(out, vt, amat[0][t % 2][:, h, :], start=True, stop=False)
                k.mm(out, vt, amat[1][t % 2][:, h, :], start=False, stop=False)
                k.mm(out, sin_bf[0][es, t, pair * 128:(pair + 1) * 128], qd[0][es, pair, ts_], start=False, stop=False)
                k.mm(out, sin_bf[1][es, t, pair * 128:(pair + 1) * 128], qd[1][es, pair, ts_], start=False, stop=True)
            for e in range(2):
                k.cp(og[:, e::2, ts_], pos_[e][:, 0:256].re("p (h q) -> p h q", h=2), E=(k.act if (t + e) % 2 else k.dve))
        k.pop()
        self.dbg("gla_og", og[:], [128, 4, TOK])
        k.pop()
        gs = k.sb("gla_gs", [128, 4, TOK], BF16)
        yT = k.sb("gla_yT", [128, 4, TOK], BF16)
        for s in range(2):
            wb = self.ws.next()
            for m in range(2):
                for h in range(2):
                    pb = self.proj_fm(wb, m * 128, 128, h)
                    k.actf(gs[:, s * 2 + m, h * 512:(h + 1) * 512], pb[:], AF.Silu)
        k.push()
        sq = [k.sb(f"gla_sq{j}", [128, TOK], BF16) for j in range(2)]
        rstd = [k.sb(f"gla_rstd{j}", [128, TOK], F32) for j in range(2)]
        tmp = [k.sb(f"gla_tmp{j}", [128, TOK], F32) for j in range(2)]
        for h in range(4):
            sqh, rs, tp = sq[h % 2], rstd[h % 2], tmp[h % 2]
            k.actf(sqh[:], og[:, h, :], AF.Square)
            for hf in range(2):
                pb = self.bank()
                k.mm(pb[:], self.ones_bf[:], sqh[:, hf * 512:(hf + 1) * 512])
                k.actf(rs[:, hf * 512:(hf + 1) * 512], pb[:], AF.Ln, scale=1.0 / 128, bias=self.eps6)
            k.actf(rs[:], rs[:], AF.Exp, scale=-0.5)
            k.stt(tp[:], og[:, h, :], self.pp(l, "gla_norm"), rs[:], ALU.mult, ALU.mult)
            k.tt(yT[:, h, :], tp[:], gs[:, h, :], ALU.mult)
        k.pop()
        self.dbg("gla_yT", yT[:], [128, 4, TOK])
        self.branch_out(l, yT, 1)
        k.pop()

    ALPHA = float(np.exp(-0.5))

    def plan_rw(self, l):
        S = self.SEC
        for nm in ("rr", "rv"):
            for c0 in (0, 256):
                self.win_slab(l, S[nm] + c0, 256)
        self.win_slab(l, S["wlr"], 192)
        self.win_slab(l, S["glr"], 128)
        for c0 in (0, 256):
            self.win_slab(l, S["rk"] + c0, 256)
        self.plan_branch_out(l, "w_rw_o", 2)

    def rw_branch(self, l):
        k = self.k
        fl = self.flags
        A = self.ALPHA
        k.push()
        OT = k.sb("rw_OT", [128, 4, TOK], BF16)
        bonus = k.sb("rw_bonus", [128, 4, TOK], BF16)
        sgl = k.sb("rw_sgl", [128, TOK], BF16)
        k.push()
        rT = k.sb("rw_rT", [128, 4, TOK], BF16)
        kapT = k.sb("rw_kapT", [128, 4, TOK], BF16)
        kpT = k.sb("rw_kpT", [128, 4, TOK], BF16)
        nbT = k.sb("rw_nbT", [128, 4, TOK], BF16)
        thT = k.sb("rw_thT", [128, TOK], BF16)
        vtok = k.sb("rw_vtok", [128, NT, 512], BF16)
        w2bf = k.sb("rw_w2bf", [128, 512], BF16)
        k.push()
        a2bf = k.sb("rw_a2bf", [64, 512], BF16)
        k.push()
        w2f = k.sb("rw_w2f", [128, 512], F32)
        a2f = k.sb("rw_a2f", [64, 512], F32)
        k.dma(k.sp, w2f[:], self.rw_w2_d[l].re("d r c -> (d r) c"))
        k.dma(k.sp, a2f[:], self.rw_a2_d[l])
        k.cp(w2bf[:], w2f[:], E=k.pool)
        k.cp(a2bf[:], a2f[:], E=k.pool)
        k.pop()
        kT = k.sb("rw_kT", [128, 2, TOK], BF16)
        vT = k.sb("rw_vT", [128, 4, TOK], BF16)
        aT = k.sb("rw_aT", [128, TOK], BF16)
        wl = k.sb("rw_wl", [128, TOK], BF16)
        alT = k.sb("rw_alT", [64, TOK], BF16)
        glT = k.sb("rw_glT", [128, TOK], BF16)
        hb = k.sb("rw_hb", [128, 4, 258], BF16)
        mt = k.sb("rw_mt", [128, 4, 256], BF16)
        omu = k.sb("rw_omu", [128, 15], F32)
        hmu = k.sb("rw_hmu", [128, 15], F32)
        omka = k.sb("rw_omka", [128, 4], F32)
        k.ts(omu[:], self.pp(l, "rw_mu"), -1.0, ALU.mult, 1.0, ALU.add)
        k.ts(hmu[:], self.pp(l, "rw_mu"), 0.5, ALU.mult)
        k.ts(omka[:], self.pp(l, "rw_ka"), -1.0, ALU.mult, 1.0, ALU.add)
        k.memset(hb[:, :, 0:1], 0.0, E=k.pool)
        k.memset(hb[:, :, 257:258], 0.0, E=k.pool)

        def mix(wb, m0, mc, mucol, dst):
            b = hb
            t = mt
            for h in range(2):
                pb = self.proj_fm(wb, m0, mc, h)
                k.cp(b[0:mc, 2 * h:2 * h + 2, 1:257], pb[0:mc, :].re("p (q t) -> p q t", q=2),
                     E=(k.act if h == 0 else k.dve))
            k.tt(b[0:mc, 1:4, 0], b[0:mc, 0:3, 256], V(fl, fl.h[0:mc, 2:8:2]), ALU.mult)
            k.tt(b[0:mc, 0:3, 257], b[0:mc, 1:4, 1], V(fl, fl.h[0:mc, 9:15:2]), ALU.mult)
            k.tt(t[0:mc], b[0:mc, :, 0:256], b[0:mc, :, 2:258], ALU.add)
            k.ts(t[0:mc], t[0:mc], hmu[0:mc, mucol:mucol + 1], ALU.mult)
            k.stt(dst.re("p (q t) -> p q t", q=4), b[0:mc, :, 1:257], omu[0:mc, mucol:mucol + 1], t[0:mc],
                  ALU.mult, ALU.add)

        for bi, dstT in ((0, rT), (2, vT)):
            for s_ in range(2):
                wb = self.ws.next()
                for m in range(2):
                    c = s_ * 2 + m
                    mix(wb, m * 128, 128, bi * 4 + c, dstT[:, c, :])
        wb = self.ws.next()
        mix(wb, 0, 128, 12, wl[:, :])
        mix(wb, 128, 64, 13, alT[:, :])
        wb = self.ws.next()
        mix(wb, 0, 128, 14, glT[:, :])
        k.actf(thT[:], wl[:], AF.Tanh)
        k.actf(sgl[:], glT[:], AF.Sigmoid)
        f1 = k.sb("rw_f1", [128, TOK], F32)
        f2 = k.sb("rw_f2", [128, TOK], F32)
        b1 = k.sb("rw_b1", [128, TOK], BF16)
        for s_ in range(2):
            wb = self.ws.next()
            for m in range(2):
                mix(wb, m * 128, 128, 4 + s_ * 2 + m, kT[:, m, :])
            for m in range(2):
                c = s_ * 2 + m
                for h in range(2):
                    pb = self.bank()
                    k.mm(pb[:], a2bf[:, c * 128:(c + 1) * 128], alT[:, h * 512:(h + 1) * 512])
                    k.actf(aT[:, h * 512:(h + 1) * 512], pb[:], AF.Sigmoid, bias=self.pp(l, "rw_a0", c, c + 1))
                k.ts(f1[:], kT[:, m, :], self.pp(l, "rw_kk", c, c + 1), ALU.mult)
                k.actf(b1[:], f1[:], AF.Square)
                for h in range(2):
                    pb = self.bank()
                    k.mm(pb[:], self.blockones[:], b1[:, h * 512:(h + 1) * 512])
                    k.ts(f2[:, h * 512:(h + 1) * 512], pb[:], 1e-24, ALU.max)
                k.actf(f2[:], f2[:], AF.Sqrt)
                k.recip(f2[:], f2[:])
                k.tt(kapT[:, c, :], f1[:], f2[:], ALU.mult)
                k.stt(nbT[:, c, :], kapT[:, c, :], -1.0, aT[:], ALU.mult, ALU.mult)
                k.ts(f1[:], aT[:], self.pp(l, "rw_ka", c, c + 1), ALU.mult, omka[:, c:c + 1], ALU.add)
                k.tt(kpT[:, c, :], kT[:, m, :], f1[:], ALU.mult)
                k.stt(b1[:], rT[:, c, :], self.pp(l, "rw_rk", c, c + 1), kpT[:, c, :], ALU.mult, ALU.mult)
                for h in range(2):
                    pb = self.bank()
                    k.mm(pb[:], self.blockones[:], b1[:, h * 512:(h + 1) * 512])
                    k.tt(bonus[:, c, h * 512:(h + 1) * 512], pb[:], vT[:, c, h * 512:(h + 1) * 512], ALU.mult)
        for t in range(NT):
            pb = self.bank()
            pv = pb[:].bitcast(BF16)
            for c in range(4):
                k.tr(pv[:, c * 128:(c + 1) * 128], vT[:, c, t * 128:(t + 1) * 128], self.ident_bf[:], inc=(c == 3))
            k.cp(vtok[:, t, :], pv[:, 0:512], E=(k.act if t % 2 else k.dve))
        k.pop()
        self.dbg("rw_kapT", kapT[:], [128, 4, TOK])
        self.dbg("rw_kpT", kpT[:], [128, 4, TOK])
        self.dbg("rw_rT", rT[:], [128, 4, TOK])
        self.dbg("rw_nbT", nbT[:], [128, 4, TOK])
        k.push()
        sig = k.sb("rw_sig", [128, 4, 128], F32)
        Pc = k.sb("rw_P", [128, 4, 128], F32)
        Cx = k.sb("rw_Cx", [128, 4, 128], F32)
        Ea = k.sb("rw_Ea", [128, 4, 128], BF16)
        Eb = k.sb("rw_Eb", [128, 4, 128], BF16)
        Ec = Ea
        gam = k.sb("rw_gam", [128, 4], F32)
        RKt = k.sb("rw_RKt", [128, 4, 2, 128], BF16)
        kt = k.sb("rw_kt", [128, 4, 128], BF16)
        nbt = k.sb("rw_nbt", [128, 4, 128], BF16)
        tok = k.sb("rw_tok", [128, 3, 512], BF16)
        M1 = k.sb("rw_M1", [128, 8, 2, 128], BF16)
        M2 = k.sb("rw_M2", [128, 8, 2, 128], BF16)
        X0g = [k.sb(f"rw_X0_{g}", [128, 4, 128], BF16) for g in range(2)]
        NCH = 4
        CH = [dict(XM=k.sb(f"rw_XM{q}", [128, 2, 128], BF16),
                   Zs=k.sb(f"rw_Zs{q}", [128, 2, 128], BF16), Ts=k.sb(f"rw_Ts{q}", [128, 2, 128], BF16),
                   Tc=[k.sb(f"rw_Tc{q}_{j}", [128, 2, 128], BF16) for j in range(2)]) for q in range(NCH)]
        TT = k.sb("rw_TT", [128, 8, 128], BF16)
        AV = k.sb("rw_AV", [128, 512], BF16)
        U = k.sb("rw_U", [128, 512], BF16)
        WT = k.sb("rw_WT", [128, 4, 128], BF16)
        Et = U
        S = k.sb("rw_S", [128, 256], F32)
        Sin = k.sb("rw_Sin", [128, 256], F32)
        Sbf = k.sb("rw_Sbf", [128, 256], BF16)
        nev = [0]

        def evac(dst, src):
            E = k.act if nev[0] % 2 else k.dve
            nev[0] += 1
            k.cp(dst, src, E=E)

        for d in range(2):
            m2 = self.mask2[d]
            mx = self.maskSL if d == 0 else self.maskSU
            k.dma(k.sp, Sin[:], self.st_rw_d[l, d])
            for step in range(NT):
                t = step if d == 0 else NT - 1 - step
                ts_ = slice(t * 128, (t + 1) * 128)
                ds_ = slice(d * 64, (d + 1) * 64)
                pb = self.bank()
                for c in range(4):
                    k.mm(pb[:, c * 128:(c + 1) * 128], w2bf[ds_, c * 128:(c + 1) * 128], thT[ds_, ts_])
                for c in range(4):
                    k.actf(sig[:, c, :], pb[:, c * 128:(c + 1) * 128], AF.Sigmoid,
                           bias=self.pp(l, "rw_w0", d * 4 + c, d * 4 + c + 1))
                for c in range(4):
                    k.scan(Pc[:, c, :], self.ones_f[:], sig[:, c, :], 0.0, ALU.mult, ALU.add)
                if d == 0:
                    k.tt(Cx[:], Pc[:], sig[:], ALU.subtract)
                    cin, cex = Pc, Cx
                    k.actf(gam[:], Pc[:, :, 127], AF.Exp, scale=-A)
                else:
                    k.actf(gam[:], Pc[:, :, 127], AF.Exp, scale=-A)
                    k.tt(Cx[:], V(Pc, Pc.h[:, :, 127:128].to_broadcast([128, 4, 128])), Pc[:], ALU.subtract)
                    k.tt(Pc[:], Cx[:], sig[:], ALU.add)
                    cin, cex = Pc, Cx
                k.actf(Ea[:], cin[:], AF.Exp, scale=-A)
                k.actf(Eb[:], cex[:], AF.Exp, scale=-A)
                k.tt(RKt[:, :, 0, :], rT[:, :, ts_], Ea[:], ALU.mult)
                k.actf(Ec[:], cin[:], AF.Exp, scale=A)
                k.tt(RKt[:, :, 1, :], kapT[:, :, ts_], Eb[:], ALU.mult)
                k.tt(kt[:], kpT[:, :, ts_], Ec[:], ALU.mult)
                k.tt(nbt[:], nbT[:, :, ts_], Ec[:], ALU.mult)
                pb = self.bank()
                pv = pb[:].bitcast(BF16)
                for c in range(4):
                    k.tr(pv[:, c * 128:(c + 1) * 128], RKt[:, c, 1, :], self.ident_bf[:], inc=False)
                for c in range(4):
                    k.tr(pv[:, 512 + c * 128:512 + (c + 1) * 128], kt[:, c, :], self.ident_bf[:], inc=(c == 3))
                evac(tok[:, 0:2, :], pv[:, :].re("p (a q) -> p a q", a=2))
                pb = self.bank()
                pv = pb[:].bitcast(BF16)
                for c in range(4):
                    k.tr(pv[:, c * 128:(c + 1) * 128], nbt[:, c, :], self.ident_bf[:], inc=(c == 3))
                evac(tok[:, 2, :], pv[:, 0:512])
                lmx = self.lvlmask[d]
                l1t = self.lvlmask[1 - d]
                Tcs = [None] * 4
                for gq in range(2):
                    bA = [self.bank(), self.bank()]
                    bB = [self.bank(), self.bank()]
                    b5 = [self.bank(), self.bank()]
                    for hh in range(4):
                        h = gq * 4 + hh
                        c, e = h // 2, h % 2
                        cc = hh // 2
                        es = slice(e * 64, (e + 1) * 64)
                        cs = slice(cc * 256, cc * 256 + 256)
                        k.mm(bA[e][:, cs], kt[es, c, :], RKt[es, c, :, :])
                        k.mm(bB[e][:, cs], nbt[es, c, :], RKt[es, c, :, :])
                        k.mm(b5[e][:, cc * 128:(cc + 1) * 128], RKt[es, c, 1, :], nbt[es, c, :])
                    m2v = V(m2, m2.h[:, :, :].unsqueeze(1).to_broadcast([128, 2, 2, 128]))
                    X0 = X0g[gq]
                    for e in range(2):
                        hs = slice(gq * 4 + e, gq * 4 + 4, 2)
                        k.tt(M1[:, hs, :, :], bA[e][:].re("p (h a t) -> p h a t", h=2, a=2), m2v, ALU.mult)
                        k.tt(M2[:, hs, :, :], bB[e][:].re("p (h a t) -> p h a t", h=2, a=2), m2v, ALU.mult)
                        k.tt(X0[:, e::2, :], b5[e][:, 0:256].re("p (h t) -> p h t", h=2),
                             V(mx, mx.h[:, :].unsqueeze(1).to_broadcast([128, 2, 128])), ALU.mult)
                    for q2 in range(2):
                        q = gq * 2 + q2
                        Y0 = M2[:, q * 2:q * 2 + 2, 1, :]
                        Tc = CH[q]["Tc"][0]
                        k.tt(Tc[:], Y0, V(l1t, l1t.h[:, 0, :].unsqueeze(1).to_broadcast([128, 2, 128])), ALU.mult)
                        k.tt(Tc[:], Tc[:],
                             V(self.ident_bf, self.ident_bf.h[:, :].unsqueeze(1).to_broadcast([128, 2, 128])), ALU.add)
                        Tcs[q] = Tc
                for lvl in range(1, 7):
                    bzs, bts, bps = [], [], []
                    for q in range(NCH):
                        k.tt(CH[q]["XM"][:], X0g[q // 2][:, (q % 2) * 2:(q % 2) * 2 + 2, :],
                             V(lmx, lmx.h[:, lvl, :].unsqueeze(1).to_broadcast([128, 2, 128])), ALU.mult, E=k.pool)
                    for q in range(NCH):
                        Tc = Tcs[q]
                        bz = self.bank()
                        for hh in range(2):
                            k.mm(bz[:, hh * 128:(hh + 1) * 128], CH[q]["XM"][:, hh, :], Tc[:, hh, :])
                        btv = bz[:, 256:512].bitcast(BF16)
                        for hh in range(2):
                            k.tr(btv[:, hh * 128:(hh + 1) * 128], Tc[:, hh, :], self.ident_bf[:], inc=(hh == 1))
                        bzs.append(bz)
                        bts.append(btv)
                    for q in range(NCH):
                        k.cp(CH[q]["Zs"][:], bzs[q][:, 0:256].re("p (h t) -> p h t", h=2), E=(k.dve if q % 2 == 0 else k.act))
                        k.cp(CH[q]["Ts"][:], bts[q][:, 0:256].re("p (h t) -> p h t", h=2), E=(k.dve if q % 2 == 0 else k.act))
                    for q in range(NCH):
                        Tc = Tcs[q]
                        bp = self.bank()
                        for hh in range(2):
                            o_ = bp[:, hh * 128:(hh + 1) * 128]
                            k.mm(o_, self.ident_bf[:], Tc[:, hh, :], start=True, stop=False)
                            k.mm(o_, CH[q]["Ts"][:, hh, :], CH[q]["Zs"][:, hh, :], start=False, stop=True)
                        bps.append(bp)
                    for q in range(NCH):
                        E_ = k.dve if q % 2 == 0 else k.act
                        if lvl < 6:
                            Tn = CH[q]["Tc"][lvl % 2]
                            k.cp(Tn[:], bps[q][:, 0:256].re("p (h t) -> p h t", h=2), E=E_)
                            Tcs[q] = Tn
                        else:
                            k.cp(TT[:, q * 2:q * 2 + 2, :], bps[q][:, 0:256].re("p (h t) -> p h t", h=2), E=E_)
                pb = self.bank()
                for h in range(8):
                    k.mm(pb[:, h * 64:(h + 1) * 64], M1[:, h, 1, :], vtok[:, t, h * 64:(h + 1) * 64])
                evac(AV[:], pb[:])
                pb = self.bank()
                for h in range(8):
                    k.mm(pb[:, h * 64:(h + 1) * 64], TT[:, h, :], AV[:, h * 64:(h + 1) * 64])
                evac(U[:], pb[:])
                pb = self.bank()
                for h in range(8):
                    c, e = h // 2, h % 2
                    k.mm(pb[e * 64:(e + 1) * 64, c * 128:(c + 1) * 128], tok[:, 0, h * 64:(h + 1) * 64], TT[:, h, :])
                evac(WT[:], pb[:].re("p (c t) -> p c t", c=4))
                if step > 0:
                    fcol = t if d == 0 else 8 + t
                    k.ts(Sin[:], S[:], fl[:, fcol:fcol + 1], ALU.mult)
                k.cp(Sbf[:], Sin[:], E=k.act)
                pbe = [self.bank(), self.bank()]
                for h in range(8):
                    c, e = h // 2, h % 2
                    es = slice(e * 64, (e + 1) * 64)
                    k.mm(pbe[e][:, c * 64:(c + 1) * 64], WT[es, c, :], Sbf[es, c * 64:(c + 1) * 64])
                for e in range(2):
                    k.tt(Et[:].re("p (c e v) -> p c e v", c=4, e=2)[:, :, e, :],
                         pbe[e][:, 0:256].re("p (c v) -> p c v", c=4),
                         U[:].re("p (c e v) -> p c e v", c=4, e=2)[:, :, e, :], ALU.add)
                if self.stop == "R1":
                    for nm, tl_, shp in (("rwd_tok", tok, [128, 3, 512]), ("rwd_E", Et, [128, 512]), ("rwd_U", U, [128, 512]),
                                         ("rwd_TT", TT, [128, 8, 128]), ("rwd_M1", M1, [128, 8, 2, 128]),
                                         ("rwd_M2", M2, [128, 8, 2, 128]), ("rwd_RKt", RKt, [128, 4, 2, 128]),
                                         ("rwd_kt", kt, [128, 4, 128]), ("rwd_nbt", nbt, [128, 4, 128]),
                                         ("rwd_sig", sig, [128, 4, 128]), ("rwd_P", Pc, [128, 4, 128]),
                                         ("rwd_WT", WT, [128, 4, 128]), ("rwd_AV", AV, [128, 512])):
                        self.debug.add(nm)
                        self.dbg(nm, tl_[:], shp)
                    self.chk("R1")
                pos_ = [self.bank(), self.bank()]
                for h in range(8):
                    c, e = h // 2, h % 2
                    es = slice(e * 64, (e + 1) * 64)
                    o_ = pos_[e][es, c * 128:(c + 1) * 128]
                    k.mm(o_, Sbf[es, c * 64:(c + 1) * 64], RKt[es, c, 0, :], start=True, stop=False)
                    k.mm(o_, vtok[:, t, h * 64:(h + 1) * 64], M1[:, h, 0, :], start=False, stop=False)
                    k.mm(o_, Et[:, h * 64:(h + 1) * 64], M2[:, h, 0, :], start=False, stop=True)
                for e in range(2):
                    es = slice(e * 64, (e + 1) * 64)
                    if d == 0:
                        evac(OT[es, :, ts_], pos_[e][es, :].re("p (c t) -> p c t", c=4))
                    else:
                        k.tt(OT[es, :, ts_], pos_[e][es, :].re("p (c t) -> p c t", c=4), OT[es, :, ts_], ALU.add)
                pb = self.bank()
                for h in range(8):
                    c, e = h // 2, h % 2
                    o_ = pb[e * 64:(e + 1) * 64, c * 64:(c + 1) * 64]
                    k.mm(o_, tok[:, 1, h * 64:(h + 1) * 64], vtok[:, t, h * 64:(h + 1) * 64], start=True, stop=False)
                    k.mm(o_, tok[:, 2, h * 64:(h + 1) * 64], Et[:, h * 64:(h + 1) * 64], start=False, stop=True)
                k.tt(S[:], Sin[:], pb[:, 0:256], ALU.add)
                k.tt(S[:].re("p (c v) -> p c v", c=4), S[:].re("p (c v) -> p c v", c=4),
                     V(gam, gam.h[:, :].unsqueeze(2).to_broadcast([128, 4, 64])), ALU.mult)
                if (d == 0 and t % 2 == 1) or (d == 1 and t % 2 == 0):
                    k.dma(k.sp, self.ns_rw_d[l, t // 2, d], S[:])
        k.pop()
        k.pop()
        self.dbg("rw_OT", OT[:], [128, 4, TOK])
        yT = k.sb("rw_yT", [128, 4, TOK], BF16)
        g2f = k.sb("rw_g2f", [128, 512], F32)
        g2bf = k.sb("rw_g2bf", [128, 512], BF16)
        k.dma(k.sp, g2f[:], self.rw_g2_d[l])
        k.cp(g2bf[:], g2f[:], E=k.pool)
        k.push()
        dd = k.sb("rw_dd", [128, TOK], F32)
        sq = k.sb("rw_sq", [128, TOK], BF16)
        rs = k.sb("rw_rs", [128, TOK], F32)
        for c in range(4):
            for h in range(2):
                hs = slice(h * 512, (h + 1) * 512)
                pb = self.bank()
                k.mm(pb[:], self.blockmean[:], OT[:, c, hs])
                k.tt(dd[:, hs], OT[:, c, hs], pb[:], ALU.subtract)
            k.actf(sq[:], dd[:], AF.Square)
            for h in range(2):
                hs = slice(h * 512, (h + 1) * 512)
                pb = self.bank()
                k.mm(pb[:], self.blockmean[:], sq[:, hs])
                k.actf(rs[:, hs], pb[:], AF.Ln, bias=self.epsgn)
            k.actf(rs[:], rs[:], AF.Exp, scale=-0.5)
            k.tt(dd[:], dd[:], rs[:], ALU.mult)
            k.ts(dd[:], dd[:], self.pp(l, "rw_ln_w", c, c + 1), ALU.mult, self.pp(l, "rw_ln_b", c, c + 1), ALU.add)
            k.tt(dd[:], dd[:], bonus[:, c, :], ALU.add)
            for h in range(2):
                hs = slice(h * 512, (h + 1) * 512)
                pb = self.bank()
                k.mm(pb[:], g2bf[:, c * 128:(c + 1) * 128], sgl[:, hs])
                k.tt(yT[:, c, hs], pb[:], dd[:, hs], ALU.mult)
        k.pop()
        self.dbg("rw_yT", yT[:], [128, 4, TOK])
        self.branch_out(l, yT, 2)
        k.pop()

    def group_rmsnorm(self, src, dst, nchunk, inv_n, eps, gains):
        k = self.k
        k.push()
        sq = k.sb("grn_sq", [128, nchunk, TOK], BF16)
        rstd = k.sb("grn_rstd", [128, TOK], F32)
        for c in range(nchunk):
            k.actf(sq[:, c, :], src[:, c, :], AF.Square)
        for h in range(2):
            pb = self.bank()
            for c in range(nchunk):
                k.mm(pb[:], self.ones_bf[:], sq[:, c, h * 512:(h + 1) * 512], start=(c == 0), stop=(c == nchunk - 1))
            k.actf(rstd[:, h * 512:(h + 1) * 512], pb[:], AF.Ln, scale=inv_n, bias=eps)
        k.actf(rstd[:], rstd[:], AF.Exp, scale=-0.5)
        for c in range(nchunk):
            k.stt(dst[:, c, :], src[:, c, :], gains[c], rstd[:], ALU.mult, ALU.mult)
        k.pop()


_PROG = {}


def get_prog(debug=()):
    key = tuple(sorted(debug))
    if key not in _PROG:
        p1 = Prog(debug)
        needed = p1.k.needed
        if os.environ.get("KALLINC", ""):
            needed = {E.name: set(range(1, E.cnt + 2)) for E in p1.k.engs}
        _PROG[key] = Prog(debug, needed)
    return _PROG[key]


def make_in_maps(inp):
    inp = {k_: np.asarray(v) for k_, v in inp.items()}
    pp = np.stack([pack_params(inp, l) for l in range(DEPTH)], axis=0)
    gp = _cm(inp["final_norm"])
    ti = np.arange(128)[:, None]
    si = np.arange(128)[None, :]
    lm = np.zeros((2, 128, 7, 128), np.float32)
    for lvl in range(7):
        m_ = 1 << lvl
        msk = ((ti // (2 * m_)) == (si // (2 * m_))) & ((ti % (2 * m_)) >= m_) & ((si % (2 * m_)) < m_)
        lm[0, :, lvl, :] = msk
        lm[1, :, lvl, :] = msk.T
    shared = {"pp": pp, "gp": gp, "lvlmask": lm}
    shared["gla_gk_w"] = np.ascontiguousarray(inp["gla_gk_w"], dtype=np.float32)
    for nm in ("rw_w2", "rw_a2", "rw_g2"):
        shared[nm] = np.ascontiguousarray(inp[nm], dtype=np.float32)
    for name in ["w_ada", "ffn_gate", "ffn_up", "ffn_down", "w_in", "w_ssd_o", "w_gla_o", "w_rw_o", "w_out"]:
        shared[name] = np.ascontiguousarray(inp[name], dtype=np.float32)
    maps = []
    for core in range(8):
        m = dict(shared)
        flags = np.zeros((128, 32), np.float32)
        if core < 4:
            x = inp["x_prompt"][4 * core:4 * core + 4].reshape(TOK, D)
            cond = inp["c_ctx"]
            cf = np.array([0, 1, 0, 1, 0, 1, 0, 1], np.float32)
            cb = np.array([1, 0, 1, 0, 1, 0, 1, 0], np.float32)
            posf = 0.0
        else:
            x = inp["x_sample"][core - 4]
            cond = inp["c"][core - 4]
            cf = np.array([0, 1, 1, 1, 1, 1, 1, 1], np.float32)
            cb = np.array([1, 1, 1, 1, 1, 1, 1, 0], np.float32)
            posf = 1.0
        flags[:, 0:8] = cf[None]
        flags[:, 8:16] = cb[None]
        flags[:, 16] = posf
        if core < 4:
            st_ssd = np.zeros((DEPTH, 2, 128, 256), np.float32)
        else:
            ss = inp["state_ssd"][core - 4]
            st_ssd = np.ascontiguousarray(
                ss.reshape(DEPTH, 2, 2, 4, 64, 64).transpose(0, 1, 2, 5, 3, 4).reshape(DEPTH, 2, 128, 256))
        m["st_ssd"] = st_ssd
        if core < 4:
            st_gla = np.zeros((DEPTH, 2, 128, 256), np.float32)
        else:
            sg = inp["state_gla"][core - 4]
            st_gla = np.ascontiguousarray(
                sg.reshape(DEPTH, 2, 2, 2, 64, 128).transpose(0, 1, 3, 4, 2, 5).reshape(DEPTH, 2, 128, 256))
        m["st_gla"] = st_gla
        if core < 4:
            st_rw = np.zeros((DEPTH, 2, 128, 256), np.float32)
        else:
            sr = inp["state_rwkv"][core - 4]
            st_rw = np.ascontiguousarray(
                sr.reshape(DEPTH, 2, 4, 2, 64, 64).transpose(0, 1, 3, 5, 2, 4).reshape(DEPTH, 2, 128, 256))
        m["st_rw"] = st_rw
        m["xT"] = np.ascontiguousarray(x.T, dtype=np.float32)
        m["cond"] = _cm(cond)
        m["flags"] = flags
        maps.append(m)
    return maps


def run(inp, debug=(), trace=False):
    prog = get_prog(debug)
    maps = make_in_maps(inp)
    res = run_bass_kernel_spmd(prog.k.nc, maps, core_ids=list(range(8)), trace=trace)
    return prog, res


def kernel(**inputs):
    prog, res = run(inputs)
    r = res.results
    y_prompt = np.zeros((16, 256, D), np.float32)
    y_sample = np.zeros((4, 1024, D), np.float32)
    for core in range(8):
        y = np.ascontiguousarray(r[core]["yT"].T)
        if core < 4:
            y_prompt[4 * core:4 * core + 4] = y.reshape(4, 256, D)
        else:
            y_sample[core - 4] = y
    ns_ssd = np.zeros((16, DEPTH, 2, 8, 64, 64), np.float32)
    for core in range(4):
        raw = r[core]["ns_ssd"]
        v = raw.reshape(DEPTH, 4, 2, 2, 64, 4, 64).transpose(1, 0, 2, 3, 5, 6, 4)
        ns_ssd[4 * core:4 * core + 4] = v.reshape(4, DEPTH, 2, 8, 64, 64)
    ns_gla = np.zeros((16, DEPTH, 2, 4, 64, 128), np.float32)
    for core in range(4):
        raw = r[core]["ns_gla"]
        v = raw.reshape(DEPTH, 4, 2, 2, 64, 2, 128).transpose(1, 0, 2, 5, 3, 4, 6)
        ns_gla[4 * core:4 * core + 4] = v.reshape(4, DEPTH, 2, 4, 64, 128)
    ns_rw = np.zeros((16, DEPTH, 2, 8, 64, 64), np.float32)
    for core in range(4):
        raw = r[core]["ns_rw"]
        v = raw.reshape(DEPTH, 4, 2, 2, 64, 4, 64).transpose(1, 0, 2, 5, 3, 6, 4)
        ns_rw[4 * core:4 * core + 4] = v.reshape(4, DEPTH, 2, 8, 64, 64)
    return (y_prompt, y_sample, ns_ssd, ns_gla, ns_rw)
```
